# Optimizing a Trainium2 kernel written in Bass

```python
import math
import jax, jax.numpy as jnp
from jax import lax
import numpy as np

D_MODEL = 1024
BATCH = 32
SEQ = 256
DEPTH = 2
DEC_BATCH = 8
DEC_SEQ = 4096
PAST_LEN = 256

GRID_W = 64
POS_BASE = 10000.0
D_MIX = D_MODEL
W_GROUP = D_MIX // 4
N_DIR = 2
M_HEADS = 4
M_HD = W_GROUP // M_HEADS
M_CHUNK = 64
POOL_WINDOWS = (2, 4, 8, 16)
POOL_GC = W_GROUP // len(POOL_WINDOWS)
G_HEADS = 4
G_HD = W_GROUP // G_HEADS
G_CHUNK = 128
S_HEADS = 4
S_HD = W_GROUP // S_HEADS
S_GROUPS = 2
S_STATE = 128
S_CONV = 3
S_CHUNK = 64
D_FF = 2816
FFN_CONV = 3
ALPHA = (2 * DEPTH) ** 0.25
BETA = (8 * DEPTH) ** -0.25
LN_EPS = 1e-5
M_COLS = 4 * W_GROUP + 2 * N_DIR * M_HEADS
P_COLS = W_GROUP
G_COLS = 2 * W_GROUP
S_XBC = W_GROUP + 2 * S_GROUPS * S_STATE
S_COLS = W_GROUP + S_XBC + N_DIR * S_HEADS
D_IN = M_COLS + P_COLS + G_COLS + S_COLS

kernel_name = 'hybrid_mlstm_pool_gmlp_ssd_diffusion_step'


def layer_norm(x, g=None, b=None):
    xf = x.astype(jnp.float32)
    mu = jnp.mean(xf, -1, keepdims=True)
    var = jnp.mean(jnp.square(xf - mu), -1, keepdims=True)
    y = (xf - mu) * lax.rsqrt(var + LN_EPS)
    if g is not None:
        y = y * g + b
    return y.astype(x.dtype)


def modulate(x, shift, scale):
    return layer_norm(x) * (1 + scale) + shift


def dwconv_centered(x, w, b):
    k = w.shape[0]
    pad = k // 2
    y = lax.conv_general_dilated(x, w[:, None, :], window_strides=(1,), padding=[(pad, pad)],
                                 dimension_numbers=('NWC', 'WIO', 'NWC'), feature_group_count=x.shape[-1])
    return y + b


def grid_pos_embed(rows, dim):
    f32 = jnp.float32
    quarter = dim // 4
    freq = 1.0 / (POS_BASE ** (jnp.arange(quarter, dtype=f32) / quarter))
    r = jnp.repeat(jnp.arange(rows, dtype=f32), GRID_W)
    col = jnp.tile(jnp.arange(GRID_W, dtype=f32), rows)
    def enc(pos):
        ang = pos[:, None] * freq[None, :]
        return jnp.concatenate([jnp.sin(ang), jnp.cos(ang)], -1)
    return jnp.concatenate([enc(r), enc(col)], -1)


def to_chunks(a, n_chunks, size):
    return jnp.moveaxis(a.reshape(a.shape[0], a.shape[1], n_chunks, size, *a.shape[3:]), 2, 0)


def mlstm_scan(q, k, v, ig, lf, c0, n0, m0):
    f32 = jnp.float32
    B, H, T, D = q.shape
    L = M_CHUNK
    nc = T // L
    causal = jnp.tril(jnp.ones((L, L), dtype=bool))

    def step(carry, inp):
        c, n, m = carry
        qc, kc, vc, ic, fc = inp
        b = jnp.cumsum(fc, -1)
        dmat = jnp.where(causal, b[..., :, None] - b[..., None, :] + ic[..., None, :], -jnp.inf)
        inter = b + m[..., None]
        m_q = jnp.maximum(inter, jnp.max(dmat, -1))
        w_inter = jnp.exp(inter - m_q)
        s = jnp.einsum('bhtd,bhsd->bhts', qc, kc) * jnp.exp(dmat - m_q[..., None])
        num = jnp.einsum('bhts,bhse->bhte', s, vc) + w_inter[..., None] * jnp.einsum('bhtd,bhde->bhte', qc, c)
        den = jnp.sum(s, -1) + w_inter * jnp.einsum('bhtd,bhd->bht', qc, n)
        h = num / jnp.maximum(jnp.abs(den), jnp.exp(-m_q))[..., None]
        b_last = b[..., -1]
        dec_s = b_last[..., None] - b + ic
        m_new = jnp.maximum(b_last + m, jnp.max(dec_s, -1))
        w_s = jnp.exp(dec_s - m_new[..., None])
        w_c = jnp.exp(b_last + m - m_new)
        c_new = w_c[..., None, None] * c + jnp.einsum('bhs,bhsd,bhse->bhde', w_s, kc, vc)
        n_new = w_c[..., None] * n + jnp.einsum('bhs,bhsd->bhd', w_s, kc)
        return (c_new, n_new, m_new), h

    init = (c0.astype(f32), n0.astype(f32), m0.astype(f32))
    xs = tuple(to_chunks(a, nc, L) for a in (q, k, v, ig, lf))
    (c, n, m), hs = lax.scan(step, init, xs)
    return jnp.moveaxis(hs, 0, 2).reshape(B, H, T, D), c, n, m


def mlstm_mixer(cols, b_i, b_f, norm_g, state):
    f32 = jnp.float32
    B, T, _ = cols.shape
    q, k, v, o, gates = jnp.split(cols, [W_GROUP, 2 * W_GROUP, 3 * W_GROUP, 4 * W_GROUP], axis=-1)
    heads = lambda a: a.reshape(B, T, M_HEADS, M_HD).transpose(0, 2, 1, 3).astype(f32)
    q, k, v = heads(q), heads(k) * (M_HD ** -0.5), heads(v)
    gates = gates.astype(f32).reshape(B, T, 2, N_DIR, M_HEADS)
    ig = (gates[:, :, 0] + b_i).transpose(0, 2, 3, 1)
    lf = jax.nn.log_sigmoid(gates[:, :, 1] + b_f).transpose(0, 2, 3, 1)
    c0, n0, m0 = state
    rev = lambda a: jnp.flip(a, axis=2)
    hf, cf, nf, mf = mlstm_scan(q, k, v, ig[:, 0], lf[:, 0], c0[:, 0], n0[:, 0], m0[:, 0])
    hb, cb, nb, mb = mlstm_scan(rev(q), rev(k), rev(v), rev(ig[:, 1]), rev(lf[:, 1]), c0[:, 1], n0[:, 1], m0[:, 1])
    h = hf + rev(hb)
    h = layer_norm(h).transpose(0, 2, 1, 3).reshape(B, T, W_GROUP) * norm_g
    out = jax.nn.sigmoid(o.astype(f32)) * h
    new_state = (jnp.stack([cf, cb], 1), jnp.stack([nf, nb], 1), jnp.stack([mf, mb], 1))
    return out.astype(cols.dtype), new_state


def pool_mixer(x, w_pool, scale):
    f32 = jnp.float32
    B, T, _ = x.shape
    xg = x.reshape(B, T, len(POOL_WINDOWS), POOL_GC)
    csum = jnp.concatenate([jnp.zeros((B, 1) + xg.shape[2:], f32), jnp.cumsum(xg.astype(f32), axis=1)], axis=1)
    t = jnp.arange(T)
    outs = []
    for g, w in enumerate(POOL_WINDOWS):
        lo = jnp.clip(t - w // 2, 0, T)
        hi = jnp.clip(t - w // 2 + w, 0, T)
        cg = csum[:, :, g]
        mean = (cg[:, hi] - cg[:, lo]) / (hi - lo).astype(f32)[:, None]
        outs.append(mean - xg[:, :, g].astype(f32))
    pooled = jnp.stack(outs, 2).astype(x.dtype)
    y = jnp.einsum('btgc,gcd->btgd', pooled, w_pool).reshape(B, T, W_GROUP)
    return y * scale


def gmlp_mixer(cols, w_s, b_s):
    B, T, _ = cols.shape
    u, v = jnp.split(cols, 2, axis=-1)
    v = layer_norm(v)
    vc = v.reshape(B, T // G_CHUNK, G_CHUNK, G_HEADS, G_HD)
    mixed = jnp.einsum('hts,bcshd->bcthd', w_s, vc) + b_s.T[None, None, :, :, None]
    return u * mixed.reshape(B, T, W_GROUP)


def ssd_scan(x, dt, a, bm, cm, h0):
    B, H, T, P = x.shape
    L = S_CHUNK
    nc = T // L
    causal = jnp.tril(jnp.ones((L, L), dtype=bool))

    def step(h, inp):
        xc, dtc, bc, cc = inp
        lc = jnp.cumsum(dtc * a[:, None], axis=-1)
        decay = jnp.exp(jnp.where(causal, lc[..., :, None] - lc[..., None, :], -jnp.inf))
        scores = jnp.einsum('bhtn,bhsn->bhts', cc, bc) * decay * dtc[..., None, :]
        y = jnp.einsum('bhts,bhsp->bhtp', scores, xc) + jnp.exp(lc)[..., None] * jnp.einsum('bhtn,bhpn->bhtp', cc, h)
        w_end = jnp.exp(lc[..., -1:] - lc) * dtc
        h_new = jnp.exp(lc[..., -1])[..., None, None] * h + jnp.einsum('bhs,bhsp,bhsn->bhpn', w_end, xc, bc)
        return h_new, y

    xs = tuple(to_chunks(arr, nc, L) for arr in (x, dt, bm, cm))
    h, ys = lax.scan(step, h0.astype(jnp.float32), xs)
    return jnp.moveaxis(ys, 0, 2).reshape(B, H, T, P), h


def ssd_mixer(cols, conv_w, conv_b, dt_bias, a_log, d_skip, norm_g, state):
    f32 = jnp.float32
    B, T, _ = cols.shape
    z, xbc, dtr = jnp.split(cols, [W_GROUP, W_GROUP + S_XBC], axis=-1)
    xbc = jax.nn.silu(dwconv_centered(xbc, conv_w, conv_b))
    xs, bm, cm = jnp.split(xbc, [W_GROUP, W_GROUP + S_GROUPS * S_STATE], axis=-1)
    xh = xs.reshape(B, T, S_HEADS, S_HD).transpose(0, 2, 1, 3).astype(f32)
    rep = S_HEADS // S_GROUPS
    grp = lambda arr: jnp.repeat(arr.reshape(B, T, S_GROUPS, S_STATE), rep, axis=2).transpose(0, 2, 1, 3).astype(f32)
    bh, ch = grp(bm), grp(cm)
    dt = jax.nn.softplus(dtr.astype(f32).reshape(B, T, N_DIR, S_HEADS) + dt_bias).transpose(0, 2, 3, 1)
    a = -jnp.exp(a_log.astype(f32))
    rev = lambda arr: jnp.flip(arr, axis=2)
    yf, hf = ssd_scan(xh, dt[:, 0], a[0], bh, ch, state[:, 0])
    yb, hb = ssd_scan(rev(xh), rev(dt[:, 1]), a[1], rev(bh), rev(ch), state[:, 1])
    y = yf + rev(yb) + d_skip[:, None, None] * xh
    y = y.transpose(0, 2, 1, 3).reshape(B, T, W_GROUP) * jax.nn.silu(z.astype(f32))
    yg = y.reshape(B, T, S_GROUPS, W_GROUP // S_GROUPS)
    yg = yg * lax.rsqrt(jnp.mean(jnp.square(yg), -1, keepdims=True) + LN_EPS)
    out = yg.reshape(B, T, W_GROUP) * norm_g
    return out.astype(cols.dtype), jnp.stack([hf, hb], 1)


def conv_ffn(h, w_up, conv_w, conv_b, w_down):
    u = jnp.einsum('btd,df->btf', h, w_up)
    u = dwconv_centered(u, conv_w, conv_b)
    g, val = jnp.split(u, 2, axis=-1)
    return jnp.einsum('btf,fd->btd', jax.nn.silu(g) * val, w_down)


def trunk_layer(x, cond, p, m_state, s_state):
    mod = (jnp.dot(cond, p['w_ada']) + p['b_ada'])[:, None, :]
    sh1, sc1, g1, sh2, sc2, g2 = jnp.split(mod, 6, axis=-1)
    h = modulate(x, sh1, sc1)
    cols = jnp.einsum('btd,de->bte', h, p['w_in'])
    mc, pc, gc, sc = jnp.split(cols, [M_COLS, M_COLS + P_COLS, M_COLS + P_COLS + G_COLS], axis=-1)
    y_m, m_new = mlstm_mixer(mc, p['b_igate'], p['b_fgate'], p['mlstm_norm_g'], m_state)
    y_p = pool_mixer(pc, p['w_pool'], p['pool_scale'])
    y_g = gmlp_mixer(gc, p['w_spatial'], p['b_spatial'])
    y_s, s_new = ssd_mixer(sc, p['ssd_conv_w'], p['ssd_conv_b'], p['ssd_dt_bias'], p['ssd_a_log'],
                           p['ssd_d'], p['ssd_norm_g'], s_state)
    mix = jnp.einsum('bte,ed->btd', jnp.concatenate([y_m, y_p, y_g, y_s], -1), p['w_out'])
    x = layer_norm(ALPHA * x + g1 * mix, p['ln1_g'], p['ln1_b'])
    h = modulate(x, sh2, sc2)
    ffn = conv_ffn(h, p['ffn_w_up'], p['ffn_conv_w'], p['ffn_conv_b'], p['ffn_w_down'])
    x = layer_norm(ALPHA * x + g2 * ffn, p['ln2_g'], p['ln2_b'])
    return x, m_new, s_new


def setup_inputs(seed: int = 0) -> dict:
    key = jax.random.key(seed)
    ks = iter(jax.random.split(key, 40))
    f32 = jnp.float32
    def nrm(shape, s=1.0):
        return s * jax.random.normal(next(ks), shape, f32)
    def gain(shape):
        return 1.0 + nrm(shape, 0.1)
    L = DEPTH
    x_prompt = nrm((BATCH, SEQ, D_MODEL))
    x_sample = nrm((DEC_BATCH, DEC_SEQ, D_MODEL))
    state_mlstm_c = nrm((DEC_BATCH, L, N_DIR, M_HEADS, M_HD, M_HD), 0.5)
    state_mlstm_n = nrm((DEC_BATCH, L, N_DIR, M_HEADS, M_HD), 0.5)
    state_mlstm_m = nrm((DEC_BATCH, L, N_DIR, M_HEADS), 0.5)
    state_ssd = nrm((DEC_BATCH, L, N_DIR, S_HEADS, S_HD, S_STATE), 0.5)
    c = nrm((DEC_BATCH, D_MODEL))
    c_ctx = nrm((D_MODEL,))
    w_ada = nrm((L, D_MODEL, 6 * D_MODEL), D_MODEL ** -0.5)
    b_ada = nrm((L, 6 * D_MODEL), 0.02)
    w_in = nrm((L, D_MODEL, D_IN), D_MODEL ** -0.5)
    b_igate = nrm((L, N_DIR, M_HEADS), 0.1)
    b_fgate = jnp.linspace(3.0, 6.0, M_HEADS, dtype=f32) + nrm((L, N_DIR, M_HEADS), 0.1)
    mlstm_norm_g = gain((L, W_GROUP))
    w_pool = nrm((L, len(POOL_WINDOWS), POOL_GC, POOL_GC), POOL_GC ** -0.5)
    pool_scale = gain((L, W_GROUP))
    w_spatial = nrm((L, G_HEADS, G_CHUNK, G_CHUNK), G_CHUNK ** -0.5)
    b_spatial = gain((L, G_HEADS, G_CHUNK))
    ssd_conv_w = nrm((L, S_CONV, S_XBC), S_CONV ** -0.5)
    ssd_conv_b = nrm((L, S_XBC), 0.02)
    dt0 = jnp.exp(jax.random.uniform(next(ks), (L, N_DIR, S_HEADS), f32, math.log(1e-3), math.log(1e-1)))
    ssd_dt_bias = dt0 + jnp.log(-jnp.expm1(-dt0))
    ssd_a_log = jnp.log(jax.random.uniform(next(ks), (L, N_DIR, S_HEADS), f32, 1.0, 16.0))
    ssd_d = gain((L, S_HEADS))
    ssd_norm_g = gain((L, W_GROUP))
    w_out = nrm((L, D_MIX, D_MODEL), D_MIX ** -0.5 * BETA)
    ln1_g = gain((L, D_MODEL))
    ln1_b = nrm((L, D_MODEL), 0.02)
    ffn_w_up = nrm((L, D_MODEL, 2 * D_FF), D_MODEL ** -0.5)
    ffn_conv_w = nrm((L, FFN_CONV, 2 * D_FF), FFN_CONV ** -0.5)
    ffn_conv_b = nrm((L, 2 * D_FF), 0.02)
    ffn_w_down = nrm((L, D_FF, D_MODEL), D_FF ** -0.5 * BETA)
    ln2_g = gain((L, D_MODEL))
    ln2_b = nrm((L, D_MODEL), 0.02)
    return {'x_prompt': x_prompt, 'x_sample': x_sample, 'state_mlstm_c': state_mlstm_c,
            'state_mlstm_n': state_mlstm_n, 'state_mlstm_m': state_mlstm_m, 'state_ssd': state_ssd,
            'c': c, 'c_ctx': c_ctx, 'w_ada': w_ada, 'b_ada': b_ada, 'w_in': w_in, 'b_igate': b_igate,
            'b_fgate': b_fgate, 'mlstm_norm_g': mlstm_norm_g, 'w_pool': w_pool, 'pool_scale': pool_scale,
            'w_spatial': w_spatial, 'b_spatial': b_spatial, 'ssd_conv_w': ssd_conv_w, 'ssd_conv_b': ssd_conv_b,
            'ssd_dt_bias': ssd_dt_bias, 'ssd_a_log': ssd_a_log, 'ssd_d': ssd_d, 'ssd_norm_g': ssd_norm_g,
            'w_out': w_out, 'ln1_g': ln1_g, 'ln1_b': ln1_b, 'ffn_w_up': ffn_w_up, 'ffn_conv_w': ffn_conv_w,
            'ffn_conv_b': ffn_conv_b, 'ffn_w_down': ffn_w_down, 'ln2_g': ln2_g, 'ln2_b': ln2_b}


def reference(x_prompt, x_sample, state_mlstm_c, state_mlstm_n, state_mlstm_m, state_ssd, c, c_ctx,
              w_ada, b_ada, w_in, b_igate, b_fgate, mlstm_norm_g, w_pool, pool_scale, w_spatial, b_spatial,
              ssd_conv_w, ssd_conv_b, ssd_dt_bias, ssd_a_log, ssd_d, ssd_norm_g, w_out, ln1_g, ln1_b,
              ffn_w_up, ffn_conv_w, ffn_conv_b, ffn_w_down, ln2_g, ln2_b):
    f32 = jnp.float32
    n_ctx = x_prompt.shape[0]
    t_lat = x_sample.shape[1]
    rows = t_lat // GRID_W
    y_p = x_prompt
    y_s = x_sample + grid_pos_embed(rows, D_MODEL).astype(x_sample.dtype)[None]
    cond_ctx = jax.nn.silu(c_ctx)[None, :]
    cond_lat = jax.nn.silu(c)
    zero_m = (jnp.zeros((n_ctx, N_DIR, M_HEADS, M_HD, M_HD), f32),
              jnp.zeros((n_ctx, N_DIR, M_HEADS, M_HD), f32),
              jnp.zeros((n_ctx, N_DIR, M_HEADS), f32))
    zero_s = jnp.zeros((n_ctx, N_DIR, S_HEADS, S_HD, S_STATE), f32)
    new_c, new_n, new_m, new_s = [], [], [], []
    for l in range(DEPTH):
        p = {'w_ada': w_ada[l], 'b_ada': b_ada[l], 'w_in': w_in[l], 'b_igate': b_igate[l],
             'b_fgate': b_fgate[l], 'mlstm_norm_g': mlstm_norm_g[l], 'w_pool': w_pool[l],
             'pool_scale': pool_scale[l], 'w_spatial': w_spatial[l], 'b_spatial': b_spatial[l],
             'ssd_conv_w': ssd_conv_w[l], 'ssd_conv_b': ssd_conv_b[l], 'ssd_dt_bias': ssd_dt_bias[l],
             'ssd_a_log': ssd_a_log[l], 'ssd_d': ssd_d[l], 'ssd_norm_g': ssd_norm_g[l], 'w_out': w_out[l],
             'ln1_g': ln1_g[l], 'ln1_b': ln1_b[l], 'ffn_w_up': ffn_w_up[l], 'ffn_conv_w': ffn_conv_w[l],
             'ffn_conv_b': ffn_conv_b[l], 'ffn_w_down': ffn_w_down[l], 'ln2_g': ln2_g[l], 'ln2_b': ln2_b[l]}
        y_p, (mc, mn, mm), ss = trunk_layer(y_p, cond_ctx, p, zero_m, zero_s)
        new_c.append(mc)
        new_n.append(mn)
        new_m.append(mm)
        new_s.append(ss)
        y_s, _, _ = trunk_layer(y_s, cond_lat, p,
                                (state_mlstm_c[:, l], state_mlstm_n[:, l], state_mlstm_m[:, l]),
                                state_ssd[:, l])
    return (y_p, y_s, jnp.stack(new_c, 1), jnp.stack(new_n, 1), jnp.stack(new_m, 1), jnp.stack(new_s, 1))
```

```python
import math
import numpy as np
import ml_dtypes
from contextlib import ExitStack
import concourse.bass as bass
import concourse.mybir as mybir
from concourse.bass_utils import run_bass_kernel_spmd

F32 = mybir.dt.float32
BF16 = mybir.dt.bfloat16
AF = mybir.ActivationFunctionType
ALU = mybir.AluOpType
AX = mybir.AxisListType
DTSZ = {F32: 4, BF16: 2}

D = 1024
DIN = 2840
DFF = 2816
EPS = 1e-5
ALPHA = 4.0 ** 0.25
NEG = -30000.0

ENGS = ("pe", "act", "dve", "pool", "sp")
EPOCH = 30000
NDMA_SEM = 8


def prod(l):
    r = 1
    for x in l:
        r *= int(x)
    return r


class Buf:
    __slots__ = ("name", "v", "last_w", "readers", "off", "excl")

    def __init__(self, name, v, floor=None):
        self.off = -1
        self.excl = False
        self.name = name
        self.v = v
        self.last_w = floor
        self.readers = []

    def __getitem__(self, k):
        return self.v[k]


class Op:
    __slots__ = ("eng", "fn", "deps", "is_dma", "idx", "sig", "has_dep", "vc", "name")

    def __init__(self, eng, fn, is_dma, name):
        self.eng = eng
        self.fn = fn
        self.is_dma = is_dma
        self.deps = []
        self.sig = None
        self.has_dep = False
        self.vc = None
        self.name = name


class Sched:
    def __init__(self, nc, es, arena_bytes):
        self.nc = nc
        self.es = es
        self.ops = []
        self.floor = None
        self.bufs = []
        self.arena = es.enter_context(nc.sbuf_tensor("arena", [128, arena_bytes // 4], F32))
        self.arena_bytes = arena_bytes
        self.off = 0
        self.peak = 0
        self.banks = []
        for i in range(8):
            t = es.enter_context(nc.psum_tensor(f"bank{i}", [128, 512], F32))
            self.banks.append(Buf(f"bank{i}", t))
            self.banks[-1].excl = True
        self.bank_i = 0
        self.dram_bufs = {}

    def sb(self, name, shape, dtype=F32):
        shape = [int(s) for s in shape]
        if getattr(self, "verbose", False):
            print(f"  sb {name} {shape} {prod(shape[1:]) * DTSZ[dtype]} at {self.off}")
        n = prod(shape[1:])
        nb = n * DTSZ[dtype]
        off = (self.off + 31) // 32 * 32
        assert off + nb <= self.arena_bytes, f"arena overflow allocating {name}: {off + nb}"
        self.off = off + nb
        self.peak = max(self.peak, self.off)
        h = self.arena if dtype == F32 else self.arena.bitcast(dtype)
        e0 = off // DTSZ[dtype]
        v = h[0:shape[0], e0:e0 + n]
        if len(shape) > 2:
            names = " ".join(f"d{i}" for i in range(len(shape) - 1))
            kw = {f"d{i}": shape[i + 1] for i in range(len(shape) - 1)}
            v = v.rearrange(f"p ({names}) -> p {names}", **kw)
        b = Buf(name, v, self.floor)
        b.off = off
        self.bufs.append(b)
        return b

    def mark(self):
        return self.off

    def release(self, mark):
        self.barrier()
        self.bufs = [b for b in self.bufs if b.off < mark]
        self.off = mark

    def bank(self):
        b = self.banks[self.bank_i]
        self.bank_i = (self.bank_i + 1) % 8
        return b

    def dbuf(self, key):
        if key not in self.dram_bufs:
            self.dram_bufs[key] = Buf(str(key), None, None)
        return self.dram_bufs[key]

    def op(self, eng, fn, reads=(), writes=(), name=None, dma=False):
        o = Op(eng, fn, dma, name)
        deps = set()
        ex = [b for b in reads if b.excl]
        if ex:
            reads = [b for b in reads if not b.excl]
            writes = list(writes) + [b for b in ex if b not in writes]
        for b in reads:
            if b.last_w is not None:
                deps.add(b.last_w)
        for b in writes:
            if b.last_w is not None:
                deps.add(b.last_w)
            for r in b.readers:
                deps.add(r)
        o.deps = sorted(deps, key=lambda d: d.idx)
        o.idx = len(self.ops)
        for d in o.deps:
            d.has_dep = True
        for b in reads:
            b.readers.append(o)
        for b in writes:
            b.last_w = o
            b.readers = []
        self.ops.append(o)
        return o

    def dma(self, q, out, in_, reads=(), writes=(), name=None, **kw):
        return self.op(q, lambda e: e.dma_start(out=out, in_=in_, **kw), reads, writes, name=name, dma=True)

    def barrier(self):
        allb = self.bufs + self.banks + list(self.dram_bufs.values())
        o = self.op("sp", None, reads=[], writes=allb, name="barrier")
        self.floor = o
        return o

    def emit(self, final_bufs):
        nc, es = self.nc, self.es
        fin = self.op("sp", None, reads=list(final_bufs), name="final")
        cnt, dma_n, semkeys = {}, {}, []
        for o in self.ops:
            if not o.has_dep:
                continue
            if o.is_dma:
                n = dma_n.get(o.eng, 0)
                dma_n[o.eng] = n + 1
                key = ("dma", o.eng, n % NDMA_SEM)
                cnt[key] = cnt.get(key, 0) + 16
                o.sig = (key, cnt[key])
            else:
                tot = cnt.get(("n", o.eng), 0)
                cnt[("n", o.eng)] = tot + 1
                key = ("c", o.eng, tot // EPOCH)
                o.sig = (key, tot % EPOCH + 1)
            if o.sig[0] not in semkeys:
                semkeys.append(o.sig[0])
        sems = {k: es.enter_context(nc.semaphore("s_" + "_".join(str(x) for x in k))) for k in semkeys}
        per_eng = {e: [] for e in ENGS}
        seen = {e: {} for e in ENGS}
        nwaits = 0
        for o in self.ops:
            s = seen[o.eng]
            need = {}
            for d in o.deps:
                k, c = d.sig
                if s.get(k, 0) < c:
                    need[k] = max(need.get(k, 0), c)
            if o.is_dma and o.sig is not None:
                k, c = o.sig
                if c > 16 and s.get(k, 0) < c - 16:
                    need[k] = max(need.get(k, 0), c - 16)
            for d in o.deps:
                for k, c in d.vc.items():
                    if s.get(k, 0) < c:
                        s[k] = c
            for k, c in need.items():
                if s.get(k, 0) < c:
                    s[k] = c
            o.deps = need
            nwaits += len(need)
            vc = dict(s)
            if o.sig is not None:
                vc[o.sig[0]] = max(vc.get(o.sig[0], 0), o.sig[1])
                if not o.is_dma:
                    for ep in range(o.sig[0][2]):
                        vc[("c", o.eng, ep)] = EPOCH
            o.vc = vc
            per_eng[o.eng].append(o)

        def body_for(engname):
            def body(eng):
                for o in per_eng[engname]:
                    for k, c in o.deps.items():
                        eng.wait_ge(sems[k], c)
                    if o.fn is not None:
                        ins = o.fn(eng)
                        if o.sig is not None:
                            ins.then_inc(sems[o.sig[0]], 16 if o.is_dma else 1)
                    elif o.sig is not None:
                        eng.nop().then_inc(sems[o.sig[0]], 1)
            return body

        with nc.Block() as block:
            block.sync(body_for("sp"))
            block.scalar(body_for("act"))
            block.vector(body_for("dve"))
            block.gpsimd(body_for("pool"))
            block.tensor(body_for("pe"))
        return {"ops": len(self.ops), "waits": nwaits, "sems": len(sems),
                "per_eng": {e: len(v) for e, v in per_eng.items()}, "sbuf_peak": self.peak}


def ACT(S, r, w, out, in_, func, bias=None, scale=None):
    kw = {}
    if bias is not None:
        kw["bias"] = bias
    if scale is not None:
        kw["scale"] = scale
    return S.op("act", lambda e: e.activation(out, in_, func, **kw), r, w)


def TS(S, eng, r, w, out, in0, s1, s2, op0, op1=None):
    if op1 is None:
        return S.op(eng, lambda e: e.tensor_scalar(out, in0, s1, None, op0), r, w)
    return S.op(eng, lambda e: e.tensor_scalar(out, in0, s1, s2, op0, op1), r, w)


def TT(S, eng, r, w, out, in0, in1, op):
    return S.op(eng, lambda e: e.tensor_tensor(out, in0, in1, op), r, w)


def STT(S, eng, r, w, out, in0, scalar, in1, op0, op1):
    return S.op(eng, lambda e: e.scalar_tensor_tensor(out, in0, scalar, in1, op0, op1), r, w)


def CP(S, eng, r, w, out, in_):
    if eng == "act":
        return S.op("act", lambda e: e.copy(out, in_), r, w)
    return S.op(eng, lambda e: e.tensor_copy(out, in_), r, w)


def MSET(S, eng, w, out, val):
    return S.op(eng, lambda e: e.memset(out, val), [], w)


def MM(S, r, w, mms):
    mms = list(mms)

    def fn(e):
        ins = None
        for (o, l, rh, st, sp) in mms:
            ins = e.matmul(o, l, rh, start=st, stop=sp)
        return ins
    return S.op("pe", fn, r, w)


def TR(S, r, w, trs):
    trs = list(trs)

    def fn(e):
        ins = None
        for (o, i, idt) in trs:
            ins = e.transpose(o, i, idt)
        return ins
    return S.op("pe", fn, r, w)


def bc(ap, shape):
    return ap.to_broadcast([int(s) for s in shape])


POOL_W = (2, 4, 8, 16)


def make_consts():
    cols = {}
    parts = []
    off = [0]

    def add(name, arr):
        a = np.zeros((128, arr.shape[1]), np.float32)
        a[:arr.shape[0]] = arr
        cols[name] = (off[0], arr.shape[1])
        off[0] += arr.shape[1]
        parts.append(a)

    idx = np.arange(128)
    s_, t_ = idx[:, None], idx[None, :]
    add("ident", np.eye(128, dtype=np.float32))
    add("ones", np.ones((128, 128), np.float32))
    add("tri0", (s_ <= t_).astype(np.float32))
    add("tri1", (s_ >= t_).astype(np.float32))
    add("neg0", np.where(s_ <= t_, 0.0, NEG).astype(np.float32))
    add("neg1", np.where(s_ >= t_, 0.0, NEG).astype(np.float32))
    n1 = off[0]
    for g, w in enumerate(POOL_W):
        h = w // 2
        band = ((s_ >= t_ - h) & (s_ < t_ + h)).astype(np.float32)
        cnt_int = np.full(128, float(w))
        cnt_first = (idx + h) - np.maximum(idx - h, 0)
        cnt_last = np.minimum(idx + h, 128) - (idx - h)
        eye = np.eye(128, dtype=np.float32)
        add(f"pA{g}int", band / cnt_int[None, :] - eye)
        add(f"pA{g}first", band / cnt_first[None, :] - eye)
        add(f"pA{g}last", band / cnt_last[None, :] - eye)
        sp = np.arange(8)[:, None]
        add(f"pP{g}", (((sp - 8) >= t_ - h) & ((sp - 8) < t_ + h)).astype(np.float32) / w)
        add(f"pN{g}", (((128 + sp) >= t_ - h) & ((128 + sp) < t_ + h)).astype(np.float32) / w)
    add("jrow", np.tile(np.arange(256, dtype=np.float32)[None, :], (128, 1)))
    add("pcol", (idx % 64).astype(np.float32)[:, None])
    full = np.concatenate(parts, axis=1)
    cols2 = {kk: (o - n1, n) for kk, (o, n) in cols.items() if o >= n1}
    cols1 = {kk: (o, n) for kk, (o, n) in cols.items() if o < n1}
    return full[:, :n1].copy(), cols1, full[:, n1:].copy(), cols2


class Cfg:
    def __init__(self, Ts=4096, NP=4, Tp=256, L=2, debug=()):
        self.Ts, self.NP, self.Tp, self.L = Ts, NP, Tp, L
        self.debug = tuple(debug)
        self.seqs = [("s", Ts, 0, 0)] + [(f"p{j}", Tp, 1, Ts + j * Tp) for j in range(NP)]
        self.Ttot = Ts + NP * Tp


INPUT_SPECS = lambda c: [
    ("xs", [c.Ts, D]), ("xp", [c.NP * c.Tp, D]),
    ("st_c", [2, 2, 4, 64, 64]), ("st_n", [2, 2, 4, 64]), ("st_m", [2, 8]), ("st_s", [2, 2, 4, 64, 128]),
    ("cond", [2, D]),
    ("w_ada", [2, D, 6 * D]), ("b_ada", [2, 6 * D]), ("w_in", [2, D, DIN]), ("b_ig", [2, 8]), ("b_fg", [2, 8]),
    ("mnorm_g", [2, 256]), ("w_pool", [2, 4, 64, 64]), ("pool_scale", [2, 256]), ("w_sp", [2, 4, 128, 128]),
    ("b_sp", [2, 4, 128]), ("sconv_w", [2, 3, 768]), ("sconv_b", [2, 768]), ("dt_bias", [2, 8]),
    ("a_log", [2, 8]), ("ssd_d", [2, 4]), ("snorm_g", [2, 256]), ("w_out", [2, D, D]),
    ("ln1_g", [2, D]), ("ln1_b", [2, D]), ("w_up", [2, D, 2 * DFF]), ("fconv_w", [2, 3, 2 * DFF]),
    ("fconv_b", [2, 2 * DFF]), ("w_dn", [2, DFF, D]), ("ln2_g", [2, D]), ("ln2_b", [2, D]),
]
OUTPUT_SPECS = lambda c: [
    ("ys", [c.Ts, D]), ("yp", [c.NP * c.Tp, D]), ("oc", [c.NP, 2, 2, 4, 64, 64]), ("on", [c.NP, 2, 2, 4, 64]),
    ("om", [c.NP, 2, 8]), ("os", [c.NP, 2, 2, 4, 64, 128]),
]


class K:
    pass


def build(cfg):
    nc = bass.Bass("TRN2", target_bir_lowering=False)
    cst_np, ccols, cst2_np, ccols2 = make_consts()
    I = {}
    for name, shape in INPUT_SPECS(cfg):
        I[name] = nc.dram_tensor(name, shape, F32, kind="ExternalInput")
    I["cst"] = nc.dram_tensor("cst", list(cst_np.shape), F32, kind="ExternalInput")
    I["cst2"] = nc.dram_tensor("cst2", list(cst2_np.shape), F32, kind="ExternalInput")
    O = {}
    for name, shape in OUTPUT_SPECS(cfg):
        O[name] = nc.dram_tensor(name, shape, F32, kind="ExternalOutput")
    DBG = {}
    for name, shape in cfg.debug:
        DBG[name] = nc.dram_tensor("dbg_" + name, shape, F32, kind="ExternalOutput")
    ntile = cfg.Ttot // 128
    XA = nc.dram_tensor("scr_xa", [cfg.Ttot, D], F32, kind="Internal")
    XB = nc.dram_tensor("scr_xb", [cfg.Ttot, D], F32, kind="Internal")
    HF = nc.dram_tensor("scr_hf", [cfg.Ttot, 512], F32, kind="Internal")
    HT = nc.dram_tensor("scr_ht", [ntile, 128, 8 * 128], BF16, kind="Internal")
    ED = nc.dram_tensor("scr_e", [64, 512], F32, kind="Internal")

    with ExitStack() as es:
        S = Sched(nc, es, 204 * 1024)
        k = K()
        k.S, k.cfg, k.I, k.O, k.DBG = S, cfg, I, O, DBG
        k.XA, k.XB, k.HF, k.HT, k.ED = XA, XB, HF, HT, ED
        k.final_bufs = []
        k.ccols2 = ccols2
        k.W2 = cst2_np.shape[1]
        setup(k, ccols)
        for l in range(cfg.L):
            layer(k, l)
        stats = S.emit(k.final_bufs)
    return nc, stats


def cview(k, name, rows=128):
    if name in k.ccols:
        o, n = k.ccols[name]
        return k.cst[0:rows, o:o + n]
    o, n = k.ccols2[name]
    return k.cst2[0:rows, o:o + n]


def load_cst2(k):
    S = k.S
    b = S.sb("cst2", [128, k.W2])
    k.cst2b = b
    k.cst2 = b.v
    S.dma("sp", b[:, :], k.I["cst2"][:, :], writes=[b])
    return b


def setup(k, ccols):
    S, I, cfg = k.S, k.I, k.cfg
    k.ccols = ccols
    W = sum(n for (_, n) in ccols.values())
    cstb = S.sb("cst", [128, W])
    k.cstb = cstb
    k.cst = cstb.v
    S.dma("sp", cstb[:, :], I["cst"][:, :], writes=[cstb])
    k.identb = S.sb("identb", [128, 128], BF16)
    CP(S, "dve", [cstb], [k.identb], k.identb[:, :], cview(k, "ident"))
    k.cm05 = S.sb("cm05", [128, 1])
    MSET(S, "pool", [k.cm05], k.cm05[:, :], -0.5)
    k.cm1 = S.sb("cm1", [128, 1])
    MSET(S, "pool", [k.cm1], k.cm1[:, :], -1.0)

    L = cfg.L
    k.modT = S.sb("modT", [128, L, 48, 2])
    layer_consts_alloc(k)
    m1 = S.mark()
    c2b = load_cst2(k)
    fr = S.sb("pe_fr", [64, 256])
    ang = S.sb("pe_ang", [64, 256])
    et = S.sb("pe_e", [64, 512])
    et2 = S.sb("pe_e2", [64, 512])
    sq = S.sb("pe_sq", [64, 256])
    ACT(S, [c2b], [fr], fr[:, :], cview(k, "jrow", 64), AF.Exp, scale=-math.log(10000.0) / 256.0)
    TS(S, "dve", [fr, c2b], [ang], ang[:, :], fr[:, :], cview(k, "pcol", 64), None, ALU.mult)
    ACT(S, [ang], [et], et[:, 0:256], ang[:, :], AF.Sin, scale=1.0 / 32.0)
    ACT(S, [ang], [et], et[:, 256:512], ang[:, :], AF.Sin, scale=-1.0 / 32.0, bias=math.pi / 2.0)
    cur, nxt = et, et2
    for it in range(5):
        TT(S, "dve", [cur], [sq], sq[:, :], cur[:, 0:256], cur[:, 0:256], ALU.mult)
        STT(S, "dve", [cur], [nxt], nxt[:, 0:256], cur[:, 0:256], 2.0, cur[:, 256:512], ALU.mult, ALU.mult)
        TS(S, "dve", [sq], [nxt], nxt[:, 256:512], sq[:, :], -2.0, 1.0, ALU.mult, ALU.add)
        cur, nxt = nxt, cur
    et = cur
    edb = S.dbuf("ED")
    S.dma("sp", k.ED[:, :], et[:, :], reads=[et], writes=[edb])

    condT = S.sb("condT", [128, 8, 2])
    for c in range(2):
        S.dma("sp", condT[:, :, c], I["cond"][c].rearrange("(kc p) -> p kc", p=128), writes=[condT],
              allow_slow_non_contiguous=True)
    esg = S.sb("cond_e", [128, 8, 2])
    ACT(S, [condT], [esg], esg[:, :, :], condT[:, :, :], AF.Exp, scale=-1.0)
    TS(S, "dve", [esg], [esg], esg[:, :, :], esg[:, :, :], 1.0, None, ALU.add)
    TT(S, "pool", [esg, k.cm1], [esg], esg[:, :, :], esg[:, :, :], bc(k.cm1[:, 0:1].unsqueeze(2), [128, 8, 2]), ALU.pow)
    TT(S, "pool", [condT, esg], [condT], condT[:, :, :], condT[:, :, :], esg[:, :, :], ALU.mult)
    badaT = S.sb("badaT", [128, L, 48])
    k.cf_st = S.sb("cf_st0", [128, 128])
    for l in range(L):
        colform(k, badaT, badaT[:, l, :], I["b_ada"][l].rearrange("(j p) -> j p", p=128), 48)
    wst = [S.sb(f"wada_st{j}", [128, 8, 512]) for j in range(2)]
    n = 0
    for l in range(L):
        for cb in range(12):
            st = wst[n % 2]
            n += 1
            S.dma("sp", st[:, :, :], I["w_ada"][l, :, cb * 512:(cb + 1) * 512].rearrange("(kc p) n -> p kc n", p=128),
                  writes=[st])
            pb = S.bank()
            mms = []
            for sub in range(4):
                for kc in range(8):
                    mms.append((pb[:, sub * 2:sub * 2 + 2], st[:, kc, sub * 128:(sub + 1) * 128], condT[:, kc, :],
                                kc == 0, kc == 7))
            MM(S, [st, condT], [pb], mms)
            TT(S, "dve", [pb, badaT], [k.modT],
               k.modT[:, l, cb * 4:cb * 4 + 4, :],
               pb[:, 0:8].rearrange("p (s c) -> p s c", c=2),
               bc(badaT[:, l, cb * 4:cb * 4 + 4].unsqueeze(2), [128, 4, 2]), ALU.add)
    for l in range(L):
        for grp in (1, 4):
            TS(S, "dve", [k.modT], [k.modT], k.modT[:, l, grp * 8:(grp + 1) * 8, :],
               k.modT[:, l, grp * 8:(grp + 1) * 8, :], 1.0, None, ALU.add)
    S.release(m1)


class nc_allow:
    def __init__(self, k):
        pass

    def __enter__(self):
        return self

    def __exit__(self, *a):
        return False


def bview(bank, dtype=F32):
    return bank.v if dtype == F32 else bank.v.bitcast(dtype)


def layer_consts_alloc(k):
    S = k.S
    k.bif = S.sb("bif", [128, 16])
    k.coef = S.sb("coef", [128, 2, 8])
    k.dtb = S.sb("dtb", [128, 8])
    k.dsk = S.sb("dsk", [128, 4])
    k.mng = S.sb("mng", [128, 256])
    k.sng = S.sb("sng", [128, 256])
    k.psc = S.sb("psc", [128, 2])
    k.wpb = S.sb("wpb", [128, 2, 128], BF16)
    k.wsT = S.sb("wsT", [128, 4, 128], BF16)
    k.bsT = S.sb("bsT", [128, 4])
    k.scw = S.sb("scw", [128, 4, 6])
    k.fcw = S.sb("fcw", [128, 4, 44])
    k.lng = S.sb("lng", [128, D])
    k.lnb = S.sb("lnb", [128, D])
    k.gbc = S.sb("gbc", [128, D])
    k.ttmp = [S.sb(f"ttmp{j}", [128, D]) for j in range(2)]
    k.small = {}


def colform(k, dst_buf, dst_ap, src_ap, nb):
    S = k.S
    st = k.cf_st
    S.dma("sp", st[0:nb, :], src_ap, writes=[st])
    pb = S.bank()
    TR(S, [st, k.cstb], [pb], [(pb[:, 0:nb], st[0:nb, :], cview(k, "ident")[0:nb, 0:nb])])
    CP(S, "dve", [pb], [dst_buf], dst_ap, pb[:, 0:nb])


def load_layer_consts(k, l):
    S, I = k.S, k.I
    m = S.mark()
    k.cf_st = S.sb("cf_st", [128, 128])
    row = lambda name, a, b: I[name][l:l + 1, a:b].partition_broadcast(128)
    S.dma("sp", k.bif[:, 0:8], row("b_ig", 0, 8), writes=[k.bif])
    S.dma("sp", k.bif[:, 8:16], row("b_fg", 0, 8), writes=[k.bif])
    TS(S, "dve", [k.bif], [k.bif], k.bif[:, 8:16], k.bif[:, 8:16], -1.0, None, ALU.mult)
    al = S.sb("al_tmp", [128, 8])
    S.dma("sp", al[:, :], row("a_log", 0, 8), writes=[al])
    ACT(S, [al], [al], al[:, :], al[:, :], AF.Exp)
    MSET(S, "pool", [k.coef], k.coef[:, :, :], -1.0)
    TS(S, "dve", [al, k.coef], [k.coef], k.coef[:, :, 4:8], al[:, :].rearrange("p (d h) -> p d h", d=2), -1.0, None, ALU.mult)
    S.dma("sp", k.dtb[:, :], row("dt_bias", 0, 8), writes=[k.dtb])
    S.dma("sp", k.dsk[:, :], row("ssd_d", 0, 4), writes=[k.dsk])
    S.dma("sp", k.mng[:, :], row("mnorm_g", 0, 256), writes=[k.mng])
    S.dma("sp", k.sng[:, :], row("snorm_g", 0, 256), writes=[k.sng])
    colform(k, k.psc, k.psc[:, :], I["pool_scale"][l].rearrange("(j p) -> j p", p=128), 2)
    wp32 = S.sb("wp32", [128, 2, 128])
    MSET(S, "pool", [wp32], wp32[:, :, :], 0.0)
    for g in range(4):
        pr = slice((g % 2) * 64, (g % 2) * 64 + 64)
        S.dma("sp", wp32[pr, g // 2, (g % 2) * 64:(g % 2) * 64 + 64], I["w_pool"][l, g], writes=[wp32])
    CP(S, "dve", [wp32], [k.wpb], k.wpb[:, :, :], wp32[:, :, :])
    ws32 = S.sb("ws32", [128, 4, 128])
    S.dma("sp", ws32[:, :, :], I["w_sp"][l].rearrange("h t s -> t h s"), writes=[ws32])
    pb = S.bank()
    TR(S, [ws32, k.cstb], [pb], [(pb[:, h * 128:(h + 1) * 128], ws32[:, h, :], cview(k, "ident")) for h in range(4)])
    CP(S, "act", [pb], [k.wsT], k.wsT[:, :, :], pb[:, :].rearrange("p (h t) -> p h t", h=4))
    colform(k, k.bsT, k.bsT[:, :], I["b_sp"][l], 4)
    for tap in range(3):
        colform(k, k.scw, k.scw[:, tap, :], I["sconv_w"][l, tap].rearrange("(b p) -> b p", p=128), 6)
        colform(k, k.fcw, k.fcw[:, tap, :], I["fconv_w"][l, tap].rearrange("(b p) -> b p", p=128), 44)
    colform(k, k.scw, k.scw[:, 3, :], I["sconv_b"][l].rearrange("(b p) -> b p", p=128), 6)
    colform(k, k.fcw, k.fcw[:, 3, :], I["fconv_b"][l].rearrange("(b p) -> b p", p=128), 44)
    S.release(m)


def load_weight(k, dst, src2d, nkc, ncols, scope_stage):
    S = k.S
    engs = ("dve", "pool", "act")
    piece = 2840
    for kc in range(nkc):
        for c0 in range(0, ncols, piece):
            c1 = min(ncols, c0 + piece)
            st = scope_stage[k.wl_n % 2]
            S.dma("sp", st[:, 0:c1 - c0], src2d[kc * 128:(kc + 1) * 128, c0:c1], writes=[st])
            CP(S, engs[k.wl_n % 3], [st], [dst], dst[:, kc, c0:c1], st[:, 0:c1 - c0])
            k.wl_n += 1


def gate_table(k, l, grp, cond):
    S = k.S
    dg = k.ttmp[0]
    for j in range(8):
        TS(S, "dve", [k.cstb, k.modT], [dg], dg[:, 0:128], cview(k, "ident"), k.modT[:, l, grp * 8 + j, cond:cond + 1], None, ALU.mult)
        if j % 4 == 0:
            pb = S.bank()
        MM(S, [dg, k.cstb], [pb], [(pb[:, (j % 4) * 128:(j % 4 + 1) * 128], cview(k, "ones"), dg[:, 0:128], True, True)])
        if j % 4 == 3:
            CP(S, "act", [pb], [k.gbc], k.gbc[:, (j // 4) * 512:(j // 4 + 1) * 512], pb[:, :])


def make_hT(k, l, which, cond, rows, xbuf, dsts, dst_buf, src_bufs, pos_tile=None):
    S = k.S
    n = 0
    for r, nr in rows:
        S.dma("sp", xbuf[n:n + nr, :], r, reads=src_bufs, writes=[xbuf])
        n += nr
    if pos_tile is not None:
        TT(S, "pool", [xbuf, pos_tile], [xbuf], xbuf[0:n, :], xbuf[0:n, :], pos_tile[0:n, :], ALU.add)
    sm = k.hsm
    st, mv, ve, rstd, xnb = sm["st"], sm["mv"], sm["ve"], sm["rstd"], sm["xnb"]
    S.op("dve", lambda e: e.bn_stats(st[0:n, 0, :], xbuf[0:n, 0:512]), [xbuf], [st])
    S.op("dve", lambda e: e.bn_stats(st[0:n, 1, :], xbuf[0:n, 512:1024]), [xbuf], [st])
    S.op("dve", lambda e: e.bn_aggr(mv[0:n, :], st[0:n, :, :].rearrange("p a b -> p (a b)")), [st], [mv])
    TS(S, "dve", [mv], [ve], ve[0:n, :], mv[0:n, 1:2], EPS, None, ALU.add)
    TT(S, "pool", [ve, k.cm05], [rstd], rstd[0:n, :], ve[0:n, :], k.cm05[0:n, :], ALU.pow)
    TS(S, "dve", [xbuf, mv, rstd], [xnb], xnb[0:n, :], xbuf[0:n, :], mv[0:n, 0:1], rstd[0:n, 0:1], ALU.subtract, ALU.mult)
    pb = S.bank()
    pv = bview(pb, BF16)
    TR(S, [xnb, k.identb], [pb],
       [(pv[:, kc * 128:kc * 128 + n], xnb[0:n, kc * 128:(kc + 1) * 128], k.identb[0:n, 0:n]) for kc in range(8)])
    gsh, gsc = (0, 1) if which == 1 else (3, 4)
    for kc in range(8):
        sc = k.modT[:, l, gsc * 8 + kc, cond:cond + 1]
        sh = k.modT[:, l, gsh * 8 + kc, cond:cond + 1]
        if kc % 2 == 0:
            ACT(S, [pb, k.modT], [dst_buf], dsts[kc], pv[:, kc * 128:kc * 128 + n], AF.Identity, bias=sh, scale=sc)
        else:
            TS(S, "dve", [pb, k.modT], [dst_buf], dsts[kc], pv[:, kc * 128:kc * 128 + n], sc, sh, ALU.mult, ALU.add)


def alloc_hsm(k):
    S = k.S
    k.hsm = {"st": S.sb("h_st", [128, 2, 6]), "mv": S.sb("h_mv", [128, 2]), "ve": S.sb("h_ve", [128, 1]),
             "rstd": S.sb("h_rstd", [128, 1]), "xnb": S.sb("h_xnb", [128, D], BF16)}


def resid_ln(k, x_buf, psum_halves, out_buf, nb_small):
    S = k.S
    t0, t1 = k.ttmp[0], out_buf
    for hlf, pb in enumerate(psum_halves):
        sl = slice(hlf * 512, (hlf + 1) * 512)
        TT(S, "dve", [pb, k.gbc], [t0], t0[:, sl], pb[:, :], k.gbc[:, sl], ALU.mult)
    TS(S, "pool", [x_buf], [x_buf], x_buf[:, :], x_buf[:, :], ALPHA, None, ALU.mult)
    TT(S, "pool", [x_buf, t0], [t0], t0[:, :], t0[:, :], x_buf[:, :], ALU.add)
    st, mv, ve, rstd, nb = nb_small["st"], nb_small["mv"], nb_small["ve"], nb_small["rstd"], nb_small["nb"]
    S.op("dve", lambda e: e.bn_stats(st[:, 0, :], t0[:, 0:512]), [t0], [st])
    S.op("dve", lambda e: e.bn_stats(st[:, 1, :], t0[:, 512:1024]), [t0], [st])
    S.op("dve", lambda e: e.bn_aggr(mv[:, :], st[:, :, :].rearrange("p a b -> p (a b)")), [st], [mv])
    TS(S, "dve", [mv], [ve], ve[:, :], mv[:, 1:2], EPS, None, ALU.add)
    TT(S, "pool", [ve, k.cm05], [rstd], rstd[:, :], ve[:, :], k.cm05[:, :], ALU.pow)
    STT(S, "dve", [mv, rstd], [nb], nb[:, :], mv[:, 0:1], -1.0, rstd[:, :], ALU.mult, ALU.mult)
    ACT(S, [t0, rstd, nb], [t1], t1[:, :], t0[:, :], AF.Identity, bias=nb[:, 0:1], scale=rstd[:, 0:1])
    TT(S, "pool", [t1, k.lng], [t1], t1[:, :], t1[:, :], k.lng[:, :], ALU.mult)
    TT(S, "pool", [t1, k.lnb], [out_buf], out_buf[:, :], t1[:, :], k.lnb[:, :], ALU.add)


def alloc_rsm(k):
    S = k.S
    return {"st": S.sb("r_st", [128, 2, 6]), "mv": S.sb("r_mv", [128, 2]), "ve": S.sb("r_ve", [128, 1]),
            "rstd": S.sb("r_rstd", [128, 1]), "nb": S.sb("r_nb", [128, 1])}


def seq_src_dst(k, l, phase):
    cfg = k.cfg
    mode = getattr(cfg, "mode", "full")
    if phase == "A":
        src = None if l == 0 else k.XB
        dst = k.XA if mode == "full" else None
    else:
        src = k.XA if mode == "full" else None
        dst = None if l == cfg.L - 1 else k.XB
    return src, dst


def rows_ap(k, handle, which_io, r0, n):
    cfg = k.cfg
    if handle is not None:
        return handle[r0:r0 + n, :]
    if r0 < cfg.Ts:
        t = k.I["xs"] if which_io == "in" else k.O["ys"]
        return t[r0:r0 + n, :]
    t = k.I["xp"] if which_io == "in" else k.O["yp"]
    return t[r0 - cfg.Ts:r0 - cfg.Ts + n, :]


def phaseB(k, l):
    S, I, cfg = k.S, k.I, k.cfg
    m = S.mark()
    w_up = S.sb("w_up", [128, 8, 2 * DFF], BF16)
    w_dn = S.sb("w_dn", [128, 22, D], BF16)
    m2 = S.mark()
    stage = [S.sb(f"wstage{j}", [128, 2840]) for j in range(2)]
    k.wl_n = 0
    load_weight(k, w_up, I["w_up"][l], 8, 2 * DFF, stage)
    load_weight(k, w_dn, I["w_dn"][l], 22, D, stage)
    S.release(m2)
    S.dma("sp", k.lng[:, :], I["ln2_g"][l:l + 1, :].partition_broadcast(128), writes=[k.lng])
    S.dma("sp", k.lnb[:, :], I["ln2_b"][l:l + 1, :].partition_broadcast(128), writes=[k.lnb])
    alloc_hsm(k)
    rsm = alloc_rsm(k)
    SEG = 256
    h2T = [S.sb(f"h2T{j}", [128, 8, SEG + 2], BF16) for j in range(2)]
    actT = S.sb("actT", [128, 22, SEG], BF16)
    xt = [S.sb(f"xtB{j}", [128, D]) for j in range(3)]
    xh = k.ttmp[1]
    cg = [S.sb(f"cg{j}", [128, SEG]) for j in range(2)]
    cv = [S.sb(f"cv{j}", [128, SEG]) for j in range(2)]
    th = [S.sb(f"th{j}", [128, SEG]) for j in range(2)]
    src, dst = seq_src_dst(k, l, "B")
    xt = xt + [S.sb("xtB3", [128, D])]
    segs = []
    for (sname, T, cond, off) in cfg.seqs:
        for t0 in range(0, T, SEG):
            segs.append((T, cond, off, t0))
    state = {}

    def prep(si):
        T, cond, off, t0 = segs[si]
        hT = h2T[si % 2]
        r0 = off + t0
        xts = []
        for j in range(SEG // 128):
            xb = xt[(2 * si + j) % 4]
            xts.append(xb)
            make_hT(k, l, 2, cond, [(rows_ap(k, src, "in", r0 + 128 * j, 128), 128)], xb,
                    [hT[:, kc, 1 + 128 * j:1 + 128 * (j + 1)] for kc in range(8)], hT, [])
        rows, cols = [], []
        if t0 > 0:
            rows.append((rows_ap(k, src, "in", r0 - 1, 1), 1))
            cols.append(0)
        else:
            MSET(S, "pool", [hT], hT[:, :, 0:1], 0.0)
        if t0 + SEG < T:
            rows.append((rows_ap(k, src, "in", r0 + SEG, 1), 1))
            cols.append(SEG + 1)
        else:
            MSET(S, "pool", [hT], hT[:, :, SEG + 1:SEG + 2], 0.0)
        if len(rows) == 2:
            make_hT(k, l, 2, cond, rows, xh, [hT[:, kc, 0:SEG + 2:SEG + 1] for kc in range(8)], hT, [])
        elif len(rows) == 1:
            c = cols[0]
            make_hT(k, l, 2, cond, rows, xh, [hT[:, kc, c:c + 1] for kc in range(8)], hT, [])
        state[si] = xts

    def ffn(si):
        T, cond, off, t0 = segs[si]
        hT = h2T[si % 2]
        r0 = off + t0
        xts = state.pop(si)
        for c in range(22):
            pg, pv = S.bank(), S.bank()
            MM(S, [w_up, hT], [pg], [(pg[:, 0:SEG + 2], w_up[:, kc, c * 128:(c + 1) * 128], hT[:, kc, :], kc == 0, kc == 7)
                                      for kc in range(8)])
            MM(S, [w_up, hT], [pv], [(pv[:, 0:SEG + 2], w_up[:, kc, DFF + c * 128:DFF + (c + 1) * 128], hT[:, kc, :], kc == 0, kc == 7)
                                      for kc in range(8)])
            g_, v_, t_ = cg[c % 2], cv[c % 2], th[c % 2]
            fw = k.fcw
            ACT(S, [pg, fw], [g_], g_[:, :], pg[:, 1:SEG + 1], AF.Identity, bias=fw[:, 3, c:c + 1], scale=fw[:, 1, c:c + 1])
            STT(S, "dve", [pg, fw, g_], [g_], g_[:, :], pg[:, 0:SEG], fw[:, 0, c:c + 1], g_[:, :], ALU.mult, ALU.add)
            STT(S, "dve", [pg, fw, g_], [g_], g_[:, :], pg[:, 2:SEG + 2], fw[:, 2, c:c + 1], g_[:, :], ALU.mult, ALU.add)
            cc = 22 + c
            ACT(S, [pv, fw], [v_], v_[:, :], pv[:, 1:SEG + 1], AF.Identity, bias=fw[:, 3, cc:cc + 1], scale=fw[:, 1, cc:cc + 1])
            STT(S, "dve", [pv, fw, v_], [v_], v_[:, :], pv[:, 0:SEG], fw[:, 0, cc:cc + 1], v_[:, :], ALU.mult, ALU.add)
            STT(S, "dve", [pv, fw, v_], [v_], v_[:, :], pv[:, 2:SEG + 2], fw[:, 2, cc:cc + 1], v_[:, :], ALU.mult, ALU.add)
            ACT(S, [g_], [t_], t_[:, :], g_[:, :], AF.Sigmoid)
            TT(S, "pool", [t_, g_], [t_], t_[:, :], t_[:, :], g_[:, :], ALU.mult)
            TT(S, "pool", [t_, v_], [actT], actT[:, c, :], t_[:, :], v_[:, :], ALU.mult)
        for j in range(SEG // 128):
            p0, p1 = S.bank(), S.bank()
            for hlf, pb in enumerate((p0, p1)):
                MM(S, [actT, w_dn], [pb], [(pb[:, :], actT[:, c, 128 * j:128 * (j + 1)], w_dn[:, c, hlf * 512:(hlf + 1) * 512],
                                              c == 0, c == 21) for c in range(22)])
            ob = xts[j]
            resid_ln(k, xts[j], (p0, p1), ob, rsm)
            db = S.dbuf(("xout", l, (r0 + 128 * j) // 128))
            S.dma("pool", rows_ap(k, dst, "out", r0 + 128 * j, 128), ob[:, :], reads=[ob], writes=[db])
            if dst is None:
                k.final_bufs.append(db)

    cur_cond = None
    prep(0)
    for si in range(len(segs)):
        if si + 1 < len(segs):
            prep(si + 1)
        if segs[si][1] != cur_cond:
            cur_cond = segs[si][1]
            gate_table(k, l, 5, cur_cond)
        ffn(si)
    S.release(m)


def layer(k, l):
    cfg = k.cfg
    load_layer_consts(k, l)
    mode = getattr(cfg, "mode", "full")
    if mode in ("full", "A"):
        phaseA(k, l)
    if mode in ("full", "B"):
        phaseB(k, l)


def shard_inputs(inp, cfg, core):
    f = lambda a: np.ascontiguousarray(np.asarray(a), dtype=np.float32)
    NP = cfg.NP
    cst, _, cst2, _ = make_consts()
    m = {
        "xs": f(inp["x_sample"][core]),
        "xp": f(inp["x_prompt"][NP * core:NP * (core + 1)]).reshape(NP * cfg.Tp, D),
        "st_c": f(inp["state_mlstm_c"][core]), "st_n": f(inp["state_mlstm_n"][core]),
        "st_m": f(inp["state_mlstm_m"][core]).reshape(2, 8), "st_s": f(inp["state_ssd"][core]),
        "cond": f(np.stack([np.asarray(inp["c"])[core], np.asarray(inp["c_ctx"])], 0)),
        "w_ada": f(inp["w_ada"]), "b_ada": f(inp["b_ada"]), "w_in": f(inp["w_in"]),
        "b_ig": f(inp["b_igate"]).reshape(2, 8), "b_fg": f(inp["b_fgate"]).reshape(2, 8),
        "mnorm_g": f(inp["mlstm_norm_g"]), "w_pool": f(inp["w_pool"]), "pool_scale": f(inp["pool_scale"]),
        "w_sp": f(inp["w_spatial"]), "b_sp": f(inp["b_spatial"]), "sconv_w": f(inp["ssd_conv_w"]),
        "sconv_b": f(inp["ssd_conv_b"]), "dt_bias": f(inp["ssd_dt_bias"]).reshape(2, 8),
        "a_log": f(inp["ssd_a_log"]).reshape(2, 8), "ssd_d": f(inp["ssd_d"]), "snorm_g": f(inp["ssd_norm_g"]),
        "w_out": f(inp["w_out"]), "ln1_g": f(inp["ln1_g"]), "ln1_b": f(inp["ln1_b"]), "w_up": f(inp["ffn_w_up"]),
        "fconv_w": f(inp["ffn_conv_w"]), "fconv_b": f(inp["ffn_conv_b"]), "w_dn": f(inp["ffn_w_down"]),
        "ln2_g": f(inp["ln2_g"]), "ln2_b": f(inp["ln2_b"]), "cst": cst, "cst2": cst2,
    }
    return m


def phaseA(k, l):
    S, I, cfg = k.S, k.I, k.cfg
    m = S.mark()
    w_in = S.sb("w_in", [128, 8, DIN], BF16)
    w_out = S.sb("w_out", [128, 8, D], BF16)
    m2 = S.mark()
    stage = [S.sb(f"wstageA{j}", [128, 2840]) for j in range(2)]
    k.wl_n = 0
    load_weight(k, w_in, I["w_in"][l], 8, DIN, stage)
    load_weight(k, w_out, I["w_out"][l], 8, D, stage)
    S.release(m2)
    c2b = load_cst2(k)
    S.dma("sp", k.lng[:, :], I["ln1_g"][l:l + 1, :].partition_broadcast(128), writes=[k.lng])
    S.dma("sp", k.lnb[:, :], I["ln1_b"][l:l + 1, :].partition_broadcast(128), writes=[k.lnb])
    alloc_hsm(k)
    rsm = alloc_rsm(k)
    a = K()
    a.w_in, a.w_out, a.c2b, a.rsm, a.l = w_in, w_out, c2b, rsm, l
    a.hb = [S.sb(f"hb{j}", [128, 8, 130], BF16) for j in range(3)]
    a.xq = [S.sb(f"xq{j}", [128, D]) for j in range(2)]
    a.pet = [S.sb(f"petile{j}", [128, D]) for j in range(2)] if l == 0 else None
    if l == 0:
        edb = S.dbuf("ED")
        for j in range(2):
            S.dma("sp", a.pet[j][0:64, 512:1024], k.ED[:, :], reads=[edb], writes=[a.pet[j]])
            S.dma("sp", a.pet[j][64:128, 512:1024], k.ED[:, :], reads=[edb], writes=[a.pet[j]])
    sb = S.sb
    a.G8, a.E8, a.SP8, a.igb = sb("G8", [128, 8]), sb("E8", [128, 8]), sb("SP8", [128, 8]), sb("igb", [128, 4])
    a.r8, a.logdec, a.cum, a.e8 = sb("r8", [128, 8]), sb("logdec", [128, 8]), sb("cum", [128, 8]), sb("e8", [128, 8])
    a.wend, a.aL, a.tmp8 = sb("wend", [128, 8]), sb("aL", [128, 8]), sb("tmp8", [128, 8])
    a.L1 = sb("L1", [128, 8, 128])
    a.DIFF = sb("DIFF", [128, 8, 128])
    a.qkT = sb("qkT", [128, 4, 128], BF16)
    a.PTm = sb("PTm", [128, 4, 128], BF16)
    a.PTs = sb("PTs", [128, 4, 128], BF16)
    a.k_tm = sb("k_tm", [128, 256], BF16)
    a.xt_m, a.xh_m = sb("xt_m", [128, 4, 66], BF16), sb("xh_m", [128, 4, 66], BF16)
    a.XBC, a.XBCe = sb("XBC", [128, 6, 128]), sb("XBCe", [128, 6, 128])
    a.XBCb = sb("XBCb", [128, 6, 128], BF16)
    a.x_tm, a.B_tm = sb("x_tm", [128, 256], BF16), sb("B_tm", [128, 2, 128], BF16)
    a.xt_s, a.xh_s = sb("xt_s", [128, 4, 64], BF16), sb("xh_s", [128, 4, 64], BF16)
    a.Cn, a.Cnb = sb("Cn", [128, 2, 65]), sb("Cnb", [128, 2, 66], BF16)
    a.Hs, a.Hsb = sb("Hs", [128, 4, 64]), sb("Hsb", [128, 4, 64], BF16)
    a.NUM = sb("NUM", [128, 4, 65])
    a.den = sb("den", [128, 4])
    a.Ysc = sb("Ysc", [128, 4, 64])
    a.HY = [sb(f"HY{j}", [128, 512]) for j in range(2)]
    a.HYf = [sb(f"HYf{j}", [128, 512]) for j in range(2)]
    a.eo, a.z_sb, a.ez, a.gu = sb("eo", [128, 256]), sb("z_sb", [128, 256]), sb("ez", [128, 256]), sb("gu", [128, 256])
    a.gvb = sb("gvb", [128, 256], BF16)
    a.pc, a.pcP, a.pcN = sb("pc", [128, 256]), sb("pcP", [8, 256]), sb("pcN", [8, 256])
    a.plb, a.plT = sb("plb", [128, 256], BF16), sb("plT", [128, 2, 128], BF16)
    a.fin1, a.fin2, a.fin3 = sb("fin1", [128, 256]), sb("fin2", [128, 256]), sb("fin3", [128, 256])
    a.st4, a.st4b = sb("st4", [128, 4]), sb("st4b", [128, 4])
    a.yall = sb("yall", [128, 3, 256], BF16)
    a.concatT = sb("concatT", [128, 8, 128], BF16)
    a.gst, a.gmv, a.gve, a.grs = sb("gst", [128, 6]), sb("gmv", [128, 2]), sb("gve", [128, 1]), sb("grs", [128, 1])
    a.mrun = sb("mrun", [4, 1])
    a.mt = sb("mt", [4, 2])
    a.dec = sb("dec", [128, 8])
    a.sio = sb("sio", [128, 4, 128])
    src, dst = seq_src_dst(k, l, "A")
    a.src, a.dst = src, dst
    cur_cond = None
    for si, (sname, T, cond, off) in enumerate(cfg.seqs):
        if cond != cur_cond:
            gate_table(k, l, 2, cond)
            cur_cond = cond
        runseq(k, a, si, T, cond, off)
    S.release(m)


def runseq(k, a, si, T, cond, off):
    S, cfg, l = k.S, k.cfg, a.l
    nt = T // 128
    is_sample = (si == 0)
    tile0 = off // 128
    w_in = a.w_in

    def hbuf(i):
        return a.hb[i % 3]

    def fix_halo(lo, hi):
        CP(S, "pool", [hbuf(hi)], [hbuf(lo)], hbuf(lo)[:, :, 129:130], hbuf(hi)[:, :, 1:2])
        CP(S, "pool", [hbuf(lo)], [hbuf(hi)], hbuf(hi)[:, :, 0:1], hbuf(lo)[:, :, 128:129])

    def ensure1(i):
        hb = hbuf(i)
        xb = a.xq[i % 2]
        pos = None
        if l == 0 and is_sample:
            pos = a.pet[i % 2]
            edb = S.dbuf("ED")
            S.dma("sp", pos[0:64, 0:512], k.ED[2 * i:2 * i + 1, :].partition_broadcast(64), reads=[edb], writes=[pos])
            S.dma("sp", pos[64:128, 0:512], k.ED[2 * i + 1:2 * i + 2, :].partition_broadcast(64), reads=[edb], writes=[pos])
        make_hT(k, l, 1, cond, [(rows_ap(k, a.src, "in", off + 128 * i, 128), 128)], xb,
                [hb[:, kc, 1:129] for kc in range(8)], hb, [], pos_tile=pos)
        db = S.dbuf(("HT", tile0 + i))
        S.dma("pool", k.HT[tile0 + i].rearrange("p (kc t) -> p kc t", kc=8), hb[:, :, 1:129], reads=[hb], writes=[db])
        if i == 0:
            MSET(S, "pool", [hb], hb[:, :, 0:1], 0.0)
        else:
            fix_halo(i - 1, i)
        if i == nt - 1:
            MSET(S, "pool", [hb], hb[:, :, 129:130], 0.0)

    def ensure2(i):
        hb = hbuf(i)
        db = S.dbuf(("HT", tile0 + i))
        S.dma("sp", hb[:, :, 1:129], k.HT[tile0 + i].rearrange("p (kc t) -> p kc t", kc=8), reads=[db], writes=[hb])
        xb = a.xq[i % 2]
        S.dma("sp", xb[:, :], rows_ap(k, a.src, "in", off + 128 * i, 128), writes=[xb])
        if l == 0 and is_sample:
            pos = a.pet[i % 2]
            edb = S.dbuf("ED")
            S.dma("sp", pos[0:64, 0:512], k.ED[2 * i:2 * i + 1, :].partition_broadcast(64), reads=[edb], writes=[pos])
            S.dma("sp", pos[64:128, 0:512], k.ED[2 * i + 1:2 * i + 2, :].partition_broadcast(64), reads=[edb], writes=[pos])
            TT(S, "pool", [xb, pos], [xb], xb[:, :], xb[:, :], pos[:, :], ALU.add)
        hf = a.HYf[i % 2]
        S.dma("sp", hf[:, :], k.HF[off + 128 * i:off + 128 * (i + 1), :], reads=[S.dbuf(("HF", tile0 + i))], writes=[hf])
        if i == nt - 1:
            MSET(S, "pool", [hb], hb[:, :, 129:130], 0.0)
        else:
            fix_halo(i, i + 1)
        if i == 0:
            MSET(S, "pool", [hb], hb[:, :, 0:1], 0.0)

    for d in ((0,) if getattr(cfg, "stop", 99) <= 4 else (0, 1)):
        init_state(k, a, si, d, is_sample)
        order = list(range(nt)) if d == 0 else list(range(nt - 1, -1, -1))
        ens = ensure1 if d == 0 else ensure2
        ens(order[0])
        for n, i in enumerate(order):
            if n + 1 < len(order):
                ens(order[n + 1])
            tileA(k, a, si, T, cond, off, i, d, nt, is_sample)
        if not is_sample and getattr(cfg, "stop", 99) > 5:
            final_state(k, a, si, d)


def init_state(k, a, si, d, is_sample):
    S, I, l = k.S, k.I, a.l
    if not is_sample:
        MSET(S, "pool", [a.Cn], a.Cn[:, :, :], 0.0)
        MSET(S, "pool", [a.Cnb], a.Cnb[:, :, :], 0.0)
        MSET(S, "pool", [a.Hs], a.Hs[:, :, :], 0.0)
        MSET(S, "pool", [a.Hsb], a.Hsb[:, :, :], 0.0)
        MSET(S, "pool", [a.mrun], a.mrun[:, :], 0.0)
        return
    for h in range(4):
        pr = slice((h % 2) * 64, (h % 2) * 64 + 64)
        S.dma("sp", a.Cn[pr, h // 2, 0:64], I["st_c"][l, d, h], writes=[a.Cn])
        S.dma("sp", a.Cn[pr, h // 2, 64:65], I["st_n"][l, d, h].rearrange("(p o) -> p o", o=1), writes=[a.Cn])
    S.dma("sp", a.st4[:, :], I["st_m"][l:l + 1, 4 * d:4 * d + 4].partition_broadcast(128), writes=[a.st4])
    ACT(S, [a.st4], [a.st4b], a.st4b[:, :], a.st4[:, :], AF.Exp)
    for h in range(4):
        pr = slice((h % 2) * 64, (h % 2) * 64 + 64)
        TS(S, "dve", [a.Cn, a.st4b], [a.Cn], a.Cn[pr, h // 2, :], a.Cn[pr, h // 2, :], a.st4b[pr, h:h + 1], None, ALU.mult)
    CP(S, "pool", [a.Cn], [a.Cnb], a.Cnb[:, :, 0:65], a.Cn[:, :, :])
    S.dma("sp", a.sio[0:64, :, :], I["st_s"][l, d].rearrange("h p n -> p h n"), writes=[a.sio])
    pb = S.bank()
    TR(S, [a.sio, k.cstb], [pb], [(pb[:, h * 64:(h + 1) * 64], a.sio[0:64, h, :], cview(k, "ident")[0:64, 0:64]) for h in range(4)])
    CP(S, "dve", [pb], [a.Hs], a.Hs[:, :, :], pb[:, 0:256].rearrange("p (h q) -> p h q", h=4))
    CP(S, "act", [pb], [a.Hsb], a.Hsb[:, :, :], pb[:, 0:256].rearrange("p (h q) -> p h q", h=4))


def final_state(k, a, si, d):
    S, O, l = k.S, k.O, a.l
    j = si - 1
    dg = a.tmp8
    TS(S, "dve", [k.cstb, a.mrun], [dg], dg[0:4, 0:4], cview(k, "ident")[0:4, 0:4], a.mrun[0:4, 0:1], None, ALU.mult)
    pb = S.bank()
    MM(S, [dg, k.cstb], [pb], [(pb[:, 0:4], cview(k, "ones")[0:4, :], dg[0:4, 0:4], True, True)])
    ACT(S, [pb], [a.st4b], a.st4b[:, :], pb[:, 0:4], AF.Exp, scale=-1.0)
    stg = a.sio
    sv = stg[:, 0:2, 0:65]
    for h in range(4):
        pr = slice((h % 2) * 64, (h % 2) * 64 + 64)
        TS(S, "dve", [a.Cn, a.st4b], [stg], stg[pr, h // 2, 0:65], a.Cn[pr, h // 2, :], a.st4b[pr, h:h + 1], None, ALU.mult)
    outs = []
    for h in range(4):
        pr = slice((h % 2) * 64, (h % 2) * 64 + 64)
        db = S.dbuf(("oc", j, l, d, h))
        S.dma("pool", O["oc"][j, l, d, h], stg[pr, h // 2, 0:64], reads=[stg], writes=[db])
        db2 = S.dbuf(("on", j, l, d, h))
        S.dma("pool", O["on"][j, l, d, h].rearrange("(p o) -> p o", o=1), stg[pr, h // 2, 64:65], reads=[stg], writes=[db2])
        outs += [db, db2]
    db = S.dbuf(("om", j, l, d))
    S.dma("pool", O["om"][j, l, 4 * d:4 * d + 4].rearrange("(p o) -> p o", o=1), a.mrun[0:4, 0:1], reads=[a.mrun], writes=[db])
    outs.append(db)
    pb2 = S.bank()
    TR(S, [a.Hs, k.cstb], [pb2], [(pb2[0:64, h * 128:(h + 1) * 128], a.Hs[:, h, :], cview(k, "ident")) for h in range(4)])
    CP(S, "dve", [pb2, stg], [stg], stg[0:64, :, :], pb2[0:64, :].rearrange("p (h n) -> p h n", h=4))
    db = S.dbuf(("os", j, l, d))
    S.dma("pool", O["os"][j, l, d].rearrange("h p n -> p h n"), stg[0:64, :, :], reads=[stg], writes=[db])
    outs.append(db)
    k.final_bufs += outs


def tileA(k, a, si, T, cond, off, i, d, nt, is_sample):
    S, l = k.S, a.l
    w_in = a.w_in
    hb = a.hb[i % 3]
    hcur = lambda kc: hb[:, kc, 1:129]
    tri = cview(k, "tri%d" % d)
    neg = cview(k, "neg%d" % d)
    endc = 127 if d == 0 else 0
    full = (d == 1)
    cst = k.cstb

    ps1, ps2 = S.bank(), S.bank()
    MM(S, [hb, w_in], [ps1], [(ps1[:, 0:512], hcur(kc), w_in[:, kc, 256:768], kc == 0, kc == 7) for kc in range(8)])
    MM(S, [hb, w_in], [ps2], [(ps2[:, 0:272], hcur(kc), w_in[:, kc, 768:1040], kc == 0, kc == 7) for kc in range(8)]
       + [(ps2[:, 272:280], hcur(kc), w_in[:, kc, 2832:2840], kc == 0, kc == 7) for kc in range(8)])
    G8, E8, SP8, igb, r8, logdec, cum, e8, wend, aL, tmp8 = (a.G8, a.E8, a.SP8, a.igb, a.r8, a.logdec, a.cum, a.e8,
                                                             a.wend, a.aL, a.tmp8)
    STT(S, "dve", [ps2, k.bif], [G8], G8[:, 0:4], ps2[:, 264 + 4 * d:268 + 4 * d], -1.0, k.bif[:, 8 + 4 * d:12 + 4 * d], ALU.mult, ALU.add)
    TT(S, "dve", [ps2, k.dtb], [G8], G8[:, 4:8], ps2[:, 272 + 4 * d:276 + 4 * d], k.dtb[:, 4 * d:4 * d + 4], ALU.add)
    TT(S, "dve", [ps2, k.bif], [igb], igb[:, :], ps2[:, 256 + 4 * d:260 + 4 * d], k.bif[:, 4 * d:4 * d + 4], ALU.add)
    if full:
        ACT(S, [ps2], [a.eo], a.eo[:, :], ps2[:, 0:256], AF.Exp, scale=-1.0)
    ACT(S, [G8], [E8], E8[:, :], G8[:, :], AF.Exp)
    ACT(S, [E8], [SP8], SP8[:, :], E8[:, :], AF.Ln, bias=1.0)
    ACT(S, [igb], [r8], r8[:, 0:4], igb[:, :], AF.Exp)
    CP(S, "pool", [SP8], [r8], r8[:, 4:8], SP8[:, 4:8])
    TT(S, "dve", [SP8, k.coef], [logdec], logdec[:, :], SP8[:, :], k.coef[:, d, :], ALU.mult)
    CP(S, "pool", [logdec], [a.L1], a.L1[:, :, :], bc(logdec[:, 0:8].unsqueeze(2), [128, 8, 128]))
    psL = [S.bank(), S.bank()]
    for hh in range(2):
        MM(S, [a.L1, cst], [psL[hh]], [(psL[hh][:, q * 128:(q + 1) * 128], a.L1[:, hh * 4 + q, :], tri, True, True) for q in range(4)])
    psC = S.bank()
    MM(S, [logdec, cst], [psC], [(psC[:, 0:8], tri, logdec[:, 0:8], True, True)])
    CP(S, "dve", [psC], [cum], cum[:, :], psC[:, 0:8])
    for h in range(8):
        pl = psL[h // 4]
        q = h % 4
        STT(S, "dve", [pl, cum, cst], [a.DIFF], a.DIFF[:, h, :], pl[:, q * 128:(q + 1) * 128], cum[:, h:h + 1], neg, ALU.subtract, ALU.add)
    ACT(S, [a.DIFF], [a.DIFF], a.DIFF[:, :, :], a.DIFF[:, :, :], AF.Exp)
    ACT(S, [cum], [e8], e8[:, :], cum[:, :], AF.Exp)
    for hh in range(2):
        TT(S, "dve", [psL[hh], cum], [tmp8], tmp8[:, hh * 4:hh * 4 + 4], psL[hh][:, endc:512:128], cum[:, hh * 4:hh * 4 + 4], ALU.subtract)
        ACT(S, [psL[hh]], [aL], aL[:, hh * 4:hh * 4 + 4], psL[hh][:, endc:512:128], AF.Exp)
    ACT(S, [tmp8], [wend], wend[:, :], tmp8[:, :], AF.Exp)
    TT(S, "dve", [wend, r8], [wend], wend[:, :], wend[:, :], r8[:, :], ALU.mult)
    if not is_sample:
        TT(S, "dve", [tmp8, igb], [a.dec], a.dec[:, 0:4], tmp8[:, 0:4], igb[:, :], ALU.add)
        TT(S, "dve", [tmp8, cum], [a.dec], a.dec[:, 4:8], tmp8[:, 0:4], cum[:, 0:4], ALU.add)
        pm = S.bank()
        TR(S, [a.dec, cst], [pm], [(pm[0:4, 0:128], a.dec[:, 0:4], cview(k, "ident")),
                                   (pm[0:4, 128:256], a.dec[:, 4:8], cview(k, "ident"))])
        S.op("dve", lambda e: e.tensor_reduce(a.mt[0:4, 0:1], pm[0:4, 0:128], AX.X, ALU.max), [pm], [a.mt])
        TT(S, "dve", [pm, a.mrun], [a.mt], a.mt[0:4, 1:2], pm[0:4, 128:129], a.mrun[0:4, 0:1], ALU.add)
        TT(S, "dve", [a.mt], [a.mrun], a.mrun[0:4, 0:1], a.mt[0:4, 0:1], a.mt[0:4, 1:2], ALU.max)

    if getattr(k.cfg, "stop", 99) <= 1:
        return
    psQ = [S.bank(), S.bank()]
    for hh in range(2):
        MM(S, [hb, w_in], [psQ[hh]], [(psQ[hh][:, q * 130:(q + 1) * 130], w_in[:, kc, (hh * 2 + q) * 128:(hh * 2 + q + 1) * 128],
                                        hb[:, kc, 0:130], kc == 0, kc == 7) for q in range(2) for kc in range(8)])
    if getattr(k.cfg, "stop", 99) <= 1.05:
        return
    qkT = a.qkT
    CP(S, "act", [psQ[0]], [qkT], qkT[:, 0:2, :], psQ[0][:, 0:260].rearrange("p (b t) -> p b t", b=2)[:, :, 1:129])
    ACT(S, [psQ[1]], [qkT], qkT[:, 2:4, :], psQ[1][:, 0:260].rearrange("p (b t) -> p b t", b=2)[:, :, 1:129], AF.Identity, scale=0.125)
    ACT(S, [ps1], [a.k_tm], a.k_tm[:, :], ps1[:, 0:256], AF.Identity, scale=0.125)
    if getattr(k.cfg, "stop", 99) <= 1.1:
        return
    v4 = ps1[:, 256:512].rearrange("p (h e) -> p h e", h=4)
    TT(S, "dve", [ps1, r8], [a.xt_m], a.xt_m[:, :, 0:64], v4, bc(r8[:, 0:4].unsqueeze(2), [128, 4, 64]), ALU.mult)
    if getattr(k.cfg, "stop", 99) <= 1.12:
        return
    CP(S, "pool", [r8], [a.xt_m], a.xt_m[:, :, 64:65], r8[:, 0:4].unsqueeze(2))
    if getattr(k.cfg, "stop", 99) <= 1.15:
        return
    EXP = getattr(k.cfg, "exp", "")
    if EXP != "noTT":
        TT(S, "dve", [ps1, wend], [a.xh_m], a.xh_m[:, :, 0:64], v4, bc((r8 if EXP == "r8" else wend)[:, 0:4].unsqueeze(2), [128, 4, 64]), ALU.mult)
    if EXP != "noCP":
        CP(S, "pool", [wend], [a.xh_m], a.xh_m[:, :, 64:65], wend[:, 0:4].unsqueeze(2))
    if getattr(k.cfg, "stop", 99) <= 1.2:
        return
    psS = [S.bank(), S.bank()]
    hp = lambda h: slice((h % 2) * 64, (h % 2) * 64 + 64)
    for par in range(2):
        MM(S, [qkT], [psS[par]], [(psS[par][:, (h // 2) * 128:(h // 2 + 1) * 128], qkT[hp(h), 2 + h // 2, :], qkT[hp(h), h // 2, :], True, True)
                                  for h in (par, par + 2)])
    for par in range(2):
        TT(S, "dve", [psS[par], a.DIFF], [a.PTm], a.PTm[:, par:4:2, :], psS[par][:, 0:256].rearrange("p (h t) -> p h t", h=2),
           a.DIFF[:, par:4:2, :], ALU.mult)
    if getattr(k.cfg, "stop", 99) <= 1.4:
        return
    psO = S.bank()
    psI = [S.bank(), S.bank()]
    MM(S, [a.PTm, a.xt_m], [psO], [(psO[:, h * 65:h * 65 + 65], a.PTm[:, h, :], a.xt_m[:, h, 0:65], True, True) for h in range(4)])
    for par in range(2):
        MM(S, [qkT, a.Cnb], [psI[par]], [(psI[par][:, (h // 2) * 65:(h // 2) * 65 + 65], qkT[hp(h), h // 2, :], a.Cnb[hp(h), h // 2, 0:65], True, True)
                                         for h in (par, par + 2)])
    NUM = a.NUM
    for par in range(2):
        TT(S, "dve", [psI[par], e8], [NUM], NUM[:, par:4:2, :], psI[par][:, 0:130].rearrange("p (h e) -> p h e", h=2),
           bc(e8[:, par:4:2].unsqueeze(2), [128, 2, 65]), ALU.mult)
    TT(S, "dve", [psO, NUM], [NUM], NUM[:, :, :], psO[:, 0:260].rearrange("p (h e) -> p h e", h=4), NUM[:, :, :], ALU.add)
    if getattr(k.cfg, "stop", 99) <= 1.6:
        return
    HY = a.HY[i % 2]
    ACT(S, [NUM], [a.den], a.den[:, :].unsqueeze(2), NUM[:, :, 64:65], AF.Abs)
    TS(S, "dve", [a.den], [a.den], a.den[:, :], a.den[:, :], 1.0, None, ALU.max)
    TT(S, "pool", [a.den, k.cm1], [a.den], a.den[:, :], a.den[:, :], bc(k.cm1[:, 0:1], [128, 4]), ALU.pow)
    TT(S, "pool", [NUM, a.den], [HY], HY[:, 0:256].rearrange("p (h e) -> p h e", h=4), NUM[:, :, 0:64],
       bc(a.den[:, :].unsqueeze(2), [128, 4, 64]), ALU.mult)
    if getattr(k.cfg, "stop", 99) <= 1.8:
        return
    psU = S.bank()
    MM(S, [a.k_tm, a.xh_m], [psU], [(psU[:, h * 65:h * 65 + 65], a.k_tm[:, (h // 2) * 128:(h // 2 + 1) * 128], a.xh_m[:, h, 0:65], True, True)
                                     for h in range(4)])
    for h in range(4):
        STT(S, "dve", [a.Cn, aL, psU], [a.Cn], a.Cn[hp(h), h // 2, :], a.Cn[hp(h), h // 2, :], aL[hp(h), h:h + 1],
            psU[hp(h), h * 65:h * 65 + 65], ALU.mult, ALU.add)
    CP(S, "pool", [a.Cn], [a.Cnb], a.Cnb[:, :, 0:65], a.Cn[:, :, :])

    if getattr(k.cfg, "stop", 99) <= 2:
        return
    psX = [S.bank(), S.bank()]
    for hh in range(2):
        MM(S, [hb, w_in], [psX[hh]], [(psX[hh][:, q * 130:q * 130 + 130], w_in[:, kc, 2064 + (hh * 3 + q) * 128:2064 + (hh * 3 + q + 1) * 128],
                                        hb[:, kc, 0:130], kc == 0, kc == 7) for q in range(3) for kc in range(8)])
    XBC, XBCe, XBCb = a.XBC, a.XBCe, a.XBCb
    for b in range(6):
        pb, c0 = psX[b // 3], (b % 3) * 130
        ACT(S, [pb, k.scw], [XBC], XBC[:, b, :], pb[:, c0 + 1:c0 + 129], AF.Identity, bias=k.scw[:, 3, b:b + 1], scale=k.scw[:, 1, b:b + 1])
        STT(S, "dve", [pb, k.scw, XBC], [XBC], XBC[:, b, :], pb[:, c0:c0 + 128], k.scw[:, 0, b:b + 1], XBC[:, b, :], ALU.mult, ALU.add)
        STT(S, "dve", [pb, k.scw, XBC], [XBC], XBC[:, b, :], pb[:, c0 + 2:c0 + 130], k.scw[:, 2, b:b + 1], XBC[:, b, :], ALU.mult, ALU.add)
    ACT(S, [XBC], [XBCe], XBCe[:, :, :], XBC[:, :, :], AF.Exp, scale=-1.0)
    TS(S, "pool", [XBCe], [XBCe], XBCe[:, :, :], XBCe[:, :, :], 1.0, None, ALU.add)
    TT(S, "pool", [XBCe, k.cm1], [XBCe], XBCe[:, :, :], XBCe[:, :, :], bc(k.cm1[:, 0:1].unsqueeze(2), [128, 6, 128]), ALU.pow)
    TT(S, "pool", [XBC, XBCe], [XBCb], XBCb[:, :, :], XBC[:, :, :], XBCe[:, :, :], ALU.mult)
    psT = S.bank()
    pTv = bview(psT, BF16)
    TR(S, [XBCb, k.identb], [psT], [(pTv[:, b * 128:(b + 1) * 128], XBCb[:, b, :], k.identb[:, :]) for b in range(4)])
    CP(S, "act", [psT], [a.x_tm], a.x_tm[:, :], pTv[:, 0:256])
    CP(S, "act", [psT], [a.B_tm], a.B_tm[:, :, :], pTv[:, 256:512].rearrange("p (g n) -> p g n", g=2))
    x4 = a.x_tm[:, :].rearrange("p (h e) -> p h e", h=4)
    TT(S, "pool", [a.x_tm, r8], [a.xt_s], a.xt_s[:, :, :], x4, bc(r8[:, 4:8].unsqueeze(2), [128, 4, 64]), ALU.mult)
    TT(S, "pool", [a.x_tm, wend], [a.xh_s], a.xh_s[:, :, :], x4, bc(wend[:, 4:8].unsqueeze(2), [128, 4, 64]), ALU.mult)
    psS2 = S.bank()
    MM(S, [XBCb], [psS2], [(psS2[:, g * 128:(g + 1) * 128], XBCb[:, 2 + g, :], XBCb[:, 4 + g, :], True, True) for g in range(2)])
    for g in range(2):
        TT(S, "dve", [psS2, a.DIFF], [a.PTs], a.PTs[:, 2 * g:2 * g + 2, :],
           bc(psS2[:, g * 128:(g + 1) * 128].unsqueeze(1), [128, 2, 128]), a.DIFF[:, 4 + 2 * g:6 + 2 * g, :], ALU.mult)
    psY = S.bank()
    MM(S, [a.PTs, a.xt_s, XBCb, a.Hsb], [psY],
       [(psY[:, h * 64:(h + 1) * 64], a.PTs[:, h, :], a.xt_s[:, h, :], True, True) for h in range(4)]
       + [(psY[:, 256 + g * 128:256 + (g + 1) * 128], XBCb[:, 4 + g, :], a.Hsb[:, 2 * g:2 * g + 2, :].rearrange("p h e -> p (h e)"), True, True)
          for g in range(2)])
    Ysc = a.Ysc
    TT(S, "dve", [psY, e8], [Ysc], Ysc[:, :, :], psY[:, 256:512].rearrange("p (h e) -> p h e", h=4),
       bc(e8[:, 4:8].unsqueeze(2), [128, 4, 64]), ALU.mult)
    TT(S, "dve", [psY, Ysc], [HY], HY[:, 256:512], psY[:, 0:256], Ysc[:, :, :].rearrange("p h e -> p (h e)"), ALU.add)
    psU2 = S.bank()
    MM(S, [a.B_tm, a.xh_s], [psU2], [(psU2[:, g * 128:(g + 1) * 128], a.B_tm[:, g, :], a.xh_s[:, 2 * g:2 * g + 2, :].rearrange("p h e -> p (h e)"),
                                       True, True) for g in range(2)])
    TT(S, "dve", [a.Hs, aL], [a.Hs], a.Hs[:, :, :], a.Hs[:, :, :], bc(aL[:, 4:8].unsqueeze(2), [128, 4, 64]), ALU.mult)
    TT(S, "dve", [a.Hs, psU2], [a.Hs], a.Hs[:, :, :], psU2[:, 0:256].rearrange("p (h e) -> p h e", h=4), a.Hs[:, :, :], ALU.add)
    CP(S, "pool", [a.Hs], [a.Hsb], a.Hsb[:, :, :], a.Hs[:, :, :])

    if getattr(k.cfg, "stop", 99) <= 3:
        return
    tile_g = (off // 128) + i
    if not full:
        db = S.dbuf(("HF", tile_g))
        S.dma("pool", k.HF[off + 128 * i:off + 128 * (i + 1), :], HY[:, :], reads=[HY], writes=[db])
        return
    finalizeA(k, a, si, T, cond, off, i, nt, ps1, ps2, HY)


def finalizeA(k, a, si, T, cond, off, i, nt, ps1, ps2, HY):
    S, l = k.S, a.l
    w_in, w_out = a.w_in, a.w_out
    hb = a.hb[i % 3]
    hcur = lambda kc: hb[:, kc, 1:129]
    cst = k.cstb
    HYf = a.HYf[i % 2]
    f1, f2, f3 = a.fin1, a.fin2, a.fin3
    v4 = lambda ap: ap.rearrange("p (h e) -> p h e", h=4)
    ym, yg, ys = a.yall[:, 0, :], a.yall[:, 1, :], a.yall[:, 2, :]

    ps3, ps4 = S.bank(), S.bank()
    MM(S, [hb, w_in], [ps3], [(ps3[:, 0:512], hcur(kc), w_in[:, kc, 1296:1808], kc == 0, kc == 7) for kc in range(8)])
    MM(S, [hb, w_in], [ps4], [(ps4[:, 0:256], hcur(kc), w_in[:, kc, 1808:2064], kc == 0, kc == 7) for kc in range(8)]
       + [(ps4[:, 256:512], hcur(kc), w_in[:, kc, 1040:1296], kc == 0, kc == 7) for kc in range(8)])
    has_p, has_n = i > 0, i < nt - 1
    ps5 = S.bank()
    mm5 = []
    if has_p:
        hp_ = a.hb[(i - 1) % 3]
        mm5 += [(ps5[0:8, 0:256], hp_[:, kc, 121:129], w_in[:, kc, 1040:1296], kc == 0, kc == 7) for kc in range(8)]
    if has_n:
        hn_ = a.hb[(i + 1) % 3]
        mm5 += [(ps5[0:8, 256:512], hn_[:, kc, 1:9], w_in[:, kc, 1040:1296], kc == 0, kc == 7) for kc in range(8)]
    if mm5:
        rd = [w_in] + ([a.hb[(i - 1) % 3]] if has_p else []) + ([a.hb[(i + 1) % 3]] if has_n else [])
        MM(S, rd, [ps5], mm5)

    TT(S, "pool", [HY, HYf], [f1], f1[:, :], HY[:, 0:256], HYf[:, 0:256], ALU.add)
    S.op("dve", lambda e: e.tensor_reduce(a.st4[:, :], v4(f1[:, :]), AX.X, ALU.add), [f1], [a.st4])
    TS(S, "dve", [a.st4], [a.st4], a.st4[:, :], a.st4[:, :], 1.0 / 64.0, None, ALU.mult)
    TT(S, "pool", [f1, a.st4], [f1], v4(f1[:, :]), v4(f1[:, :]), bc(a.st4[:, :].unsqueeze(2), [128, 4, 64]), ALU.subtract)
    TT(S, "pool", [f1], [f2], f2[:, :], f1[:, :], f1[:, :], ALU.mult)
    S.op("dve", lambda e: e.tensor_reduce(a.st4b[:, :], v4(f2[:, :]), AX.X, ALU.add), [f2], [a.st4b])
    TS(S, "dve", [a.st4b], [a.st4b], a.st4b[:, :], a.st4b[:, :], 1.0 / 64.0, EPS, ALU.mult, ALU.add)
    TT(S, "pool", [a.st4b, k.cm05], [a.st4b], a.st4b[:, :], a.st4b[:, :], bc(k.cm05[:, 0:1], [128, 4]), ALU.pow)
    TT(S, "pool", [f1, a.st4b], [f1], v4(f1[:, :]), v4(f1[:, :]), bc(a.st4b[:, :].unsqueeze(2), [128, 4, 64]), ALU.mult)
    TT(S, "pool", [f1, k.mng], [f1], f1[:, :], f1[:, :], k.mng[:, :], ALU.mult)
    TS(S, "pool", [a.eo], [a.eo], a.eo[:, :], a.eo[:, :], 1.0, None, ALU.add)
    TT(S, "pool", [a.eo, k.cm1], [a.eo], a.eo[:, :], a.eo[:, :], bc(k.cm1[:, 0:1], [128, 256]), ALU.pow)
    TT(S, "pool", [f1, a.eo], [a.yall], ym, f1[:, :], a.eo[:, :], ALU.mult)

    CP(S, "act", [ps4], [a.z_sb], a.z_sb[:, :], ps4[:, 0:256])
    ACT(S, [ps4], [a.ez], a.ez[:, :], ps4[:, 0:256], AF.Exp, scale=-1.0)
    TT(S, "pool", [HY, HYf], [f2], f2[:, :], HY[:, 256:512], HYf[:, 256:512], ALU.add)
    TT(S, "pool", [a.x_tm, k.dsk], [f3], v4(f3[:, :]), v4(a.x_tm[:, :]), bc(k.dsk[:, :].unsqueeze(2), [128, 4, 64]), ALU.mult)
    TT(S, "pool", [f2, f3], [f2], f2[:, :], f2[:, :], f3[:, :], ALU.add)
    TS(S, "pool", [a.ez], [a.ez], a.ez[:, :], a.ez[:, :], 1.0, None, ALU.add)
    TT(S, "pool", [a.ez, k.cm1], [a.ez], a.ez[:, :], a.ez[:, :], bc(k.cm1[:, 0:1], [128, 256]), ALU.pow)
    TT(S, "pool", [a.ez, a.z_sb], [a.ez], a.ez[:, :], a.ez[:, :], a.z_sb[:, :], ALU.mult)
    TT(S, "pool", [f2, a.ez], [f2], f2[:, :], f2[:, :], a.ez[:, :], ALU.mult)
    TT(S, "pool", [f2], [f3], f3[:, :], f2[:, :], f2[:, :], ALU.mult)
    S.op("dve", lambda e: e.tensor_reduce(a.st4[:, 0:2], f3[:, :].rearrange("p (g e) -> p g e", g=2), AX.X, ALU.add), [f3], [a.st4])
    TS(S, "dve", [a.st4], [a.st4], a.st4[:, 0:2], a.st4[:, 0:2], 1.0 / 128.0, EPS, ALU.mult, ALU.add)
    TT(S, "pool", [a.st4, k.cm05], [a.st4], a.st4[:, 0:2], a.st4[:, 0:2], bc(k.cm05[:, 0:1], [128, 2]), ALU.pow)
    TT(S, "pool", [f2, a.st4], [f2], f2[:, :].rearrange("p (g e) -> p g e", g=2), f2[:, :].rearrange("p (g e) -> p g e", g=2),
       bc(a.st4[:, 0:2].unsqueeze(2), [128, 2, 128]), ALU.mult)
    TT(S, "pool", [f2, k.sng], [a.yall], ys, f2[:, :], k.sng[:, :], ALU.mult)

    CP(S, "act", [ps3], [a.gu], a.gu[:, :], ps3[:, 0:256])
    S.op("dve", lambda e: e.bn_stats(a.gst[:, :], ps3[:, 256:512]), [ps3], [a.gst])
    S.op("dve", lambda e: e.bn_aggr(a.gmv[:, :], a.gst[:, :]), [a.gst], [a.gmv])
    TS(S, "dve", [a.gmv], [a.gve], a.gve[:, :], a.gmv[:, 1:2], EPS, None, ALU.add)
    TT(S, "pool", [a.gve, k.cm05], [a.grs], a.grs[:, :], a.gve[:, :], k.cm05[:, :], ALU.pow)
    TS(S, "dve", [ps3, a.gmv, a.grs], [a.gvb], a.gvb[:, :], ps3[:, 256:512], a.gmv[:, 0:1], a.grs[:, 0:1], ALU.subtract, ALU.mult)
    psG = S.bank()
    MM(S, [k.wsT, a.gvb], [psG], [(psG[:, h * 64:(h + 1) * 64], k.wsT[:, h, :], a.gvb[:, h * 64:(h + 1) * 64], True, True) for h in range(4)])
    TT(S, "dve", [psG, k.bsT], [f3], v4(f3[:, :]), v4(psG[:, 0:256]), bc(k.bsT[:, :].unsqueeze(2), [128, 4, 64]), ALU.add)
    TT(S, "pool", [f3, a.gu], [a.yall], yg, f3[:, :], a.gu[:, :], ALU.mult)

    CP(S, "act", [ps4], [a.pc], a.pc[:, :], ps4[:, 256:512])
    if has_p:
        CP(S, "act", [ps5], [a.pcP], a.pcP[:, :], ps5[0:8, 0:256])
    if has_n:
        CP(S, "act", [ps5], [a.pcN], a.pcN[:, :], ps5[0:8, 256:512])
    var = "int" if (has_p and has_n) else ("first" if has_n else ("last" if has_p else "int"))
    psP = S.bank()
    mmp = []
    for g in range(4):
        o_ = psP[:, g * 64:(g + 1) * 64]
        seqm = [(cview(k, f"pA{g}{var}"), a.pc[:, g * 64:(g + 1) * 64])]
        if has_p:
            seqm.append((cview(k, f"pP{g}", 8), a.pcP[0:8, g * 64:(g + 1) * 64]))
        if has_n:
            seqm.append((cview(k, f"pN{g}", 8), a.pcN[0:8, g * 64:(g + 1) * 64]))
        for n_, (lh, rh) in enumerate(seqm):
            mmp.append((o_, lh, rh, n_ == 0, n_ == len(seqm) - 1))
    MM(S, [a.c2b, a.pc, a.pcP, a.pcN], [psP], mmp)
    CP(S, "act", [psP], [a.plb], a.plb[:, :], psP[:, 0:256])
    psT2 = S.bank()
    t2v = bview(psT2, BF16)
    TR(S, [a.plb, k.identb], [psT2], [(t2v[:, j * 128:(j + 1) * 128], a.plb[:, j * 128:(j + 1) * 128], k.identb[:, :]) for j in range(2)])
    CP(S, "dve", [psT2], [a.plT], a.plT[:, :, :], t2v[:, 0:256].rearrange("p (j t) -> p j t", j=2))
    psW = S.bank()
    MM(S, [k.wpb, a.plT], [psW], [(psW[:, j * 128:(j + 1) * 128], k.wpb[:, j, :], a.plT[:, j, :], True, True) for j in range(2)])
    cT = a.concatT
    for j in range(2):
        ACT(S, [psW, k.psc], [cT], cT[:, 2 + j, :], psW[:, j * 128:(j + 1) * 128], AF.Identity, scale=k.psc[:, j:j + 1])

    psT3 = S.bank()
    t3v = bview(psT3, BF16)
    TR(S, [a.yall, k.identb], [psT3], [(t3v[:, (m3 * 2 + j) * 128:(m3 * 2 + j + 1) * 128], a.yall[:, m3, j * 128:(j + 1) * 128], k.identb[:, :])
                                      for m3 in range(3) for j in range(2)])
    CP(S, "dve", [psT3], [cT], cT[:, 0:2, :], t3v[:, 0:256].rearrange("p (j t) -> p j t", j=2))
    CP(S, "act", [psT3], [cT], cT[:, 4:8, :], t3v[:, 256:768].rearrange("p (j t) -> p j t", j=4))

    p0, p1 = S.bank(), S.bank()
    for hlf, pb in enumerate((p0, p1)):
        MM(S, [cT, w_out], [pb], [(pb[:, :], cT[:, kc, :], w_out[:, kc, hlf * 512:(hlf + 1) * 512], kc == 0, kc == 7) for kc in range(8)])
    xb = a.xq[i % 2]
    resid_ln(k, xb, (p0, p1), xb, a.rsm)
    r0 = off + 128 * i
    db = S.dbuf(("xoutA", l, r0 // 128))
    S.dma("pool", rows_ap(k, a.dst, "out", r0, 128), xb[:, :], reads=[xb], writes=[db])
    if a.dst is None:
        k.final_bufs.append(db)


_CACHE = {}


def gather_outputs(results, cfg, n):
    NP, Tp, Ts = cfg.NP, cfg.Tp, cfg.Ts
    y_p = np.concatenate([r["yp"].reshape(NP, Tp, D) for r in results], 0)
    y_s = np.stack([r["ys"].reshape(Ts, D) for r in results], 0)
    oc = np.concatenate([r["oc"] for r in results], 0)
    on = np.concatenate([r["on"] for r in results], 0)
    om = np.concatenate([r["om"].reshape(NP, 2, 2, 4) for r in results], 0)
    os_ = np.concatenate([r["os"] for r in results], 0)
    f = lambda a: np.ascontiguousarray(a, dtype=np.float32)
    return (f(y_p), f(y_s), f(oc), f(on), f(om), f(os_))


def kernel(**inputs):
    n = 8
    xs = np.asarray(inputs["x_sample"])
    xp = np.asarray(inputs["x_prompt"])
    cfg = Cfg(Ts=xs.shape[1], NP=xp.shape[0] // n, Tp=xp.shape[1], L=2)
    key = (cfg.Ts, cfg.NP, cfg.Tp)
    if key not in _CACHE:
        _CACHE[key] = build(cfg)
    nc, _ = _CACHE[key]
    in_maps = [shard_inputs(inputs, cfg, c) for c in range(n)]
    res = run_bass_kernel_spmd(nc, in_maps, core_ids=list(range(n)))
    return gather_outputs(res.results, cfg, n)
```

```python
import math
import numpy as np
import ml_dtypes
from contextlib import ExitStack
import concourse.bass as bass
import concourse.mybir as mybir
from concourse.bass_utils import run_bass_kernel_spmd

F32 = mybir.dt.float32
BF16 = mybir.dt.bfloat16
AF = mybir.ActivationFunctionType
ALU = mybir.AluOpType
AX = mybir.AxisListType
DTSZ = {F32: 4, BF16: 2}

D = 1024
DIN = 2840
DFF = 2816
EPS = 1e-5
ALPHA = 4.0 ** 0.25
NEG = -30000.0

ENGS = ("pe", "act", "dve", "pool", "sp")
EPOCH = 30000
NDMA_SEM = 8


def prod(l):
    r = 1
    for x in l:
        r *= int(x)
    return r


class Buf:
    __slots__ = ("name", "v", "last_w", "readers", "off", "excl")

    def __init__(self, name, v, floor=None):
        self.off = -1
        self.excl = False
        self.name = name
        self.v = v
        self.last_w = floor
        self.readers = []

    def __getitem__(self, k):
        return self.v[k]


class Op:
    __slots__ = ("eng", "fn", "deps", "is_dma", "idx", "sig", "has_dep", "vc", "name", "cost")

    def __init__(self, eng, fn, is_dma, name):
        self.cost = 0.4
        self.eng = eng
        self.fn = fn
        self.is_dma = is_dma
        self.deps = []
        self.sig = None
        self.has_dep = False
        self.vc = None
        self.name = name


class Sched:
    def __init__(self, nc, es, arena_bytes):
        self.nc = nc
        self.es = es
        self.ops = []
        self.floor = None
        self.bufs = []
        self.arena = es.enter_context(nc.sbuf_tensor("arena", [128, arena_bytes // 4], F32))
        self.arena_bytes = arena_bytes
        self.off = 0
        self.peak = 0
        self.banks = []
        for i in range(8):
            t = es.enter_context(nc.psum_tensor(f"bank{i}", [128, 512], F32))
            self.banks.append(Buf(f"bank{i}", t))
            self.banks[-1].excl = True
        self.bank_i = 0
        self.dram_bufs = {}

    def sb(self, name, shape, dtype=F32):
        shape = [int(s) for s in shape]
        if getattr(self, "verbose", False):
            print(f"  sb {name} {shape} {prod(shape[1:]) * DTSZ[dtype]} at {self.off}")
        n = prod(shape[1:])
        nb = n * DTSZ[dtype]
        off = (self.off + 31) // 32 * 32
        assert off + nb <= self.arena_bytes, f"arena overflow allocating {name}: {off + nb}"
        self.off = off + nb
        self.peak = max(self.peak, self.off)
        h = self.arena if dtype == F32 else self.arena.bitcast(dtype)
        e0 = off // DTSZ[dtype]
        v = h[0:shape[0], e0:e0 + n]
        if len(shape) > 2:
            names = " ".join(f"d{i}" for i in range(len(shape) - 1))
            kw = {f"d{i}": shape[i + 1] for i in range(len(shape) - 1)}
            v = v.rearrange(f"p ({names}) -> p {names}", **kw)
        b = Buf(name, v, self.floor)
        b.off = off
        self.bufs.append(b)
        return b

    def mark(self):
        return self.off

    def release(self, mark):
        self.barrier()
        self.bufs = [b for b in self.bufs if b.off < mark]
        self.off = mark

    def bank(self):
        b = self.banks[self.bank_i]
        self.bank_i = (self.bank_i + 1) % 8
        return b

    def dbuf(self, key):
        if key not in self.dram_bufs:
            self.dram_bufs[key] = Buf(str(key), None, None)
        return self.dram_bufs[key]

    def op(self, eng, fn, reads=(), writes=(), name=None, dma=False, cost=None):
        o = Op(eng, fn, dma, name)
        if cost is not None:
            o.cost = cost
        deps = set()
        ex = [b for b in reads if b.excl]
        if ex:
            reads = [b for b in reads if not b.excl]
            writes = list(writes) + [b for b in ex if b not in writes]
        for b in reads:
            if b.last_w is not None:
                deps.add(b.last_w)
        for b in writes:
            if b.last_w is not None:
                deps.add(b.last_w)
            for r in b.readers:
                deps.add(r)
        o.deps = list(deps)
        o.idx = len(self.ops)
        for d in o.deps:
            d.has_dep = True
        for b in reads:
            b.readers.append(o)
        for b in writes:
            b.last_w = o
            b.readers = []
        self.ops.append(o)
        return o

    def dma(self, q, out, in_, reads=(), writes=(), name=None, **kw):
        nbytes = prod(out.shape) * 4
        return self.op(q, lambda e: e.dma_start(out=out, in_=in_, **kw), reads, writes, name=name, dma=True,
                       cost=2.0 + nbytes / 150e3)

    def barrier(self):
        allb = self.bufs + self.banks + list(self.dram_bufs.values())
        o = self.op("sp", None, reads=[], writes=allb, name="barrier")
        self.floor = o
        return o

    def list_schedule(self, ops):
        import heapq
        LAT = 1.2
        out = []
        seg = []
        segs = []
        for o in ops:
            if o.fn is None:
                segs.append(seg)
                segs.append([o])
                seg = []
            else:
                seg.append(o)
        segs.append(seg)
        finish = {}
        for seg in segs:
            if len(seg) <= 1:
                for o in seg:
                    finish[o] = 0.0
                    out.append(o)
                continue
            inseg = set(seg)
            indeg = {}
            users = {}
            for o in seg:
                n = 0
                for d in o.deps:
                    if d in inseg:
                        n += 1
                        users.setdefault(d, []).append(o)
                indeg[o] = n
            eng_time = {e: 0.0 for e in ENGS}
            ready_at = {}
            heap = []
            for o in seg:
                if indeg[o] == 0:
                    ready_at[o] = 0.0
                    heapq.heappush(heap, (0.0, o.idx, o))
            while heap:
                best = None
                cand = []
                while heap and len(cand) < 24:
                    cand.append(heapq.heappop(heap))
                bi = None
                for ci, (ra, idx, o) in enumerate(cand):
                    stt = max(ra, eng_time[o.eng])
                    key = (stt, idx)
                    if best is None or key < best:
                        best, bi = key, ci
                ra, idx, o = cand.pop(bi)
                for c in cand:
                    heapq.heappush(heap, c)
                stt = best[0]
                if o.is_dma:
                    eng_time[o.eng] = stt + 0.15
                    fin_t = stt + o.cost
                else:
                    fin_t = stt + o.cost
                    eng_time[o.eng] = fin_t
                finish[o] = fin_t
                out.append(o)
                for u in users.get(o, ()):
                    t = fin_t + (LAT if u.eng != o.eng else 0.3)
                    if ready_at.get(u, 0.0) < t:
                        ready_at[u] = t
                    indeg[u] -= 1
                    if indeg[u] == 0:
                        heapq.heappush(heap, (ready_at[u], u.idx, u))
            self.est_time = getattr(self, "est_time", 0.0) + max(eng_time.values())
        for i, o in enumerate(out):
            o.idx = i
        return out

    def emit(self, final_bufs):
        nc, es = self.nc, self.es
        fin = self.op("sp", None, reads=list(final_bufs), name="final")
        if getattr(self, "reorder", True):
            self.ops = self.list_schedule(self.ops)
        cnt, dma_n, semkeys = {}, {}, []
        for o in self.ops:
            if not o.has_dep:
                continue
            if o.is_dma:
                n = dma_n.get(o.eng, 0)
                dma_n[o.eng] = n + 1
                key = ("dma", o.eng, n % NDMA_SEM)
                cnt[key] = cnt.get(key, 0) + 16
                o.sig = (key, cnt[key])
            else:
                tot = cnt.get(("n", o.eng), 0)
                cnt[("n", o.eng)] = tot + 1
                key = ("c", o.eng, tot // EPOCH)
                o.sig = (key, tot % EPOCH + 1)
            if o.sig[0] not in semkeys:
                semkeys.append(o.sig[0])
        sems = {k: es.enter_context(nc.semaphore("s_" + "_".join(str(x) for x in k))) for k in semkeys}
        per_eng = {e: [] for e in ENGS}
        seen = {e: {} for e in ENGS}
        nwaits = 0
        for o in self.ops:
            s = seen[o.eng]
            need = {}
            for d in o.deps:
                k, c = d.sig
                if s.get(k, 0) < c:
                    need[k] = max(need.get(k, 0), c)
            if o.is_dma and o.sig is not None:
                k, c = o.sig
                if c > 16 and s.get(k, 0) < c - 16:
                    need[k] = max(need.get(k, 0), c - 16)
            for d in o.deps:
                for k, c in d.vc.items():
                    if s.get(k, 0) < c:
                        s[k] = c
            for k, c in need.items():
                if s.get(k, 0) < c:
                    s[k] = c
            o.deps = need
            nwaits += len(need)
            vc = dict(s)
            if o.sig is not None:
                vc[o.sig[0]] = max(vc.get(o.sig[0], 0), o.sig[1])
                if not o.is_dma:
                    for ep in range(o.sig[0][2]):
                        vc[("c", o.eng, ep)] = EPOCH
            o.vc = vc
            per_eng[o.eng].append(o)

        def body_for(engname):
            def body(eng):
                for o in per_eng[engname]:
                    for k, c in o.deps.items():
                        eng.wait_ge(sems[k], c)
                    if o.fn is not None:
                        ins = o.fn(eng)
                        if o.sig is not None:
                            ins.then_inc(sems[o.sig[0]], 16 if o.is_dma else 1)
                    elif o.sig is not None:
                        eng.nop().then_inc(sems[o.sig[0]], 1)
            return body

        with nc.Block() as block:
            block.sync(body_for("sp"))
            block.scalar(body_for("act"))
            block.vector(body_for("dve"))
            block.gpsimd(body_for("pool"))
            block.tensor(body_for("pe"))
        return {"ops": len(self.ops), "waits": nwaits, "sems": len(sems),
                "per_eng": {e: len(v) for e, v in per_eng.items()}, "sbuf_peak": self.peak}


def _c(out, base=0.25, per=1.0 / 1000.0):
    return base + prod(out.shape[1:]) * per


def ACT(S, r, w, out, in_, func, bias=None, scale=None):
    kw = {}
    if bias is not None:
        kw["bias"] = bias
    if scale is not None:
        kw["scale"] = scale
    return S.op("act", lambda e: e.activation(out, in_, func, **kw), r, w, cost=_c(out, 0.3, 1 / 1200.0))


def TS(S, eng, r, w, out, in0, s1, s2, op0, op1=None):
    if op1 is None:
        return S.op(eng, lambda e: e.tensor_scalar(out, in0, s1, None, op0), r, w, cost=_c(out))
    return S.op(eng, lambda e: e.tensor_scalar(out, in0, s1, s2, op0, op1), r, w, cost=_c(out))


def TT(S, eng, r, w, out, in0, in1, op):
    return S.op(eng, lambda e: e.tensor_tensor(out, in0, in1, op), r, w, cost=_c(out))


def STT(S, eng, r, w, out, in0, scalar, in1, op0, op1):
    return S.op(eng, lambda e: e.scalar_tensor_tensor(out, in0, scalar, in1, op0, op1), r, w, cost=_c(out))


def CP(S, eng, r, w, out, in_):
    if eng == "act":
        return S.op("act", lambda e: e.copy(out, in_), r, w, cost=_c(out, 0.3, 1 / 1200.0))
    return S.op(eng, lambda e: e.tensor_copy(out, in_), r, w, cost=_c(out))


def MSET(S, eng, w, out, val):
    return S.op(eng, lambda e: e.memset(out, val), [], w)


def MM(S, r, w, mms):
    mms = list(mms)

    def fn(e):
        ins = None
        for (o, l, rh, st, sp) in mms:
            ins = e.matmul(o, l, rh, start=st, stop=sp)
        return ins
    cost = 0.1
    for (o, l, rh, st, sp) in mms:
        cost += max(64, prod(rh.shape[1:])) / 2400.0 * (4.0 if rh.dtype == F32 else 1.0) + 0.02
    return S.op("pe", fn, r, w, cost=cost)


def TR(S, r, w, trs):
    trs = list(trs)

    def fn(e):
        ins = None
        for (o, i, idt) in trs:
            ins = e.transpose(o, i, idt)
        return ins
    return S.op("pe", fn, r, w, cost=0.1 + 0.12 * len(trs))


def bc(ap, shape):
    return ap.to_broadcast([int(s) for s in shape])


POOL_W = (2, 4, 8, 16)


def make_consts():
    cols = {}
    parts = []
    off = [0]

    def add(name, arr):
        a = np.zeros((128, arr.shape[1]), np.float32)
        a[:arr.shape[0]] = arr
        cols[name] = (off[0], arr.shape[1])
        off[0] += arr.shape[1]
        parts.append(a)

    idx = np.arange(128)
    s_, t_ = idx[:, None], idx[None, :]
    add("ident", np.eye(128, dtype=np.float32))
    add("ones", np.ones((128, 128), np.float32))
    add("tri0", (s_ <= t_).astype(np.float32))
    add("tri1", (s_ >= t_).astype(np.float32))
    add("neg0", np.where(s_ <= t_, 0.0, NEG).astype(np.float32))
    add("neg1", np.where(s_ >= t_, 0.0, NEG).astype(np.float32))
    n1 = off[0]
    for g, w in enumerate(POOL_W):
        h = w // 2
        band = ((s_ >= t_ - h) & (s_ < t_ + h)).astype(np.float32)
        cnt_int = np.full(128, float(w))
        cnt_first = (idx + h) - np.maximum(idx - h, 0)
        cnt_last = np.minimum(idx + h, 128) - (idx - h)
        eye = np.eye(128, dtype=np.float32)
        add(f"pA{g}int", band / cnt_int[None, :] - eye)
        add(f"pA{g}first", band / cnt_first[None, :] - eye)
        add(f"pA{g}last", band / cnt_last[None, :] - eye)
        sp = np.arange(8)[:, None]
        add(f"pP{g}", (((sp - 8) >= t_ - h) & ((sp - 8) < t_ + h)).astype(np.float32) / w)
        add(f"pN{g}", (((128 + sp) >= t_ - h) & ((128 + sp) < t_ + h)).astype(np.float32) / w)
    add("jrow", np.tile(np.arange(256, dtype=np.float32)[None, :], (128, 1)))
    add("pcol", (idx % 64).astype(np.float32)[:, None])
    full = np.concatenate(parts, axis=1)
    cols2 = {kk: (o - n1, n) for kk, (o, n) in cols.items() if o >= n1}
    cols1 = {kk: (o, n) for kk, (o, n) in cols.items() if o < n1}
    return full[:, :n1].copy(), cols1, full[:, n1:].copy(), cols2


class Cfg:
    def __init__(self, Ts=4096, NP=4, Tp=256, L=2, debug=()):
        self.Ts, self.NP, self.Tp, self.L = Ts, NP, Tp, L
        self.debug = tuple(debug)
        self.seqs = [("s", Ts, 0, 0)] + [(f"p{j}", Tp, 1, Ts + j * Tp) for j in range(NP)]
        self.Ttot = Ts + NP * Tp


INPUT_SPECS = lambda c: [
    ("xs", [c.Ts, D]), ("xp", [c.NP * c.Tp, D]),
    ("st_c", [2, 2, 4, 64, 64]), ("st_n", [2, 2, 4, 64]), ("st_m", [2, 8]), ("st_s", [2, 2, 4, 64, 128]),
    ("cond", [2, D]),
    ("w_ada", [2, D, 6 * D]), ("b_ada", [2, 6 * D]), ("w_in", [2, D, DIN]), ("b_ig", [2, 8]), ("b_fg", [2, 8]),
    ("mnorm_g", [2, 256]), ("w_pool", [2, 4, 64, 64]), ("pool_scale", [2, 256]), ("w_sp", [2, 4, 128, 128]),
    ("b_sp", [2, 4, 128]), ("sconv_w", [2, 3, 768]), ("sconv_b", [2, 768]), ("dt_bias", [2, 8]),
    ("a_log", [2, 8]), ("ssd_d", [2, 4]), ("snorm_g", [2, 256]), ("w_out", [2, D, D]),
    ("ln1_g", [2, D]), ("ln1_b", [2, D]), ("w_up", [2, D, 2 * DFF]), ("fconv_w", [2, 3, 2 * DFF]),
    ("fconv_b", [2, 2 * DFF]), ("w_dn", [2, DFF, D]), ("ln2_g", [2, D]), ("ln2_b", [2, D]),
]
OUTPUT_SPECS = lambda c: [
    ("ys", [c.Ts, D]), ("yp", [c.NP * c.Tp, D]), ("oc", [c.NP, 2, 2, 4, 64, 64]), ("on", [c.NP, 2, 2, 4, 64]),
    ("om", [c.NP, 2, 8]), ("os", [c.NP, 2, 2, 4, 64, 128]),
]


class K:
    pass


def build(cfg):
    nc = bass.Bass("TRN2", target_bir_lowering=False)
    cst_np, ccols, cst2_np, ccols2 = make_consts()
    I = {}
    for name, shape in INPUT_SPECS(cfg):
        I[name] = nc.dram_tensor(name, shape, F32, kind="ExternalInput")
    I["cst"] = nc.dram_tensor("cst", list(cst_np.shape), F32, kind="ExternalInput")
    I["cst2"] = nc.dram_tensor("cst2", list(cst2_np.shape), F32, kind="ExternalInput")
    O = {}
    for name, shape in OUTPUT_SPECS(cfg):
        O[name] = nc.dram_tensor(name, shape, F32, kind="ExternalOutput")
    DBG = {}
    for name, shape in cfg.debug:
        DBG[name] = nc.dram_tensor("dbg_" + name, shape, F32, kind="ExternalOutput")
    ntile = cfg.Ttot // 128
    XA = nc.dram_tensor("scr_xa", [cfg.Ttot, D], F32, kind="Internal")
    XB = nc.dram_tensor("scr_xb", [cfg.Ttot, D], F32, kind="Internal")
    HF = nc.dram_tensor("scr_hf", [cfg.Ttot, 512], F32, kind="Internal")
    HT = nc.dram_tensor("scr_ht", [ntile, 128, 8 * 128], BF16, kind="Internal")
    ED = nc.dram_tensor("scr_e", [64, 512], F32, kind="Internal")

    with ExitStack() as es:
        S = Sched(nc, es, 204 * 1024)
        k = K()
        k.S, k.cfg, k.I, k.O, k.DBG = S, cfg, I, O, DBG
        k.XA, k.XB, k.HF, k.HT, k.ED = XA, XB, HF, HT, ED
        k.final_bufs = []
        k.ccols2 = ccols2
        k.W2 = cst2_np.shape[1]
        setup(k, ccols)
        for l in range(cfg.L):
            layer(k, l)
        stats = S.emit(k.final_bufs)
    return nc, stats


def cview(k, name, rows=128):
    if name in k.ccols:
        o, n = k.ccols[name]
        return k.cst[0:rows, o:o + n]
    o, n = k.ccols2[name]
    return k.cst2[0:rows, o:o + n]


def load_cst2(k):
    S = k.S
    b = S.sb("cst2", [128, k.W2])
    k.cst2b = b
    k.cst2 = b.v
    S.dma("sp", b[:, :], k.I["cst2"][:, :], writes=[b])
    return b


def setup(k, ccols):
    S, I, cfg = k.S, k.I, k.cfg
    k.ccols = ccols
    W = sum(n for (_, n) in ccols.values())
    cstb = S.sb("cst", [128, W])
    k.cstb = cstb
    k.cst = cstb.v
    S.dma("sp", cstb[:, :], I["cst"][:, :], writes=[cstb])
    k.identb = S.sb("identb", [128, 128], BF16)
    CP(S, "dve", [cstb], [k.identb], k.identb[:, :], cview(k, "ident"))
    k.cm05 = S.sb("cm05", [128, 1])
    MSET(S, "pool", [k.cm05], k.cm05[:, :], -0.5)
    k.cm1 = S.sb("cm1", [128, 1])
    MSET(S, "pool", [k.cm1], k.cm1[:, :], -1.0)

    L = cfg.L
    k.modT = S.sb("modT", [128, L, 48, 2])
    layer_consts_alloc(k)
    m1 = S.mark()
    c2b = load_cst2(k)
    fr = S.sb("pe_fr", [64, 256])
    ang = S.sb("pe_ang", [64, 256])
    et = S.sb("pe_e", [64, 512])
    et2 = S.sb("pe_e2", [64, 512])
    sq = S.sb("pe_sq", [64, 256])
    ACT(S, [c2b], [fr], fr[:, :], cview(k, "jrow", 64), AF.Exp, scale=-math.log(10000.0) / 256.0)
    TS(S, "dve", [fr, c2b], [ang], ang[:, :], fr[:, :], cview(k, "pcol", 64), None, ALU.mult)
    ACT(S, [ang], [et], et[:, 0:256], ang[:, :], AF.Sin, scale=1.0 / 32.0)
    ACT(S, [ang], [et], et[:, 256:512], ang[:, :], AF.Sin, scale=-1.0 / 32.0, bias=math.pi / 2.0)
    cur, nxt = et, et2
    for it in range(5):
        TT(S, "dve", [cur], [sq], sq[:, :], cur[:, 0:256], cur[:, 0:256], ALU.mult)
        STT(S, "dve", [cur], [nxt], nxt[:, 0:256], cur[:, 0:256], 2.0, cur[:, 256:512], ALU.mult, ALU.mult)
        TS(S, "dve", [sq], [nxt], nxt[:, 256:512], sq[:, :], -2.0, 1.0, ALU.mult, ALU.add)
        cur, nxt = nxt, cur
    et = cur
    edb = S.dbuf("ED")
    S.dma("sp", k.ED[:, :], et[:, :], reads=[et], writes=[edb])

    condT = S.sb("condT", [128, 8, 2])
    for c in range(2):
        S.dma("sp", condT[:, :, c], I["cond"][c].rearrange("(kc p) -> p kc", p=128), writes=[condT],
              allow_slow_non_contiguous=True)
    esg = S.sb("cond_e", [128, 8, 2])
    ACT(S, [condT], [esg], esg[:, :, :], condT[:, :, :], AF.Exp, scale=-1.0)
    TS(S, "dve", [esg], [esg], esg[:, :, :], esg[:, :, :], 1.0, None, ALU.add)
    TT(S, "pool", [esg, k.cm1], [esg], esg[:, :, :], esg[:, :, :], bc(k.cm1[:, 0:1].unsqueeze(2), [128, 8, 2]), ALU.pow)
    TT(S, "pool", [condT, esg], [condT], condT[:, :, :], condT[:, :, :], esg[:, :, :], ALU.mult)
    badaT = S.sb("badaT", [128, L, 48])
    k.cf_st = S.sb("cf_st0", [128, 128])
    for l in range(L):
        colform(k, badaT, badaT[:, l, :], I["b_ada"][l].rearrange("(j p) -> j p", p=128), 48)
    wst = [S.sb(f"wada_st{j}", [128, 8, 512]) for j in range(2)]
    n = 0
    for l in range(L):
        for cb in range(12):
            st = wst[n % 2]
            n += 1
            S.dma("sp", st[:, :, :], I["w_ada"][l, :, cb * 512:(cb + 1) * 512].rearrange("(kc p) n -> p kc n", p=128),
                  writes=[st])
            pb = S.bank()
            mms = []
            for sub in range(4):
                for kc in range(8):
                    mms.append((pb[:, sub * 2:sub * 2 + 2], st[:, kc, sub * 128:(sub + 1) * 128], condT[:, kc, :],
                                kc == 0, kc == 7))
            MM(S, [st, condT], [pb], mms)
            TT(S, "dve", [pb, badaT], [k.modT],
               k.modT[:, l, cb * 4:cb * 4 + 4, :],
               pb[:, 0:8].rearrange("p (s c) -> p s c", c=2),
               bc(badaT[:, l, cb * 4:cb * 4 + 4].unsqueeze(2), [128, 4, 2]), ALU.add)
    for l in range(L):
        for grp in (1, 4):
            TS(S, "dve", [k.modT], [k.modT], k.modT[:, l, grp * 8:(grp + 1) * 8, :],
               k.modT[:, l, grp * 8:(grp + 1) * 8, :], 1.0, None, ALU.add)
    S.release(m1)


class nc_allow:
    def __init__(self, k):
        pass

    def __enter__(self):
        return self

    def __exit__(self, *a):
        return False


def bview(bank, dtype=F32):
    return bank.v if dtype == F32 else bank.v.bitcast(dtype)


def layer_consts_alloc(k):
    S = k.S
    k.bif = S.sb("bif", [128, 16])
    k.coef = S.sb("coef", [128, 2, 8])
    k.dtb = S.sb("dtb", [128, 8])
    k.dsk = S.sb("dsk", [128, 4])
    k.mng = S.sb("mng", [128, 256])
    k.sng = S.sb("sng", [128, 256])
    k.psc = S.sb("psc", [128, 2])
    k.wpb = S.sb("wpb", [128, 2, 128], BF16)
    k.wsT = S.sb("wsT", [128, 4, 128], BF16)
    k.bsT = S.sb("bsT", [128, 4])
    k.scw = S.sb("scw", [128, 4, 6])
    k.fcw = S.sb("fcw", [128, 4, 44])
    k.lng = S.sb("lng", [128, D])
    k.lnb = S.sb("lnb", [128, D])
    k.gbc = S.sb("gbc", [128, D])
    k.ttmp = [S.sb(f"ttmp{j}", [128, D]) for j in range(2)]
    k.small = {}


def colform(k, dst_buf, dst_ap, src_ap, nb):
    S = k.S
    st = k.cf_st
    S.dma("sp", st[0:nb, :], src_ap, writes=[st])
    pb = S.bank()
    TR(S, [st, k.cstb], [pb], [(pb[:, 0:nb], st[0:nb, :], cview(k, "ident")[0:nb, 0:nb])])
    CP(S, "dve", [pb], [dst_buf], dst_ap, pb[:, 0:nb])


def load_layer_consts(k, l):
    S, I = k.S, k.I
    m = S.mark()
    k.cf_st = S.sb("cf_st", [128, 128])
    row = lambda name, a, b: I[name][l:l + 1, a:b].partition_broadcast(128)
    S.dma("sp", k.bif[:, 0:8], row("b_ig", 0, 8), writes=[k.bif])
    S.dma("sp", k.bif[:, 8:16], row("b_fg", 0, 8), writes=[k.bif])
    TS(S, "dve", [k.bif], [k.bif], k.bif[:, 8:16], k.bif[:, 8:16], -1.0, None, ALU.mult)
    al = S.sb("al_tmp", [128, 8])
    S.dma("sp", al[:, :], row("a_log", 0, 8), writes=[al])
    ACT(S, [al], [al], al[:, :], al[:, :], AF.Exp)
    MSET(S, "pool", [k.coef], k.coef[:, :, :], -1.0)
    TS(S, "dve", [al, k.coef], [k.coef], k.coef[:, :, 4:8], al[:, :].rearrange("p (d h) -> p d h", d=2), -1.0, None, ALU.mult)
    S.dma("sp", k.dtb[:, :], row("dt_bias", 0, 8), writes=[k.dtb])
    S.dma("sp", k.dsk[:, :], row("ssd_d", 0, 4), writes=[k.dsk])
    S.dma("sp", k.mng[:, :], row("mnorm_g", 0, 256), writes=[k.mng])
    S.dma("sp", k.sng[:, :], row("snorm_g", 0, 256), writes=[k.sng])
    colform(k, k.psc, k.psc[:, :], I["pool_scale"][l].rearrange("(j p) -> j p", p=128), 2)
    wp32 = S.sb("wp32", [128, 2, 128])
    MSET(S, "pool", [wp32], wp32[:, :, :], 0.0)
    for g in range(4):
        pr = slice((g % 2) * 64, (g % 2) * 64 + 64)
        S.dma("sp", wp32[pr, g // 2, (g % 2) * 64:(g % 2) * 64 + 64], I["w_pool"][l, g], writes=[wp32])
    CP(S, "dve", [wp32], [k.wpb], k.wpb[:, :, :], wp32[:, :, :])
    ws32 = S.sb("ws32", [128, 4, 128])
    S.dma("sp", ws32[:, :, :], I["w_sp"][l].rearrange("h t s -> t h s"), writes=[ws32])
    pb = S.bank()
    TR(S, [ws32, k.cstb], [pb], [(pb[:, h * 128:(h + 1) * 128], ws32[:, h, :], cview(k, "ident")) for h in range(4)])
    CP(S, "act", [pb], [k.wsT], k.wsT[:, :, :], pb[:, :].rearrange("p (h t) -> p h t", h=4))
    colform(k, k.bsT, k.bsT[:, :], I["b_sp"][l], 4)
    for tap in range(3):
        colform(k, k.scw, k.scw[:, tap, :], I["sconv_w"][l, tap].rearrange("(b p) -> b p", p=128), 6)
        colform(k, k.fcw, k.fcw[:, tap, :], I["fconv_w"][l, tap].rearrange("(b p) -> b p", p=128), 44)
    colform(k, k.scw, k.scw[:, 3, :], I["sconv_b"][l].rearrange("(b p) -> b p", p=128), 6)
    colform(k, k.fcw, k.fcw[:, 3, :], I["fconv_b"][l].rearrange("(b p) -> b p", p=128), 44)
    S.release(m)


def load_weight(k, dst, src2d, nkc, ncols, scope_stage):
    S = k.S
    engs = ("dve", "pool", "act")
    piece = 2840
    for kc in range(nkc):
        for c0 in range(0, ncols, piece):
            c1 = min(ncols, c0 + piece)
            st = scope_stage[k.wl_n % 2]
            S.dma("sp", st[:, 0:c1 - c0], src2d[kc * 128:(kc + 1) * 128, c0:c1], writes=[st])
            CP(S, engs[k.wl_n % 3], [st], [dst], dst[:, kc, c0:c1], st[:, 0:c1 - c0])
            k.wl_n += 1


def gate_table(k, l, grp, cond):
    S = k.S
    dg = k.ttmp[0]
    for j in range(8):
        TS(S, "dve", [k.cstb, k.modT], [dg], dg[:, 0:128], cview(k, "ident"), k.modT[:, l, grp * 8 + j, cond:cond + 1], None, ALU.mult)
        if j % 4 == 0:
            pb = S.bank()
        MM(S, [dg, k.cstb], [pb], [(pb[:, (j % 4) * 128:(j % 4 + 1) * 128], cview(k, "ones"), dg[:, 0:128], True, True)])
        if j % 4 == 3:
            CP(S, "act", [pb], [k.gbc], k.gbc[:, (j // 4) * 512:(j // 4 + 1) * 512], pb[:, :])


def make_hT(k, l, which, cond, rows, xbuf, dsts, dst_buf, src_bufs, pos_tile=None):
    S = k.S
    n = 0
    for r, nr in rows:
        S.dma("sp", xbuf[n:n + nr, :], r, reads=src_bufs, writes=[xbuf])
        n += nr
    if pos_tile is not None:
        TT(S, "pool", [xbuf, pos_tile], [xbuf], xbuf[0:n, :], xbuf[0:n, :], pos_tile[0:n, :], ALU.add)
    sm = k.hsm
    st, mv, ve, rstd, xnb = sm["st"], sm["mv"], sm["ve"], sm["rstd"], sm["xnb"]
    S.op("dve", lambda e: e.bn_stats(st[0:n, 0, :], xbuf[0:n, 0:512]), [xbuf], [st])
    S.op("dve", lambda e: e.bn_stats(st[0:n, 1, :], xbuf[0:n, 512:1024]), [xbuf], [st])
    S.op("dve", lambda e: e.bn_aggr(mv[0:n, :], st[0:n, :, :].rearrange("p a b -> p (a b)")), [st], [mv])
    TS(S, "dve", [mv], [ve], ve[0:n, :], mv[0:n, 1:2], EPS, None, ALU.add)
    TT(S, "pool", [ve, k.cm05], [rstd], rstd[0:n, :], ve[0:n, :], k.cm05[0:n, :], ALU.pow)
    TS(S, "dve", [xbuf, mv, rstd], [xnb], xnb[0:n, :], xbuf[0:n, :], mv[0:n, 0:1], rstd[0:n, 0:1], ALU.subtract, ALU.mult)
    pb = S.bank()
    pv = bview(pb, BF16)
    TR(S, [xnb, k.identb], [pb],
       [(pv[:, kc * 128:kc * 128 + n], xnb[0:n, kc * 128:(kc + 1) * 128], k.identb[0:n, 0:n]) for kc in range(8)])
    gsh, gsc = (0, 1) if which == 1 else (3, 4)
    for kc in range(8):
        sc = k.modT[:, l, gsc * 8 + kc, cond:cond + 1]
        sh = k.modT[:, l, gsh * 8 + kc, cond:cond + 1]
        if kc % 2 == 0:
            ACT(S, [pb, k.modT], [dst_buf], dsts[kc], pv[:, kc * 128:kc * 128 + n], AF.Identity, bias=sh, scale=sc)
        else:
            TS(S, "dve", [pb, k.modT], [dst_buf], dsts[kc], pv[:, kc * 128:kc * 128 + n], sc, sh, ALU.mult, ALU.add)


def alloc_hsm(k):
    S = k.S
    k.hsm = {"st": S.sb("h_st", [128, 2, 6]), "mv": S.sb("h_mv", [128, 2]), "ve": S.sb("h_ve", [128, 1]),
             "rstd": S.sb("h_rstd", [128, 1]), "xnb": S.sb("h_xnb", [128, D], BF16)}


def resid_ln(k, x_buf, psum_halves, out_buf, nb_small):
    S = k.S
    t0, t1 = k.ttmp[0], out_buf
    for hlf, pb in enumerate(psum_halves):
        sl = slice(hlf * 512, (hlf + 1) * 512)
        TT(S, "dve", [pb, k.gbc], [t0], t0[:, sl], pb[:, :], k.gbc[:, sl], ALU.mult)
    TS(S, "pool", [x_buf], [x_buf], x_buf[:, :], x_buf[:, :], ALPHA, None, ALU.mult)
    TT(S, "pool", [x_buf, t0], [t0], t0[:, :], t0[:, :], x_buf[:, :], ALU.add)
    st, mv, ve, rstd, nb = nb_small["st"], nb_small["mv"], nb_small["ve"], nb_small["rstd"], nb_small["nb"]
    S.op("dve", lambda e: e.bn_stats(st[:, 0, :], t0[:, 0:512]), [t0], [st])
    S.op("dve", lambda e: e.bn_stats(st[:, 1, :], t0[:, 512:1024]), [t0], [st])
    S.op("dve", lambda e: e.bn_aggr(mv[:, :], st[:, :, :].rearrange("p a b -> p (a b)")), [st], [mv])
    TS(S, "dve", [mv], [ve], ve[:, :], mv[:, 1:2], EPS, None, ALU.add)
    TT(S, "pool", [ve, k.cm05], [rstd], rstd[:, :], ve[:, :], k.cm05[:, :], ALU.pow)
    STT(S, "dve", [mv, rstd], [nb], nb[:, :], mv[:, 0:1], -1.0, rstd[:, :], ALU.mult, ALU.mult)
    ACT(S, [t0, rstd, nb], [t1], t1[:, :], t0[:, :], AF.Identity, bias=nb[:, 0:1], scale=rstd[:, 0:1])
    TT(S, "pool", [t1, k.lng], [t1], t1[:, :], t1[:, :], k.lng[:, :], ALU.mult)
    TT(S, "pool", [t1, k.lnb], [out_buf], out_buf[:, :], t1[:, :], k.lnb[:, :], ALU.add)


def alloc_rsm(k):
    S = k.S
    return {"st": S.sb("r_st", [128, 2, 6]), "mv": S.sb("r_mv", [128, 2]), "ve": S.sb("r_ve", [128, 1]),
            "rstd": S.sb("r_rstd", [128, 1]), "nb": S.sb("r_nb", [128, 1])}


def seq_src_dst(k, l, phase):
    cfg = k.cfg
    mode = getattr(cfg, "mode", "full")
    if phase == "A":
        src = None if l == 0 else k.XB
        dst = k.XA if mode == "full" else None
    else:
        src = k.XA if mode == "full" else None
        dst = None if l == cfg.L - 1 else k.XB
    return src, dst


def rows_ap(k, handle, which_io, r0, n):
    cfg = k.cfg
    if handle is not None:
        return handle[r0:r0 + n, :]
    if r0 < cfg.Ts:
        t = k.I["xs"] if which_io == "in" else k.O["ys"]
        return t[r0:r0 + n, :]
    t = k.I["xp"] if which_io == "in" else k.O["yp"]
    return t[r0 - cfg.Ts:r0 - cfg.Ts + n, :]


def phaseB(k, l):
    S, I, cfg = k.S, k.I, k.cfg
    m = S.mark()
    w_up = S.sb("w_up", [128, 8, 2 * DFF], BF16)
    w_dn = S.sb("w_dn", [128, 22, D], BF16)
    m2 = S.mark()
    stage = [S.sb(f"wstage{j}", [128, 2840]) for j in range(2)]
    k.wl_n = 0
    load_weight(k, w_up, I["w_up"][l], 8, 2 * DFF, stage)
    load_weight(k, w_dn, I["w_dn"][l], 22, D, stage)
    S.release(m2)
    S.dma("sp", k.lng[:, :], I["ln2_g"][l:l + 1, :].partition_broadcast(128), writes=[k.lng])
    S.dma("sp", k.lnb[:, :], I["ln2_b"][l:l + 1, :].partition_broadcast(128), writes=[k.lnb])
    alloc_hsm(k)
    rsm = alloc_rsm(k)
    SEG = 256
    h2T = [S.sb(f"h2T{j}", [128, 8, SEG + 2], BF16) for j in range(2)]
    actT = S.sb("actT", [128, 22, SEG], BF16)
    xt = [S.sb(f"xtB{j}", [128, D]) for j in range(3)]
    xh = k.ttmp[1]
    cg = [S.sb(f"cg{j}", [128, SEG]) for j in range(2)]
    cv = [S.sb(f"cv{j}", [128, SEG]) for j in range(2)]
    th = [S.sb(f"th{j}", [128, SEG]) for j in range(2)]
    src, dst = seq_src_dst(k, l, "B")
    xt = xt + [S.sb("xtB3", [128, D])]
    segs = []
    for (sname, T, cond, off) in cfg.seqs:
        for t0 in range(0, T, SEG):
            segs.append((T, cond, off, t0))
    state = {}

    def prep(si):
        T, cond, off, t0 = segs[si]
        hT = h2T[si % 2]
        r0 = off + t0
        xts = []
        for j in range(SEG // 128):
            xb = xt[(2 * si + j) % 4]
            xts.append(xb)
            make_hT(k, l, 2, cond, [(rows_ap(k, src, "in", r0 + 128 * j, 128), 128)], xb,
                    [hT[:, kc, 1 + 128 * j:1 + 128 * (j + 1)] for kc in range(8)], hT, [])
        rows, cols = [], []
        if t0 > 0:
            rows.append((rows_ap(k, src, "in", r0 - 1, 1), 1))
            cols.append(0)
        else:
            MSET(S, "pool", [hT], hT[:, :, 0:1], 0.0)
        if t0 + SEG < T:
            rows.append((rows_ap(k, src, "in", r0 + SEG, 1), 1))
            cols.append(SEG + 1)
        else:
            MSET(S, "pool", [hT], hT[:, :, SEG + 1:SEG + 2], 0.0)
        if len(rows) == 2:
            make_hT(k, l, 2, cond, rows, xh, [hT[:, kc, 0:SEG + 2:SEG + 1] for kc in range(8)], hT, [])
        elif len(rows) == 1:
            c = cols[0]
            make_hT(k, l, 2, cond, rows, xh, [hT[:, kc, c:c + 1] for kc in range(8)], hT, [])
        state[si] = xts

    def ffn(si):
        T, cond, off, t0 = segs[si]
        hT = h2T[si % 2]
        r0 = off + t0
        xts = state.pop(si)
        for c in range(22):
            pg, pv = S.bank(), S.bank()
            MM(S, [w_up, hT], [pg], [(pg[:, 0:SEG + 2], w_up[:, kc, c * 128:(c + 1) * 128], hT[:, kc, :], kc == 0, kc == 7)
                                      for kc in range(8)])
            MM(S, [w_up, hT], [pv], [(pv[:, 0:SEG + 2], w_up[:, kc, DFF + c * 128:DFF + (c + 1) * 128], hT[:, kc, :], kc == 0, kc == 7)
                                      for kc in range(8)])
            g_, v_, t_ = cg[c % 2], cv[c % 2], th[c % 2]
            fw = k.fcw
            ACT(S, [pg, fw], [g_], g_[:, :], pg[:, 1:SEG + 1], AF.Identity, bias=fw[:, 3, c:c + 1], scale=fw[:, 1, c:c + 1])
            STT(S, "dve", [pg, fw, g_], [g_], g_[:, :], pg[:, 0:SEG], fw[:, 0, c:c + 1], g_[:, :], ALU.mult, ALU.add)
            STT(S, "dve", [pg, fw, g_], [g_], g_[:, :], pg[:, 2:SEG + 2], fw[:, 2, c:c + 1], g_[:, :], ALU.mult, ALU.add)
            cc = 22 + c
            ACT(S, [pv, fw], [v_], v_[:, :], pv[:, 1:SEG + 1], AF.Identity, bias=fw[:, 3, cc:cc + 1], scale=fw[:, 1, cc:cc + 1])
            STT(S, "dve", [pv, fw, v_], [v_], v_[:, :], pv[:, 0:SEG], fw[:, 0, cc:cc + 1], v_[:, :], ALU.mult, ALU.add)
            STT(S, "dve", [pv, fw, v_], [v_], v_[:, :], pv[:, 2:SEG + 2], fw[:, 2, cc:cc + 1], v_[:, :], ALU.mult, ALU.add)
            ACT(S, [g_], [t_], t_[:, :], g_[:, :], AF.Sigmoid)
            TT(S, "pool", [t_, g_], [t_], t_[:, :], t_[:, :], g_[:, :], ALU.mult)
            TT(S, "pool", [t_, v_], [actT], actT[:, c, :], t_[:, :], v_[:, :], ALU.mult)
        for j in range(SEG // 128):
            p0, p1 = S.bank(), S.bank()
            for hlf, pb in enumerate((p0, p1)):
                MM(S, [actT, w_dn], [pb], [(pb[:, :], actT[:, c, 128 * j:128 * (j + 1)], w_dn[:, c, hlf * 512:(hlf + 1) * 512],
                                              c == 0, c == 21) for c in range(22)])
            ob = xts[j]
            resid_ln(k, xts[j], (p0, p1), ob, rsm)
            db = S.dbuf(("xout", l, (r0 + 128 * j) // 128))
            S.dma("pool", rows_ap(k, dst, "out", r0 + 128 * j, 128), ob[:, :], reads=[ob], writes=[db])
            if dst is None:
                k.final_bufs.append(db)

    cur_cond = None
    prep(0)
    for si in range(len(segs)):
        if si + 1 < len(segs):
            prep(si + 1)
        if segs[si][1] != cur_cond:
            cur_cond = segs[si][1]
            gate_table(k, l, 5, cur_cond)
        ffn(si)
    S.release(m)


def layer(k, l):
    cfg = k.cfg
    load_layer_consts(k, l)
    mode = getattr(cfg, "mode", "full")
    if mode in ("full", "A"):
        phaseA(k, l)
    if mode in ("full", "B"):
        phaseB(k, l)


def shard_inputs(inp, cfg, core):
    f = lambda a: np.ascontiguousarray(np.asarray(a), dtype=np.float32)
    NP = cfg.NP
    cst, _, cst2, _ = make_consts()
    m = {
        "xs": f(inp["x_sample"][core]),
        "xp": f(inp["x_prompt"][NP * core:NP * (core + 1)]).reshape(NP * cfg.Tp, D),
        "st_c": f(inp["state_mlstm_c"][core]), "st_n": f(inp["state_mlstm_n"][core]),
        "st_m": f(inp["state_mlstm_m"][core]).reshape(2, 8), "st_s": f(inp["state_ssd"][core]),
        "cond": f(np.stack([np.asarray(inp["c"])[core], np.asarray(inp["c_ctx"])], 0)),
        "w_ada": f(inp["w_ada"]), "b_ada": f(inp["b_ada"]), "w_in": f(inp["w_in"]),
        "b_ig": f(inp["b_igate"]).reshape(2, 8), "b_fg": f(inp["b_fgate"]).reshape(2, 8),
        "mnorm_g": f(inp["mlstm_norm_g"]), "w_pool": f(inp["w_pool"]), "pool_scale": f(inp["pool_scale"]),
        "w_sp": f(inp["w_spatial"]), "b_sp": f(inp["b_spatial"]), "sconv_w": f(inp["ssd_conv_w"]),
        "sconv_b": f(inp["ssd_conv_b"]), "dt_bias": f(inp["ssd_dt_bias"]).reshape(2, 8),
        "a_log": f(inp["ssd_a_log"]).reshape(2, 8), "ssd_d": f(inp["ssd_d"]), "snorm_g": f(inp["ssd_norm_g"]),
        "w_out": f(inp["w_out"]), "ln1_g": f(inp["ln1_g"]), "ln1_b": f(inp["ln1_b"]), "w_up": f(inp["ffn_w_up"]),
        "fconv_w": f(inp["ffn_conv_w"]), "fconv_b": f(inp["ffn_conv_b"]), "w_dn": f(inp["ffn_w_down"]),
        "ln2_g": f(inp["ln2_g"]), "ln2_b": f(inp["ln2_b"]), "cst": cst, "cst2": cst2,
    }
    return m


def phaseA(k, l):
    S, I, cfg = k.S, k.I, k.cfg
    m = S.mark()
    w_in = S.sb("w_in", [128, 8, DIN], BF16)
    w_out = S.sb("w_out", [128, 8, D], BF16)
    m2 = S.mark()
    stage = [S.sb(f"wstageA{j}", [128, 2840]) for j in range(2)]
    k.wl_n = 0
    load_weight(k, w_in, I["w_in"][l], 8, DIN, stage)
    load_weight(k, w_out, I["w_out"][l], 8, D, stage)
    S.release(m2)
    c2b = load_cst2(k)
    S.dma("sp", k.lng[:, :], I["ln1_g"][l:l + 1, :].partition_broadcast(128), writes=[k.lng])
    S.dma("sp", k.lnb[:, :], I["ln1_b"][l:l + 1, :].partition_broadcast(128), writes=[k.lnb])
    alloc_hsm(k)
    rsm = alloc_rsm(k)
    a = K()
    a.w_in, a.w_out, a.c2b, a.rsm, a.l = w_in, w_out, c2b, rsm, l
    a.hb = [S.sb(f"hb{j}", [128, 8, 130], BF16) for j in range(3)]
    a.xq = [S.sb(f"xq{j}", [128, D]) for j in range(2)]
    a.pet = [S.sb(f"petile{j}", [128, D]) for j in range(2)] if l == 0 else None
    if l == 0:
        edb = S.dbuf("ED")
        for j in range(2):
            S.dma("sp", a.pet[j][0:64, 512:1024], k.ED[:, :], reads=[edb], writes=[a.pet[j]])
            S.dma("sp", a.pet[j][64:128, 512:1024], k.ED[:, :], reads=[edb], writes=[a.pet[j]])
    sb = S.sb
    a.G8, a.E8, a.SP8, a.igb = sb("G8", [128, 8]), sb("E8", [128, 8]), sb("SP8", [128, 8]), sb("igb", [128, 4])
    a.r8, a.logdec, a.cum, a.e8 = sb("r8", [128, 8]), sb("logdec", [128, 8]), sb("cum", [128, 8]), sb("e8", [128, 8])
    a.wend, a.aL, a.tmp8 = sb("wend", [128, 8]), sb("aL", [128, 8]), sb("tmp8", [128, 8])
    a.L1 = sb("L1", [128, 8, 128])
    a.DIFF = sb("DIFF", [128, 8, 128])
    a.qkT = sb("qkT", [128, 4, 128], BF16)
    a.PTm = sb("PTm", [128, 4, 128], BF16)
    a.PTs = sb("PTs", [128, 4, 128], BF16)
    a.k_tm = sb("k_tm", [128, 256], BF16)
    a.xt_m, a.xh_m = sb("xt_m", [128, 4, 66], BF16), sb("xh_m", [128, 4, 66], BF16)
    a.XBC, a.XBCe = sb("XBC", [128, 6, 128]), sb("XBCe", [128, 6, 128])
    a.XBCb = sb("XBCb", [128, 6, 128], BF16)
    a.x_tm, a.B_tm = sb("x_tm", [128, 256], BF16), sb("B_tm", [128, 2, 128], BF16)
    a.xt_s, a.xh_s = sb("xt_s", [128, 4, 64], BF16), sb("xh_s", [128, 4, 64], BF16)
    a.Cn, a.Cnb = sb("Cn", [128, 2, 65]), sb("Cnb", [128, 2, 66], BF16)
    a.Hs, a.Hsb = sb("Hs", [128, 4, 64]), sb("Hsb", [128, 4, 64], BF16)
    a.NUM = sb("NUM", [128, 4, 65])
    a.den = sb("den", [128, 4])
    a.Ysc = sb("Ysc", [128, 4, 64])
    a.HY = [sb(f"HY{j}", [128, 512]) for j in range(2)]
    a.HYf = [sb(f"HYf{j}", [128, 512]) for j in range(2)]
    a.eo, a.z_sb, a.ez, a.gu = sb("eo", [128, 256]), sb("z_sb", [128, 256]), sb("ez", [128, 256]), sb("gu", [128, 256])
    a.gvb = sb("gvb", [128, 256], BF16)
    a.pc, a.pcP, a.pcN = sb("pc", [128, 256]), sb("pcP", [8, 256]), sb("pcN", [8, 256])
    a.plb, a.plT = sb("plb", [128, 256], BF16), sb("plT", [128, 2, 128], BF16)
    a.fin1, a.fin2, a.fin3 = sb("fin1", [128, 256]), sb("fin2", [128, 256]), sb("fin3", [128, 256])
    a.st4, a.st4b = sb("st4", [128, 4]), sb("st4b", [128, 4])
    a.yall = sb("yall", [128, 3, 256], BF16)
    a.concatT = sb("concatT", [128, 8, 128], BF16)
    a.gst, a.gmv, a.gve, a.grs = sb("gst", [128, 6]), sb("gmv", [128, 2]), sb("gve", [128, 1]), sb("grs", [128, 1])
    a.mrun = sb("mrun", [4, 1])
    a.mt = sb("mt", [4, 2])
    a.dec = sb("dec", [128, 8])
    a.sio = sb("sio", [128, 4, 128])
    src, dst = seq_src_dst(k, l, "A")
    a.src, a.dst = src, dst
    cur_cond = None
    for si, (sname, T, cond, off) in enumerate(cfg.seqs):
        if cond != cur_cond:
            gate_table(k, l, 2, cond)
            cur_cond = cond
        runseq(k, a, si, T, cond, off)
    S.release(m)


def runseq(k, a, si, T, cond, off):
    S, cfg, l = k.S, k.cfg, a.l
    nt = T // 128
    is_sample = (si == 0)
    tile0 = off // 128
    w_in = a.w_in

    def hbuf(i):
        return a.hb[i % 3]

    def fix_halo(lo, hi):
        CP(S, "pool", [hbuf(hi)], [hbuf(lo)], hbuf(lo)[:, :, 129:130], hbuf(hi)[:, :, 1:2])
        CP(S, "pool", [hbuf(lo)], [hbuf(hi)], hbuf(hi)[:, :, 0:1], hbuf(lo)[:, :, 128:129])

    def ensure1(i):
        hb = hbuf(i)
        xb = a.xq[i % 2]
        pos = None
        if l == 0 and is_sample:
            pos = a.pet[i % 2]
            edb = S.dbuf("ED")
            S.dma("sp", pos[0:64, 0:512], k.ED[2 * i:2 * i + 1, :].partition_broadcast(64), reads=[edb], writes=[pos])
            S.dma("sp", pos[64:128, 0:512], k.ED[2 * i + 1:2 * i + 2, :].partition_broadcast(64), reads=[edb], writes=[pos])
        make_hT(k, l, 1, cond, [(rows_ap(k, a.src, "in", off + 128 * i, 128), 128)], xb,
                [hb[:, kc, 1:129] for kc in range(8)], hb, [], pos_tile=pos)
        db = S.dbuf(("HT", tile0 + i))
        S.dma("pool", k.HT[tile0 + i].rearrange("p (kc t) -> p kc t", kc=8), hb[:, :, 1:129], reads=[hb], writes=[db])
        if i == 0:
            MSET(S, "pool", [hb], hb[:, :, 0:1], 0.0)
        else:
            fix_halo(i - 1, i)
        if i == nt - 1:
            MSET(S, "pool", [hb], hb[:, :, 129:130], 0.0)

    def ensure2(i):
        hb = hbuf(i)
        db = S.dbuf(("HT", tile0 + i))
        S.dma("sp", hb[:, :, 1:129], k.HT[tile0 + i].rearrange("p (kc t) -> p kc t", kc=8), reads=[db], writes=[hb])
        xb = a.xq[i % 2]
        S.dma("sp", xb[:, :], rows_ap(k, a.src, "in", off + 128 * i, 128), writes=[xb])
        if l == 0 and is_sample:
            pos = a.pet[i % 2]
            edb = S.dbuf("ED")
            S.dma("sp", pos[0:64, 0:512], k.ED[2 * i:2 * i + 1, :].partition_broadcast(64), reads=[edb], writes=[pos])
            S.dma("sp", pos[64:128, 0:512], k.ED[2 * i + 1:2 * i + 2, :].partition_broadcast(64), reads=[edb], writes=[pos])
            TT(S, "pool", [xb, pos], [xb], xb[:, :], xb[:, :], pos[:, :], ALU.add)
        hf = a.HYf[i % 2]
        S.dma("sp", hf[:, :], k.HF[off + 128 * i:off + 128 * (i + 1), :], reads=[S.dbuf(("HF", tile0 + i))], writes=[hf])
        if i == nt - 1:
            MSET(S, "pool", [hb], hb[:, :, 129:130], 0.0)
        else:
            fix_halo(i, i + 1)
        if i == 0:
            MSET(S, "pool", [hb], hb[:, :, 0:1], 0.0)

    for d in ((0,) if getattr(cfg, "stop", 99) <= 4 else (0, 1)):
        init_state(k, a, si, d, is_sample)
        order = list(range(nt)) if d == 0 else list(range(nt - 1, -1, -1))
        ens = ensure1 if d == 0 else ensure2
        ens(order[0])
        for n, i in enumerate(order):
            if n + 1 < len(order):
                ens(order[n + 1])
            tileA(k, a, si, T, cond, off, i, d, nt, is_sample)
        if not is_sample and getattr(cfg, "stop", 99) > 5:
            final_state(k, a, si, d)


def init_state(k, a, si, d, is_sample):
    S, I, l = k.S, k.I, a.l
    if not is_sample:
        MSET(S, "pool", [a.Cn], a.Cn[:, :, :], 0.0)
        MSET(S, "pool", [a.Cnb], a.Cnb[:, :, :], 0.0)
        MSET(S, "pool", [a.Hs], a.Hs[:, :, :], 0.0)
        MSET(S, "pool", [a.Hsb], a.Hsb[:, :, :], 0.0)
        MSET(S, "pool", [a.mrun], a.mrun[:, :], 0.0)
        return
    for h in range(4):
        pr = slice((h % 2) * 64, (h % 2) * 64 + 64)
        S.dma("sp", a.Cn[pr, h // 2, 0:64], I["st_c"][l, d, h], writes=[a.Cn])
        S.dma("sp", a.Cn[pr, h // 2, 64:65], I["st_n"][l, d, h].rearrange("(p o) -> p o", o=1), writes=[a.Cn])
    S.dma("sp", a.st4[:, :], I["st_m"][l:l + 1, 4 * d:4 * d + 4].partition_broadcast(128), writes=[a.st4])
    ACT(S, [a.st4], [a.st4b], a.st4b[:, :], a.st4[:, :], AF.Exp)
    for h in range(4):
        pr = slice((h % 2) * 64, (h % 2) * 64 + 64)
        TS(S, "dve", [a.Cn, a.st4b], [a.Cn], a.Cn[pr, h // 2, :], a.Cn[pr, h // 2, :], a.st4b[pr, h:h + 1], None, ALU.mult)
    CP(S, "pool", [a.Cn], [a.Cnb], a.Cnb[:, :, 0:65], a.Cn[:, :, :])
    S.dma("sp", a.sio[0:64, :, :], I["st_s"][l, d].rearrange("h p n -> p h n"), writes=[a.sio])
    pb = S.bank()
    TR(S, [a.sio, k.cstb], [pb], [(pb[:, h * 64:(h + 1) * 64], a.sio[0:64, h, :], cview(k, "ident")[0:64, 0:64]) for h in range(4)])
    CP(S, "dve", [pb], [a.Hs], a.Hs[:, :, :], pb[:, 0:256].rearrange("p (h q) -> p h q", h=4))
    CP(S, "act", [pb], [a.Hsb], a.Hsb[:, :, :], pb[:, 0:256].rearrange("p (h q) -> p h q", h=4))


def final_state(k, a, si, d):
    S, O, l = k.S, k.O, a.l
    j = si - 1
    dg = a.tmp8
    TS(S, "dve", [k.cstb, a.mrun], [dg], dg[0:4, 0:4], cview(k, "ident")[0:4, 0:4], a.mrun[0:4, 0:1], None, ALU.mult)
    pb = S.bank()
    MM(S, [dg, k.cstb], [pb], [(pb[:, 0:4], cview(k, "ones")[0:4, :], dg[0:4, 0:4], True, True)])
    ACT(S, [pb], [a.st4b], a.st4b[:, :], pb[:, 0:4], AF.Exp, scale=-1.0)
    stg = a.sio
    sv = stg[:, 0:2, 0:65]
    for h in range(4):
        pr = slice((h % 2) * 64, (h % 2) * 64 + 64)
        TS(S, "dve", [a.Cn, a.st4b], [stg], stg[pr, h // 2, 0:65], a.Cn[pr, h // 2, :], a.st4b[pr, h:h + 1], None, ALU.mult)
    outs = []
    for h in range(4):
        pr = slice((h % 2) * 64, (h % 2) * 64 + 64)
        db = S.dbuf(("oc", j, l, d, h))
        S.dma("pool", O["oc"][j, l, d, h], stg[pr, h // 2, 0:64], reads=[stg], writes=[db])
        db2 = S.dbuf(("on", j, l, d, h))
        S.dma("pool", O["on"][j, l, d, h].rearrange("(p o) -> p o", o=1), stg[pr, h // 2, 64:65], reads=[stg], writes=[db2])
        outs += [db, db2]
    db = S.dbuf(("om", j, l, d))
    S.dma("pool", O["om"][j, l, 4 * d:4 * d + 4].rearrange("(p o) -> p o", o=1), a.mrun[0:4, 0:1], reads=[a.mrun], writes=[db])
    outs.append(db)
    pb2 = S.bank()
    TR(S, [a.Hs, k.cstb], [pb2], [(pb2[0:64, h * 128:(h + 1) * 128], a.Hs[:, h, :], cview(k, "ident")) for h in range(4)])
    CP(S, "dve", [pb2, stg], [stg], stg[0:64, :, :], pb2[0:64, :].rearrange("p (h n) -> p h n", h=4))
    db = S.dbuf(("os", j, l, d))
    S.dma("pool", O["os"][j, l, d].rearrange("h p n -> p h n"), stg[0:64, :, :], reads=[stg], writes=[db])
    outs.append(db)
    k.final_bufs += outs


def tileA(k, a, si, T, cond, off, i, d, nt, is_sample):
    S, l = k.S, a.l
    w_in = a.w_in
    hb = a.hb[i % 3]
    hcur = lambda kc: hb[:, kc, 1:129]
    tri = cview(k, "tri%d" % d)
    neg = cview(k, "neg%d" % d)
    endc = 127 if d == 0 else 0
    full = (d == 1)
    cst = k.cstb

    ps1, ps2 = S.bank(), S.bank()
    MM(S, [hb, w_in], [ps1], [(ps1[:, 0:512], hcur(kc), w_in[:, kc, 256:768], kc == 0, kc == 7) for kc in range(8)])
    MM(S, [hb, w_in], [ps2], [(ps2[:, 0:272], hcur(kc), w_in[:, kc, 768:1040], kc == 0, kc == 7) for kc in range(8)]
       + [(ps2[:, 272:280], hcur(kc), w_in[:, kc, 2832:2840], kc == 0, kc == 7) for kc in range(8)])
    G8, E8, SP8, igb, r8, logdec, cum, e8, wend, aL, tmp8 = (a.G8, a.E8, a.SP8, a.igb, a.r8, a.logdec, a.cum, a.e8,
                                                             a.wend, a.aL, a.tmp8)
    STT(S, "dve", [ps2, k.bif], [G8], G8[:, 0:4], ps2[:, 264 + 4 * d:268 + 4 * d], -1.0, k.bif[:, 8 + 4 * d:12 + 4 * d], ALU.mult, ALU.add)
    TT(S, "dve", [ps2, k.dtb], [G8], G8[:, 4:8], ps2[:, 272 + 4 * d:276 + 4 * d], k.dtb[:, 4 * d:4 * d + 4], ALU.add)
    TT(S, "dve", [ps2, k.bif], [igb], igb[:, :], ps2[:, 256 + 4 * d:260 + 4 * d], k.bif[:, 4 * d:4 * d + 4], ALU.add)
    if full:
        ACT(S, [ps2], [a.eo], a.eo[:, :], ps2[:, 0:256], AF.Exp, scale=-1.0)
    ACT(S, [G8], [E8], E8[:, :], G8[:, :], AF.Exp)
    ACT(S, [E8], [SP8], SP8[:, :], E8[:, :], AF.Ln, bias=1.0)
    ACT(S, [igb], [r8], r8[:, 0:4], igb[:, :], AF.Exp)
    CP(S, "pool", [SP8], [r8], r8[:, 4:8], SP8[:, 4:8])
    TT(S, "dve", [SP8, k.coef], [logdec], logdec[:, :], SP8[:, :], k.coef[:, d, :], ALU.mult)
    CP(S, "pool", [logdec], [a.L1], a.L1[:, :, :], bc(logdec[:, 0:8].unsqueeze(2), [128, 8, 128]))
    psL = [S.bank(), S.bank()]
    for hh in range(2):
        MM(S, [a.L1, cst], [psL[hh]], [(psL[hh][:, q * 128:(q + 1) * 128], a.L1[:, hh * 4 + q, :], tri, True, True) for q in range(4)])
    psC = S.bank()
    MM(S, [logdec, cst], [psC], [(psC[:, 0:8], tri, logdec[:, 0:8], True, True)])
    CP(S, "dve", [psC], [cum], cum[:, :], psC[:, 0:8])
    for h in range(8):
        pl = psL[h // 4]
        q = h % 4
        STT(S, "dve", [pl, cum, cst], [a.DIFF], a.DIFF[:, h, :], pl[:, q * 128:(q + 1) * 128], cum[:, h:h + 1], neg, ALU.subtract, ALU.add)
    ACT(S, [a.DIFF], [a.DIFF], a.DIFF[:, :, :], a.DIFF[:, :, :], AF.Exp)
    ACT(S, [cum], [e8], e8[:, :], cum[:, :], AF.Exp)
    for hh in range(2):
        TT(S, "dve", [psL[hh], cum], [tmp8], tmp8[:, hh * 4:hh * 4 + 4], psL[hh][:, endc:512:128], cum[:, hh * 4:hh * 4 + 4], ALU.subtract)
        ACT(S, [psL[hh]], [aL], aL[:, hh * 4:hh * 4 + 4], psL[hh][:, endc:512:128], AF.Exp)
    ACT(S, [tmp8], [wend], wend[:, :], tmp8[:, :], AF.Exp)
    TT(S, "dve", [wend, r8], [wend], wend[:, :], wend[:, :], r8[:, :], ALU.mult)
    if not is_sample:
        TT(S, "dve", [tmp8, igb], [a.dec], a.dec[:, 0:4], tmp8[:, 0:4], igb[:, :], ALU.add)
        TT(S, "dve", [tmp8, cum], [a.dec], a.dec[:, 4:8], tmp8[:, 0:4], cum[:, 0:4], ALU.add)
        pm = S.bank()
        TR(S, [a.dec, cst], [pm], [(pm[0:4, 0:128], a.dec[:, 0:4], cview(k, "ident")),
                                   (pm[0:4, 128:256], a.dec[:, 4:8], cview(k, "ident"))])
        S.op("dve", lambda e: e.tensor_reduce(a.mt[0:4, 0:1], pm[0:4, 0:128], AX.X, ALU.max), [pm], [a.mt])
        TT(S, "dve", [pm, a.mrun], [a.mt], a.mt[0:4, 1:2], pm[0:4, 128:129], a.mrun[0:4, 0:1], ALU.add)
        TT(S, "dve", [a.mt], [a.mrun], a.mrun[0:4, 0:1], a.mt[0:4, 0:1], a.mt[0:4, 1:2], ALU.max)

    if getattr(k.cfg, "stop", 99) <= 1:
        return
    psQ = [S.bank(), S.bank()]
    for hh in range(2):
        MM(S, [hb, w_in], [psQ[hh]], [(psQ[hh][:, q * 130:(q + 1) * 130], w_in[:, kc, (hh * 2 + q) * 128:(hh * 2 + q + 1) * 128],
                                        hb[:, kc, 0:130], kc == 0, kc == 7) for q in range(2) for kc in range(8)])
    if getattr(k.cfg, "stop", 99) <= 1.05:
        return
    qkT = a.qkT
    CP(S, "act", [psQ[0]], [qkT], qkT[:, 0:2, :], psQ[0][:, 0:260].rearrange("p (b t) -> p b t", b=2)[:, :, 1:129])
    ACT(S, [psQ[1]], [qkT], qkT[:, 2:4, :], psQ[1][:, 0:260].rearrange("p (b t) -> p b t", b=2)[:, :, 1:129], AF.Identity, scale=0.125)
    ACT(S, [ps1], [a.k_tm], a.k_tm[:, :], ps1[:, 0:256], AF.Identity, scale=0.125)
    if getattr(k.cfg, "stop", 99) <= 1.1:
        return
    v4 = ps1[:, 256:512].rearrange("p (h e) -> p h e", h=4)
    TT(S, "dve", [ps1, r8], [a.xt_m], a.xt_m[:, :, 0:64], v4, bc(r8[:, 0:4].unsqueeze(2), [128, 4, 64]), ALU.mult)
    if getattr(k.cfg, "stop", 99) <= 1.12:
        return
    CP(S, "pool", [r8], [a.xt_m], a.xt_m[:, :, 64:65], r8[:, 0:4].unsqueeze(2))
    if getattr(k.cfg, "stop", 99) <= 1.15:
        return
    EXP = getattr(k.cfg, "exp", "")
    if EXP != "noTT":
        TT(S, "dve", [ps1, wend], [a.xh_m], a.xh_m[:, :, 0:64], v4, bc((r8 if EXP == "r8" else wend)[:, 0:4].unsqueeze(2), [128, 4, 64]), ALU.mult)
    if EXP != "noCP":
        CP(S, "pool", [wend], [a.xh_m], a.xh_m[:, :, 64:65], wend[:, 0:4].unsqueeze(2))
    if getattr(k.cfg, "stop", 99) <= 1.2:
        return
    psS = [S.bank(), S.bank()]
    hp = lambda h: slice((h % 2) * 64, (h % 2) * 64 + 64)
    for par in range(2):
        MM(S, [qkT], [psS[par]], [(psS[par][:, (h // 2) * 128:(h // 2 + 1) * 128], qkT[hp(h), 2 + h // 2, :], qkT[hp(h), h // 2, :], True, True)
                                  for h in (par, par + 2)])
    for par in range(2):
        TT(S, "dve", [psS[par], a.DIFF], [a.PTm], a.PTm[:, par:4:2, :], psS[par][:, 0:256].rearrange("p (h t) -> p h t", h=2),
           a.DIFF[:, par:4:2, :], ALU.mult)
    if getattr(k.cfg, "stop", 99) <= 1.4:
        return
    psO = S.bank()
    psI = [S.bank(), S.bank()]
    MM(S, [a.PTm, a.xt_m], [psO], [(psO[:, h * 65:h * 65 + 65], a.PTm[:, h, :], a.xt_m[:, h, 0:65], True, True) for h in range(4)])
    for par in range(2):
        MM(S, [qkT, a.Cnb], [psI[par]], [(psI[par][:, (h // 2) * 65:(h // 2) * 65 + 65], qkT[hp(h), h // 2, :], a.Cnb[hp(h), h // 2, 0:65], True, True)
                                         for h in (par, par + 2)])
    NUM = a.NUM
    for par in range(2):
        TT(S, "dve", [psI[par], e8], [NUM], NUM[:, par:4:2, :], psI[par][:, 0:130].rearrange("p (h e) -> p h e", h=2),
           bc(e8[:, par:4:2].unsqueeze(2), [128, 2, 65]), ALU.mult)
    TT(S, "dve", [psO, NUM], [NUM], NUM[:, :, :], psO[:, 0:260].rearrange("p (h e) -> p h e", h=4), NUM[:, :, :], ALU.add)
    if getattr(k.cfg, "stop", 99) <= 1.6:
        return
    HY = a.HY[i % 2]
    ACT(S, [NUM], [a.den], a.den[:, :].unsqueeze(2), NUM[:, :, 64:65], AF.Abs)
    TS(S, "dve", [a.den], [a.den], a.den[:, :], a.den[:, :], 1.0, None, ALU.max)
    TT(S, "pool", [a.den, k.cm1], [a.den], a.den[:, :], a.den[:, :], bc(k.cm1[:, 0:1], [128, 4]), ALU.pow)
    TT(S, "pool", [NUM, a.den], [HY], HY[:, 0:256].rearrange("p (h e) -> p h e", h=4), NUM[:, :, 0:64],
       bc(a.den[:, :].unsqueeze(2), [128, 4, 64]), ALU.mult)
    if getattr(k.cfg, "stop", 99) <= 1.8:
        return
    psU = S.bank()
    MM(S, [a.k_tm, a.xh_m], [psU], [(psU[:, h * 65:h * 65 + 65], a.k_tm[:, (h // 2) * 128:(h // 2 + 1) * 128], a.xh_m[:, h, 0:65], True, True)
                                     for h in range(4)])
    for h in range(4):
        STT(S, "dve", [a.Cn, aL, psU], [a.Cn], a.Cn[hp(h), h // 2, :], a.Cn[hp(h), h // 2, :], aL[hp(h), h:h + 1],
            psU[hp(h), h * 65:h * 65 + 65], ALU.mult, ALU.add)
    CP(S, "pool", [a.Cn], [a.Cnb], a.Cnb[:, :, 0:65], a.Cn[:, :, :])

    if getattr(k.cfg, "stop", 99) <= 2:
        return
    psX = [S.bank(), S.bank()]
    for hh in range(2):
        MM(S, [hb, w_in], [psX[hh]], [(psX[hh][:, q * 130:q * 130 + 130], w_in[:, kc, 2064 + (hh * 3 + q) * 128:2064 + (hh * 3 + q + 1) * 128],
                                        hb[:, kc, 0:130], kc == 0, kc == 7) for q in range(3) for kc in range(8)])
    XBC, XBCe, XBCb = a.XBC, a.XBCe, a.XBCb
    for b in range(6):
        pb, c0 = psX[b // 3], (b % 3) * 130
        ACT(S, [pb, k.scw], [XBC], XBC[:, b, :], pb[:, c0 + 1:c0 + 129], AF.Identity, bias=k.scw[:, 3, b:b + 1], scale=k.scw[:, 1, b:b + 1])
        STT(S, "dve", [pb, k.scw, XBC], [XBC], XBC[:, b, :], pb[:, c0:c0 + 128], k.scw[:, 0, b:b + 1], XBC[:, b, :], ALU.mult, ALU.add)
        STT(S, "dve", [pb, k.scw, XBC], [XBC], XBC[:, b, :], pb[:, c0 + 2:c0 + 130], k.scw[:, 2, b:b + 1], XBC[:, b, :], ALU.mult, ALU.add)
    ACT(S, [XBC], [XBCe], XBCe[:, :, :], XBC[:, :, :], AF.Exp, scale=-1.0)
    ACT(S, [XBCe], [XBCe], XBCe[:, :, :], XBCe[:, :, :], AF.Ln, bias=1.0)
    ACT(S, [XBCe], [XBCe], XBCe[:, :, :], XBCe[:, :, :], AF.Exp, scale=-1.0)
    TT(S, "pool", [XBC, XBCe], [XBCb], XBCb[:, :, :], XBC[:, :, :], XBCe[:, :, :], ALU.mult)
    psT = S.bank()
    pTv = bview(psT, BF16)
    TR(S, [XBCb, k.identb], [psT], [(pTv[:, b * 128:(b + 1) * 128], XBCb[:, b, :], k.identb[:, :]) for b in range(4)])
    CP(S, "act", [psT], [a.x_tm], a.x_tm[:, :], pTv[:, 0:256])
    CP(S, "act", [psT], [a.B_tm], a.B_tm[:, :, :], pTv[:, 256:512].rearrange("p (g n) -> p g n", g=2))
    x4 = a.x_tm[:, :].rearrange("p (h e) -> p h e", h=4)
    TT(S, "pool", [a.x_tm, r8], [a.xt_s], a.xt_s[:, :, :], x4, bc(r8[:, 4:8].unsqueeze(2), [128, 4, 64]), ALU.mult)
    TT(S, "pool", [a.x_tm, wend], [a.xh_s], a.xh_s[:, :, :], x4, bc(wend[:, 4:8].unsqueeze(2), [128, 4, 64]), ALU.mult)
    psS2 = S.bank()
    MM(S, [XBCb], [psS2], [(psS2[:, g * 128:(g + 1) * 128], XBCb[:, 2 + g, :], XBCb[:, 4 + g, :], True, True) for g in range(2)])
    for g in range(2):
        TT(S, "dve", [psS2, a.DIFF], [a.PTs], a.PTs[:, 2 * g:2 * g + 2, :],
           bc(psS2[:, g * 128:(g + 1) * 128].unsqueeze(1), [128, 2, 128]), a.DIFF[:, 4 + 2 * g:6 + 2 * g, :], ALU.mult)
    psY = S.bank()
    MM(S, [a.PTs, a.xt_s, XBCb, a.Hsb], [psY],
       [(psY[:, h * 64:(h + 1) * 64], a.PTs[:, h, :], a.xt_s[:, h, :], True, True) for h in range(4)]
       + [(psY[:, 256 + g * 128:256 + (g + 1) * 128], XBCb[:, 4 + g, :], a.Hsb[:, 2 * g:2 * g + 2, :].rearrange("p h e -> p (h e)"), True, True)
          for g in range(2)])
    Ysc = a.Ysc
    TT(S, "dve", [psY, e8], [Ysc], Ysc[:, :, :], psY[:, 256:512].rearrange("p (h e) -> p h e", h=4),
       bc(e8[:, 4:8].unsqueeze(2), [128, 4, 64]), ALU.mult)
    TT(S, "dve", [psY, Ysc], [HY], HY[:, 256:512], psY[:, 0:256], Ysc[:, :, :].rearrange("p h e -> p (h e)"), ALU.add)
    psU2 = S.bank()
    MM(S, [a.B_tm, a.xh_s], [psU2], [(psU2[:, g * 128:(g + 1) * 128], a.B_tm[:, g, :], a.xh_s[:, 2 * g:2 * g + 2, :].rearrange("p h e -> p (h e)"),
                                       True, True) for g in range(2)])
    TT(S, "dve", [a.Hs, aL], [a.Hs], a.Hs[:, :, :], a.Hs[:, :, :], bc(aL[:, 4:8].unsqueeze(2), [128, 4, 64]), ALU.mult)
    TT(S, "dve", [a.Hs, psU2], [a.Hs], a.Hs[:, :, :], psU2[:, 0:256].rearrange("p (h e) -> p h e", h=4), a.Hs[:, :, :], ALU.add)
    CP(S, "pool", [a.Hs], [a.Hsb], a.Hsb[:, :, :], a.Hs[:, :, :])

    if getattr(k.cfg, "stop", 99) <= 3:
        return
    tile_g = (off // 128) + i
    if not full:
        db = S.dbuf(("HF", tile_g))
        S.dma("pool", k.HF[off + 128 * i:off + 128 * (i + 1), :], HY[:, :], reads=[HY], writes=[db])
        return
    finalizeA(k, a, si, T, cond, off, i, nt, ps1, ps2, HY)


def finalizeA(k, a, si, T, cond, off, i, nt, ps1, ps2, HY):
    S, l = k.S, a.l
    w_in, w_out = a.w_in, a.w_out
    hb = a.hb[i % 3]
    hcur = lambda kc: hb[:, kc, 1:129]
    cst = k.cstb
    HYf = a.HYf[i % 2]
    f1, f2, f3 = a.fin1, a.fin2, a.fin3
    v4 = lambda ap: ap.rearrange("p (h e) -> p h e", h=4)
    ym, yg, ys = a.yall[:, 0, :], a.yall[:, 1, :], a.yall[:, 2, :]

    ps3, ps4 = S.bank(), S.bank()
    MM(S, [hb, w_in], [ps3], [(ps3[:, 0:512], hcur(kc), w_in[:, kc, 1296:1808], kc == 0, kc == 7) for kc in range(8)])
    MM(S, [hb, w_in], [ps4], [(ps4[:, 0:256], hcur(kc), w_in[:, kc, 1808:2064], kc == 0, kc == 7) for kc in range(8)]
       + [(ps4[:, 256:512], hcur(kc), w_in[:, kc, 1040:1296], kc == 0, kc == 7) for kc in range(8)])
    has_p, has_n = i > 0, i < nt - 1
    ps5 = S.bank()
    mm5 = []
    if has_p:
        hp_ = a.hb[(i - 1) % 3]
        mm5 += [(ps5[0:8, 0:256], hp_[:, kc, 121:129], w_in[:, kc, 1040:1296], kc == 0, kc == 7) for kc in range(8)]
    if has_n:
        hn_ = a.hb[(i + 1) % 3]
        mm5 += [(ps5[0:8, 256:512], hn_[:, kc, 1:9], w_in[:, kc, 1040:1296], kc == 0, kc == 7) for kc in range(8)]
    if mm5:
        rd = [w_in] + ([a.hb[(i - 1) % 3]] if has_p else []) + ([a.hb[(i + 1) % 3]] if has_n else [])
        MM(S, rd, [ps5], mm5)

    TT(S, "pool", [HY, HYf], [f1], f1[:, :], HY[:, 0:256], HYf[:, 0:256], ALU.add)
    S.op("dve", lambda e: e.tensor_reduce(a.st4[:, :], v4(f1[:, :]), AX.X, ALU.add), [f1], [a.st4])
    TS(S, "dve", [a.st4], [a.st4], a.st4[:, :], a.st4[:, :], 1.0 / 64.0, None, ALU.mult)
    TT(S, "pool", [f1, a.st4], [f1], v4(f1[:, :]), v4(f1[:, :]), bc(a.st4[:, :].unsqueeze(2), [128, 4, 64]), ALU.subtract)
    TT(S, "pool", [f1], [f2], f2[:, :], f1[:, :], f1[:, :], ALU.mult)
    S.op("dve", lambda e: e.tensor_reduce(a.st4b[:, :], v4(f2[:, :]), AX.X, ALU.add), [f2], [a.st4b])
    TS(S, "dve", [a.st4b], [a.st4b], a.st4b[:, :], a.st4b[:, :], 1.0 / 64.0, EPS, ALU.mult, ALU.add)
    TT(S, "pool", [a.st4b, k.cm05], [a.st4b], a.st4b[:, :], a.st4b[:, :], bc(k.cm05[:, 0:1], [128, 4]), ALU.pow)
    TT(S, "pool", [f1, a.st4b], [f1], v4(f1[:, :]), v4(f1[:, :]), bc(a.st4b[:, :].unsqueeze(2), [128, 4, 64]), ALU.mult)
    TT(S, "pool", [f1, k.mng], [f1], f1[:, :], f1[:, :], k.mng[:, :], ALU.mult)
    ACT(S, [a.eo], [a.eo], a.eo[:, :], a.eo[:, :], AF.Ln, bias=1.0)
    ACT(S, [a.eo], [a.eo], a.eo[:, :], a.eo[:, :], AF.Exp, scale=-1.0)
    TT(S, "pool", [f1, a.eo], [a.yall], ym, f1[:, :], a.eo[:, :], ALU.mult)

    CP(S, "act", [ps4], [a.z_sb], a.z_sb[:, :], ps4[:, 0:256])
    ACT(S, [ps4], [a.ez], a.ez[:, :], ps4[:, 0:256], AF.Exp, scale=-1.0)
    TT(S, "pool", [HY, HYf], [f2], f2[:, :], HY[:, 256:512], HYf[:, 256:512], ALU.add)
    TT(S, "pool", [a.x_tm, k.dsk], [f3], v4(f3[:, :]), v4(a.x_tm[:, :]), bc(k.dsk[:, :].unsqueeze(2), [128, 4, 64]), ALU.mult)
    TT(S, "pool", [f2, f3], [f2], f2[:, :], f2[:, :], f3[:, :], ALU.add)
    ACT(S, [a.ez], [a.ez], a.ez[:, :], a.ez[:, :], AF.Ln, bias=1.0)
    ACT(S, [a.ez], [a.ez], a.ez[:, :], a.ez[:, :], AF.Exp, scale=-1.0)
    TT(S, "pool", [a.ez, a.z_sb], [a.ez], a.ez[:, :], a.ez[:, :], a.z_sb[:, :], ALU.mult)
    TT(S, "pool", [f2, a.ez], [f2], f2[:, :], f2[:, :], a.ez[:, :], ALU.mult)
    TT(S, "pool", [f2], [f3], f3[:, :], f2[:, :], f2[:, :], ALU.mult)
    S.op("dve", lambda e: e.tensor_reduce(a.st4[:, 0:2], f3[:, :].rearrange("p (g e) -> p g e", g=2), AX.X, ALU.add), [f3], [a.st4])
    TS(S, "dve", [a.st4], [a.st4], a.st4[:, 0:2], a.st4[:, 0:2], 1.0 / 128.0, EPS, ALU.mult, ALU.add)
    TT(S, "pool", [a.st4, k.cm05], [a.st4], a.st4[:, 0:2], a.st4[:, 0:2], bc(k.cm05[:, 0:1], [128, 2]), ALU.pow)
    TT(S, "pool", [f2, a.st4], [f2], f2[:, :].rearrange("p (g e) -> p g e", g=2), f2[:, :].rearrange("p (g e) -> p g e", g=2),
       bc(a.st4[:, 0:2].unsqueeze(2), [128, 2, 128]), ALU.mult)
    TT(S, "pool", [f2, k.sng], [a.yall], ys, f2[:, :], k.sng[:, :], ALU.mult)

    CP(S, "act", [ps3], [a.gu], a.gu[:, :], ps3[:, 0:256])
    S.op("dve", lambda e: e.bn_stats(a.gst[:, :], ps3[:, 256:512]), [ps3], [a.gst])
    S.op("dve", lambda e: e.bn_aggr(a.gmv[:, :], a.gst[:, :]), [a.gst], [a.gmv])
    TS(S, "dve", [a.gmv], [a.gve], a.gve[:, :], a.gmv[:, 1:2], EPS, None, ALU.add)
    TT(S, "pool", [a.gve, k.cm05], [a.grs], a.grs[:, :], a.gve[:, :], k.cm05[:, :], ALU.pow)
    TS(S, "dve", [ps3, a.gmv, a.grs], [a.gvb], a.gvb[:, :], ps3[:, 256:512], a.gmv[:, 0:1], a.grs[:, 0:1], ALU.subtract, ALU.mult)
    psG = S.bank()
    MM(S, [k.wsT, a.gvb], [psG], [(psG[:, h * 64:(h + 1) * 64], k.wsT[:, h, :], a.gvb[:, h * 64:(h + 1) * 64], True, True) for h in range(4)])
    TT(S, "dve", [psG, k.bsT], [f3], v4(f3[:, :]), v4(psG[:, 0:256]), bc(k.bsT[:, :].unsqueeze(2), [128, 4, 64]), ALU.add)
    TT(S, "pool", [f3, a.gu], [a.yall], yg, f3[:, :], a.gu[:, :], ALU.mult)

    CP(S, "act", [ps4], [a.pc], a.pc[:, :], ps4[:, 256:512])
    if has_p:
        CP(S, "act", [ps5], [a.pcP], a.pcP[:, :], ps5[0:8, 0:256])
    if has_n:
        CP(S, "act", [ps5], [a.pcN], a.pcN[:, :], ps5[0:8, 256:512])
    var = "int" if (has_p and has_n) else ("first" if has_n else ("last" if has_p else "int"))
    psP = S.bank()
    mmp = []
    for g in range(4):
        o_ = psP[:, g * 64:(g + 1) * 64]
        seqm = [(cview(k, f"pA{g}{var}"), a.pc[:, g * 64:(g + 1) * 64])]
        if has_p:
            seqm.append((cview(k, f"pP{g}", 8), a.pcP[0:8, g * 64:(g + 1) * 64]))
        if has_n:
            seqm.append((cview(k, f"pN{g}", 8), a.pcN[0:8, g * 64:(g + 1) * 64]))
        for n_, (lh, rh) in enumerate(seqm):
            mmp.append((o_, lh, rh, n_ == 0, n_ == len(seqm) - 1))
    MM(S, [a.c2b, a.pc, a.pcP, a.pcN], [psP], mmp)
    CP(S, "act", [psP], [a.plb], a.plb[:, :], psP[:, 0:256])
    psT2 = S.bank()
    t2v = bview(psT2, BF16)
    TR(S, [a.plb, k.identb], [psT2], [(t2v[:, j * 128:(j + 1) * 128], a.plb[:, j * 128:(j + 1) * 128], k.identb[:, :]) for j in range(2)])
    CP(S, "dve", [psT2], [a.plT], a.plT[:, :, :], t2v[:, 0:256].rearrange("p (j t) -> p j t", j=2))
    psW = S.bank()
    MM(S, [k.wpb, a.plT], [psW], [(psW[:, j * 128:(j + 1) * 128], k.wpb[:, j, :], a.plT[:, j, :], True, True) for j in range(2)])
    cT = a.concatT
    for j in range(2):
        ACT(S, [psW, k.psc], [cT], cT[:, 2 + j, :], psW[:, j * 128:(j + 1) * 128], AF.Identity, scale=k.psc[:, j:j + 1])

    psT3 = S.bank()
    t3v = bview(psT3, BF16)
    TR(S, [a.yall, k.identb], [psT3], [(t3v[:, (m3 * 2 + j) * 128:(m3 * 2 + j + 1) * 128], a.yall[:, m3, j * 128:(j + 1) * 128], k.identb[:, :])
                                      for m3 in range(3) for j in range(2)])
    CP(S, "dve", [psT3], [cT], cT[:, 0:2, :], t3v[:, 0:256].rearrange("p (j t) -> p j t", j=2))
    CP(S, "act", [psT3], [cT], cT[:, 4:8, :], t3v[:, 256:768].rearrange("p (j t) -> p j t", j=4))

    p0, p1 = S.bank(), S.bank()
    for hlf, pb in enumerate((p0, p1)):
        MM(S, [cT, w_out], [pb], [(pb[:, :], cT[:, kc, :], w_out[:, kc, hlf * 512:(hlf + 1) * 512], kc == 0, kc == 7) for kc in range(8)])
    xb = a.xq[i % 2]
    resid_ln(k, xb, (p0, p1), xb, a.rsm)
    r0 = off + 128 * i
    db = S.dbuf(("xoutA", l, r0 // 128))
    S.dma("pool", rows_ap(k, a.dst, "out", r0, 128), xb[:, :], reads=[xb], writes=[db])
    if a.dst is None:
        k.final_bufs.append(db)


_CACHE = {}


def gather_outputs(results, cfg, n):
    NP, Tp, Ts = cfg.NP, cfg.Tp, cfg.Ts
    y_p = np.concatenate([r["yp"].reshape(NP, Tp, D) for r in results], 0)
    y_s = np.stack([r["ys"].reshape(Ts, D) for r in results], 0)
    oc = np.concatenate([r["oc"] for r in results], 0)
    on = np.concatenate([r["on"] for r in results], 0)
    om = np.concatenate([r["om"].reshape(NP, 2, 2, 4) for r in results], 0)
    os_ = np.concatenate([r["os"] for r in results], 0)
    f = lambda a: np.ascontiguousarray(a, dtype=np.float32)
    return (f(y_p), f(y_s), f(oc), f(on), f(om), f(os_))


def kernel(**inputs):
    n = 8
    xs = np.asarray(inputs["x_sample"])
    xp = np.asarray(inputs["x_prompt"])
    cfg = Cfg(Ts=xs.shape[1], NP=xp.shape[0] // n, Tp=xp.shape[1], L=2)
    key = (cfg.Ts, cfg.NP, cfg.Tp)
    if key not in _CACHE:
        _CACHE[key] = build(cfg)
    nc, _ = _CACHE[key]
    in_maps = [shard_inputs(inputs, cfg, c) for c in range(n)]
    res = run_bass_kernel_spmd(nc, in_maps, core_ids=list(range(n)))
    return gather_outputs(res.results, cfg, n)
```

```python
import math
import numpy as np
import ml_dtypes
from contextlib import ExitStack
import concourse.bass as bass
import concourse.mybir as mybir
from concourse.bass_utils import run_bass_kernel_spmd

F32 = mybir.dt.float32
BF16 = mybir.dt.bfloat16
AF = mybir.ActivationFunctionType
ALU = mybir.AluOpType
AX = mybir.AxisListType
DTSZ = {F32: 4, BF16: 2}

D = 1024
DIN = 2840
DFF = 2816
EPS = 1e-5
ALPHA = 4.0 ** 0.25
NEG = -30000.0

ENGS = ("pe", "act", "dve", "pool", "sp")
EPOCH = 30000
NDMA_SEM = 8


def prod(l):
    r = 1
    for x in l:
        r *= int(x)
    return r


class Buf:
    __slots__ = ("name", "v", "last_w", "readers", "off", "excl")

    def __init__(self, name, v, floor=None):
        self.off = -1
        self.excl = False
        self.name = name
        self.v = v
        self.last_w = floor
        self.readers = []

    def __getitem__(self, k):
        return self.v[k]


class Op:
    __slots__ = ("eng", "fn", "deps", "is_dma", "idx", "sig", "has_dep", "vc", "name", "cost")

    def __init__(self, eng, fn, is_dma, name):
        self.cost = 0.4
        self.eng = eng
        self.fn = fn
        self.is_dma = is_dma
        self.deps = []
        self.sig = None
        self.has_dep = False
        self.vc = None
        self.name = name


class Sched:
    def __init__(self, nc, es, arena_bytes):
        self.nc = nc
        self.es = es
        self.ops = []
        self.floor = None
        self.bufs = []
        self.arena = es.enter_context(nc.sbuf_tensor("arena", [128, arena_bytes // 4], F32))
        self.arena_bytes = arena_bytes
        self.off = 0
        self.peak = 0
        self.banks = []
        for i in range(8):
            t = es.enter_context(nc.psum_tensor(f"bank{i}", [128, 512], F32))
            self.banks.append(Buf(f"bank{i}", t))
            self.banks[-1].excl = True
        self.bank_i = 0
        self.dram_bufs = {}

    def sb(self, name, shape, dtype=F32):
        shape = [int(s) for s in shape]
        if getattr(self, "verbose", False):
            print(f"  sb {name} {shape} {prod(shape[1:]) * DTSZ[dtype]} at {self.off}")
        n = prod(shape[1:])
        nb = n * DTSZ[dtype]
        off = (self.off + 31) // 32 * 32
        assert off + nb <= self.arena_bytes, f"arena overflow allocating {name}: {off + nb}"
        self.off = off + nb
        self.peak = max(self.peak, self.off)
        h = self.arena if dtype == F32 else self.arena.bitcast(dtype)
        e0 = off // DTSZ[dtype]
        v = h[0:shape[0], e0:e0 + n]
        if len(shape) > 2:
            names = " ".join(f"d{i}" for i in range(len(shape) - 1))
            kw = {f"d{i}": shape[i + 1] for i in range(len(shape) - 1)}
            v = v.rearrange(f"p ({names}) -> p {names}", **kw)
        b = Buf(name, v, self.floor)
        b.off = off
        self.bufs.append(b)
        return b

    def mark(self):
        return self.off

    def release(self, mark):
        self.barrier()
        self.bufs = [b for b in self.bufs if b.off < mark]
        self.off = mark

    def bank(self):
        b = self.banks[self.bank_i]
        self.bank_i = (self.bank_i + 1) % 8
        return b

    def dbuf(self, key):
        if key not in self.dram_bufs:
            self.dram_bufs[key] = Buf(str(key), None, None)
        return self.dram_bufs[key]

    def op(self, eng, fn, reads=(), writes=(), name=None, dma=False, cost=None):
        o = Op(eng, fn, dma, name)
        if cost is not None:
            o.cost = cost
        deps = set()
        ex = [b for b in reads if b.excl]
        if ex:
            reads = [b for b in reads if not b.excl]
            writes = list(writes) + [b for b in ex if b not in writes]
        for b in reads:
            if b.last_w is not None:
                deps.add(b.last_w)
        for b in writes:
            if b.last_w is not None:
                deps.add(b.last_w)
            for r in b.readers:
                deps.add(r)
        o.deps = list(deps)
        o.idx = len(self.ops)
        for d in o.deps:
            d.has_dep = True
        for b in reads:
            b.readers.append(o)
        for b in writes:
            b.last_w = o
            b.readers = []
        self.ops.append(o)
        return o

    def dma(self, q, out, in_, reads=(), writes=(), name=None, **kw):
        nbytes = prod(out.shape) * 4
        return self.op(q, lambda e: e.dma_start(out=out, in_=in_, **kw), reads, writes, name=name, dma=True,
                       cost=2.0 + nbytes / 150e3)

    def barrier(self):
        allb = self.bufs + self.banks + list(self.dram_bufs.values())
        o = self.op("sp", None, reads=[], writes=allb, name="barrier")
        self.floor = o
        return o

    def list_schedule(self, ops):
        import heapq
        LAT = 1.2
        out = []
        seg = []
        segs = []
        for o in ops:
            if o.fn is None:
                segs.append(seg)
                segs.append([o])
                seg = []
            else:
                seg.append(o)
        segs.append(seg)
        finish = {}
        for seg in segs:
            if len(seg) <= 1:
                for o in seg:
                    finish[o] = 0.0
                    out.append(o)
                continue
            inseg = set(seg)
            indeg = {}
            users = {}
            for o in seg:
                n = 0
                for d in o.deps:
                    if d in inseg:
                        n += 1
                        users.setdefault(d, []).append(o)
                indeg[o] = n
            eng_time = {e: 0.0 for e in ENGS}
            ready_at = {}
            heap = []
            for o in seg:
                if indeg[o] == 0:
                    ready_at[o] = 0.0
                    heapq.heappush(heap, (0.0, o.idx, o))
            while heap:
                best = None
                cand = []
                while heap and len(cand) < 24:
                    cand.append(heapq.heappop(heap))
                bi = None
                for ci, (ra, idx, o) in enumerate(cand):
                    stt = max(ra, eng_time[o.eng])
                    key = (stt, idx)
                    if best is None or key < best:
                        best, bi = key, ci
                ra, idx, o = cand.pop(bi)
                for c in cand:
                    heapq.heappush(heap, c)
                stt = best[0]
                if o.is_dma:
                    eng_time[o.eng] = stt + 0.15
                    fin_t = stt + o.cost
                else:
                    fin_t = stt + o.cost
                    eng_time[o.eng] = fin_t
                finish[o] = fin_t
                out.append(o)
                for u in users.get(o, ()):
                    t = fin_t + (LAT if u.eng != o.eng else 0.3)
                    if ready_at.get(u, 0.0) < t:
                        ready_at[u] = t
                    indeg[u] -= 1
                    if indeg[u] == 0:
                        heapq.heappush(heap, (ready_at[u], u.idx, u))
            self.est_time = getattr(self, "est_time", 0.0) + max(eng_time.values())
        for i, o in enumerate(out):
            o.idx = i
        return out

    def emit(self, final_bufs):
        nc, es = self.nc, self.es
        fin = self.op("sp", None, reads=list(final_bufs), name="final")
        if getattr(self, "reorder", True):
            self.ops = self.list_schedule(self.ops)
        cnt, dma_n, semkeys = {}, {}, []
        for o in self.ops:
            if not o.has_dep:
                continue
            if o.is_dma:
                n = dma_n.get(o.eng, 0)
                dma_n[o.eng] = n + 1
                key = ("dma", o.eng, n % NDMA_SEM)
                cnt[key] = cnt.get(key, 0) + 16
                o.sig = (key, cnt[key])
            else:
                tot = cnt.get(("n", o.eng), 0)
                cnt[("n", o.eng)] = tot + 1
                key = ("c", o.eng, tot // EPOCH)
                o.sig = (key, tot % EPOCH + 1)
            if o.sig[0] not in semkeys:
                semkeys.append(o.sig[0])
        sems = {k: es.enter_context(nc.semaphore("s_" + "_".join(str(x) for x in k))) for k in semkeys}
        per_eng = {e: [] for e in ENGS}
        seen = {e: {} for e in ENGS}
        nwaits = 0
        for o in self.ops:
            s = seen[o.eng]
            need = {}
            for d in o.deps:
                k, c = d.sig
                if s.get(k, 0) < c:
                    need[k] = max(need.get(k, 0), c)
            if o.is_dma and o.sig is not None:
                k, c = o.sig
                if c > 16 and s.get(k, 0) < c - 16:
                    need[k] = max(need.get(k, 0), c - 16)
            for d in o.deps:
                for k, c in d.vc.items():
                    if s.get(k, 0) < c:
                        s[k] = c
            for k, c in need.items():
                if s.get(k, 0) < c:
                    s[k] = c
            o.deps = need
            nwaits += len(need)
            vc = dict(s)
            if o.sig is not None:
                vc[o.sig[0]] = max(vc.get(o.sig[0], 0), o.sig[1])
                if not o.is_dma:
                    for ep in range(o.sig[0][2]):
                        vc[("c", o.eng, ep)] = EPOCH
            o.vc = vc
            per_eng[o.eng].append(o)

        def body_for(engname):
            def body(eng):
                for o in per_eng[engname]:
                    for k, c in o.deps.items():
                        eng.wait_ge(sems[k], c)
                    if o.fn is not None:
                        ins = o.fn(eng)
                        if o.sig is not None:
                            ins.then_inc(sems[o.sig[0]], 16 if o.is_dma else 1)
                    elif o.sig is not None:
                        eng.nop().then_inc(sems[o.sig[0]], 1)
            return body

        with nc.Block() as block:
            block.sync(body_for("sp"))
            block.scalar(body_for("act"))
            block.vector(body_for("dve"))
            block.gpsimd(body_for("pool"))
            block.tensor(body_for("pe"))
        return {"ops": len(self.ops), "waits": nwaits, "sems": len(sems),
                "per_eng": {e: len(v) for e, v in per_eng.items()}, "sbuf_peak": self.peak}


def _c(out, base=0.25, per=1.0 / 1000.0):
    return base + prod(out.shape[1:]) * per


def ACT(S, r, w, out, in_, func, bias=None, scale=None):
    kw = {}
    if bias is not None:
        kw["bias"] = bias
    if scale is not None:
        kw["scale"] = scale
    return S.op("act", lambda e: e.activation(out, in_, func, **kw), r, w, cost=_c(out, 0.3, 1 / 1200.0))


def TS(S, eng, r, w, out, in0, s1, s2, op0, op1=None):
    if op1 is None:
        return S.op(eng, lambda e: e.tensor_scalar(out, in0, s1, None, op0), r, w, cost=_c(out))
    return S.op(eng, lambda e: e.tensor_scalar(out, in0, s1, s2, op0, op1), r, w, cost=_c(out))


def TT(S, eng, r, w, out, in0, in1, op):
    return S.op(eng, lambda e: e.tensor_tensor(out, in0, in1, op), r, w, cost=_c(out))


def STT(S, eng, r, w, out, in0, scalar, in1, op0, op1):
    return S.op(eng, lambda e: e.scalar_tensor_tensor(out, in0, scalar, in1, op0, op1), r, w, cost=_c(out))


def CP(S, eng, r, w, out, in_):
    if eng == "act":
        return S.op("act", lambda e: e.copy(out, in_), r, w, cost=_c(out, 0.3, 1 / 1200.0))
    return S.op(eng, lambda e: e.tensor_copy(out, in_), r, w, cost=_c(out))


def MSET(S, eng, w, out, val):
    return S.op(eng, lambda e: e.memset(out, val), [], w)


def MM(S, r, w, mms):
    mms = list(mms)

    def fn(e):
        ins = None
        for (o, l, rh, st, sp) in mms:
            ins = e.matmul(o, l, rh, start=st, stop=sp)
        return ins
    cost = 0.1
    for (o, l, rh, st, sp) in mms:
        cost += max(64, prod(rh.shape[1:])) / 2400.0 * (4.0 if rh.dtype == F32 else 1.0) + 0.02
    return S.op("pe", fn, r, w, cost=cost)


def TR(S, r, w, trs):
    trs = list(trs)

    def fn(e):
        ins = None
        for (o, i, idt) in trs:
            ins = e.transpose(o, i, idt)
        return ins
    return S.op("pe", fn, r, w, cost=0.1 + 0.12 * len(trs))


def bc(ap, shape):
    return ap.to_broadcast([int(s) for s in shape])


POOL_W = (2, 4, 8, 16)


def make_consts():
    cols = {}
    parts = []
    off = [0]

    def add(name, arr):
        a = np.zeros((128, arr.shape[1]), np.float32)
        a[:arr.shape[0]] = arr
        cols[name] = (off[0], arr.shape[1])
        off[0] += arr.shape[1]
        parts.append(a)

    idx = np.arange(128)
    s_, t_ = idx[:, None], idx[None, :]
    add("ident", np.eye(128, dtype=np.float32))
    add("ones", np.ones((128, 128), np.float32))
    add("tri0", (s_ <= t_).astype(np.float32))
    add("tri1", (s_ >= t_).astype(np.float32))
    add("neg0", np.where(s_ <= t_, 0.0, NEG).astype(np.float32))
    add("neg1", np.where(s_ >= t_, 0.0, NEG).astype(np.float32))
    n1 = off[0]
    for g, w in enumerate(POOL_W):
        h = w // 2
        band = ((s_ >= t_ - h) & (s_ < t_ + h)).astype(np.float32)
        cnt_int = np.full(128, float(w))
        cnt_first = (idx + h) - np.maximum(idx - h, 0)
        cnt_last = np.minimum(idx + h, 128) - (idx - h)
        eye = np.eye(128, dtype=np.float32)
        add(f"pA{g}int", band / cnt_int[None, :] - eye)
        add(f"pA{g}first", band / cnt_first[None, :] - eye)
        add(f"pA{g}last", band / cnt_last[None, :] - eye)
        sp = np.arange(8)[:, None]
        add(f"pP{g}", (((sp - 8) >= t_ - h) & ((sp - 8) < t_ + h)).astype(np.float32) / w)
        add(f"pN{g}", (((128 + sp) >= t_ - h) & ((128 + sp) < t_ + h)).astype(np.float32) / w)
    add("jrow", np.tile(np.arange(256, dtype=np.float32)[None, :], (128, 1)))
    add("pcol", (idx % 64).astype(np.float32)[:, None])
    full = np.concatenate(parts, axis=1)
    cols2 = {kk: (o - n1, n) for kk, (o, n) in cols.items() if o >= n1}
    cols1 = {kk: (o, n) for kk, (o, n) in cols.items() if o < n1}
    return full[:, :n1].copy(), cols1, full[:, n1:].copy(), cols2


class Cfg:
    def __init__(self, Ts=4096, NP=4, Tp=256, L=2, debug=()):
        self.Ts, self.NP, self.Tp, self.L = Ts, NP, Tp, L
        self.debug = tuple(debug)
        self.seqs = [("s", Ts, 0, 0)] + [(f"p{j}", Tp, 1, Ts + j * Tp) for j in range(NP)]
        self.Ttot = Ts + NP * Tp


INPUT_SPECS = lambda c: [
    ("xs", [c.Ts, D]), ("xp", [c.NP * c.Tp, D]),
    ("st_c", [2, 2, 4, 64, 64]), ("st_n", [2, 2, 4, 64]), ("st_m", [2, 8]), ("st_s", [2, 2, 4, 64, 128]),
    ("cond", [2, D]),
    ("w_ada", [2, D, 6 * D]), ("b_ada", [2, 6 * D]), ("w_in", [2, D, DIN]), ("b_ig", [2, 8]), ("b_fg", [2, 8]),
    ("mnorm_g", [2, 256]), ("w_pool", [2, 4, 64, 64]), ("pool_scale", [2, 256]), ("w_sp", [2, 4, 128, 128]),
    ("b_sp", [2, 4, 128]), ("sconv_w", [2, 3, 768]), ("sconv_b", [2, 768]), ("dt_bias", [2, 8]),
    ("a_log", [2, 8]), ("ssd_d", [2, 4]), ("snorm_g", [2, 256]), ("w_out", [2, D, D]),
    ("ln1_g", [2, D]), ("ln1_b", [2, D]), ("w_up", [2, D, 2 * DFF]), ("fconv_w", [2, 3, 2 * DFF]),
    ("fconv_b", [2, 2 * DFF]), ("w_dn", [2, DFF, D]), ("ln2_g", [2, D]), ("ln2_b", [2, D]),
]
OUTPUT_SPECS = lambda c: [
    ("ys", [c.Ts, D]), ("yp", [c.NP * c.Tp, D]), ("oc", [c.NP, 2, 2, 4, 64, 64]), ("on", [c.NP, 2, 2, 4, 64]),
    ("om", [c.NP, 2, 8]), ("os", [c.NP, 2, 2, 4, 64, 128]),
]


class K:
    pass


def build(cfg):
    nc = bass.Bass("TRN2", target_bir_lowering=False)
    cst_np, ccols, cst2_np, ccols2 = make_consts()
    I = {}
    for name, shape in INPUT_SPECS(cfg):
        I[name] = nc.dram_tensor(name, shape, F32, kind="ExternalInput")
    I["cst"] = nc.dram_tensor("cst", list(cst_np.shape), F32, kind="ExternalInput")
    I["cst2"] = nc.dram_tensor("cst2", list(cst2_np.shape), F32, kind="ExternalInput")
    O = {}
    for name, shape in OUTPUT_SPECS(cfg):
        O[name] = nc.dram_tensor(name, shape, F32, kind="ExternalOutput")
    DBG = {}
    for name, shape in cfg.debug:
        DBG[name] = nc.dram_tensor("dbg_" + name, shape, F32, kind="ExternalOutput")
    ntile = cfg.Ttot // 128
    XA = nc.dram_tensor("scr_xa", [cfg.Ttot, D], F32, kind="Internal")
    XB = nc.dram_tensor("scr_xb", [cfg.Ttot, D], F32, kind="Internal")
    HF = nc.dram_tensor("scr_hf", [cfg.Ttot, 512], F32, kind="Internal")
    HT = nc.dram_tensor("scr_ht", [ntile, 128, 8 * 128], BF16, kind="Internal")
    ED = nc.dram_tensor("scr_e", [64, 512], F32, kind="Internal")

    with ExitStack() as es:
        S = Sched(nc, es, 204 * 1024)
        k = K()
        k.S, k.cfg, k.I, k.O, k.DBG = S, cfg, I, O, DBG
        k.XA, k.XB, k.HF, k.HT, k.ED = XA, XB, HF, HT, ED
        k.final_bufs = []
        k.ccols2 = ccols2
        k.W2 = cst2_np.shape[1]
        setup(k, ccols)
        for l in range(cfg.L):
            layer(k, l)
        stats = S.emit(k.final_bufs)
    return nc, stats


def cview(k, name, rows=128):
    if name in k.ccols:
        o, n = k.ccols[name]
        return k.cst[0:rows, o:o + n]
    o, n = k.ccols2[name]
    return k.cst2[0:rows, o:o + n]


def load_cst2(k):
    S = k.S
    b = S.sb("cst2", [128, k.W2])
    k.cst2b = b
    k.cst2 = b.v
    S.dma("sp", b[:, :], k.I["cst2"][:, :], writes=[b])
    return b


def setup(k, ccols):
    S, I, cfg = k.S, k.I, k.cfg
    k.ccols = ccols
    W = sum(n for (_, n) in ccols.values())
    cstb = S.sb("cst", [128, W])
    k.cstb = cstb
    k.cst = cstb.v
    S.dma("sp", cstb[:, :], I["cst"][:, :], writes=[cstb])
    k.identb = S.sb("identb", [128, 128], BF16)
    CP(S, "dve", [cstb], [k.identb], k.identb[:, :], cview(k, "ident"))
    k.cm05 = S.sb("cm05", [128, 1])
    MSET(S, "pool", [k.cm05], k.cm05[:, :], -0.5)
    k.cm1 = S.sb("cm1", [128, 1])
    MSET(S, "pool", [k.cm1], k.cm1[:, :], -1.0)

    L = cfg.L
    k.modT = S.sb("modT", [128, L, 48, 2])
    layer_consts_alloc(k)
    m1 = S.mark()
    c2b = load_cst2(k)
    fr = S.sb("pe_fr", [64, 256])
    ang = S.sb("pe_ang", [64, 256])
    et = S.sb("pe_e", [64, 512])
    et2 = S.sb("pe_e2", [64, 512])
    sq = S.sb("pe_sq", [64, 256])
    ACT(S, [c2b], [fr], fr[:, :], cview(k, "jrow", 64), AF.Exp, scale=-math.log(10000.0) / 256.0)
    TS(S, "dve", [fr, c2b], [ang], ang[:, :], fr[:, :], cview(k, "pcol", 64), None, ALU.mult)
    ACT(S, [ang], [et], et[:, 0:256], ang[:, :], AF.Sin, scale=1.0 / 32.0)
    ACT(S, [ang], [et], et[:, 256:512], ang[:, :], AF.Sin, scale=-1.0 / 32.0, bias=math.pi / 2.0)
    cur, nxt = et, et2
    for it in range(5):
        TT(S, "dve", [cur], [sq], sq[:, :], cur[:, 0:256], cur[:, 0:256], ALU.mult)
        STT(S, "dve", [cur], [nxt], nxt[:, 0:256], cur[:, 0:256], 2.0, cur[:, 256:512], ALU.mult, ALU.mult)
        TS(S, "dve", [sq], [nxt], nxt[:, 256:512], sq[:, :], -2.0, 1.0, ALU.mult, ALU.add)
        cur, nxt = nxt, cur
    et = cur
    edb = S.dbuf("ED")
    S.dma("sp", k.ED[:, :], et[:, :], reads=[et], writes=[edb])

    condT = S.sb("condT", [128, 8, 2])
    for c in range(2):
        S.dma("sp", condT[:, :, c], I["cond"][c].rearrange("(kc p) -> p kc", p=128), writes=[condT],
              allow_slow_non_contiguous=True)
    esg = S.sb("cond_e", [128, 8, 2])
    ACT(S, [condT], [esg], esg[:, :, :], condT[:, :, :], AF.Exp, scale=-1.0)
    TS(S, "dve", [esg], [esg], esg[:, :, :], esg[:, :, :], 1.0, None, ALU.add)
    TT(S, "pool", [esg, k.cm1], [esg], esg[:, :, :], esg[:, :, :], bc(k.cm1[:, 0:1].unsqueeze(2), [128, 8, 2]), ALU.pow)
    TT(S, "pool", [condT, esg], [condT], condT[:, :, :], condT[:, :, :], esg[:, :, :], ALU.mult)
    badaT = S.sb("badaT", [128, L, 48])
    k.cf_st = S.sb("cf_st0", [128, 128])
    for l in range(L):
        colform(k, badaT, badaT[:, l, :], I["b_ada"][l].rearrange("(j p) -> j p", p=128), 48)
    wst = [S.sb(f"wada_st{j}", [128, 8, 512]) for j in range(2)]
    n = 0
    for l in range(L):
        for cb in range(12):
            st = wst[n % 2]
            n += 1
            S.dma("sp", st[:, :, :], I["w_ada"][l, :, cb * 512:(cb + 1) * 512].rearrange("(kc p) n -> p kc n", p=128),
                  writes=[st])
            pb = S.bank()
            mms = []
            for sub in range(4):
                for kc in range(8):
                    mms.append((pb[:, sub * 2:sub * 2 + 2], st[:, kc, sub * 128:(sub + 1) * 128], condT[:, kc, :],
                                kc == 0, kc == 7))
            MM(S, [st, condT], [pb], mms)
            TT(S, "dve", [pb, badaT], [k.modT],
               k.modT[:, l, cb * 4:cb * 4 + 4, :],
               pb[:, 0:8].rearrange("p (s c) -> p s c", c=2),
               bc(badaT[:, l, cb * 4:cb * 4 + 4].unsqueeze(2), [128, 4, 2]), ALU.add)
    for l in range(L):
        for grp in (1, 4):
            TS(S, "dve", [k.modT], [k.modT], k.modT[:, l, grp * 8:(grp + 1) * 8, :],
               k.modT[:, l, grp * 8:(grp + 1) * 8, :], 1.0, None, ALU.add)
    S.release(m1)


class nc_allow:
    def __init__(self, k):
        pass

    def __enter__(self):
        return self

    def __exit__(self, *a):
        return False


def bview(bank, dtype=F32):
    return bank.v if dtype == F32 else bank.v.bitcast(dtype)


def layer_consts_alloc(k):
    S = k.S
    k.bif = S.sb("bif", [128, 16])
    k.coef = S.sb("coef", [128, 2, 8])
    k.dtb = S.sb("dtb", [128, 8])
    k.dsk = S.sb("dsk", [128, 4])
    k.mng = S.sb("mng", [128, 256])
    k.sng = S.sb("sng", [128, 256])
    k.psc = S.sb("psc", [128, 2])
    k.wpb = S.sb("wpb", [128, 2, 128], BF16)
    k.wsT = S.sb("wsT", [128, 4, 128], BF16)
    k.bsT = S.sb("bsT", [128, 4])
    k.scw = S.sb("scw", [128, 4, 6])
    k.fcw = S.sb("fcw", [128, 4, 44])
    k.lng = S.sb("lng", [128, D])
    k.lnb = S.sb("lnb", [128, D])
    k.gbc = S.sb("gbc", [128, D])
    k.ttmp = [S.sb(f"ttmp{j}", [128, D]) for j in range(1)]
    k.small = {}


def colform(k, dst_buf, dst_ap, src_ap, nb):
    S = k.S
    st = k.cf_st
    S.dma("sp", st[0:nb, :], src_ap, writes=[st])
    pb = S.bank()
    TR(S, [st, k.cstb], [pb], [(pb[:, 0:nb], st[0:nb, :], cview(k, "ident")[0:nb, 0:nb])])
    CP(S, "dve", [pb], [dst_buf], dst_ap, pb[:, 0:nb])


def load_layer_consts(k, l):
    S, I = k.S, k.I
    m = S.mark()
    k.cf_st = S.sb("cf_st", [128, 128])
    row = lambda name, a, b: I[name][l:l + 1, a:b].partition_broadcast(128)
    S.dma("sp", k.bif[:, 0:8], row("b_ig", 0, 8), writes=[k.bif])
    S.dma("sp", k.bif[:, 8:16], row("b_fg", 0, 8), writes=[k.bif])
    TS(S, "dve", [k.bif], [k.bif], k.bif[:, 8:16], k.bif[:, 8:16], -1.0, None, ALU.mult)
    al = S.sb("al_tmp", [128, 8])
    S.dma("sp", al[:, :], row("a_log", 0, 8), writes=[al])
    ACT(S, [al], [al], al[:, :], al[:, :], AF.Exp)
    MSET(S, "pool", [k.coef], k.coef[:, :, :], -1.0)
    TS(S, "dve", [al, k.coef], [k.coef], k.coef[:, :, 4:8], al[:, :].rearrange("p (d h) -> p d h", d=2), -1.0, None, ALU.mult)
    S.dma("sp", k.dtb[:, :], row("dt_bias", 0, 8), writes=[k.dtb])
    S.dma("sp", k.dsk[:, :], row("ssd_d", 0, 4), writes=[k.dsk])
    S.dma("sp", k.mng[:, :], row("mnorm_g", 0, 256), writes=[k.mng])
    S.dma("sp", k.sng[:, :], row("snorm_g", 0, 256), writes=[k.sng])
    colform(k, k.psc, k.psc[:, :], I["pool_scale"][l].rearrange("(j p) -> j p", p=128), 2)
    wp32 = S.sb("wp32", [128, 2, 128])
    MSET(S, "pool", [wp32], wp32[:, :, :], 0.0)
    for g in range(4):
        pr = slice((g % 2) * 64, (g % 2) * 64 + 64)
        S.dma("sp", wp32[pr, g // 2, (g % 2) * 64:(g % 2) * 64 + 64], I["w_pool"][l, g], writes=[wp32])
    CP(S, "dve", [wp32], [k.wpb], k.wpb[:, :, :], wp32[:, :, :])
    ws32 = S.sb("ws32", [128, 4, 128])
    S.dma("sp", ws32[:, :, :], I["w_sp"][l].rearrange("h t s -> t h s"), writes=[ws32])
    pb = S.bank()
    TR(S, [ws32, k.cstb], [pb], [(pb[:, h * 128:(h + 1) * 128], ws32[:, h, :], cview(k, "ident")) for h in range(4)])
    CP(S, "act", [pb], [k.wsT], k.wsT[:, :, :], pb[:, :].rearrange("p (h t) -> p h t", h=4))
    colform(k, k.bsT, k.bsT[:, :], I["b_sp"][l], 4)
    for tap in range(3):
        colform(k, k.scw, k.scw[:, tap, :], I["sconv_w"][l, tap].rearrange("(b p) -> b p", p=128), 6)
        colform(k, k.fcw, k.fcw[:, tap, :], I["fconv_w"][l, tap].rearrange("(b p) -> b p", p=128), 44)
    colform(k, k.scw, k.scw[:, 3, :], I["sconv_b"][l].rearrange("(b p) -> b p", p=128), 6)
    colform(k, k.fcw, k.fcw[:, 3, :], I["fconv_b"][l].rearrange("(b p) -> b p", p=128), 44)
    S.release(m)


def load_weight(k, dst, src2d, nkc, ncols, scope_stage):
    S = k.S
    engs = ("dve", "act")
    piece = 2840
    for kc in range(nkc):
        for c0 in range(0, ncols, piece):
            c1 = min(ncols, c0 + piece)
            st = scope_stage[k.wl_n % 2]
            S.dma("sp", st[:, 0:c1 - c0], src2d[kc * 128:(kc + 1) * 128, c0:c1], writes=[st])
            CP(S, engs[k.wl_n % 2], [st], [dst], dst[:, kc, c0:c1], st[:, 0:c1 - c0])
            k.wl_n += 1


def gate_table(k, l, grp, cond):
    S = k.S
    dg = k.ttmp[0]
    for j in range(8):
        TS(S, "dve", [k.cstb, k.modT], [dg], dg[:, 0:128], cview(k, "ident"), k.modT[:, l, grp * 8 + j, cond:cond + 1], None, ALU.mult)
        if j % 4 == 0:
            pb = S.bank()
        MM(S, [dg, k.cstb], [pb], [(pb[:, (j % 4) * 128:(j % 4 + 1) * 128], cview(k, "ones"), dg[:, 0:128], True, True)])
        if j % 4 == 3:
            CP(S, "act", [pb], [k.gbc], k.gbc[:, (j // 4) * 512:(j // 4 + 1) * 512], pb[:, :])


def make_hT(k, l, which, cond, rows, xbuf, dsts, dst_buf, src_bufs, pos_tile=None):
    S = k.S
    n = 0
    for r, nr in rows:
        S.dma("sp", xbuf[n:n + nr, :], r, reads=src_bufs, writes=[xbuf])
        n += nr
    if pos_tile is not None:
        TT(S, "pool", [xbuf, pos_tile], [xbuf], xbuf[0:n, :], xbuf[0:n, :], pos_tile[0:n, :], ALU.add)
    sm = k.hsm
    st, mv, ve, rstd, xnb = sm["st"], sm["mv"], sm["ve"], sm["rstd"], sm["xnb"]
    S.op("dve", lambda e: e.bn_stats(st[0:n, 0, :], xbuf[0:n, 0:512]), [xbuf], [st])
    S.op("dve", lambda e: e.bn_stats(st[0:n, 1, :], xbuf[0:n, 512:1024]), [xbuf], [st])
    S.op("dve", lambda e: e.bn_aggr(mv[0:n, :], st[0:n, :, :].rearrange("p a b -> p (a b)")), [st], [mv])
    TS(S, "dve", [mv], [ve], ve[0:n, :], mv[0:n, 1:2], EPS, None, ALU.add)
    TT(S, "pool", [ve, k.cm05], [rstd], rstd[0:n, :], ve[0:n, :], k.cm05[0:n, :], ALU.pow)
    TS(S, "dve", [xbuf, mv, rstd], [xnb], xnb[0:n, :], xbuf[0:n, :], mv[0:n, 0:1], rstd[0:n, 0:1], ALU.subtract, ALU.mult)
    pb = S.bank()
    pv = bview(pb, BF16)
    TR(S, [xnb, k.identb], [pb],
       [(pv[:, kc * 128:kc * 128 + n], xnb[0:n, kc * 128:(kc + 1) * 128], k.identb[0:n, 0:n]) for kc in range(8)])
    gsh, gsc = (0, 1) if which == 1 else (3, 4)
    for kc in range(8):
        sc = k.modT[:, l, gsc * 8 + kc, cond:cond + 1]
        sh = k.modT[:, l, gsh * 8 + kc, cond:cond + 1]
        if kc % 2 == 0:
            ACT(S, [pb, k.modT], [dst_buf], dsts[kc], pv[:, kc * 128:kc * 128 + n], AF.Identity, bias=sh, scale=sc)
        else:
            TS(S, "dve", [pb, k.modT], [dst_buf], dsts[kc], pv[:, kc * 128:kc * 128 + n], sc, sh, ALU.mult, ALU.add)


def alloc_hsm(k):
    S = k.S
    k.hsm = {"st": S.sb("h_st", [128, 2, 6]), "mv": S.sb("h_mv", [128, 2]), "ve": S.sb("h_ve", [128, 1]),
             "rstd": S.sb("h_rstd", [128, 1]), "xnb": S.sb("h_xnb", [128, D], BF16)}


def resid_ln(k, x_buf, psum_halves, out_buf, nb_small):
    S = k.S
    t0, t1 = k.ttmp[0], out_buf
    for hlf, pb in enumerate(psum_halves):
        sl = slice(hlf * 512, (hlf + 1) * 512)
        TT(S, "dve", [pb, k.gbc], [t0], t0[:, sl], pb[:, :], k.gbc[:, sl], ALU.mult)
    TS(S, "pool", [x_buf], [x_buf], x_buf[:, :], x_buf[:, :], ALPHA, None, ALU.mult)
    TT(S, "pool", [x_buf, t0], [t0], t0[:, :], t0[:, :], x_buf[:, :], ALU.add)
    st, mv, ve, rstd, nb = nb_small["st"], nb_small["mv"], nb_small["ve"], nb_small["rstd"], nb_small["nb"]
    S.op("dve", lambda e: e.bn_stats(st[:, 0, :], t0[:, 0:512]), [t0], [st])
    S.op("dve", lambda e: e.bn_stats(st[:, 1, :], t0[:, 512:1024]), [t0], [st])
    S.op("dve", lambda e: e.bn_aggr(mv[:, :], st[:, :, :].rearrange("p a b -> p (a b)")), [st], [mv])
    TS(S, "dve", [mv], [ve], ve[:, :], mv[:, 1:2], EPS, None, ALU.add)
    TT(S, "pool", [ve, k.cm05], [rstd], rstd[:, :], ve[:, :], k.cm05[:, :], ALU.pow)
    STT(S, "dve", [mv, rstd], [nb], nb[:, :], mv[:, 0:1], -1.0, rstd[:, :], ALU.mult, ALU.mult)
    ACT(S, [t0, rstd, nb], [t1], t1[:, :], t0[:, :], AF.Identity, bias=nb[:, 0:1], scale=rstd[:, 0:1])
    TT(S, "pool", [t1, k.lng], [t1], t1[:, :], t1[:, :], k.lng[:, :], ALU.mult)
    TT(S, "pool", [t1, k.lnb], [out_buf], out_buf[:, :], t1[:, :], k.lnb[:, :], ALU.add)


def alloc_rsm(k):
    S = k.S
    return {"st": S.sb("r_st", [128, 2, 6]), "mv": S.sb("r_mv", [128, 2]), "ve": S.sb("r_ve", [128, 1]),
            "rstd": S.sb("r_rstd", [128, 1]), "nb": S.sb("r_nb", [128, 1])}


def seq_src_dst(k, l, phase):
    cfg = k.cfg
    mode = getattr(cfg, "mode", "full")
    if phase == "A":
        src = None if l == 0 else k.XB
        dst = k.XA if mode == "full" else None
    else:
        src = k.XA if mode == "full" else None
        dst = None if l == cfg.L - 1 else k.XB
    return src, dst


def rows_ap(k, handle, which_io, r0, n):
    cfg = k.cfg
    if handle is not None:
        return handle[r0:r0 + n, :]
    if r0 < cfg.Ts:
        t = k.I["xs"] if which_io == "in" else k.O["ys"]
        return t[r0:r0 + n, :]
    t = k.I["xp"] if which_io == "in" else k.O["yp"]
    return t[r0 - cfg.Ts:r0 - cfg.Ts + n, :]


def phaseB(k, l):
    S, I, cfg = k.S, k.I, k.cfg
    m = S.mark()
    w_up = S.sb("w_up", [128, 8, 2 * DFF], BF16)
    w_dn = S.sb("w_dn", [128, 22, D], BF16)
    m2 = S.mark()
    stage = [S.sb(f"wstage{j}", [128, 2840]) for j in range(2)]
    k.wl_n = 0
    load_weight(k, w_up, I["w_up"][l], 8, 2 * DFF, stage)
    load_weight(k, w_dn, I["w_dn"][l], 22, D, stage)
    S.release(m2)
    S.dma("sp", k.lng[:, :], I["ln2_g"][l:l + 1, :].partition_broadcast(128), writes=[k.lng])
    S.dma("sp", k.lnb[:, :], I["ln2_b"][l:l + 1, :].partition_broadcast(128), writes=[k.lnb])
    alloc_hsm(k)
    rsm = alloc_rsm(k)
    SEG = 256
    h2T = [S.sb(f"h2T{j}", [128, 8, SEG + 2], BF16) for j in range(2)]
    actT = S.sb("actT", [128, 22, SEG], BF16)
    xt = [S.sb(f"xtB{j}", [128, D]) for j in range(4)]
    xh = k.ttmp[0]
    NBUF = 3
    cg = [S.sb(f"cg{j}", [128, SEG]) for j in range(NBUF)]
    cv = [S.sb(f"cv{j}", [128, SEG]) for j in range(NBUF)]
    th = [S.sb(f"th{j}", [128, SEG]) for j in range(NBUF)]
    src, dst = seq_src_dst(k, l, "B")
    segs = []
    for (sname, T, cond, off) in cfg.seqs:
        for t0 in range(0, T, SEG):
            segs.append((T, cond, off, t0))
    state = {}

    def prep(si):
        T, cond, off, t0 = segs[si]
        hT = h2T[si % 2]
        r0 = off + t0
        xts = []
        for j in range(SEG // 128):
            xb = xt[(2 * si + j) % 4]
            xts.append(xb)
            make_hT(k, l, 2, cond, [(rows_ap(k, src, "in", r0 + 128 * j, 128), 128)], xb,
                    [hT[:, kc, 1 + 128 * j:1 + 128 * (j + 1)] for kc in range(8)], hT, [])
        rows, cols = [], []
        if t0 > 0:
            rows.append((rows_ap(k, src, "in", r0 - 1, 1), 1))
            cols.append(0)
        else:
            MSET(S, "pool", [hT], hT[:, :, 0:1], 0.0)
        if t0 + SEG < T:
            rows.append((rows_ap(k, src, "in", r0 + SEG, 1), 1))
            cols.append(SEG + 1)
        else:
            MSET(S, "pool", [hT], hT[:, :, SEG + 1:SEG + 2], 0.0)
        if len(rows) == 2:
            make_hT(k, l, 2, cond, rows, xh, [hT[:, kc, 0:SEG + 2:SEG + 1] for kc in range(8)], hT, [])
        elif len(rows) == 1:
            c = cols[0]
            make_hT(k, l, 2, cond, rows, xh, [hT[:, kc, c:c + 1] for kc in range(8)], hT, [])
        state[si] = xts

    def ffn(si):
        T, cond, off, t0 = segs[si]
        hT = h2T[si % 2]
        r0 = off + t0
        xts = state.pop(si)
        for c in range(22):
            pg, pv = S.bank(), S.bank()
            MM(S, [w_up, hT], [pg], [(pg[:, 0:SEG + 2], w_up[:, kc, c * 128:(c + 1) * 128], hT[:, kc, :], kc == 0, kc == 7)
                                      for kc in range(8)])
            MM(S, [w_up, hT], [pv], [(pv[:, 0:SEG + 2], w_up[:, kc, DFF + c * 128:DFF + (c + 1) * 128], hT[:, kc, :], kc == 0, kc == 7)
                                      for kc in range(8)])
            g_, v_, t_ = cg[c % NBUF], cv[c % NBUF], th[c % NBUF]
            fw = k.fcw
            ACT(S, [pg, fw], [g_], g_[:, :], pg[:, 1:SEG + 1], AF.Identity, bias=fw[:, 3, c:c + 1], scale=fw[:, 1, c:c + 1])
            STT(S, "dve", [pg, fw, g_], [g_], g_[:, :], pg[:, 0:SEG], fw[:, 0, c:c + 1], g_[:, :], ALU.mult, ALU.add)
            STT(S, "dve", [pg, fw, g_], [g_], g_[:, :], pg[:, 2:SEG + 2], fw[:, 2, c:c + 1], g_[:, :], ALU.mult, ALU.add)
            cc = 22 + c
            ACT(S, [pv, fw], [v_], v_[:, :], pv[:, 1:SEG + 1], AF.Identity, bias=fw[:, 3, cc:cc + 1], scale=fw[:, 1, cc:cc + 1])
            STT(S, "dve", [pv, fw, v_], [v_], v_[:, :], pv[:, 0:SEG], fw[:, 0, cc:cc + 1], v_[:, :], ALU.mult, ALU.add)
            STT(S, "dve", [pv, fw, v_], [v_], v_[:, :], pv[:, 2:SEG + 2], fw[:, 2, cc:cc + 1], v_[:, :], ALU.mult, ALU.add)
            ACT(S, [g_], [t_], t_[:, :], g_[:, :], AF.Silu)
            TT(S, "pool", [t_, v_], [actT], actT[:, c, :], t_[:, :], v_[:, :], ALU.mult)
        for j in range(SEG // 128):
            p0, p1 = S.bank(), S.bank()
            for hlf, pb in enumerate((p0, p1)):
                MM(S, [actT, w_dn], [pb], [(pb[:, :], actT[:, c, 128 * j:128 * (j + 1)], w_dn[:, c, hlf * 512:(hlf + 1) * 512],
                                              c == 0, c == 21) for c in range(22)])
            ob = xts[j]
            resid_ln(k, xts[j], (p0, p1), ob, rsm)
            db = S.dbuf(("xout", l, (r0 + 128 * j) // 128))
            S.dma("pool", rows_ap(k, dst, "out", r0 + 128 * j, 128), ob[:, :], reads=[ob], writes=[db])
            if dst is None:
                k.final_bufs.append(db)

    cur_cond = None
    prep(0)
    for si in range(len(segs)):
        if si + 1 < len(segs):
            prep(si + 1)
        if segs[si][1] != cur_cond:
            cur_cond = segs[si][1]
            gate_table(k, l, 5, cur_cond)
        ffn(si)
    S.release(m)


def layer(k, l):
    cfg = k.cfg
    load_layer_consts(k, l)
    mode = getattr(cfg, "mode", "full")
    if mode in ("full", "A"):
        phaseA(k, l)
    if mode in ("full", "B"):
        phaseB(k, l)


def shard_inputs(inp, cfg, core):
    f = lambda a: np.ascontiguousarray(np.asarray(a), dtype=np.float32)
    NP = cfg.NP
    cst, _, cst2, _ = make_consts()
    m = {
        "xs": f(inp["x_sample"][core]),
        "xp": f(inp["x_prompt"][NP * core:NP * (core + 1)]).reshape(NP * cfg.Tp, D),
        "st_c": f(inp["state_mlstm_c"][core]), "st_n": f(inp["state_mlstm_n"][core]),
        "st_m": f(inp["state_mlstm_m"][core]).reshape(2, 8), "st_s": f(inp["state_ssd"][core]),
        "cond": f(np.stack([np.asarray(inp["c"])[core], np.asarray(inp["c_ctx"])], 0)),
        "w_ada": f(inp["w_ada"]), "b_ada": f(inp["b_ada"]), "w_in": f(inp["w_in"]),
        "b_ig": f(inp["b_igate"]).reshape(2, 8), "b_fg": f(inp["b_fgate"]).reshape(2, 8),
        "mnorm_g": f(inp["mlstm_norm_g"]), "w_pool": f(inp["w_pool"]), "pool_scale": f(inp["pool_scale"]),
        "w_sp": f(inp["w_spatial"]), "b_sp": f(inp["b_spatial"]), "sconv_w": f(inp["ssd_conv_w"]),
        "sconv_b": f(inp["ssd_conv_b"]), "dt_bias": f(inp["ssd_dt_bias"]).reshape(2, 8),
        "a_log": f(inp["ssd_a_log"]).reshape(2, 8), "ssd_d": f(inp["ssd_d"]), "snorm_g": f(inp["ssd_norm_g"]),
        "w_out": f(inp["w_out"]), "ln1_g": f(inp["ln1_g"]), "ln1_b": f(inp["ln1_b"]), "w_up": f(inp["ffn_w_up"]),
        "fconv_w": f(inp["ffn_conv_w"]), "fconv_b": f(inp["ffn_conv_b"]), "w_dn": f(inp["ffn_w_down"]),
        "ln2_g": f(inp["ln2_g"]), "ln2_b": f(inp["ln2_b"]), "cst": cst, "cst2": cst2,
    }
    return m


def phaseA(k, l):
    S, I, cfg = k.S, k.I, k.cfg
    m = S.mark()
    w_in = S.sb("w_in", [128, 8, DIN], BF16)
    w_out = S.sb("w_out", [128, 8, D], BF16)
    m2 = S.mark()
    stage = [S.sb(f"wstageA{j}", [128, 2840]) for j in range(2)]
    k.wl_n = 0
    load_weight(k, w_in, I["w_in"][l], 8, DIN, stage)
    load_weight(k, w_out, I["w_out"][l], 8, D, stage)
    S.release(m2)
    c2b = load_cst2(k)
    S.dma("sp", k.lng[:, :], I["ln1_g"][l:l + 1, :].partition_broadcast(128), writes=[k.lng])
    S.dma("sp", k.lnb[:, :], I["ln1_b"][l:l + 1, :].partition_broadcast(128), writes=[k.lnb])
    alloc_hsm(k)
    rsm = alloc_rsm(k)
    a = K()
    a.w_in, a.w_out, a.c2b, a.rsm, a.l = w_in, w_out, c2b, rsm, l
    a.hb = [S.sb(f"hb{j}", [128, 8, 130], BF16) for j in range(3)]
    a.xq = [S.sb(f"xq{j}", [128, D]) for j in range(2)]
    a.pet = [S.sb(f"petile{j}", [128, D]) for j in range(2)] if l == 0 else None
    if l == 0:
        edb = S.dbuf("ED")
        for j in range(2):
            S.dma("sp", a.pet[j][0:64, 512:1024], k.ED[:, :], reads=[edb], writes=[a.pet[j]])
            S.dma("sp", a.pet[j][64:128, 512:1024], k.ED[:, :], reads=[edb], writes=[a.pet[j]])
    sb = S.sb
    a.Cn, a.Cnb = sb("Cn", [128, 2, 65]), sb("Cnb", [128, 2, 66], BF16)
    a.Hs, a.Hsb = sb("Hs", [128, 4, 64]), sb("Hsb", [128, 4, 64], BF16)
    a.p = []
    for par in range(2):
        q = K()
        a.p.append(q)
        q.G8, q.E8, q.SP8, q.igb = sb(f"G8{par}", [128, 8]), sb(f"E8{par}", [128, 8]), sb(f"SP8{par}", [128, 8]), sb(f"igb{par}", [128, 4])
        q.r8, q.logdec, q.cum, q.e8 = sb(f"r8{par}", [128, 8]), sb(f"logdec{par}", [128, 8]), sb(f"cum{par}", [128, 8]), sb(f"e8{par}", [128, 8])
        q.wend, q.aL, q.tmp8 = sb(f"wend{par}", [128, 8]), sb(f"aL{par}", [128, 8]), sb(f"tmp8{par}", [128, 8])
        q.L1 = sb(f"L1{par}", [128, 8, 128])
        q.DIFF = sb(f"DIFF{par}", [128, 8, 128])
        q.qkT = sb(f"qkT{par}", [128, 4, 128], BF16)
        q.PTm = sb(f"PTm{par}", [128, 4, 128], BF16)
        q.PTs = sb(f"PTs{par}", [128, 4, 128], BF16)
        q.k_tm = sb(f"k_tm{par}", [128, 256], BF16)
        q.xt_m, q.xh_m = sb(f"xt_m{par}", [128, 4, 66], BF16), sb(f"xh_m{par}", [128, 4, 66], BF16)
        q.XBC, q.XBCe = sb(f"XBC{par}", [128, 6, 128]), sb(f"XBCe{par}", [128, 6, 128])
        q.XBCb = sb(f"XBCb{par}", [128, 6, 128], BF16)
        q.x_tm, q.B_tm = sb(f"x_tm{par}", [128, 256], BF16), sb(f"B_tm{par}", [128, 2, 128], BF16)
        q.xt_s, q.xh_s = sb(f"xt_s{par}", [128, 4, 64], BF16), sb(f"xh_s{par}", [128, 4, 64], BF16)
        q.NUM = sb(f"NUM{par}", [128, 4, 65])
        q.den = sb(f"den{par}", [128, 4])
        q.Ysc = sb(f"Ysc{par}", [128, 4, 64])
    a.HY = [sb(f"HY{j}", [128, 512]) for j in range(2)]
    a.HYf = [sb(f"HYf{j}", [128, 512]) for j in range(2)]
    a.eo, a.z_sb, a.ez, a.gu = sb("eo", [128, 256]), sb("z_sb", [128, 256]), sb("ez", [128, 256]), sb("gu", [128, 256])
    a.gvb = sb("gvb", [128, 256], BF16)
    a.pc, a.pcP, a.pcN = sb("pc", [128, 256]), sb("pcP", [8, 256]), sb("pcN", [8, 256])
    a.plb, a.plT = sb("plb", [128, 256], BF16), sb("plT", [128, 2, 128], BF16)
    a.fin1, a.fin2, a.fin3 = sb("fin1", [128, 256]), sb("fin2", [128, 256]), sb("fin3", [128, 256])
    a.st4, a.st4b = sb("st4", [128, 4]), sb("st4b", [128, 4])
    a.yall = sb("yall", [128, 3, 256], BF16)
    a.concatT = sb("concatT", [128, 8, 128], BF16)
    a.gst, a.gmv, a.gve, a.grs = sb("gst", [128, 6]), sb("gmv", [128, 2]), sb("gve", [128, 1]), sb("grs", [128, 1])
    a.mrun = sb("mrun", [4, 1])
    a.mt = sb("mt", [4, 2])
    for par in range(2):
        a.p[par].dec = sb(f"dec{par}", [128, 8])
    a.sio = sb("sio", [128, 4, 128])
    src, dst = seq_src_dst(k, l, "A")
    a.src, a.dst = src, dst
    cur_cond = None
    for si, (sname, T, cond, off) in enumerate(cfg.seqs):
        if cond != cur_cond:
            gate_table(k, l, 2, cond)
            cur_cond = cond
        runseq(k, a, si, T, cond, off)
    S.release(m)


def runseq(k, a, si, T, cond, off):
    S, cfg, l = k.S, k.cfg, a.l
    nt = T // 128
    is_sample = (si == 0)
    tile0 = off // 128
    w_in = a.w_in

    def hbuf(i):
        return a.hb[i % 3]

    def fix_halo(lo, hi):
        CP(S, "pool", [hbuf(hi)], [hbuf(lo)], hbuf(lo)[:, :, 129:130], hbuf(hi)[:, :, 1:2])
        CP(S, "pool", [hbuf(lo)], [hbuf(hi)], hbuf(hi)[:, :, 0:1], hbuf(lo)[:, :, 128:129])

    def ensure1(i):
        hb = hbuf(i)
        xb = a.xq[i % 2]
        pos = None
        if l == 0 and is_sample:
            pos = a.pet[i % 2]
            edb = S.dbuf("ED")
            S.dma("sp", pos[0:64, 0:512], k.ED[2 * i:2 * i + 1, :].partition_broadcast(64), reads=[edb], writes=[pos])
            S.dma("sp", pos[64:128, 0:512], k.ED[2 * i + 1:2 * i + 2, :].partition_broadcast(64), reads=[edb], writes=[pos])
        make_hT(k, l, 1, cond, [(rows_ap(k, a.src, "in", off + 128 * i, 128), 128)], xb,
                [hb[:, kc, 1:129] for kc in range(8)], hb, [], pos_tile=pos)
        db = S.dbuf(("HT", tile0 + i))
        S.dma("pool", k.HT[tile0 + i].rearrange("p (kc t) -> p kc t", kc=8), hb[:, :, 1:129], reads=[hb], writes=[db])
        if i == 0:
            MSET(S, "pool", [hb], hb[:, :, 0:1], 0.0)
        else:
            fix_halo(i - 1, i)
        if i == nt - 1:
            MSET(S, "pool", [hb], hb[:, :, 129:130], 0.0)

    def ensure2(i):
        hb = hbuf(i)
        db = S.dbuf(("HT", tile0 + i))
        S.dma("sp", hb[:, :, 1:129], k.HT[tile0 + i].rearrange("p (kc t) -> p kc t", kc=8), reads=[db], writes=[hb])
        xb = a.xq[i % 2]
        S.dma("sp", xb[:, :], rows_ap(k, a.src, "in", off + 128 * i, 128), writes=[xb])
        if l == 0 and is_sample:
            pos = a.pet[i % 2]
            edb = S.dbuf("ED")
            S.dma("sp", pos[0:64, 0:512], k.ED[2 * i:2 * i + 1, :].partition_broadcast(64), reads=[edb], writes=[pos])
            S.dma("sp", pos[64:128, 0:512], k.ED[2 * i + 1:2 * i + 2, :].partition_broadcast(64), reads=[edb], writes=[pos])
            TT(S, "pool", [xb, pos], [xb], xb[:, :], xb[:, :], pos[:, :], ALU.add)
        hf = a.HYf[i % 2]
        S.dma("sp", hf[:, :], k.HF[off + 128 * i:off + 128 * (i + 1), :], reads=[S.dbuf(("HF", tile0 + i))], writes=[hf])
        if i == nt - 1:
            MSET(S, "pool", [hb], hb[:, :, 129:130], 0.0)
        else:
            fix_halo(i, i + 1)
        if i == 0:
            MSET(S, "pool", [hb], hb[:, :, 0:1], 0.0)

    for d in ((0,) if getattr(cfg, "stop", 99) <= 4 else (0, 1)):
        init_state(k, a, si, d, is_sample)
        order = list(range(nt)) if d == 0 else list(range(nt - 1, -1, -1))
        ens = ensure1 if d == 0 else ensure2
        ens(order[0])
        for n, i in enumerate(order):
            if n + 1 < len(order):
                ens(order[n + 1])
            tileA(k, a, si, T, cond, off, i, d, nt, is_sample)
        if not is_sample and getattr(cfg, "stop", 99) > 5:
            final_state(k, a, si, d)


def init_state(k, a, si, d, is_sample):
    S, I, l = k.S, k.I, a.l
    if not is_sample:
        MSET(S, "pool", [a.Cn], a.Cn[:, :, :], 0.0)
        MSET(S, "pool", [a.Cnb], a.Cnb[:, :, :], 0.0)
        MSET(S, "pool", [a.Hs], a.Hs[:, :, :], 0.0)
        MSET(S, "pool", [a.Hsb], a.Hsb[:, :, :], 0.0)
        MSET(S, "pool", [a.mrun], a.mrun[:, :], 0.0)
        return
    for h in range(4):
        pr = slice((h % 2) * 64, (h % 2) * 64 + 64)
        S.dma("sp", a.Cn[pr, h // 2, 0:64], I["st_c"][l, d, h], writes=[a.Cn])
        S.dma("sp", a.Cn[pr, h // 2, 64:65], I["st_n"][l, d, h].rearrange("(p o) -> p o", o=1), writes=[a.Cn])
    S.dma("sp", a.st4[:, :], I["st_m"][l:l + 1, 4 * d:4 * d + 4].partition_broadcast(128), writes=[a.st4])
    ACT(S, [a.st4], [a.st4b], a.st4b[:, :], a.st4[:, :], AF.Exp)
    for h in range(4):
        pr = slice((h % 2) * 64, (h % 2) * 64 + 64)
        TS(S, "dve", [a.Cn, a.st4b], [a.Cn], a.Cn[pr, h // 2, :], a.Cn[pr, h // 2, :], a.st4b[pr, h:h + 1], None, ALU.mult)
    CP(S, "pool", [a.Cn], [a.Cnb], a.Cnb[:, :, 0:65], a.Cn[:, :, :])
    S.dma("sp", a.sio[0:64, :, :], I["st_s"][l, d].rearrange("h p n -> p h n"), writes=[a.sio])
    pb = S.bank()
    TR(S, [a.sio, k.cstb], [pb], [(pb[:, h * 64:(h + 1) * 64], a.sio[0:64, h, :], cview(k, "ident")[0:64, 0:64]) for h in range(4)])
    CP(S, "dve", [pb], [a.Hs], a.Hs[:, :, :], pb[:, 0:256].rearrange("p (h q) -> p h q", h=4))
    CP(S, "act", [pb], [a.Hsb], a.Hsb[:, :, :], pb[:, 0:256].rearrange("p (h q) -> p h q", h=4))


def final_state(k, a, si, d):
    S, O, l = k.S, k.O, a.l
    j = si - 1
    dg = a.p[0].tmp8
    TS(S, "dve", [k.cstb, a.mrun], [dg], dg[0:4, 0:4], cview(k, "ident")[0:4, 0:4], a.mrun[0:4, 0:1], None, ALU.mult)
    pb = S.bank()
    MM(S, [dg, k.cstb], [pb], [(pb[:, 0:4], cview(k, "ones")[0:4, :], dg[0:4, 0:4], True, True)])
    ACT(S, [pb], [a.st4b], a.st4b[:, :], pb[:, 0:4], AF.Exp, scale=-1.0)
    stg = a.sio
    sv = stg[:, 0:2, 0:65]
    for h in range(4):
        pr = slice((h % 2) * 64, (h % 2) * 64 + 64)
        TS(S, "dve", [a.Cn, a.st4b], [stg], stg[pr, h // 2, 0:65], a.Cn[pr, h // 2, :], a.st4b[pr, h:h + 1], None, ALU.mult)
    outs = []
    for h in range(4):
        pr = slice((h % 2) * 64, (h % 2) * 64 + 64)
        db = S.dbuf(("oc", j, l, d, h))
        S.dma("pool", O["oc"][j, l, d, h], stg[pr, h // 2, 0:64], reads=[stg], writes=[db])
        db2 = S.dbuf(("on", j, l, d, h))
        S.dma("pool", O["on"][j, l, d, h].rearrange("(p o) -> p o", o=1), stg[pr, h // 2, 64:65], reads=[stg], writes=[db2])
        outs += [db, db2]
    db = S.dbuf(("om", j, l, d))
    S.dma("pool", O["om"][j, l, 4 * d:4 * d + 4].rearrange("(p o) -> p o", o=1), a.mrun[0:4, 0:1], reads=[a.mrun], writes=[db])
    outs.append(db)
    pb2 = S.bank()
    TR(S, [a.Hs, k.cstb], [pb2], [(pb2[0:64, h * 128:(h + 1) * 128], a.Hs[:, h, :], cview(k, "ident")) for h in range(4)])
    CP(S, "dve", [pb2, stg], [stg], stg[0:64, :, :], pb2[0:64, :].rearrange("p (h n) -> p h n", h=4))
    db = S.dbuf(("os", j, l, d))
    S.dma("pool", O["os"][j, l, d].rearrange("h p n -> p h n"), stg[0:64, :, :], reads=[stg], writes=[db])
    outs.append(db)
    k.final_bufs += outs


def tileA(k, a, si, T, cond, off, i, d, nt, is_sample):
    S, l = k.S, a.l
    PS = a.p[i % 2]
    w_in = a.w_in
    hb = a.hb[i % 3]
    hcur = lambda kc: hb[:, kc, 1:129]
    tri = cview(k, "tri%d" % d)
    neg = cview(k, "neg%d" % d)
    endc = 127 if d == 0 else 0
    full = (d == 1)
    cst = k.cstb

    ps1, ps2 = S.bank(), S.bank()
    MM(S, [hb, w_in], [ps1], [(ps1[:, 0:512], hcur(kc), w_in[:, kc, 256:768], kc == 0, kc == 7) for kc in range(8)])
    MM(S, [hb, w_in], [ps2], [(ps2[:, 0:272], hcur(kc), w_in[:, kc, 768:1040], kc == 0, kc == 7) for kc in range(8)]
       + [(ps2[:, 272:280], hcur(kc), w_in[:, kc, 2832:2840], kc == 0, kc == 7) for kc in range(8)])
    G8, E8, SP8, igb, r8, logdec, cum, e8, wend, aL, tmp8 = (PS.G8, PS.E8, PS.SP8, PS.igb, PS.r8, PS.logdec, PS.cum, PS.e8,
                                                             PS.wend, PS.aL, PS.tmp8)
    STT(S, "dve", [ps2, k.bif], [G8], G8[:, 0:4], ps2[:, 264 + 4 * d:268 + 4 * d], -1.0, k.bif[:, 8 + 4 * d:12 + 4 * d], ALU.mult, ALU.add)
    TT(S, "dve", [ps2, k.dtb], [G8], G8[:, 4:8], ps2[:, 272 + 4 * d:276 + 4 * d], k.dtb[:, 4 * d:4 * d + 4], ALU.add)
    TT(S, "dve", [ps2, k.bif], [igb], igb[:, :], ps2[:, 256 + 4 * d:260 + 4 * d], k.bif[:, 4 * d:4 * d + 4], ALU.add)
    if full:
        ACT(S, [ps2], [a.eo], a.eo[:, :], ps2[:, 0:256], AF.Exp, scale=-1.0)
    ACT(S, [G8], [E8], E8[:, :], G8[:, :], AF.Exp)
    ACT(S, [E8], [SP8], SP8[:, :], E8[:, :], AF.Ln, bias=1.0)
    ACT(S, [igb], [r8], r8[:, 0:4], igb[:, :], AF.Exp)
    CP(S, "pool", [SP8], [r8], r8[:, 4:8], SP8[:, 4:8])
    TT(S, "dve", [SP8, k.coef], [logdec], logdec[:, :], SP8[:, :], k.coef[:, d, :], ALU.mult)
    CP(S, "pool", [logdec], [PS.L1], PS.L1[:, :, :], bc(logdec[:, 0:8].unsqueeze(2), [128, 8, 128]))
    psL = [S.bank(), S.bank()]
    for hh in range(2):
        MM(S, [PS.L1, cst], [psL[hh]], [(psL[hh][:, q * 128:(q + 1) * 128], PS.L1[:, hh * 4 + q, :], tri, True, True) for q in range(4)])
    psC = S.bank()
    MM(S, [logdec, cst], [psC], [(psC[:, 0:8], tri, logdec[:, 0:8], True, True)])
    CP(S, "dve", [psC], [cum], cum[:, :], psC[:, 0:8])
    for h in range(8):
        pl = psL[h // 4]
        q = h % 4
        STT(S, "dve", [pl, cum, cst], [PS.DIFF], PS.DIFF[:, h, :], pl[:, q * 128:(q + 1) * 128], cum[:, h:h + 1], neg, ALU.subtract, ALU.add)
    ACT(S, [PS.DIFF], [PS.DIFF], PS.DIFF[:, :, :], PS.DIFF[:, :, :], AF.Exp)
    ACT(S, [cum], [e8], e8[:, :], cum[:, :], AF.Exp)
    for hh in range(2):
        TT(S, "dve", [psL[hh], cum], [tmp8], tmp8[:, hh * 4:hh * 4 + 4], psL[hh][:, endc:512:128], cum[:, hh * 4:hh * 4 + 4], ALU.subtract)
        ACT(S, [psL[hh]], [aL], aL[:, hh * 4:hh * 4 + 4], psL[hh][:, endc:512:128], AF.Exp)
    ACT(S, [tmp8], [wend], wend[:, :], tmp8[:, :], AF.Exp)
    TT(S, "dve", [wend, r8], [wend], wend[:, :], wend[:, :], r8[:, :], ALU.mult)
    if not is_sample:
        TT(S, "dve", [tmp8, igb], [PS.dec], PS.dec[:, 0:4], tmp8[:, 0:4], igb[:, :], ALU.add)
        TT(S, "dve", [tmp8, cum], [PS.dec], PS.dec[:, 4:8], tmp8[:, 0:4], cum[:, 0:4], ALU.add)
        pm = S.bank()
        TR(S, [PS.dec, cst], [pm], [(pm[0:4, 0:128], PS.dec[:, 0:4], cview(k, "ident")),
                                   (pm[0:4, 128:256], PS.dec[:, 4:8], cview(k, "ident"))])
        S.op("dve", lambda e: e.tensor_reduce(a.mt[0:4, 0:1], pm[0:4, 0:128], AX.X, ALU.max), [pm], [a.mt])
        TT(S, "dve", [pm, a.mrun], [a.mt], a.mt[0:4, 1:2], pm[0:4, 128:129], a.mrun[0:4, 0:1], ALU.add)
        TT(S, "dve", [a.mt], [a.mrun], a.mrun[0:4, 0:1], a.mt[0:4, 0:1], a.mt[0:4, 1:2], ALU.max)

    if getattr(k.cfg, "stop", 99) <= 1:
        return
    psQ = [S.bank(), S.bank()]
    for hh in range(2):
        MM(S, [hb, w_in], [psQ[hh]], [(psQ[hh][:, q * 130:(q + 1) * 130], w_in[:, kc, (hh * 2 + q) * 128:(hh * 2 + q + 1) * 128],
                                        hb[:, kc, 0:130], kc == 0, kc == 7) for q in range(2) for kc in range(8)])
    if getattr(k.cfg, "stop", 99) <= 1.05:
        return
    qkT = PS.qkT
    CP(S, "act", [psQ[0]], [qkT], qkT[:, 0:2, :], psQ[0][:, 0:260].rearrange("p (b t) -> p b t", b=2)[:, :, 1:129])
    ACT(S, [psQ[1]], [qkT], qkT[:, 2:4, :], psQ[1][:, 0:260].rearrange("p (b t) -> p b t", b=2)[:, :, 1:129], AF.Identity, scale=0.125)
    ACT(S, [ps1], [PS.k_tm], PS.k_tm[:, :], ps1[:, 0:256], AF.Identity, scale=0.125)
    if getattr(k.cfg, "stop", 99) <= 1.1:
        return
    v4 = ps1[:, 256:512].rearrange("p (h e) -> p h e", h=4)
    TT(S, "dve", [ps1, r8], [PS.xt_m], PS.xt_m[:, :, 0:64], v4, bc(r8[:, 0:4].unsqueeze(2), [128, 4, 64]), ALU.mult)
    if getattr(k.cfg, "stop", 99) <= 1.12:
        return
    CP(S, "pool", [r8], [PS.xt_m], PS.xt_m[:, :, 64:65], r8[:, 0:4].unsqueeze(2))
    if getattr(k.cfg, "stop", 99) <= 1.15:
        return
    EXP = getattr(k.cfg, "exp", "")
    if EXP != "noTT":
        TT(S, "dve", [ps1, wend], [PS.xh_m], PS.xh_m[:, :, 0:64], v4, bc((r8 if EXP == "r8" else wend)[:, 0:4].unsqueeze(2), [128, 4, 64]), ALU.mult)
    if EXP != "noCP":
        CP(S, "pool", [wend], [PS.xh_m], PS.xh_m[:, :, 64:65], wend[:, 0:4].unsqueeze(2))
    if getattr(k.cfg, "stop", 99) <= 1.2:
        return
    psS = [S.bank(), S.bank()]
    hp = lambda h: slice((h % 2) * 64, (h % 2) * 64 + 64)
    for par in range(2):
        MM(S, [qkT], [psS[par]], [(psS[par][:, (h // 2) * 128:(h // 2 + 1) * 128], qkT[hp(h), 2 + h // 2, :], qkT[hp(h), h // 2, :], True, True)
                                  for h in (par, par + 2)])
    for par in range(2):
        TT(S, "dve", [psS[par], PS.DIFF], [PS.PTm], PS.PTm[:, par:4:2, :], psS[par][:, 0:256].rearrange("p (h t) -> p h t", h=2),
           PS.DIFF[:, par:4:2, :], ALU.mult)
    if getattr(k.cfg, "stop", 99) <= 1.4:
        return
    psO = S.bank()
    psI = [S.bank(), S.bank()]
    MM(S, [PS.PTm, PS.xt_m], [psO], [(psO[:, h * 65:h * 65 + 65], PS.PTm[:, h, :], PS.xt_m[:, h, 0:65], True, True) for h in range(4)])
    for par in range(2):
        MM(S, [qkT, a.Cnb], [psI[par]], [(psI[par][:, (h // 2) * 65:(h // 2) * 65 + 65], qkT[hp(h), h // 2, :], a.Cnb[hp(h), h // 2, 0:65], True, True)
                                         for h in (par, par + 2)])
    NUM = PS.NUM
    for par in range(2):
        TT(S, "dve", [psI[par], e8], [NUM], NUM[:, par:4:2, :], psI[par][:, 0:130].rearrange("p (h e) -> p h e", h=2),
           bc(e8[:, par:4:2].unsqueeze(2), [128, 2, 65]), ALU.mult)
    TT(S, "dve", [psO, NUM], [NUM], NUM[:, :, :], psO[:, 0:260].rearrange("p (h e) -> p h e", h=4), NUM[:, :, :], ALU.add)
    if getattr(k.cfg, "stop", 99) <= 1.6:
        return
    HY = a.HY[i % 2]
    ACT(S, [NUM], [PS.den], PS.den[:, :].unsqueeze(2), NUM[:, :, 64:65], AF.Abs)
    TS(S, "dve", [PS.den], [PS.den], PS.den[:, :], PS.den[:, :], 1.0, None, ALU.max)
    TT(S, "pool", [PS.den, k.cm1], [PS.den], PS.den[:, :], PS.den[:, :], bc(k.cm1[:, 0:1], [128, 4]), ALU.pow)
    TT(S, "pool", [NUM, PS.den], [HY], HY[:, 0:256].rearrange("p (h e) -> p h e", h=4), NUM[:, :, 0:64],
       bc(PS.den[:, :].unsqueeze(2), [128, 4, 64]), ALU.mult)
    if getattr(k.cfg, "stop", 99) <= 1.8:
        return
    psU = S.bank()
    MM(S, [PS.k_tm, PS.xh_m], [psU], [(psU[:, h * 65:h * 65 + 65], PS.k_tm[:, (h // 2) * 128:(h // 2 + 1) * 128], PS.xh_m[:, h, 0:65], True, True)
                                     for h in range(4)])
    for h in range(4):
        STT(S, "dve", [a.Cn, aL, psU], [a.Cn], a.Cn[hp(h), h // 2, :], a.Cn[hp(h), h // 2, :], aL[hp(h), h:h + 1],
            psU[hp(h), h * 65:h * 65 + 65], ALU.mult, ALU.add)
    CP(S, "pool", [a.Cn], [a.Cnb], a.Cnb[:, :, 0:65], a.Cn[:, :, :])

    if getattr(k.cfg, "stop", 99) <= 2:
        return
    psX = [S.bank(), S.bank()]
    for hh in range(2):
        MM(S, [hb, w_in], [psX[hh]], [(psX[hh][:, q * 130:q * 130 + 130], w_in[:, kc, 2064 + (hh * 3 + q) * 128:2064 + (hh * 3 + q + 1) * 128],
                                        hb[:, kc, 0:130], kc == 0, kc == 7) for q in range(3) for kc in range(8)])
    XBC, XBCe, XBCb = PS.XBC, PS.XBCe, PS.XBCb
    for b in range(6):
        pb, c0 = psX[b // 3], (b % 3) * 130
        ACT(S, [pb, k.scw], [XBC], XBC[:, b, :], pb[:, c0 + 1:c0 + 129], AF.Identity, bias=k.scw[:, 3, b:b + 1], scale=k.scw[:, 1, b:b + 1])
        STT(S, "dve", [pb, k.scw, XBC], [XBC], XBC[:, b, :], pb[:, c0:c0 + 128], k.scw[:, 0, b:b + 1], XBC[:, b, :], ALU.mult, ALU.add)
        STT(S, "dve", [pb, k.scw, XBC], [XBC], XBC[:, b, :], pb[:, c0 + 2:c0 + 130], k.scw[:, 2, b:b + 1], XBC[:, b, :], ALU.mult, ALU.add)
    ACT(S, [XBC], [XBCe], XBCe[:, :, :], XBC[:, :, :], AF.Exp, scale=-1.0)
    ACT(S, [XBCe], [XBCe], XBCe[:, :, :], XBCe[:, :, :], AF.Ln, bias=1.0)
    ACT(S, [XBCe], [XBCe], XBCe[:, :, :], XBCe[:, :, :], AF.Exp, scale=-1.0)
    TT(S, "pool", [XBC, XBCe], [XBCb], XBCb[:, :, :], XBC[:, :, :], XBCe[:, :, :], ALU.mult)
    psT = S.bank()
    pTv = bview(psT, BF16)
    TR(S, [XBCb, k.identb], [psT], [(pTv[:, b * 128:(b + 1) * 128], XBCb[:, b, :], k.identb[:, :]) for b in range(4)])
    CP(S, "act", [psT], [PS.x_tm], PS.x_tm[:, :], pTv[:, 0:256])
    CP(S, "act", [psT], [PS.B_tm], PS.B_tm[:, :, :], pTv[:, 256:512].rearrange("p (g n) -> p g n", g=2))
    x4 = PS.x_tm[:, :].rearrange("p (h e) -> p h e", h=4)
    TT(S, "pool", [PS.x_tm, r8], [PS.xt_s], PS.xt_s[:, :, :], x4, bc(r8[:, 4:8].unsqueeze(2), [128, 4, 64]), ALU.mult)
    TT(S, "pool", [PS.x_tm, wend], [PS.xh_s], PS.xh_s[:, :, :], x4, bc(wend[:, 4:8].unsqueeze(2), [128, 4, 64]), ALU.mult)
    psS2 = S.bank()
    MM(S, [XBCb], [psS2], [(psS2[:, g * 128:(g + 1) * 128], XBCb[:, 2 + g, :], XBCb[:, 4 + g, :], True, True) for g in range(2)])
    for g in range(2):
        TT(S, "dve", [psS2, PS.DIFF], [PS.PTs], PS.PTs[:, 2 * g:2 * g + 2, :],
           bc(psS2[:, g * 128:(g + 1) * 128].unsqueeze(1), [128, 2, 128]), PS.DIFF[:, 4 + 2 * g:6 + 2 * g, :], ALU.mult)
    psY = S.bank()
    MM(S, [PS.PTs, PS.xt_s, XBCb, a.Hsb], [psY],
       [(psY[:, h * 64:(h + 1) * 64], PS.PTs[:, h, :], PS.xt_s[:, h, :], True, True) for h in range(4)]
       + [(psY[:, 256 + g * 128:256 + (g + 1) * 128], XBCb[:, 4 + g, :], a.Hsb[:, 2 * g:2 * g + 2, :].rearrange("p h e -> p (h e)"), True, True)
          for g in range(2)])
    Ysc = PS.Ysc
    TT(S, "dve", [psY, e8], [Ysc], Ysc[:, :, :], psY[:, 256:512].rearrange("p (h e) -> p h e", h=4),
       bc(e8[:, 4:8].unsqueeze(2), [128, 4, 64]), ALU.mult)
    TT(S, "dve", [psY, Ysc], [HY], HY[:, 256:512], psY[:, 0:256], Ysc[:, :, :].rearrange("p h e -> p (h e)"), ALU.add)
    psU2 = S.bank()
    MM(S, [PS.B_tm, PS.xh_s], [psU2], [(psU2[:, g * 128:(g + 1) * 128], PS.B_tm[:, g, :], PS.xh_s[:, 2 * g:2 * g + 2, :].rearrange("p h e -> p (h e)"),
                                       True, True) for g in range(2)])
    TT(S, "dve", [a.Hs, aL], [a.Hs], a.Hs[:, :, :], a.Hs[:, :, :], bc(aL[:, 4:8].unsqueeze(2), [128, 4, 64]), ALU.mult)
    TT(S, "dve", [a.Hs, psU2], [a.Hs], a.Hs[:, :, :], psU2[:, 0:256].rearrange("p (h e) -> p h e", h=4), a.Hs[:, :, :], ALU.add)
    CP(S, "pool", [a.Hs], [a.Hsb], a.Hsb[:, :, :], a.Hs[:, :, :])

    if getattr(k.cfg, "stop", 99) <= 3:
        return
    tile_g = (off // 128) + i
    if not full:
        db = S.dbuf(("HF", tile_g))
        S.dma("pool", k.HF[off + 128 * i:off + 128 * (i + 1), :], HY[:, :], reads=[HY], writes=[db])
        return
    finalizeA(k, a, si, T, cond, off, i, nt, ps1, ps2, HY, PS)


def finalizeA(k, a, si, T, cond, off, i, nt, ps1, ps2, HY, pset):
    S, l = k.S, a.l
    w_in, w_out = a.w_in, a.w_out
    hb = a.hb[i % 3]
    hcur = lambda kc: hb[:, kc, 1:129]
    cst = k.cstb
    HYf = a.HYf[i % 2]
    f1, f2, f3 = a.fin1, a.fin2, a.fin3
    v4 = lambda ap: ap.rearrange("p (h e) -> p h e", h=4)
    ym, yg, ys = a.yall[:, 0, :], a.yall[:, 1, :], a.yall[:, 2, :]

    ps3, ps4 = S.bank(), S.bank()
    MM(S, [hb, w_in], [ps3], [(ps3[:, 0:512], hcur(kc), w_in[:, kc, 1296:1808], kc == 0, kc == 7) for kc in range(8)])
    MM(S, [hb, w_in], [ps4], [(ps4[:, 0:256], hcur(kc), w_in[:, kc, 1808:2064], kc == 0, kc == 7) for kc in range(8)]
       + [(ps4[:, 256:512], hcur(kc), w_in[:, kc, 1040:1296], kc == 0, kc == 7) for kc in range(8)])
    has_p, has_n = i > 0, i < nt - 1
    ps5 = S.bank()
    mm5 = []
    if has_p:
        hp_ = a.hb[(i - 1) % 3]
        mm5 += [(ps5[0:8, 0:256], hp_[:, kc, 121:129], w_in[:, kc, 1040:1296], kc == 0, kc == 7) for kc in range(8)]
    if has_n:
        hn_ = a.hb[(i + 1) % 3]
        mm5 += [(ps5[0:8, 256:512], hn_[:, kc, 1:9], w_in[:, kc, 1040:1296], kc == 0, kc == 7) for kc in range(8)]
    if mm5:
        rd = [w_in] + ([a.hb[(i - 1) % 3]] if has_p else []) + ([a.hb[(i + 1) % 3]] if has_n else [])
        MM(S, rd, [ps5], mm5)

    TT(S, "pool", [HY, HYf], [f1], f1[:, :], HY[:, 0:256], HYf[:, 0:256], ALU.add)
    S.op("dve", lambda e: e.tensor_reduce(a.st4[:, :], v4(f1[:, :]), AX.X, ALU.add), [f1], [a.st4])
    TS(S, "dve", [a.st4], [a.st4], a.st4[:, :], a.st4[:, :], 1.0 / 64.0, None, ALU.mult)
    TT(S, "pool", [f1, a.st4], [f1], v4(f1[:, :]), v4(f1[:, :]), bc(a.st4[:, :].unsqueeze(2), [128, 4, 64]), ALU.subtract)
    TT(S, "pool", [f1], [f2], f2[:, :], f1[:, :], f1[:, :], ALU.mult)
    S.op("dve", lambda e: e.tensor_reduce(a.st4b[:, :], v4(f2[:, :]), AX.X, ALU.add), [f2], [a.st4b])
    TS(S, "dve", [a.st4b], [a.st4b], a.st4b[:, :], a.st4b[:, :], 1.0 / 64.0, EPS, ALU.mult, ALU.add)
    TT(S, "pool", [a.st4b, k.cm05], [a.st4b], a.st4b[:, :], a.st4b[:, :], bc(k.cm05[:, 0:1], [128, 4]), ALU.pow)
    TT(S, "pool", [f1, a.st4b], [f1], v4(f1[:, :]), v4(f1[:, :]), bc(a.st4b[:, :].unsqueeze(2), [128, 4, 64]), ALU.mult)
    TT(S, "pool", [f1, k.mng], [f1], f1[:, :], f1[:, :], k.mng[:, :], ALU.mult)
    ACT(S, [a.eo], [a.eo], a.eo[:, :], a.eo[:, :], AF.Ln, bias=1.0)
    ACT(S, [a.eo], [a.eo], a.eo[:, :], a.eo[:, :], AF.Exp, scale=-1.0)
    TT(S, "pool", [f1, a.eo], [a.yall], ym, f1[:, :], a.eo[:, :], ALU.mult)

    CP(S, "act", [ps4], [a.z_sb], a.z_sb[:, :], ps4[:, 0:256])
    ACT(S, [ps4], [a.ez], a.ez[:, :], ps4[:, 0:256], AF.Exp, scale=-1.0)
    TT(S, "pool", [HY, HYf], [f2], f2[:, :], HY[:, 256:512], HYf[:, 256:512], ALU.add)
    TT(S, "pool", [pset.x_tm, k.dsk], [f3], v4(f3[:, :]), v4(pset.x_tm[:, :]), bc(k.dsk[:, :].unsqueeze(2), [128, 4, 64]), ALU.mult)
    TT(S, "pool", [f2, f3], [f2], f2[:, :], f2[:, :], f3[:, :], ALU.add)
    ACT(S, [a.ez], [a.ez], a.ez[:, :], a.ez[:, :], AF.Ln, bias=1.0)
    ACT(S, [a.ez], [a.ez], a.ez[:, :], a.ez[:, :], AF.Exp, scale=-1.0)
    TT(S, "pool", [a.ez, a.z_sb], [a.ez], a.ez[:, :], a.ez[:, :], a.z_sb[:, :], ALU.mult)
    TT(S, "pool", [f2, a.ez], [f2], f2[:, :], f2[:, :], a.ez[:, :], ALU.mult)
    TT(S, "pool", [f2], [f3], f3[:, :], f2[:, :], f2[:, :], ALU.mult)
    S.op("dve", lambda e: e.tensor_reduce(a.st4[:, 0:2], f3[:, :].rearrange("p (g e) -> p g e", g=2), AX.X, ALU.add), [f3], [a.st4])
    TS(S, "dve", [a.st4], [a.st4], a.st4[:, 0:2], a.st4[:, 0:2], 1.0 / 128.0, EPS, ALU.mult, ALU.add)
    TT(S, "pool", [a.st4, k.cm05], [a.st4], a.st4[:, 0:2], a.st4[:, 0:2], bc(k.cm05[:, 0:1], [128, 2]), ALU.pow)
    TT(S, "pool", [f2, a.st4], [f2], f2[:, :].rearrange("p (g e) -> p g e", g=2), f2[:, :].rearrange("p (g e) -> p g e", g=2),
       bc(a.st4[:, 0:2].unsqueeze(2), [128, 2, 128]), ALU.mult)
    TT(S, "pool", [f2, k.sng], [a.yall], ys, f2[:, :], k.sng[:, :], ALU.mult)

    CP(S, "act", [ps3], [a.gu], a.gu[:, :], ps3[:, 0:256])
    S.op("dve", lambda e: e.bn_stats(a.gst[:, :], ps3[:, 256:512]), [ps3], [a.gst])
    S.op("dve", lambda e: e.bn_aggr(a.gmv[:, :], a.gst[:, :]), [a.gst], [a.gmv])
    TS(S, "dve", [a.gmv], [a.gve], a.gve[:, :], a.gmv[:, 1:2], EPS, None, ALU.add)
    TT(S, "pool", [a.gve, k.cm05], [a.grs], a.grs[:, :], a.gve[:, :], k.cm05[:, :], ALU.pow)
    TS(S, "dve", [ps3, a.gmv, a.grs], [a.gvb], a.gvb[:, :], ps3[:, 256:512], a.gmv[:, 0:1], a.grs[:, 0:1], ALU.subtract, ALU.mult)
    psG = S.bank()
    MM(S, [k.wsT, a.gvb], [psG], [(psG[:, h * 64:(h + 1) * 64], k.wsT[:, h, :], a.gvb[:, h * 64:(h + 1) * 64], True, True) for h in range(4)])
    TT(S, "dve", [psG, k.bsT], [f3], v4(f3[:, :]), v4(psG[:, 0:256]), bc(k.bsT[:, :].unsqueeze(2), [128, 4, 64]), ALU.add)
    TT(S, "pool", [f3, a.gu], [a.yall], yg, f3[:, :], a.gu[:, :], ALU.mult)

    CP(S, "act", [ps4], [a.pc], a.pc[:, :], ps4[:, 256:512])
    if has_p:
        CP(S, "act", [ps5], [a.pcP], a.pcP[:, :], ps5[0:8, 0:256])
    if has_n:
        CP(S, "act", [ps5], [a.pcN], a.pcN[:, :], ps5[0:8, 256:512])
    var = "int" if (has_p and has_n) else ("first" if has_n else ("last" if has_p else "int"))
    psP = S.bank()
    mmp = []
    for g in range(4):
        o_ = psP[:, g * 64:(g + 1) * 64]
        seqm = [(cview(k, f"pA{g}{var}"), a.pc[:, g * 64:(g + 1) * 64])]
        if has_p:
            seqm.append((cview(k, f"pP{g}", 8), a.pcP[0:8, g * 64:(g + 1) * 64]))
        if has_n:
            seqm.append((cview(k, f"pN{g}", 8), a.pcN[0:8, g * 64:(g + 1) * 64]))
        for n_, (lh, rh) in enumerate(seqm):
            mmp.append((o_, lh, rh, n_ == 0, n_ == len(seqm) - 1))
    MM(S, [a.c2b, a.pc, a.pcP, a.pcN], [psP], mmp)
    CP(S, "act", [psP], [a.plb], a.plb[:, :], psP[:, 0:256])
    psT2 = S.bank()
    t2v = bview(psT2, BF16)
    TR(S, [a.plb, k.identb], [psT2], [(t2v[:, j * 128:(j + 1) * 128], a.plb[:, j * 128:(j + 1) * 128], k.identb[:, :]) for j in range(2)])
    CP(S, "dve", [psT2], [a.plT], a.plT[:, :, :], t2v[:, 0:256].rearrange("p (j t) -> p j t", j=2))
    psW = S.bank()
    MM(S, [k.wpb, a.plT], [psW], [(psW[:, j * 128:(j + 1) * 128], k.wpb[:, j, :], a.plT[:, j, :], True, True) for j in range(2)])
    cT = a.concatT
    for j in range(2):
        ACT(S, [psW, k.psc], [cT], cT[:, 2 + j, :], psW[:, j * 128:(j + 1) * 128], AF.Identity, scale=k.psc[:, j:j + 1])

    psT3 = S.bank()
    t3v = bview(psT3, BF16)
    TR(S, [a.yall, k.identb], [psT3], [(t3v[:, (m3 * 2 + j) * 128:(m3 * 2 + j + 1) * 128], a.yall[:, m3, j * 128:(j + 1) * 128], k.identb[:, :])
                                      for m3 in range(3) for j in range(2)])
    CP(S, "dve", [psT3], [cT], cT[:, 0:2, :], t3v[:, 0:256].rearrange("p (j t) -> p j t", j=2))
    CP(S, "act", [psT3], [cT], cT[:, 4:8, :], t3v[:, 256:768].rearrange("p (j t) -> p j t", j=4))

    p0, p1 = S.bank(), S.bank()
    for hlf, pb in enumerate((p0, p1)):
        MM(S, [cT, w_out], [pb], [(pb[:, :], cT[:, kc, :], w_out[:, kc, hlf * 512:(hlf + 1) * 512], kc == 0, kc == 7) for kc in range(8)])
    xb = a.xq[i % 2]
    resid_ln(k, xb, (p0, p1), xb, a.rsm)
    r0 = off + 128 * i
    db = S.dbuf(("xoutA", l, r0 // 128))
    S.dma("pool", rows_ap(k, a.dst, "out", r0, 128), xb[:, :], reads=[xb], writes=[db])
    if a.dst is None:
        k.final_bufs.append(db)


_CACHE = {}


def gather_outputs(results, cfg, n):
    NP, Tp, Ts = cfg.NP, cfg.Tp, cfg.Ts
    y_p = np.concatenate([r["yp"].reshape(NP, Tp, D) for r in results], 0)
    y_s = np.stack([r["ys"].reshape(Ts, D) for r in results], 0)
    oc = np.concatenate([r["oc"] for r in results], 0)
    on = np.concatenate([r["on"] for r in results], 0)
    om = np.concatenate([r["om"].reshape(NP, 2, 2, 4) for r in results], 0)
    os_ = np.concatenate([r["os"] for r in results], 0)
    f = lambda a: np.ascontiguousarray(a, dtype=np.float32)
    return (f(y_p), f(y_s), f(oc), f(on), f(om), f(os_))


def kernel(**inputs):
    n = 8
    xs = np.asarray(inputs["x_sample"])
    xp = np.asarray(inputs["x_prompt"])
    cfg = Cfg(Ts=xs.shape[1], NP=xp.shape[0] // n, Tp=xp.shape[1], L=2)
    key = (cfg.Ts, cfg.NP, cfg.Tp)
    if key not in _CACHE:
        _CACHE[key] = build(cfg)
    nc, _ = _CACHE[key]
    in_maps = [shard_inputs(inputs, cfg, c) for c in range(n)]
    res = run_bass_kernel_spmd(nc, in_maps, core_ids=list(range(n)))
    return gather_outputs(res.results, cfg, n)
```

```python
import math
import numpy as np
import ml_dtypes
from contextlib import ExitStack
import concourse.bass as bass
import concourse.mybir as mybir
from concourse.bass_utils import run_bass_kernel_spmd

F32 = mybir.dt.float32
BF16 = mybir.dt.bfloat16
AF = mybir.ActivationFunctionType
ALU = mybir.AluOpType
AX = mybir.AxisListType
DTSZ = {F32: 4, BF16: 2}

D = 1024
DIN = 2840
DFF = 2816
EPS = 1e-5
ALPHA = 4.0 ** 0.25
NEG = -30000.0

ENGS = ("pe", "act", "dve", "pool", "sp")
EPOCH = 30000
NDMA_SEM = 8


def prod(l):
    r = 1
    for x in l:
        r *= int(x)
    return r


class Buf:
    __slots__ = ("name", "v", "last_w", "readers", "off", "excl")

    def __init__(self, name, v, floor=None):
        self.off = -1
        self.excl = False
        self.name = name
        self.v = v
        self.last_w = floor
        self.readers = []

    def __getitem__(self, k):
        return self.v[k]


class Op:
    __slots__ = ("eng", "fn", "deps", "is_dma", "idx", "sig", "has_dep", "vc", "name", "cost")

    def __init__(self, eng, fn, is_dma, name):
        self.cost = 0.4
        self.eng = eng
        self.fn = fn
        self.is_dma = is_dma
        self.deps = []
        self.sig = None
        self.has_dep = False
        self.vc = None
        self.name = name


class Sched:
    def __init__(self, nc, es, arena_bytes):
        self.nc = nc
        self.es = es
        self.ops = []
        self.floor = None
        self.bufs = []
        self.arena = es.enter_context(nc.sbuf_tensor("arena", [128, arena_bytes // 4], F32))
        self.arena_bytes = arena_bytes
        self.off = 0
        self.peak = 0
        self.banks = []
        for i in range(8):
            t = es.enter_context(nc.psum_tensor(f"bank{i}", [128, 512], F32))
            self.banks.append(Buf(f"bank{i}", t))
            self.banks[-1].excl = True
        self.bank_i = 0
        self.dram_bufs = {}

    def sb(self, name, shape, dtype=F32):
        shape = [int(s) for s in shape]
        if getattr(self, "verbose", False):
            print(f"  sb {name} {shape} {prod(shape[1:]) * DTSZ[dtype]} at {self.off}")
        n = prod(shape[1:])
        nb = n * DTSZ[dtype]
        off = (self.off + 31) // 32 * 32
        assert off + nb <= self.arena_bytes, f"arena overflow allocating {name}: {off + nb}"
        self.off = off + nb
        self.peak = max(self.peak, self.off)
        h = self.arena if dtype == F32 else self.arena.bitcast(dtype)
        e0 = off // DTSZ[dtype]
        v = h[0:shape[0], e0:e0 + n]
        if len(shape) > 2:
            names = " ".join(f"d{i}" for i in range(len(shape) - 1))
            kw = {f"d{i}": shape[i + 1] for i in range(len(shape) - 1)}
            v = v.rearrange(f"p ({names}) -> p {names}", **kw)
        b = Buf(name, v, self.floor)
        b.off = off
        self.bufs.append(b)
        return b

    def mark(self):
        return self.off

    def release(self, mark):
        self.barrier()
        self.bufs = [b for b in self.bufs if b.off < mark]
        self.off = mark

    def bank(self):
        b = self.banks[self.bank_i]
        self.bank_i = (self.bank_i + 1) % 8
        return b

    def dbuf(self, key):
        if key not in self.dram_bufs:
            self.dram_bufs[key] = Buf(str(key), None, None)
        return self.dram_bufs[key]

    def op(self, eng, fn, reads=(), writes=(), name=None, dma=False, cost=None):
        o = Op(eng, fn, dma, name)
        if cost is not None:
            o.cost = cost
        deps = set()
        ex = [b for b in reads if b.excl]
        if ex:
            reads = [b for b in reads if not b.excl]
            writes = list(writes) + [b for b in ex if b not in writes]
        for b in reads:
            if b.last_w is not None:
                deps.add(b.last_w)
        for b in writes:
            if b.last_w is not None:
                deps.add(b.last_w)
            for r in b.readers:
                deps.add(r)
        o.deps = list(deps)
        o.idx = len(self.ops)
        for d in o.deps:
            d.has_dep = True
        for b in reads:
            b.readers.append(o)
        for b in writes:
            b.last_w = o
            b.readers = []
        self.ops.append(o)
        return o

    def dma(self, q, out, in_, reads=(), writes=(), name=None, **kw):
        nbytes = prod(out.shape) * 4
        return self.op(q, lambda e: e.dma_start(out=out, in_=in_, **kw), reads, writes, name=name, dma=True,
                       cost=2.0 + nbytes / 150e3)

    def barrier(self):
        allb = self.bufs + self.banks + list(self.dram_bufs.values())
        o = self.op("sp", None, reads=[], writes=allb, name="barrier")
        self.floor = o
        return o

    def list_schedule(self, ops):
        import heapq
        LAT = 1.2
        out = []
        seg = []
        segs = []
        for o in ops:
            if o.fn is None:
                segs.append(seg)
                segs.append([o])
                seg = []
            else:
                seg.append(o)
        segs.append(seg)
        finish = {}
        for seg in segs:
            if len(seg) <= 1:
                for o in seg:
                    finish[o] = 0.0
                    out.append(o)
                continue
            inseg = set(seg)
            indeg = {}
            users = {}
            for o in seg:
                n = 0
                for d in o.deps:
                    if d in inseg:
                        n += 1
                        users.setdefault(d, []).append(o)
                indeg[o] = n
            eng_time = {e: 0.0 for e in ENGS}
            ready_at = {}
            heap = []
            for o in seg:
                if indeg[o] == 0:
                    ready_at[o] = 0.0
                    heapq.heappush(heap, (0.0, o.idx, o))
            while heap:
                best = None
                cand = []
                while heap and len(cand) < 24:
                    cand.append(heapq.heappop(heap))
                bi = None
                for ci, (ra, idx, o) in enumerate(cand):
                    stt = max(ra, eng_time[o.eng])
                    key = (stt, idx)
                    if best is None or key < best:
                        best, bi = key, ci
                ra, idx, o = cand.pop(bi)
                for c in cand:
                    heapq.heappush(heap, c)
                stt = best[0]
                if o.is_dma:
                    eng_time[o.eng] = stt + 0.15
                    fin_t = stt + o.cost
                else:
                    fin_t = stt + o.cost
                    eng_time[o.eng] = fin_t
                finish[o] = fin_t
                out.append(o)
                for u in users.get(o, ()):
                    t = fin_t + (LAT if u.eng != o.eng else 0.3)
                    if ready_at.get(u, 0.0) < t:
                        ready_at[u] = t
                    indeg[u] -= 1
                    if indeg[u] == 0:
                        heapq.heappush(heap, (ready_at[u], u.idx, u))
            self.est_time = getattr(self, "est_time", 0.0) + max(eng_time.values())
        for i, o in enumerate(out):
            o.idx = i
        return out

    def emit(self, final_bufs):
        nc, es = self.nc, self.es
        fin = self.op("sp", None, reads=list(final_bufs), name="final")
        if getattr(self, "reorder", True):
            self.ops = self.list_schedule(self.ops)
        cnt, dma_n, semkeys = {}, {}, []
        for o in self.ops:
            if not o.has_dep:
                continue
            if o.is_dma:
                n = dma_n.get(o.eng, 0)
                dma_n[o.eng] = n + 1
                key = ("dma", o.eng, n % NDMA_SEM)
                cnt[key] = cnt.get(key, 0) + 16
                o.sig = (key, cnt[key])
            else:
                tot = cnt.get(("n", o.eng), 0)
                cnt[("n", o.eng)] = tot + 1
                key = ("c", o.eng, tot // EPOCH)
                o.sig = (key, tot % EPOCH + 1)
            if o.sig[0] not in semkeys:
                semkeys.append(o.sig[0])
        sems = {k: es.enter_context(nc.semaphore("s_" + "_".join(str(x) for x in k))) for k in semkeys}
        per_eng = {e: [] for e in ENGS}
        seen = {e: {} for e in ENGS}
        nwaits = 0
        for o in self.ops:
            s = seen[o.eng]
            need = {}
            for d in o.deps:
                k, c = d.sig
                if s.get(k, 0) < c:
                    need[k] = max(need.get(k, 0), c)
            if o.is_dma and o.sig is not None:
                k, c = o.sig
                if c > 16 and s.get(k, 0) < c - 16:
                    need[k] = max(need.get(k, 0), c - 16)
            for d in o.deps:
                for k, c in d.vc.items():
                    if s.get(k, 0) < c:
                        s[k] = c
            for k, c in need.items():
                if s.get(k, 0) < c:
                    s[k] = c
            o.deps = need
            nwaits += len(need)
            vc = dict(s)
            if o.sig is not None:
                vc[o.sig[0]] = max(vc.get(o.sig[0], 0), o.sig[1])
                if not o.is_dma:
                    for ep in range(o.sig[0][2]):
                        vc[("c", o.eng, ep)] = EPOCH
            o.vc = vc
            per_eng[o.eng].append(o)

        def body_for(engname):
            def body(eng):
                for o in per_eng[engname]:
                    for k, c in o.deps.items():
                        eng.wait_ge(sems[k], c)
                    if o.fn is not None:
                        ins = o.fn(eng)
                        if o.sig is not None:
                            ins.then_inc(sems[o.sig[0]], 16 if o.is_dma else 1)
                    elif o.sig is not None:
                        eng.nop().then_inc(sems[o.sig[0]], 1)
            return body

        with nc.Block() as block:
            block.sync(body_for("sp"))
            block.scalar(body_for("act"))
            block.vector(body_for("dve"))
            block.gpsimd(body_for("pool"))
            block.tensor(body_for("pe"))
        return {"ops": len(self.ops), "waits": nwaits, "sems": len(sems),
                "per_eng": {e: len(v) for e, v in per_eng.items()}, "sbuf_peak": self.peak}


def _c(out, base=0.25, per=1.0 / 1000.0):
    return base + prod(out.shape[1:]) * per


def ACT(S, r, w, out, in_, func, bias=None, scale=None):
    kw = {}
    if bias is not None:
        kw["bias"] = bias
    if scale is not None:
        kw["scale"] = scale
    return S.op("act", lambda e: e.activation(out, in_, func, **kw), r, w, cost=_c(out, 0.3, 1 / 1200.0))


def TS(S, eng, r, w, out, in0, s1, s2, op0, op1=None):
    if op1 is None:
        return S.op(eng, lambda e: e.tensor_scalar(out, in0, s1, None, op0), r, w, cost=_c(out))
    return S.op(eng, lambda e: e.tensor_scalar(out, in0, s1, s2, op0, op1), r, w, cost=_c(out))


def TT(S, eng, r, w, out, in0, in1, op):
    return S.op(eng, lambda e: e.tensor_tensor(out, in0, in1, op), r, w, cost=_c(out))


def STT(S, eng, r, w, out, in0, scalar, in1, op0, op1):
    return S.op(eng, lambda e: e.scalar_tensor_tensor(out, in0, scalar, in1, op0, op1), r, w, cost=_c(out))


def CP(S, eng, r, w, out, in_):
    if eng == "act":
        return S.op("act", lambda e: e.copy(out, in_), r, w, cost=_c(out, 0.3, 1 / 1200.0))
    return S.op(eng, lambda e: e.tensor_copy(out, in_), r, w, cost=_c(out))


def MSET(S, eng, w, out, val):
    return S.op(eng, lambda e: e.memset(out, val), [], w)


def MM(S, r, w, mms):
    mms = list(mms)

    def fn(e):
        ins = None
        for (o, l, rh, st, sp) in mms:
            ins = e.matmul(o, l, rh, start=st, stop=sp)
        return ins
    cost = 0.1
    for (o, l, rh, st, sp) in mms:
        cost += max(64, prod(rh.shape[1:])) / 2400.0 * (4.0 if rh.dtype == F32 else 1.0) + 0.02
    return S.op("pe", fn, r, w, cost=cost)


def TR(S, r, w, trs):
    trs = list(trs)

    def fn(e):
        ins = None
        for (o, i, idt) in trs:
            ins = e.transpose(o, i, idt)
        return ins
    return S.op("pe", fn, r, w, cost=0.1 + 0.12 * len(trs))


def bc(ap, shape):
    return ap.to_broadcast([int(s) for s in shape])


POOL_W = (2, 4, 8, 16)


def make_consts():
    cols = {}
    parts = []
    off = [0]

    def add(name, arr):
        a = np.zeros((128, arr.shape[1]), np.float32)
        a[:arr.shape[0]] = arr
        cols[name] = (off[0], arr.shape[1])
        off[0] += arr.shape[1]
        parts.append(a)

    idx = np.arange(128)
    s_, t_ = idx[:, None], idx[None, :]
    add("ident", np.eye(128, dtype=np.float32))
    add("ones", np.ones((128, 128), np.float32))
    add("tri0", (s_ <= t_).astype(np.float32))
    add("tri1", (s_ >= t_).astype(np.float32))
    add("neg0", np.where(s_ <= t_, 0.0, NEG).astype(np.float32))
    add("neg1", np.where(s_ >= t_, 0.0, NEG).astype(np.float32))
    n1 = off[0]
    for g, w in enumerate(POOL_W):
        h = w // 2
        band = ((s_ >= t_ - h) & (s_ < t_ + h)).astype(np.float32)
        cnt_int = np.full(128, float(w))
        cnt_first = (idx + h) - np.maximum(idx - h, 0)
        cnt_last = np.minimum(idx + h, 128) - (idx - h)
        eye = np.eye(128, dtype=np.float32)
        add(f"pA{g}int", band / cnt_int[None, :] - eye)
        add(f"pA{g}first", band / cnt_first[None, :] - eye)
        add(f"pA{g}last", band / cnt_last[None, :] - eye)
        sp = np.arange(8)[:, None]
        add(f"pP{g}", (((sp - 8) >= t_ - h) & ((sp - 8) < t_ + h)).astype(np.float32) / w)
        add(f"pN{g}", (((128 + sp) >= t_ - h) & ((128 + sp) < t_ + h)).astype(np.float32) / w)
    add("jrow", np.tile(np.arange(256, dtype=np.float32)[None, :], (128, 1)))
    add("pcol", (idx % 64).astype(np.float32)[:, None])
    full = np.concatenate(parts, axis=1)
    cols2 = {kk: (o - n1, n) for kk, (o, n) in cols.items() if o >= n1}
    cols1 = {kk: (o, n) for kk, (o, n) in cols.items() if o < n1}
    return full[:, :n1].copy(), cols1, full[:, n1:].copy(), cols2


class Cfg:
    def __init__(self, Ts=4096, NP=4, Tp=256, L=2, debug=()):
        self.Ts, self.NP, self.Tp, self.L = Ts, NP, Tp, L
        self.debug = tuple(debug)
        self.seqs = [("s", Ts, 0, 0)] + [(f"p{j}", Tp, 1, Ts + j * Tp) for j in range(NP)]
        self.Ttot = Ts + NP * Tp


INPUT_SPECS = lambda c: [
    ("xs", [c.Ts, D]), ("xp", [c.NP * c.Tp, D]),
    ("st_c", [2, 2, 4, 64, 64]), ("st_n", [2, 2, 4, 64]), ("st_m", [2, 8]), ("st_s", [2, 2, 4, 64, 128]),
    ("cond", [2, D]),
    ("w_ada", [2, D, 6 * D]), ("b_ada", [2, 6 * D]), ("w_in", [2, D, DIN]), ("b_ig", [2, 8]), ("b_fg", [2, 8]),
    ("mnorm_g", [2, 256]), ("w_pool", [2, 4, 64, 64]), ("pool_scale", [2, 256]), ("w_sp", [2, 4, 128, 128]),
    ("b_sp", [2, 4, 128]), ("sconv_w", [2, 3, 768]), ("sconv_b", [2, 768]), ("dt_bias", [2, 8]),
    ("a_log", [2, 8]), ("ssd_d", [2, 4]), ("snorm_g", [2, 256]), ("w_out", [2, D, D]),
    ("ln1_g", [2, D]), ("ln1_b", [2, D]), ("w_up", [2, D, 2 * DFF]), ("fconv_w", [2, 3, 2 * DFF]),
    ("fconv_b", [2, 2 * DFF]), ("w_dn", [2, DFF, D]), ("ln2_g", [2, D]), ("ln2_b", [2, D]),
]
OUTPUT_SPECS = lambda c: [
    ("ys", [c.Ts, D]), ("yp", [c.NP * c.Tp, D]), ("oc", [c.NP, 2, 2, 4, 64, 64]), ("on", [c.NP, 2, 2, 4, 64]),
    ("om", [c.NP, 2, 8]), ("os", [c.NP, 2, 2, 4, 64, 128]),
]


class K:
    pass


def build(cfg):
    nc = bass.Bass("TRN2", target_bir_lowering=False)
    cst_np, ccols, cst2_np, ccols2 = make_consts()
    I = {}
    for name, shape in INPUT_SPECS(cfg):
        I[name] = nc.dram_tensor(name, shape, F32, kind="ExternalInput")
    I["cst"] = nc.dram_tensor("cst", list(cst_np.shape), F32, kind="ExternalInput")
    I["cst2"] = nc.dram_tensor("cst2", list(cst2_np.shape), F32, kind="ExternalInput")
    O = {}
    for name, shape in OUTPUT_SPECS(cfg):
        O[name] = nc.dram_tensor(name, shape, F32, kind="ExternalOutput")
    DBG = {}
    for name, shape in cfg.debug:
        DBG[name] = nc.dram_tensor("dbg_" + name, shape, F32, kind="ExternalOutput")
    ntile = cfg.Ttot // 128
    XA = nc.dram_tensor("scr_xa", [cfg.Ttot, D], F32, kind="Internal")
    XB = nc.dram_tensor("scr_xb", [cfg.Ttot, D], F32, kind="Internal")
    HF = nc.dram_tensor("scr_hf", [cfg.Ttot, 512], F32, kind="Internal")
    HT = nc.dram_tensor("scr_ht", [ntile, 128, 8 * 128], BF16, kind="Internal")
    ED = nc.dram_tensor("scr_e", [64, 512], F32, kind="Internal")
    STB = nc.dram_tensor("scr_stb", [ntile, 128, 2304], BF16, kind="Internal")
    STG = nc.dram_tensor("scr_stg", [ntile, 128, 280], F32, kind="Internal")

    with ExitStack() as es:
        S = Sched(nc, es, 204 * 1024)
        k = K()
        k.S, k.cfg, k.I, k.O, k.DBG = S, cfg, I, O, DBG
        k.XA, k.XB, k.HF, k.HT, k.ED = XA, XB, HF, HT, ED
        k.STB, k.STG = STB, STG
        k.final_bufs = []
        k.ccols2 = ccols2
        k.W2 = cst2_np.shape[1]
        setup(k, ccols)
        for l in range(cfg.L):
            layer(k, l)
        stats = S.emit(k.final_bufs)
    return nc, stats


def cview(k, name, rows=128):
    if name in k.ccols:
        o, n = k.ccols[name]
        return k.cst[0:rows, o:o + n]
    o, n = k.ccols2[name]
    return k.cst2[0:rows, o:o + n]


def load_cst2(k):
    S = k.S
    b = S.sb("cst2", [128, k.W2])
    k.cst2b = b
    k.cst2 = b.v
    S.dma("sp", b[:, :], k.I["cst2"][:, :], writes=[b])
    return b


def setup(k, ccols):
    S, I, cfg = k.S, k.I, k.cfg
    k.ccols = ccols
    W = sum(n for (_, n) in ccols.values())
    cstb = S.sb("cst", [128, W])
    k.cstb = cstb
    k.cst = cstb.v
    S.dma("sp", cstb[:, :], I["cst"][:, :], writes=[cstb])
    k.identb = S.sb("identb", [128, 128], BF16)
    CP(S, "dve", [cstb], [k.identb], k.identb[:, :], cview(k, "ident"))
    k.cm05 = S.sb("cm05", [128, 1])
    MSET(S, "pool", [k.cm05], k.cm05[:, :], -0.5)
    k.cm1 = S.sb("cm1", [128, 1])
    MSET(S, "pool", [k.cm1], k.cm1[:, :], -1.0)

    L = cfg.L
    k.modT = S.sb("modT", [128, L, 48, 2])
    layer_consts_alloc(k)
    m1 = S.mark()
    c2b = load_cst2(k)
    fr = S.sb("pe_fr", [64, 256])
    ang = S.sb("pe_ang", [64, 256])
    et = S.sb("pe_e", [64, 512])
    et2 = S.sb("pe_e2", [64, 512])
    sq = S.sb("pe_sq", [64, 256])
    ACT(S, [c2b], [fr], fr[:, :], cview(k, "jrow", 64), AF.Exp, scale=-math.log(10000.0) / 256.0)
    TS(S, "dve", [fr, c2b], [ang], ang[:, :], fr[:, :], cview(k, "pcol", 64), None, ALU.mult)
    ACT(S, [ang], [et], et[:, 0:256], ang[:, :], AF.Sin, scale=1.0 / 32.0)
    ACT(S, [ang], [et], et[:, 256:512], ang[:, :], AF.Sin, scale=-1.0 / 32.0, bias=math.pi / 2.0)
    cur, nxt = et, et2
    for it in range(5):
        TT(S, "dve", [cur], [sq], sq[:, :], cur[:, 0:256], cur[:, 0:256], ALU.mult)
        STT(S, "dve", [cur], [nxt], nxt[:, 0:256], cur[:, 0:256], 2.0, cur[:, 256:512], ALU.mult, ALU.mult)
        TS(S, "dve", [sq], [nxt], nxt[:, 256:512], sq[:, :], -2.0, 1.0, ALU.mult, ALU.add)
        cur, nxt = nxt, cur
    et = cur
    edb = S.dbuf("ED")
    S.dma("sp", k.ED[:, :], et[:, :], reads=[et], writes=[edb])

    condT = S.sb("condT", [128, 8, 2])
    for c in range(2):
        S.dma("sp", condT[:, :, c], I["cond"][c].rearrange("(kc p) -> p kc", p=128), writes=[condT],
              allow_slow_non_contiguous=True)
    esg = S.sb("cond_e", [128, 8, 2])
    ACT(S, [condT], [esg], esg[:, :, :], condT[:, :, :], AF.Exp, scale=-1.0)
    TS(S, "dve", [esg], [esg], esg[:, :, :], esg[:, :, :], 1.0, None, ALU.add)
    TT(S, "pool", [esg, k.cm1], [esg], esg[:, :, :], esg[:, :, :], bc(k.cm1[:, 0:1].unsqueeze(2), [128, 8, 2]), ALU.pow)
    TT(S, "pool", [condT, esg], [condT], condT[:, :, :], condT[:, :, :], esg[:, :, :], ALU.mult)
    badaT = S.sb("badaT", [128, L, 48])
    k.cf_st = S.sb("cf_st0", [128, 128])
    for l in range(L):
        colform(k, badaT, badaT[:, l, :], I["b_ada"][l].rearrange("(j p) -> j p", p=128), 48)
    wst = [S.sb(f"wada_st{j}", [128, 8, 512]) for j in range(2)]
    n = 0
    for l in range(L):
        for cb in range(12):
            st = wst[n % 2]
            n += 1
            S.dma("sp", st[:, :, :], I["w_ada"][l, :, cb * 512:(cb + 1) * 512].rearrange("(kc p) n -> p kc n", p=128),
                  writes=[st])
            pb = S.bank()
            mms = []
            for sub in range(4):
                for kc in range(8):
                    mms.append((pb[:, sub * 2:sub * 2 + 2], st[:, kc, sub * 128:(sub + 1) * 128], condT[:, kc, :],
                                kc == 0, kc == 7))
            MM(S, [st, condT], [pb], mms)
            TT(S, "dve", [pb, badaT], [k.modT],
               k.modT[:, l, cb * 4:cb * 4 + 4, :],
               pb[:, 0:8].rearrange("p (s c) -> p s c", c=2),
               bc(badaT[:, l, cb * 4:cb * 4 + 4].unsqueeze(2), [128, 4, 2]), ALU.add)
    for l in range(L):
        for grp in (1, 4):
            TS(S, "dve", [k.modT], [k.modT], k.modT[:, l, grp * 8:(grp + 1) * 8, :],
               k.modT[:, l, grp * 8:(grp + 1) * 8, :], 1.0, None, ALU.add)
    S.release(m1)


class nc_allow:
    def __init__(self, k):
        pass

    def __enter__(self):
        return self

    def __exit__(self, *a):
        return False


def bview(bank, dtype=F32):
    return bank.v if dtype == F32 else bank.v.bitcast(dtype)


def layer_consts_alloc(k):
    S = k.S
    k.bif = S.sb("bif", [128, 16])
    k.coef = S.sb("coef", [128, 2, 8])
    k.dtb = S.sb("dtb", [128, 8])
    k.dsk = S.sb("dsk", [128, 4])
    k.mng = S.sb("mng", [128, 256])
    k.sng = S.sb("sng", [128, 256])
    k.psc = S.sb("psc", [128, 2])
    k.wpb = S.sb("wpb", [128, 2, 128], BF16)
    k.wsT = S.sb("wsT", [128, 4, 128], BF16)
    k.bsT = S.sb("bsT", [128, 4])
    k.scw = S.sb("scw", [128, 4, 6])
    k.fcw = S.sb("fcw", [128, 4, 44])
    k.lng = S.sb("lng", [128, D])
    k.lnb = S.sb("lnb", [128, D])
    k.gbc = S.sb("gbc", [128, D])
    k.ttmp = [S.sb(f"ttmp{j}", [128, D]) for j in range(1)]
    k.small = {}


def colform(k, dst_buf, dst_ap, src_ap, nb):
    S = k.S
    st = k.cf_st
    S.dma("sp", st[0:nb, :], src_ap, writes=[st])
    pb = S.bank()
    TR(S, [st, k.cstb], [pb], [(pb[:, 0:nb], st[0:nb, :], cview(k, "ident")[0:nb, 0:nb])])
    CP(S, "dve", [pb], [dst_buf], dst_ap, pb[:, 0:nb])


def load_layer_consts(k, l):
    S, I = k.S, k.I
    m = S.mark()
    k.cf_st = S.sb("cf_st", [128, 128])
    row = lambda name, a, b: I[name][l:l + 1, a:b].partition_broadcast(128)
    S.dma("sp", k.bif[:, 0:8], row("b_ig", 0, 8), writes=[k.bif])
    S.dma("sp", k.bif[:, 8:16], row("b_fg", 0, 8), writes=[k.bif])
    TS(S, "dve", [k.bif], [k.bif], k.bif[:, 8:16], k.bif[:, 8:16], -1.0, None, ALU.mult)
    al = S.sb("al_tmp", [128, 8])
    S.dma("sp", al[:, :], row("a_log", 0, 8), writes=[al])
    ACT(S, [al], [al], al[:, :], al[:, :], AF.Exp)
    MSET(S, "pool", [k.coef], k.coef[:, :, :], -1.0)
    TS(S, "dve", [al, k.coef], [k.coef], k.coef[:, :, 4:8], al[:, :].rearrange("p (d h) -> p d h", d=2), -1.0, None, ALU.mult)
    S.dma("sp", k.dtb[:, :], row("dt_bias", 0, 8), writes=[k.dtb])
    S.dma("sp", k.dsk[:, :], row("ssd_d", 0, 4), writes=[k.dsk])
    S.dma("sp", k.mng[:, :], row("mnorm_g", 0, 256), writes=[k.mng])
    S.dma("sp", k.sng[:, :], row("snorm_g", 0, 256), writes=[k.sng])
    colform(k, k.psc, k.psc[:, :], I["pool_scale"][l].rearrange("(j p) -> j p", p=128), 2)
    wp32 = S.sb("wp32", [128, 2, 128])
    MSET(S, "pool", [wp32], wp32[:, :, :], 0.0)
    for g in range(4):
        pr = slice((g % 2) * 64, (g % 2) * 64 + 64)
        S.dma("sp", wp32[pr, g // 2, (g % 2) * 64:(g % 2) * 64 + 64], I["w_pool"][l, g], writes=[wp32])
    CP(S, "dve", [wp32], [k.wpb], k.wpb[:, :, :], wp32[:, :, :])
    ws32 = S.sb("ws32", [128, 4, 128])
    S.dma("sp", ws32[:, :, :], I["w_sp"][l].rearrange("h t s -> t h s"), writes=[ws32])
    pb = S.bank()
    TR(S, [ws32, k.cstb], [pb], [(pb[:, h * 128:(h + 1) * 128], ws32[:, h, :], cview(k, "ident")) for h in range(4)])
    CP(S, "act", [pb], [k.wsT], k.wsT[:, :, :], pb[:, :].rearrange("p (h t) -> p h t", h=4))
    colform(k, k.bsT, k.bsT[:, :], I["b_sp"][l], 4)
    for tap in range(3):
        colform(k, k.scw, k.scw[:, tap, :], I["sconv_w"][l, tap].rearrange("(b p) -> b p", p=128), 6)
        colform(k, k.fcw, k.fcw[:, tap, :], I["fconv_w"][l, tap].rearrange("(b p) -> b p", p=128), 44)
    colform(k, k.scw, k.scw[:, 3, :], I["sconv_b"][l].rearrange("(b p) -> b p", p=128), 6)
    colform(k, k.fcw, k.fcw[:, 3, :], I["fconv_b"][l].rearrange("(b p) -> b p", p=128), 44)
    S.release(m)


def load_weight(k, dst, src2d, nkc, ncols, scope_stage):
    S = k.S
    engs = ("dve", "act")
    piece = 2840
    for kc in range(nkc):
        for c0 in range(0, ncols, piece):
            c1 = min(ncols, c0 + piece)
            st = scope_stage[k.wl_n % 2]
            S.dma("sp", st[:, 0:c1 - c0], src2d[kc * 128:(kc + 1) * 128, c0:c1], writes=[st])
            CP(S, engs[k.wl_n % 2], [st], [dst], dst[:, kc, c0:c1], st[:, 0:c1 - c0])
            k.wl_n += 1


def gate_table(k, l, grp, cond):
    S = k.S
    dg = k.ttmp[0]
    for j in range(8):
        TS(S, "dve", [k.cstb, k.modT], [dg], dg[:, 0:128], cview(k, "ident"), k.modT[:, l, grp * 8 + j, cond:cond + 1], None, ALU.mult)
        if j % 4 == 0:
            pb = S.bank()
        MM(S, [dg, k.cstb], [pb], [(pb[:, (j % 4) * 128:(j % 4 + 1) * 128], cview(k, "ones"), dg[:, 0:128], True, True)])
        if j % 4 == 3:
            CP(S, "act", [pb], [k.gbc], k.gbc[:, (j // 4) * 512:(j // 4 + 1) * 512], pb[:, :])


def make_hT(k, l, which, cond, rows, xbuf, dsts, dst_buf, src_bufs, pos_tile=None):
    S = k.S
    n = 0
    for r, nr in rows:
        S.dma("sp", xbuf[n:n + nr, :], r, reads=src_bufs, writes=[xbuf])
        n += nr
    if pos_tile is not None:
        TT(S, "pool", [xbuf, pos_tile], [xbuf], xbuf[0:n, :], xbuf[0:n, :], pos_tile[0:n, :], ALU.add)
    sm = k.hsm
    st, mv, ve, rstd, xnb = sm["st"], sm["mv"], sm["ve"], sm["rstd"], sm["xnb"]
    S.op("dve", lambda e: e.bn_stats(st[0:n, 0, :], xbuf[0:n, 0:512]), [xbuf], [st])
    S.op("dve", lambda e: e.bn_stats(st[0:n, 1, :], xbuf[0:n, 512:1024]), [xbuf], [st])
    S.op("dve", lambda e: e.bn_aggr(mv[0:n, :], st[0:n, :, :].rearrange("p a b -> p (a b)")), [st], [mv])
    TS(S, "dve", [mv], [ve], ve[0:n, :], mv[0:n, 1:2], EPS, None, ALU.add)
    TT(S, "pool", [ve, k.cm05], [rstd], rstd[0:n, :], ve[0:n, :], k.cm05[0:n, :], ALU.pow)
    TS(S, "dve", [xbuf, mv, rstd], [xnb], xnb[0:n, :], xbuf[0:n, :], mv[0:n, 0:1], rstd[0:n, 0:1], ALU.subtract, ALU.mult)
    pb = S.bank()
    pv = bview(pb, BF16)
    TR(S, [xnb, k.identb], [pb],
       [(pv[:, kc * 128:kc * 128 + n], xnb[0:n, kc * 128:(kc + 1) * 128], k.identb[0:n, 0:n]) for kc in range(8)])
    gsh, gsc = (0, 1) if which == 1 else (3, 4)
    for kc in range(8):
        sc = k.modT[:, l, gsc * 8 + kc, cond:cond + 1]
        sh = k.modT[:, l, gsh * 8 + kc, cond:cond + 1]
        if kc % 2 == 0:
            ACT(S, [pb, k.modT], [dst_buf], dsts[kc], pv[:, kc * 128:kc * 128 + n], AF.Identity, bias=sh, scale=sc)
        else:
            TS(S, "dve", [pb, k.modT], [dst_buf], dsts[kc], pv[:, kc * 128:kc * 128 + n], sc, sh, ALU.mult, ALU.add)


def alloc_hsm(k):
    S = k.S
    k.hsm = {"st": S.sb("h_st", [128, 2, 6]), "mv": S.sb("h_mv", [128, 2]), "ve": S.sb("h_ve", [128, 1]),
             "rstd": S.sb("h_rstd", [128, 1]), "xnb": S.sb("h_xnb", [128, D], BF16)}


def resid_ln(k, x_buf, psum_halves, out_buf, nb_small):
    S = k.S
    t0, t1 = k.ttmp[0], out_buf
    for hlf, pb in enumerate(psum_halves):
        sl = slice(hlf * 512, (hlf + 1) * 512)
        TT(S, "dve", [pb, k.gbc], [t0], t0[:, sl], pb[:, :], k.gbc[:, sl], ALU.mult)
    STT(S, "dve", [x_buf, t0], [t0], t0[:, :], x_buf[:, :], ALPHA, t0[:, :], ALU.mult, ALU.add)
    st, mv, ve, rstd, nb = nb_small["st"], nb_small["mv"], nb_small["ve"], nb_small["rstd"], nb_small["nb"]
    S.op("dve", lambda e: e.bn_stats(st[:, 0, :], t0[:, 0:512]), [t0], [st])
    S.op("dve", lambda e: e.bn_stats(st[:, 1, :], t0[:, 512:1024]), [t0], [st])
    S.op("dve", lambda e: e.bn_aggr(mv[:, :], st[:, :, :].rearrange("p a b -> p (a b)")), [st], [mv])
    TS(S, "dve", [mv], [ve], ve[:, :], mv[:, 1:2], EPS, None, ALU.add)
    TT(S, "pool", [ve, k.cm05], [rstd], rstd[:, :], ve[:, :], k.cm05[:, :], ALU.pow)
    STT(S, "dve", [mv, rstd], [nb], nb[:, :], mv[:, 0:1], -1.0, rstd[:, :], ALU.mult, ALU.mult)
    ACT(S, [t0, rstd, nb], [t1], t1[:, :], t0[:, :], AF.Identity, bias=nb[:, 0:1], scale=rstd[:, 0:1])
    TT(S, "pool", [t1, k.lng], [t1], t1[:, :], t1[:, :], k.lng[:, :], ALU.mult)
    TT(S, "pool", [t1, k.lnb], [out_buf], out_buf[:, :], t1[:, :], k.lnb[:, :], ALU.add)


def alloc_rsm(k):
    S = k.S
    return {"st": S.sb("r_st", [128, 2, 6]), "mv": S.sb("r_mv", [128, 2]), "ve": S.sb("r_ve", [128, 1]),
            "rstd": S.sb("r_rstd", [128, 1]), "nb": S.sb("r_nb", [128, 1])}


def seq_src_dst(k, l, phase):
    cfg = k.cfg
    mode = getattr(cfg, "mode", "full")
    if phase == "A":
        src = None if l == 0 else k.XB
        dst = k.XA if mode == "full" else None
    else:
        src = k.XA if mode == "full" else None
        dst = None if l == cfg.L - 1 else k.XB
    return src, dst


def rows_ap(k, handle, which_io, r0, n):
    cfg = k.cfg
    if handle is not None:
        return handle[r0:r0 + n, :]
    if r0 < cfg.Ts:
        t = k.I["xs"] if which_io == "in" else k.O["ys"]
        return t[r0:r0 + n, :]
    t = k.I["xp"] if which_io == "in" else k.O["yp"]
    return t[r0 - cfg.Ts:r0 - cfg.Ts + n, :]


def phaseB(k, l):
    S, I, cfg = k.S, k.I, k.cfg
    m = S.mark()
    w_up = S.sb("w_up", [128, 8, 2 * DFF], BF16)
    w_dn = S.sb("w_dn", [128, 22, D], BF16)
    m2 = S.mark()
    stage = [S.sb(f"wstage{j}", [128, 2840]) for j in range(2)]
    k.wl_n = 0
    load_weight(k, w_up, I["w_up"][l], 8, 2 * DFF, stage)
    load_weight(k, w_dn, I["w_dn"][l], 22, D, stage)
    S.release(m2)
    S.dma("sp", k.lng[:, :], I["ln2_g"][l:l + 1, :].partition_broadcast(128), writes=[k.lng])
    S.dma("sp", k.lnb[:, :], I["ln2_b"][l:l + 1, :].partition_broadcast(128), writes=[k.lnb])
    alloc_hsm(k)
    rsm = alloc_rsm(k)
    SEG = 256
    h2T = [S.sb(f"h2T{j}", [128, 8, SEG + 2], BF16) for j in range(2)]
    actT = S.sb("actT", [128, 22, SEG], BF16)
    xt = [S.sb(f"xtB{j}", [128, D]) for j in range(4)]
    xh = k.ttmp[0]
    NBUF = 3
    cg = [S.sb(f"cg{j}", [128, SEG]) for j in range(NBUF)]
    cv = [S.sb(f"cv{j}", [128, SEG]) for j in range(NBUF)]
    th = [S.sb(f"th{j}", [128, SEG]) for j in range(NBUF)]
    src, dst = seq_src_dst(k, l, "B")
    segs = []
    for (sname, T, cond, off) in cfg.seqs:
        for t0 in range(0, T, SEG):
            segs.append((T, cond, off, t0))
    state = {}

    def prep(si):
        T, cond, off, t0 = segs[si]
        hT = h2T[si % 2]
        r0 = off + t0
        xts = []
        for j in range(SEG // 128):
            xb = xt[(2 * si + j) % 4]
            xts.append(xb)
            make_hT(k, l, 2, cond, [(rows_ap(k, src, "in", r0 + 128 * j, 128), 128)], xb,
                    [hT[:, kc, 1 + 128 * j:1 + 128 * (j + 1)] for kc in range(8)], hT, [])
        rows, cols = [], []
        if t0 > 0:
            rows.append((rows_ap(k, src, "in", r0 - 1, 1), 1))
            cols.append(0)
        else:
            MSET(S, "pool", [hT], hT[:, :, 0:1], 0.0)
        if t0 + SEG < T:
            rows.append((rows_ap(k, src, "in", r0 + SEG, 1), 1))
            cols.append(SEG + 1)
        else:
            MSET(S, "pool", [hT], hT[:, :, SEG + 1:SEG + 2], 0.0)
        if len(rows) == 2:
            make_hT(k, l, 2, cond, rows, xh, [hT[:, kc, 0:SEG + 2:SEG + 1] for kc in range(8)], hT, [])
        elif len(rows) == 1:
            c = cols[0]
            make_hT(k, l, 2, cond, rows, xh, [hT[:, kc, c:c + 1] for kc in range(8)], hT, [])
        state[si] = xts

    def ffn(si):
        T, cond, off, t0 = segs[si]
        hT = h2T[si % 2]
        r0 = off + t0
        xts = state.pop(si)
        for c in range(22):
            pg, pv = S.bank(), S.bank()
            MM(S, [w_up, hT], [pg], [(pg[:, 0:SEG + 2], w_up[:, kc, c * 128:(c + 1) * 128], hT[:, kc, :], kc == 0, kc == 7)
                                      for kc in range(8)])
            MM(S, [w_up, hT], [pv], [(pv[:, 0:SEG + 2], w_up[:, kc, DFF + c * 128:DFF + (c + 1) * 128], hT[:, kc, :], kc == 0, kc == 7)
                                      for kc in range(8)])
            g_, v_, t_ = cg[c % NBUF], cv[c % NBUF], th[c % NBUF]
            fw = k.fcw
            ACT(S, [pg, fw], [g_], g_[:, :], pg[:, 1:SEG + 1], AF.Identity, bias=fw[:, 3, c:c + 1], scale=fw[:, 1, c:c + 1])
            STT(S, "dve", [pg, fw, g_], [g_], g_[:, :], pg[:, 0:SEG], fw[:, 0, c:c + 1], g_[:, :], ALU.mult, ALU.add)
            STT(S, "dve", [pg, fw, g_], [g_], g_[:, :], pg[:, 2:SEG + 2], fw[:, 2, c:c + 1], g_[:, :], ALU.mult, ALU.add)
            cc = 22 + c
            ACT(S, [pv, fw], [v_], v_[:, :], pv[:, 1:SEG + 1], AF.Identity, bias=fw[:, 3, cc:cc + 1], scale=fw[:, 1, cc:cc + 1])
            STT(S, "dve", [pv, fw, v_], [v_], v_[:, :], pv[:, 0:SEG], fw[:, 0, cc:cc + 1], v_[:, :], ALU.mult, ALU.add)
            STT(S, "dve", [pv, fw, v_], [v_], v_[:, :], pv[:, 2:SEG + 2], fw[:, 2, cc:cc + 1], v_[:, :], ALU.mult, ALU.add)
            ACT(S, [g_], [t_], t_[:, :], g_[:, :], AF.Silu)
            TT(S, "pool", [t_, v_], [actT], actT[:, c, :], t_[:, :], v_[:, :], ALU.mult)
        for j in range(SEG // 128):
            p0, p1 = S.bank(), S.bank()
            for hlf, pb in enumerate((p0, p1)):
                MM(S, [actT, w_dn], [pb], [(pb[:, :], actT[:, c, 128 * j:128 * (j + 1)], w_dn[:, c, hlf * 512:(hlf + 1) * 512],
                                              c == 0, c == 21) for c in range(22)])
            ob = xts[j]
            resid_ln(k, xts[j], (p0, p1), ob, rsm)
            db = S.dbuf(("xout", l, (r0 + 128 * j) // 128))
            S.dma("pool", rows_ap(k, dst, "out", r0 + 128 * j, 128), ob[:, :], reads=[ob], writes=[db])
            if dst is None:
                k.final_bufs.append(db)

    cur_cond = None
    prep(0)
    for si in range(len(segs)):
        if si + 1 < len(segs):
            prep(si + 1)
        if segs[si][1] != cur_cond:
            cur_cond = segs[si][1]
            gate_table(k, l, 5, cur_cond)
        ffn(si)
    S.release(m)


def layer(k, l):
    cfg = k.cfg
    load_layer_consts(k, l)
    mode = getattr(cfg, "mode", "full")
    if mode in ("full", "A"):
        phaseA(k, l)
    if mode in ("full", "B"):
        phaseB(k, l)


def shard_inputs(inp, cfg, core):
    f = lambda a: np.ascontiguousarray(np.asarray(a), dtype=np.float32)
    NP = cfg.NP
    cst, _, cst2, _ = make_consts()
    m = {
        "xs": f(inp["x_sample"][core]),
        "xp": f(inp["x_prompt"][NP * core:NP * (core + 1)]).reshape(NP * cfg.Tp, D),
        "st_c": f(inp["state_mlstm_c"][core]), "st_n": f(inp["state_mlstm_n"][core]),
        "st_m": f(inp["state_mlstm_m"][core]).reshape(2, 8), "st_s": f(inp["state_ssd"][core]),
        "cond": f(np.stack([np.asarray(inp["c"])[core], np.asarray(inp["c_ctx"])], 0)),
        "w_ada": f(inp["w_ada"]), "b_ada": f(inp["b_ada"]), "w_in": f(inp["w_in"]),
        "b_ig": f(inp["b_igate"]).reshape(2, 8), "b_fg": f(inp["b_fgate"]).reshape(2, 8),
        "mnorm_g": f(inp["mlstm_norm_g"]), "w_pool": f(inp["w_pool"]), "pool_scale": f(inp["pool_scale"]),
        "w_sp": f(inp["w_spatial"]), "b_sp": f(inp["b_spatial"]), "sconv_w": f(inp["ssd_conv_w"]),
        "sconv_b": f(inp["ssd_conv_b"]), "dt_bias": f(inp["ssd_dt_bias"]).reshape(2, 8),
        "a_log": f(inp["ssd_a_log"]).reshape(2, 8), "ssd_d": f(inp["ssd_d"]), "snorm_g": f(inp["ssd_norm_g"]),
        "w_out": f(inp["w_out"]), "ln1_g": f(inp["ln1_g"]), "ln1_b": f(inp["ln1_b"]), "w_up": f(inp["ffn_w_up"]),
        "fconv_w": f(inp["ffn_conv_w"]), "fconv_b": f(inp["ffn_conv_b"]), "w_dn": f(inp["ffn_w_down"]),
        "ln2_g": f(inp["ln2_g"]), "ln2_b": f(inp["ln2_b"]), "cst": cst, "cst2": cst2,
    }
    return m


def phaseA(k, l):
    S, I, cfg = k.S, k.I, k.cfg
    m = S.mark()
    w_in = S.sb("w_in", [128, 8, DIN], BF16)
    w_out = S.sb("w_out", [128, 8, D], BF16)
    m2 = S.mark()
    stage = [S.sb(f"wstageA{j}", [128, 2840]) for j in range(2)]
    k.wl_n = 0
    load_weight(k, w_in, I["w_in"][l], 8, DIN, stage)
    load_weight(k, w_out, I["w_out"][l], 8, D, stage)
    S.release(m2)
    c2b = load_cst2(k)
    S.dma("sp", k.lng[:, :], I["ln1_g"][l:l + 1, :].partition_broadcast(128), writes=[k.lng])
    S.dma("sp", k.lnb[:, :], I["ln1_b"][l:l + 1, :].partition_broadcast(128), writes=[k.lnb])
    alloc_hsm(k)
    rsm = alloc_rsm(k)
    a = K()
    a.w_in, a.w_out, a.c2b, a.rsm, a.l = w_in, w_out, c2b, rsm, l
    a.hb = [S.sb(f"hb{j}", [128, 8, 130], BF16) for j in range(3)]
    a.xq = [S.sb(f"xq{j}", [128, D]) for j in range(2)]
    a.pet = [S.sb(f"petile{j}", [128, D]) for j in range(2)] if l == 0 else None
    if l == 0:
        edb = S.dbuf("ED")
        for j in range(2):
            S.dma("sp", a.pet[j][0:64, 512:1024], k.ED[:, :], reads=[edb], writes=[a.pet[j]])
            S.dma("sp", a.pet[j][64:128, 512:1024], k.ED[:, :], reads=[edb], writes=[a.pet[j]])
    sb = S.sb
    a.Cn, a.Cnb = sb("Cn", [128, 2, 65]), sb("Cnb", [128, 2, 66], BF16)
    a.Hs, a.Hsb = sb("Hs", [128, 4, 64]), sb("Hsb", [128, 4, 64], BF16)
    a.p = []
    for par in range(2):
        q = K()
        a.p.append(q)
        q.qkT = sb(f"qkT{par}", [128, 4, 128], BF16)
        q.k_tm = sb(f"k_tm{par}", [128, 256], BF16)
        q.v_sb = sb(f"v_sb{par}", [128, 256], BF16)
        q.XBCb = sb(f"XBCb{par}", [128, 6, 128], BF16)
        q.x_tm, q.B_tm = sb(f"x_tm{par}", [128, 256], BF16), sb(f"B_tm{par}", [128, 2, 128], BF16)
        q.stash_bufs = [q.qkT, q.k_tm, q.v_sb, q.XBCb, q.x_tm, q.B_tm]
        e0 = q.qkT.off // 2
        q.stash_ap = S.arena.bitcast(BF16)[0:128, e0:e0 + 2304]
        assert q.B_tm.off + 512 == q.qkT.off + 4608, "stash group must be contiguous"
        q.og = sb(f"og{par}", [128, 280])
        q.G8, q.E8, q.SP8, q.igb = sb(f"G8{par}", [128, 8]), sb(f"E8{par}", [128, 8]), sb(f"SP8{par}", [128, 8]), sb(f"igb{par}", [128, 4])
        q.r8, q.logdec, q.cum, q.e8 = sb(f"r8{par}", [128, 8]), sb(f"logdec{par}", [128, 8]), sb(f"cum{par}", [128, 8]), sb(f"e8{par}", [128, 8])
        q.wend, q.aL, q.tmp8 = sb(f"wend{par}", [128, 8]), sb(f"aL{par}", [128, 8]), sb(f"tmp8{par}", [128, 8])
        q.L1 = sb(f"L1{par}", [128, 8, 128])
        q.DIFF = sb(f"DIFF{par}", [128, 8, 128])
        q.PTm = sb(f"PTm{par}", [128, 4, 128], BF16)
        q.PTs = sb(f"PTs{par}", [128, 4, 128], BF16)
        q.xt_m, q.xh_m = sb(f"xt_m{par}", [128, 4, 66], BF16), sb(f"xh_m{par}", [128, 4, 66], BF16)
        q.XBC, q.XBCe = sb(f"XBC{par}", [128, 6, 128]), sb(f"XBCe{par}", [128, 6, 128])
        q.xt_s, q.xh_s = sb(f"xt_s{par}", [128, 4, 64], BF16), sb(f"xh_s{par}", [128, 4, 64], BF16)
        q.NUM = sb(f"NUM{par}", [128, 4, 65])
        q.den = sb(f"den{par}", [128, 4])
        q.Ysc = sb(f"Ysc{par}", [128, 4, 64])
    a.HY = [sb(f"HY{j}", [128, 512]) for j in range(2)]
    a.HYf = [sb(f"HYf{j}", [128, 512]) for j in range(2)]
    a.eo, a.z_sb, a.ez, a.gu = sb("eo", [128, 256]), sb("z_sb", [128, 256]), sb("ez", [128, 256]), sb("gu", [128, 256])
    a.gvb = sb("gvb", [128, 256], BF16)
    a.pc, a.pcP, a.pcN = sb("pc", [128, 256]), sb("pcP", [8, 256]), sb("pcN", [8, 256])
    a.plb, a.plT = sb("plb", [128, 256], BF16), sb("plT", [128, 2, 128], BF16)
    a.fin1, a.fin2, a.fin3 = sb("fin1", [128, 256]), sb("fin2", [128, 256]), sb("fin3", [128, 256])
    a.st4, a.st4b = sb("st4", [128, 4]), sb("st4b", [128, 4])
    a.yall = sb("yall", [128, 3, 256], BF16)
    a.concatT = sb("concatT", [128, 8, 128], BF16)
    a.gst, a.gmv, a.gve, a.grs = sb("gst", [128, 6]), sb("gmv", [128, 2]), sb("gve", [128, 1]), sb("grs", [128, 1])
    a.mrun = sb("mrun", [4, 1])
    a.mt = sb("mt", [4, 2])
    for par in range(2):
        a.p[par].dec = sb(f"dec{par}", [128, 8])
    a.sio = sb("sio", [128, 4, 128])
    src, dst = seq_src_dst(k, l, "A")
    a.src, a.dst = src, dst
    cur_cond = None
    for si, (sname, T, cond, off) in enumerate(cfg.seqs):
        if cond != cur_cond:
            gate_table(k, l, 2, cond)
            cur_cond = cond
        runseq(k, a, si, T, cond, off)
    S.release(m)


def runseq(k, a, si, T, cond, off):
    S, cfg, l = k.S, k.cfg, a.l
    nt = T // 128
    is_sample = (si == 0)
    tile0 = off // 128
    w_in = a.w_in

    def hbuf(i):
        return a.hb[i % 3]

    def fix_halo(lo, hi):
        CP(S, "pool", [hbuf(hi)], [hbuf(lo)], hbuf(lo)[:, :, 129:130], hbuf(hi)[:, :, 1:2])
        CP(S, "pool", [hbuf(lo)], [hbuf(hi)], hbuf(hi)[:, :, 0:1], hbuf(lo)[:, :, 128:129])

    def ensure1(i):
        hb = hbuf(i)
        xb = a.xq[i % 2]
        pos = None
        if l == 0 and is_sample:
            pos = a.pet[i % 2]
            edb = S.dbuf("ED")
            S.dma("sp", pos[0:64, 0:512], k.ED[2 * i:2 * i + 1, :].partition_broadcast(64), reads=[edb], writes=[pos])
            S.dma("sp", pos[64:128, 0:512], k.ED[2 * i + 1:2 * i + 2, :].partition_broadcast(64), reads=[edb], writes=[pos])
        make_hT(k, l, 1, cond, [(rows_ap(k, a.src, "in", off + 128 * i, 128), 128)], xb,
                [hb[:, kc, 1:129] for kc in range(8)], hb, [], pos_tile=pos)
        db = S.dbuf(("HT", tile0 + i))
        S.dma("pool", k.HT[tile0 + i].rearrange("p (kc t) -> p kc t", kc=8), hb[:, :, 1:129], reads=[hb], writes=[db])
        if i == 0:
            MSET(S, "pool", [hb], hb[:, :, 0:1], 0.0)
        else:
            fix_halo(i - 1, i)
        if i == nt - 1:
            MSET(S, "pool", [hb], hb[:, :, 129:130], 0.0)

    def ensure2(i):
        hb = hbuf(i)
        db = S.dbuf(("HT", tile0 + i))
        S.dma("sp", hb[:, :, 1:129], k.HT[tile0 + i].rearrange("p (kc t) -> p kc t", kc=8), reads=[db], writes=[hb])
        xb = a.xq[i % 2]
        S.dma("sp", xb[:, :], rows_ap(k, a.src, "in", off + 128 * i, 128), writes=[xb])
        if l == 0 and is_sample:
            pos = a.pet[i % 2]
            edb = S.dbuf("ED")
            S.dma("sp", pos[0:64, 0:512], k.ED[2 * i:2 * i + 1, :].partition_broadcast(64), reads=[edb], writes=[pos])
            S.dma("sp", pos[64:128, 0:512], k.ED[2 * i + 1:2 * i + 2, :].partition_broadcast(64), reads=[edb], writes=[pos])
            TT(S, "pool", [xb, pos], [xb], xb[:, :], xb[:, :], pos[:, :], ALU.add)
        hf = a.HYf[i % 2]
        S.dma("sp", hf[:, :], k.HF[off + 128 * i:off + 128 * (i + 1), :], reads=[S.dbuf(("HF", tile0 + i))], writes=[hf])
        q = a.p[i % 2]
        S.dma("sp", q.stash_ap, k.STB[tile0 + i], reads=[S.dbuf(("STB", tile0 + i))], writes=q.stash_bufs)
        S.dma("sp", q.og[:, :], k.STG[tile0 + i], reads=[S.dbuf(("STG", tile0 + i))], writes=[q.og])
        if i == nt - 1:
            MSET(S, "pool", [hb], hb[:, :, 129:130], 0.0)
        else:
            fix_halo(i, i + 1)
        if i == 0:
            MSET(S, "pool", [hb], hb[:, :, 0:1], 0.0)

    for d in ((0,) if getattr(cfg, "stop", 99) <= 4 else (0, 1)):
        init_state(k, a, si, d, is_sample)
        order = list(range(nt)) if d == 0 else list(range(nt - 1, -1, -1))
        ens = ensure1 if d == 0 else ensure2
        ens(order[0])
        for n, i in enumerate(order):
            if n + 1 < len(order):
                ens(order[n + 1])
            tileA(k, a, si, T, cond, off, i, d, nt, is_sample)
        if not is_sample and getattr(cfg, "stop", 99) > 5:
            final_state(k, a, si, d)


def init_state(k, a, si, d, is_sample):
    S, I, l = k.S, k.I, a.l
    if not is_sample:
        MSET(S, "pool", [a.Cn], a.Cn[:, :, :], 0.0)
        MSET(S, "pool", [a.Cnb], a.Cnb[:, :, :], 0.0)
        MSET(S, "pool", [a.Hs], a.Hs[:, :, :], 0.0)
        MSET(S, "pool", [a.Hsb], a.Hsb[:, :, :], 0.0)
        MSET(S, "pool", [a.mrun], a.mrun[:, :], 0.0)
        return
    for h in range(4):
        pr = slice((h % 2) * 64, (h % 2) * 64 + 64)
        S.dma("sp", a.Cn[pr, h // 2, 0:64], I["st_c"][l, d, h], writes=[a.Cn])
        S.dma("sp", a.Cn[pr, h // 2, 64:65], I["st_n"][l, d, h].rearrange("(p o) -> p o", o=1), writes=[a.Cn])
    S.dma("sp", a.st4[:, :], I["st_m"][l:l + 1, 4 * d:4 * d + 4].partition_broadcast(128), writes=[a.st4])
    ACT(S, [a.st4], [a.st4b], a.st4b[:, :], a.st4[:, :], AF.Exp)
    for h in range(4):
        pr = slice((h % 2) * 64, (h % 2) * 64 + 64)
        TS(S, "dve", [a.Cn, a.st4b], [a.Cn], a.Cn[pr, h // 2, :], a.Cn[pr, h // 2, :], a.st4b[pr, h:h + 1], None, ALU.mult)
    CP(S, "pool", [a.Cn], [a.Cnb], a.Cnb[:, :, 0:65], a.Cn[:, :, :])
    S.dma("sp", a.sio[0:64, :, :], I["st_s"][l, d].rearrange("h p n -> p h n"), writes=[a.sio])
    pb = S.bank()
    TR(S, [a.sio, k.cstb], [pb], [(pb[:, h * 64:(h + 1) * 64], a.sio[0:64, h, :], cview(k, "ident")[0:64, 0:64]) for h in range(4)])
    CP(S, "dve", [pb], [a.Hs], a.Hs[:, :, :], pb[:, 0:256].rearrange("p (h q) -> p h q", h=4))
    CP(S, "act", [pb], [a.Hsb], a.Hsb[:, :, :], pb[:, 0:256].rearrange("p (h q) -> p h q", h=4))


def final_state(k, a, si, d):
    S, O, l = k.S, k.O, a.l
    j = si - 1
    dg = a.p[0].tmp8
    TS(S, "dve", [k.cstb, a.mrun], [dg], dg[0:4, 0:4], cview(k, "ident")[0:4, 0:4], a.mrun[0:4, 0:1], None, ALU.mult)
    pb = S.bank()
    MM(S, [dg, k.cstb], [pb], [(pb[:, 0:4], cview(k, "ones")[0:4, :], dg[0:4, 0:4], True, True)])
    ACT(S, [pb], [a.st4b], a.st4b[:, :], pb[:, 0:4], AF.Exp, scale=-1.0)
    stg = a.sio
    sv = stg[:, 0:2, 0:65]
    for h in range(4):
        pr = slice((h % 2) * 64, (h % 2) * 64 + 64)
        TS(S, "dve", [a.Cn, a.st4b], [stg], stg[pr, h // 2, 0:65], a.Cn[pr, h // 2, :], a.st4b[pr, h:h + 1], None, ALU.mult)
    outs = []
    for h in range(4):
        pr = slice((h % 2) * 64, (h % 2) * 64 + 64)
        db = S.dbuf(("oc", j, l, d, h))
        S.dma("pool", O["oc"][j, l, d, h], stg[pr, h // 2, 0:64], reads=[stg], writes=[db])
        db2 = S.dbuf(("on", j, l, d, h))
        S.dma("pool", O["on"][j, l, d, h].rearrange("(p o) -> p o", o=1), stg[pr, h // 2, 64:65], reads=[stg], writes=[db2])
        outs += [db, db2]
    db = S.dbuf(("om", j, l, d))
    S.dma("pool", O["om"][j, l, 4 * d:4 * d + 4].rearrange("(p o) -> p o", o=1), a.mrun[0:4, 0:1], reads=[a.mrun], writes=[db])
    outs.append(db)
    pb2 = S.bank()
    TR(S, [a.Hs, k.cstb], [pb2], [(pb2[0:64, h * 128:(h + 1) * 128], a.Hs[:, h, :], cview(k, "ident")) for h in range(4)])
    CP(S, "dve", [pb2, stg], [stg], stg[0:64, :, :], pb2[0:64, :].rearrange("p (h n) -> p h n", h=4))
    db = S.dbuf(("os", j, l, d))
    S.dma("pool", O["os"][j, l, d].rearrange("h p n -> p h n"), stg[0:64, :, :], reads=[stg], writes=[db])
    outs.append(db)
    k.final_bufs += outs


def tileA(k, a, si, T, cond, off, i, d, nt, is_sample):
    S, l = k.S, a.l
    PS = a.p[i % 2]
    w_in = a.w_in
    hb = a.hb[i % 3]
    hcur = lambda kc: hb[:, kc, 1:129]
    tri = cview(k, "tri%d" % d)
    neg = cview(k, "neg%d" % d)
    endc = 127 if d == 0 else 0
    full = (d == 1)
    cst = k.cstb

    og = PS.og
    ps1 = ps2 = None
    if not full:
        ps1, ps2 = S.bank(), S.bank()
        MM(S, [hb, w_in], [ps1], [(ps1[:, 0:512], hcur(kc), w_in[:, kc, 256:768], kc == 0, kc == 7) for kc in range(8)])
        MM(S, [hb, w_in], [ps2], [(ps2[:, 0:272], hcur(kc), w_in[:, kc, 768:1040], kc == 0, kc == 7) for kc in range(8)]
           + [(ps2[:, 272:280], hcur(kc), w_in[:, kc, 2832:2840], kc == 0, kc == 7) for kc in range(8)])
        CP(S, "act", [ps2], [og], og[:, :], ps2[:, 0:280])
        ACT(S, [ps1], [PS.k_tm], PS.k_tm[:, :], ps1[:, 0:256], AF.Identity, scale=0.125)
        CP(S, "act", [ps1], [PS.v_sb], PS.v_sb[:, :], ps1[:, 256:512])
    G8, E8, SP8, igb, r8, logdec, cum, e8, wend, aL, tmp8 = (PS.G8, PS.E8, PS.SP8, PS.igb, PS.r8, PS.logdec, PS.cum, PS.e8,
                                                             PS.wend, PS.aL, PS.tmp8)
    STT(S, "dve", [og, k.bif], [G8], G8[:, 0:4], og[:, 264 + 4 * d:268 + 4 * d], -1.0, k.bif[:, 8 + 4 * d:12 + 4 * d], ALU.mult, ALU.add)
    TT(S, "dve", [og, k.dtb], [G8], G8[:, 4:8], og[:, 272 + 4 * d:276 + 4 * d], k.dtb[:, 4 * d:4 * d + 4], ALU.add)
    TT(S, "dve", [og, k.bif], [igb], igb[:, :], og[:, 256 + 4 * d:260 + 4 * d], k.bif[:, 4 * d:4 * d + 4], ALU.add)
    if full:
        ACT(S, [og], [a.eo], a.eo[:, :], og[:, 0:256], AF.Exp, scale=-1.0)
    ACT(S, [G8], [E8], E8[:, :], G8[:, :], AF.Exp)
    ACT(S, [E8], [SP8], SP8[:, :], E8[:, :], AF.Ln, bias=1.0)
    ACT(S, [igb], [r8], r8[:, 0:4], igb[:, :], AF.Exp)
    CP(S, "pool", [SP8], [r8], r8[:, 4:8], SP8[:, 4:8])
    TT(S, "dve", [SP8, k.coef], [logdec], logdec[:, :], SP8[:, :], k.coef[:, d, :], ALU.mult)
    CP(S, "act", [logdec], [PS.L1], PS.L1[:, :, :], bc(logdec[:, 0:8].unsqueeze(2), [128, 8, 128]))
    psL = [S.bank(), S.bank()]
    for hh in range(2):
        MM(S, [PS.L1, cst], [psL[hh]], [(psL[hh][:, q * 128:(q + 1) * 128], PS.L1[:, hh * 4 + q, :], tri, True, True) for q in range(4)])
    psC = S.bank()
    MM(S, [logdec, cst], [psC], [(psC[:, 0:8], tri, logdec[:, 0:8], True, True)])
    CP(S, "dve", [psC], [cum], cum[:, :], psC[:, 0:8])
    for h in range(8):
        pl = psL[h // 4]
        q = h % 4
        STT(S, "dve", [pl, cum, cst], [PS.DIFF], PS.DIFF[:, h, :], pl[:, q * 128:(q + 1) * 128], cum[:, h:h + 1], neg, ALU.subtract, ALU.add)
    ACT(S, [PS.DIFF], [PS.DIFF], PS.DIFF[:, :, :], PS.DIFF[:, :, :], AF.Exp)
    ACT(S, [cum], [e8], e8[:, :], cum[:, :], AF.Exp)
    for hh in range(2):
        TT(S, "dve", [psL[hh], cum], [tmp8], tmp8[:, hh * 4:hh * 4 + 4], psL[hh][:, endc:512:128], cum[:, hh * 4:hh * 4 + 4], ALU.subtract)
        ACT(S, [psL[hh]], [aL], aL[:, hh * 4:hh * 4 + 4], psL[hh][:, endc:512:128], AF.Exp)
    ACT(S, [tmp8], [wend], wend[:, :], tmp8[:, :], AF.Exp)
    TT(S, "dve", [wend, r8], [wend], wend[:, :], wend[:, :], r8[:, :], ALU.mult)
    if not is_sample:
        TT(S, "dve", [tmp8, igb], [PS.dec], PS.dec[:, 0:4], tmp8[:, 0:4], igb[:, :], ALU.add)
        TT(S, "dve", [tmp8, cum], [PS.dec], PS.dec[:, 4:8], tmp8[:, 0:4], cum[:, 0:4], ALU.add)
        pm = S.bank()
        TR(S, [PS.dec, cst], [pm], [(pm[0:4, 0:128], PS.dec[:, 0:4], cview(k, "ident")),
                                   (pm[0:4, 128:256], PS.dec[:, 4:8], cview(k, "ident"))])
        S.op("dve", lambda e: e.tensor_reduce(a.mt[0:4, 0:1], pm[0:4, 0:128], AX.X, ALU.max), [pm], [a.mt])
        TT(S, "dve", [pm, a.mrun], [a.mt], a.mt[0:4, 1:2], pm[0:4, 128:129], a.mrun[0:4, 0:1], ALU.add)
        TT(S, "dve", [a.mt], [a.mrun], a.mrun[0:4, 0:1], a.mt[0:4, 0:1], a.mt[0:4, 1:2], ALU.max)

    if getattr(k.cfg, "stop", 99) <= 1:
        return
    qkT = PS.qkT
    if not full:
        psQ = [S.bank(), S.bank()]
        for hh in range(2):
            MM(S, [hb, w_in], [psQ[hh]], [(psQ[hh][:, q * 130:(q + 1) * 130], w_in[:, kc, (hh * 2 + q) * 128:(hh * 2 + q + 1) * 128],
                                            hb[:, kc, 0:130], kc == 0, kc == 7) for q in range(2) for kc in range(8)])
        CP(S, "act", [psQ[0]], [qkT], qkT[:, 0:2, :], psQ[0][:, 0:260].rearrange("p (b t) -> p b t", b=2)[:, :, 1:129])
        ACT(S, [psQ[1]], [qkT], qkT[:, 2:4, :], psQ[1][:, 0:260].rearrange("p (b t) -> p b t", b=2)[:, :, 1:129], AF.Identity, scale=0.125)
    v4 = PS.v_sb[:, :].rearrange("p (h e) -> p h e", h=4)
    TT(S, "dve", [PS.v_sb, r8], [PS.xt_m], PS.xt_m[:, :, 0:64], v4, bc(r8[:, 0:4].unsqueeze(2), [128, 4, 64]), ALU.mult)
    if getattr(k.cfg, "stop", 99) <= 1.12:
        return
    CP(S, "pool", [r8], [PS.xt_m], PS.xt_m[:, :, 64:65], r8[:, 0:4].unsqueeze(2))
    if getattr(k.cfg, "stop", 99) <= 1.15:
        return
    EXP = getattr(k.cfg, "exp", "")
    if EXP != "noTT":
        TT(S, "dve", [PS.v_sb, wend], [PS.xh_m], PS.xh_m[:, :, 0:64], v4, bc((r8 if EXP == "r8" else wend)[:, 0:4].unsqueeze(2), [128, 4, 64]), ALU.mult)
    if EXP != "noCP":
        CP(S, "pool", [wend], [PS.xh_m], PS.xh_m[:, :, 64:65], wend[:, 0:4].unsqueeze(2))
    if getattr(k.cfg, "stop", 99) <= 1.2:
        return
    psS = [S.bank(), S.bank()]
    hp = lambda h: slice((h % 2) * 64, (h % 2) * 64 + 64)
    for par in range(2):
        MM(S, [qkT], [psS[par]], [(psS[par][:, (h // 2) * 128:(h // 2 + 1) * 128], qkT[hp(h), 2 + h // 2, :], qkT[hp(h), h // 2, :], True, True)
                                  for h in (par, par + 2)])
    for par in range(2):
        TT(S, "dve", [psS[par], PS.DIFF], [PS.PTm], PS.PTm[:, par:4:2, :], psS[par][:, 0:256].rearrange("p (h t) -> p h t", h=2),
           PS.DIFF[:, par:4:2, :], ALU.mult)
    if getattr(k.cfg, "stop", 99) <= 1.4:
        return
    psO = S.bank()
    psI = [S.bank(), S.bank()]
    MM(S, [PS.PTm, PS.xt_m], [psO], [(psO[:, h * 65:h * 65 + 65], PS.PTm[:, h, :], PS.xt_m[:, h, 0:65], True, True) for h in range(4)])
    for par in range(2):
        MM(S, [qkT, a.Cnb], [psI[par]], [(psI[par][:, (h // 2) * 65:(h // 2) * 65 + 65], qkT[hp(h), h // 2, :], a.Cnb[hp(h), h // 2, 0:65], True, True)
                                         for h in (par, par + 2)])
    NUM = PS.NUM
    for par in range(2):
        TT(S, "dve", [psI[par], e8], [NUM], NUM[:, par:4:2, :], psI[par][:, 0:130].rearrange("p (h e) -> p h e", h=2),
           bc(e8[:, par:4:2].unsqueeze(2), [128, 2, 65]), ALU.mult)
    TT(S, "dve", [psO, NUM], [NUM], NUM[:, :, :], psO[:, 0:260].rearrange("p (h e) -> p h e", h=4), NUM[:, :, :], ALU.add)
    if getattr(k.cfg, "stop", 99) <= 1.6:
        return
    HY = a.HY[i % 2]
    ACT(S, [NUM], [PS.den], PS.den[:, :].unsqueeze(2), NUM[:, :, 64:65], AF.Abs)
    TS(S, "dve", [PS.den], [PS.den], PS.den[:, :], PS.den[:, :], 1.0, None, ALU.max)
    TT(S, "pool", [PS.den, k.cm1], [PS.den], PS.den[:, :], PS.den[:, :], bc(k.cm1[:, 0:1], [128, 4]), ALU.pow)
    TT(S, "pool", [NUM, PS.den], [HY], HY[:, 0:256].rearrange("p (h e) -> p h e", h=4), NUM[:, :, 0:64],
       bc(PS.den[:, :].unsqueeze(2), [128, 4, 64]), ALU.mult)
    if getattr(k.cfg, "stop", 99) <= 1.8:
        return
    psU = S.bank()
    MM(S, [PS.k_tm, PS.xh_m], [psU], [(psU[:, h * 65:h * 65 + 65], PS.k_tm[:, (h // 2) * 128:(h // 2 + 1) * 128], PS.xh_m[:, h, 0:65], True, True)
                                     for h in range(4)])
    for h in range(4):
        STT(S, "dve", [a.Cn, aL, psU], [a.Cn], a.Cn[hp(h), h // 2, :], a.Cn[hp(h), h // 2, :], aL[hp(h), h:h + 1],
            psU[hp(h), h * 65:h * 65 + 65], ALU.mult, ALU.add)
    CP(S, "act", [a.Cn], [a.Cnb], a.Cnb[:, :, 0:65], a.Cn[:, :, :])

    if getattr(k.cfg, "stop", 99) <= 2:
        return
    XBC, XBCe, XBCb = PS.XBC, PS.XBCe, PS.XBCb
    if not full:
        psX = [S.bank(), S.bank()]
        for hh in range(2):
            MM(S, [hb, w_in], [psX[hh]], [(psX[hh][:, q * 130:q * 130 + 130], w_in[:, kc, 2064 + (hh * 3 + q) * 128:2064 + (hh * 3 + q + 1) * 128],
                                            hb[:, kc, 0:130], kc == 0, kc == 7) for q in range(3) for kc in range(8)])
        for b in range(6):
            pb, c0 = psX[b // 3], (b % 3) * 130
            ACT(S, [pb, k.scw], [XBC], XBC[:, b, :], pb[:, c0 + 1:c0 + 129], AF.Identity, bias=k.scw[:, 3, b:b + 1], scale=k.scw[:, 1, b:b + 1])
            STT(S, "dve", [pb, k.scw, XBC], [XBC], XBC[:, b, :], pb[:, c0:c0 + 128], k.scw[:, 0, b:b + 1], XBC[:, b, :], ALU.mult, ALU.add)
            STT(S, "dve", [pb, k.scw, XBC], [XBC], XBC[:, b, :], pb[:, c0 + 2:c0 + 130], k.scw[:, 2, b:b + 1], XBC[:, b, :], ALU.mult, ALU.add)
        ACT(S, [XBC], [XBCe], XBCe[:, :, :], XBC[:, :, :], AF.Exp, scale=-1.0)
        ACT(S, [XBCe], [XBCe], XBCe[:, :, :], XBCe[:, :, :], AF.Ln, bias=1.0)
        ACT(S, [XBCe], [XBCe], XBCe[:, :, :], XBCe[:, :, :], AF.Exp, scale=-1.0)
        TT(S, "pool", [XBC, XBCe], [XBCb], XBCb[:, :, :], XBC[:, :, :], XBCe[:, :, :], ALU.mult)
        psT = S.bank()
        pTv = bview(psT, BF16)
        TR(S, [XBCb, k.identb], [psT], [(pTv[:, b * 128:(b + 1) * 128], XBCb[:, b, :], k.identb[:, :]) for b in range(4)])
        CP(S, "act", [psT], [PS.x_tm], PS.x_tm[:, :], pTv[:, 0:256])
        CP(S, "act", [psT], [PS.B_tm], PS.B_tm[:, :, :], pTv[:, 256:512].rearrange("p (g n) -> p g n", g=2))
    x4 = PS.x_tm[:, :].rearrange("p (h e) -> p h e", h=4)
    TT(S, "pool", [PS.x_tm, r8], [PS.xt_s], PS.xt_s[:, :, :], x4, bc(r8[:, 4:8].unsqueeze(2), [128, 4, 64]), ALU.mult)
    TT(S, "pool", [PS.x_tm, wend], [PS.xh_s], PS.xh_s[:, :, :], x4, bc(wend[:, 4:8].unsqueeze(2), [128, 4, 64]), ALU.mult)
    psS2 = S.bank()
    MM(S, [XBCb], [psS2], [(psS2[:, g * 128:(g + 1) * 128], XBCb[:, 2 + g, :], XBCb[:, 4 + g, :], True, True) for g in range(2)])
    for g in range(2):
        TT(S, "dve", [psS2, PS.DIFF], [PS.PTs], PS.PTs[:, 2 * g:2 * g + 2, :],
           bc(psS2[:, g * 128:(g + 1) * 128].unsqueeze(1), [128, 2, 128]), PS.DIFF[:, 4 + 2 * g:6 + 2 * g, :], ALU.mult)
    psY = S.bank()
    MM(S, [PS.PTs, PS.xt_s, XBCb, a.Hsb], [psY],
       [(psY[:, h * 64:(h + 1) * 64], PS.PTs[:, h, :], PS.xt_s[:, h, :], True, True) for h in range(4)]
       + [(psY[:, 256 + g * 128:256 + (g + 1) * 128], XBCb[:, 4 + g, :], a.Hsb[:, 2 * g:2 * g + 2, :].rearrange("p h e -> p (h e)"), True, True)
          for g in range(2)])
    Ysc = PS.Ysc
    TT(S, "dve", [psY, e8], [Ysc], Ysc[:, :, :], psY[:, 256:512].rearrange("p (h e) -> p h e", h=4),
       bc(e8[:, 4:8].unsqueeze(2), [128, 4, 64]), ALU.mult)
    TT(S, "dve", [psY, Ysc], [HY], HY[:, 256:512], psY[:, 0:256], Ysc[:, :, :].rearrange("p h e -> p (h e)"), ALU.add)
    psU2 = S.bank()
    MM(S, [PS.B_tm, PS.xh_s], [psU2], [(psU2[:, g * 128:(g + 1) * 128], PS.B_tm[:, g, :], PS.xh_s[:, 2 * g:2 * g + 2, :].rearrange("p h e -> p (h e)"),
                                       True, True) for g in range(2)])
    TT(S, "dve", [a.Hs, aL], [a.Hs], a.Hs[:, :, :], a.Hs[:, :, :], bc(aL[:, 4:8].unsqueeze(2), [128, 4, 64]), ALU.mult)
    TT(S, "dve", [a.Hs, psU2], [a.Hs], a.Hs[:, :, :], psU2[:, 0:256].rearrange("p (h e) -> p h e", h=4), a.Hs[:, :, :], ALU.add)
    CP(S, "act", [a.Hs], [a.Hsb], a.Hsb[:, :, :], a.Hs[:, :, :])

    if getattr(k.cfg, "stop", 99) <= 3:
        return
    tile_g = (off // 128) + i
    if not full:
        S.dma("pool", k.STB[tile_g], PS.stash_ap, reads=PS.stash_bufs, writes=[S.dbuf(("STB", tile_g))])
        S.dma("pool", k.STG[tile_g], og[:, :], reads=[og], writes=[S.dbuf(("STG", tile_g))])
        db = S.dbuf(("HF", tile_g))
        S.dma("pool", k.HF[off + 128 * i:off + 128 * (i + 1), :], HY[:, :], reads=[HY], writes=[db])
        return
    finalizeA(k, a, si, T, cond, off, i, nt, ps1, ps2, HY, PS)


def finalizeA(k, a, si, T, cond, off, i, nt, ps1, ps2, HY, pset):
    S, l = k.S, a.l
    w_in, w_out = a.w_in, a.w_out
    hb = a.hb[i % 3]
    hcur = lambda kc: hb[:, kc, 1:129]
    cst = k.cstb
    HYf = a.HYf[i % 2]
    f1, f2, f3 = a.fin1, a.fin2, a.fin3
    v4 = lambda ap: ap.rearrange("p (h e) -> p h e", h=4)
    ym, yg, ys = a.yall[:, 0, :], a.yall[:, 1, :], a.yall[:, 2, :]

    ps3, ps4 = S.bank(), S.bank()
    MM(S, [hb, w_in], [ps3], [(ps3[:, 0:512], hcur(kc), w_in[:, kc, 1296:1808], kc == 0, kc == 7) for kc in range(8)])
    MM(S, [hb, w_in], [ps4], [(ps4[:, 0:256], hcur(kc), w_in[:, kc, 1808:2064], kc == 0, kc == 7) for kc in range(8)]
       + [(ps4[:, 256:512], hcur(kc), w_in[:, kc, 1040:1296], kc == 0, kc == 7) for kc in range(8)])
    has_p, has_n = i > 0, i < nt - 1
    ps5 = S.bank()
    mm5 = []
    if has_p:
        hp_ = a.hb[(i - 1) % 3]
        mm5 += [(ps5[0:8, 0:256], hp_[:, kc, 121:129], w_in[:, kc, 1040:1296], kc == 0, kc == 7) for kc in range(8)]
    if has_n:
        hn_ = a.hb[(i + 1) % 3]
        mm5 += [(ps5[0:8, 256:512], hn_[:, kc, 1:9], w_in[:, kc, 1040:1296], kc == 0, kc == 7) for kc in range(8)]
    if mm5:
        rd = [w_in] + ([a.hb[(i - 1) % 3]] if has_p else []) + ([a.hb[(i + 1) % 3]] if has_n else [])
        MM(S, rd, [ps5], mm5)

    TT(S, "pool", [HY, HYf], [f1], f1[:, :], HY[:, 0:256], HYf[:, 0:256], ALU.add)
    S.op("dve", lambda e: e.tensor_reduce(a.st4[:, :], v4(f1[:, :]), AX.X, ALU.add), [f1], [a.st4])
    TS(S, "dve", [a.st4], [a.st4], a.st4[:, :], a.st4[:, :], 1.0 / 64.0, None, ALU.mult)
    TT(S, "pool", [f1, a.st4], [f1], v4(f1[:, :]), v4(f1[:, :]), bc(a.st4[:, :].unsqueeze(2), [128, 4, 64]), ALU.subtract)
    TT(S, "pool", [f1], [f2], f2[:, :], f1[:, :], f1[:, :], ALU.mult)
    S.op("dve", lambda e: e.tensor_reduce(a.st4b[:, :], v4(f2[:, :]), AX.X, ALU.add), [f2], [a.st4b])
    TS(S, "dve", [a.st4b], [a.st4b], a.st4b[:, :], a.st4b[:, :], 1.0 / 64.0, EPS, ALU.mult, ALU.add)
    TT(S, "pool", [a.st4b, k.cm05], [a.st4b], a.st4b[:, :], a.st4b[:, :], bc(k.cm05[:, 0:1], [128, 4]), ALU.pow)
    TT(S, "pool", [f1, a.st4b], [f1], v4(f1[:, :]), v4(f1[:, :]), bc(a.st4b[:, :].unsqueeze(2), [128, 4, 64]), ALU.mult)
    TT(S, "pool", [f1, k.mng], [f1], f1[:, :], f1[:, :], k.mng[:, :], ALU.mult)
    ACT(S, [a.eo], [a.eo], a.eo[:, :], a.eo[:, :], AF.Ln, bias=1.0)
    ACT(S, [a.eo], [a.eo], a.eo[:, :], a.eo[:, :], AF.Exp, scale=-1.0)
    TT(S, "pool", [f1, a.eo], [a.yall], ym, f1[:, :], a.eo[:, :], ALU.mult)

    CP(S, "act", [ps4], [a.z_sb], a.z_sb[:, :], ps4[:, 0:256])
    ACT(S, [ps4], [a.ez], a.ez[:, :], ps4[:, 0:256], AF.Exp, scale=-1.0)
    TT(S, "pool", [HY, HYf], [f2], f2[:, :], HY[:, 256:512], HYf[:, 256:512], ALU.add)
    TT(S, "pool", [pset.x_tm, k.dsk], [f3], v4(f3[:, :]), v4(pset.x_tm[:, :]), bc(k.dsk[:, :].unsqueeze(2), [128, 4, 64]), ALU.mult)
    TT(S, "pool", [f2, f3], [f2], f2[:, :], f2[:, :], f3[:, :], ALU.add)
    ACT(S, [a.ez], [a.ez], a.ez[:, :], a.ez[:, :], AF.Ln, bias=1.0)
    ACT(S, [a.ez], [a.ez], a.ez[:, :], a.ez[:, :], AF.Exp, scale=-1.0)
    TT(S, "pool", [a.ez, a.z_sb], [a.ez], a.ez[:, :], a.ez[:, :], a.z_sb[:, :], ALU.mult)
    TT(S, "pool", [f2, a.ez], [f2], f2[:, :], f2[:, :], a.ez[:, :], ALU.mult)
    TT(S, "pool", [f2], [f3], f3[:, :], f2[:, :], f2[:, :], ALU.mult)
    S.op("dve", lambda e: e.tensor_reduce(a.st4[:, 0:2], f3[:, :].rearrange("p (g e) -> p g e", g=2), AX.X, ALU.add), [f3], [a.st4])
    TS(S, "dve", [a.st4], [a.st4], a.st4[:, 0:2], a.st4[:, 0:2], 1.0 / 128.0, EPS, ALU.mult, ALU.add)
    TT(S, "pool", [a.st4, k.cm05], [a.st4], a.st4[:, 0:2], a.st4[:, 0:2], bc(k.cm05[:, 0:1], [128, 2]), ALU.pow)
    TT(S, "pool", [f2, a.st4], [f2], f2[:, :].rearrange("p (g e) -> p g e", g=2), f2[:, :].rearrange("p (g e) -> p g e", g=2),
       bc(a.st4[:, 0:2].unsqueeze(2), [128, 2, 128]), ALU.mult)
    TT(S, "pool", [f2, k.sng], [a.yall], ys, f2[:, :], k.sng[:, :], ALU.mult)

    CP(S, "act", [ps3], [a.gu], a.gu[:, :], ps3[:, 0:256])
    S.op("dve", lambda e: e.bn_stats(a.gst[:, :], ps3[:, 256:512]), [ps3], [a.gst])
    S.op("dve", lambda e: e.bn_aggr(a.gmv[:, :], a.gst[:, :]), [a.gst], [a.gmv])
    TS(S, "dve", [a.gmv], [a.gve], a.gve[:, :], a.gmv[:, 1:2], EPS, None, ALU.add)
    TT(S, "pool", [a.gve, k.cm05], [a.grs], a.grs[:, :], a.gve[:, :], k.cm05[:, :], ALU.pow)
    TS(S, "dve", [ps3, a.gmv, a.grs], [a.gvb], a.gvb[:, :], ps3[:, 256:512], a.gmv[:, 0:1], a.grs[:, 0:1], ALU.subtract, ALU.mult)
    psG = S.bank()
    MM(S, [k.wsT, a.gvb], [psG], [(psG[:, h * 64:(h + 1) * 64], k.wsT[:, h, :], a.gvb[:, h * 64:(h + 1) * 64], True, True) for h in range(4)])
    TT(S, "dve", [psG, k.bsT], [f3], v4(f3[:, :]), v4(psG[:, 0:256]), bc(k.bsT[:, :].unsqueeze(2), [128, 4, 64]), ALU.add)
    TT(S, "pool", [f3, a.gu], [a.yall], yg, f3[:, :], a.gu[:, :], ALU.mult)

    CP(S, "act", [ps4], [a.pc], a.pc[:, :], ps4[:, 256:512])
    if has_p:
        CP(S, "act", [ps5], [a.pcP], a.pcP[:, :], ps5[0:8, 0:256])
    if has_n:
        CP(S, "act", [ps5], [a.pcN], a.pcN[:, :], ps5[0:8, 256:512])
    var = "int" if (has_p and has_n) else ("first" if has_n else ("last" if has_p else "int"))
    psP = S.bank()
    mmp = []
    for g in range(4):
        o_ = psP[:, g * 64:(g + 1) * 64]
        seqm = [(cview(k, f"pA{g}{var}"), a.pc[:, g * 64:(g + 1) * 64])]
        if has_p:
            seqm.append((cview(k, f"pP{g}", 8), a.pcP[0:8, g * 64:(g + 1) * 64]))
        if has_n:
            seqm.append((cview(k, f"pN{g}", 8), a.pcN[0:8, g * 64:(g + 1) * 64]))
        for n_, (lh, rh) in enumerate(seqm):
            mmp.append((o_, lh, rh, n_ == 0, n_ == len(seqm) - 1))
    MM(S, [a.c2b, a.pc, a.pcP, a.pcN], [psP], mmp)
    CP(S, "act", [psP], [a.plb], a.plb[:, :], psP[:, 0:256])
    psT2 = S.bank()
    t2v = bview(psT2, BF16)
    TR(S, [a.plb, k.identb], [psT2], [(t2v[:, j * 128:(j + 1) * 128], a.plb[:, j * 128:(j + 1) * 128], k.identb[:, :]) for j in range(2)])
    CP(S, "dve", [psT2], [a.plT], a.plT[:, :, :], t2v[:, 0:256].rearrange("p (j t) -> p j t", j=2))
    psW = S.bank()
    MM(S, [k.wpb, a.plT], [psW], [(psW[:, j * 128:(j + 1) * 128], k.wpb[:, j, :], a.plT[:, j, :], True, True) for j in range(2)])
    cT = a.concatT
    for j in range(2):
        ACT(S, [psW, k.psc], [cT], cT[:, 2 + j, :], psW[:, j * 128:(j + 1) * 128], AF.Identity, scale=k.psc[:, j:j + 1])

    psT3 = S.bank()
    t3v = bview(psT3, BF16)
    TR(S, [a.yall, k.identb], [psT3], [(t3v[:, (m3 * 2 + j) * 128:(m3 * 2 + j + 1) * 128], a.yall[:, m3, j * 128:(j + 1) * 128], k.identb[:, :])
                                      for m3 in range(3) for j in range(2)])
    CP(S, "dve", [psT3], [cT], cT[:, 0:2, :], t3v[:, 0:256].rearrange("p (j t) -> p j t", j=2))
    CP(S, "act", [psT3], [cT], cT[:, 4:8, :], t3v[:, 256:768].rearrange("p (j t) -> p j t", j=4))

    p0, p1 = S.bank(), S.bank()
    for hlf, pb in enumerate((p0, p1)):
        MM(S, [cT, w_out], [pb], [(pb[:, :], cT[:, kc, :], w_out[:, kc, hlf * 512:(hlf + 1) * 512], kc == 0, kc == 7) for kc in range(8)])
    xb = a.xq[i % 2]
    resid_ln(k, xb, (p0, p1), xb, a.rsm)
    r0 = off + 128 * i
    db = S.dbuf(("xoutA", l, r0 // 128))
    S.dma("pool", rows_ap(k, a.dst, "out", r0, 128), xb[:, :], reads=[xb], writes=[db])
    if a.dst is None:
        k.final_bufs.append(db)


_CACHE = {}


def gather_outputs(results, cfg, n):
    NP, Tp, Ts = cfg.NP, cfg.Tp, cfg.Ts
    y_p = np.concatenate([r["yp"].reshape(NP, Tp, D) for r in results], 0)
    y_s = np.stack([r["ys"].reshape(Ts, D) for r in results], 0)
    oc = np.concatenate([r["oc"] for r in results], 0)
    on = np.concatenate([r["on"] for r in results], 0)
    om = np.concatenate([r["om"].reshape(NP, 2, 2, 4) for r in results], 0)
    os_ = np.concatenate([r["os"] for r in results], 0)
    f = lambda a: np.ascontiguousarray(a, dtype=np.float32)
    return (f(y_p), f(y_s), f(oc), f(on), f(om), f(os_))


def kernel(**inputs):
    n = 8
    xs = np.asarray(inputs["x_sample"])
    xp = np.asarray(inputs["x_prompt"])
    cfg = Cfg(Ts=xs.shape[1], NP=xp.shape[0] // n, Tp=xp.shape[1], L=2)
    key = (cfg.Ts, cfg.NP, cfg.Tp)
    if key not in _CACHE:
        _CACHE[key] = build(cfg)
    nc, _ = _CACHE[key]
    in_maps = [shard_inputs(inputs, cfg, c) for c in range(n)]
    res = run_bass_kernel_spmd(nc, in_maps, core_ids=list(range(n)))
    return gather_outputs(res.results, cfg, n)
```

```python
import math
import numpy as np
import ml_dtypes
from contextlib import ExitStack
import concourse.bass as bass
import concourse.mybir as mybir
from concourse.bass_utils import run_bass_kernel_spmd

F32 = mybir.dt.float32
BF16 = mybir.dt.bfloat16
AF = mybir.ActivationFunctionType
ALU = mybir.AluOpType
AX = mybir.AxisListType
DTSZ = {F32: 4, BF16: 2}

D = 1024
DIN = 2840
DFF = 2816
EPS = 1e-5
ALPHA = 4.0 ** 0.25
NEG = -30000.0

ENGS = ("pe", "act", "dve", "pool", "sp")
EPOCH = 30000
NDMA_SEM = 8
SCHED_CP = 0.2
SCHED_LAT = 0.1


def prod(l):
    r = 1
    for x in l:
        r *= int(x)
    return r


class Buf:
    __slots__ = ("name", "v", "last_w", "readers", "off", "excl")

    def __init__(self, name, v, floor=None):
        self.off = -1
        self.excl = False
        self.name = name
        self.v = v
        self.last_w = floor
        self.readers = []

    def __getitem__(self, k):
        return self.v[k]


class Op:
    __slots__ = ("eng", "fn", "deps", "is_dma", "idx", "sig", "has_dep", "vc", "name", "cost")

    def __init__(self, eng, fn, is_dma, name):
        self.cost = 0.4
        self.eng = eng
        self.fn = fn
        self.is_dma = is_dma
        self.deps = []
        self.sig = None
        self.has_dep = False
        self.vc = None
        self.name = name


class Sched:
    def __init__(self, nc, es, arena_bytes):
        self.nc = nc
        self.es = es
        self.ops = []
        self.floor = None
        self.bufs = []
        self.arena = es.enter_context(nc.sbuf_tensor("arena", [128, arena_bytes // 4], F32))
        self.arena_bytes = arena_bytes
        self.off = 0
        self.peak = 0
        self.banks = []
        for i in range(8):
            t = es.enter_context(nc.psum_tensor(f"bank{i}", [128, 512], F32))
            self.banks.append(Buf(f"bank{i}", t))
            self.banks[-1].excl = True
        self.bank_i = 0
        self.dram_bufs = {}

    def sb(self, name, shape, dtype=F32):
        shape = [int(s) for s in shape]
        if getattr(self, "verbose", False):
            print(f"  sb {name} {shape} {prod(shape[1:]) * DTSZ[dtype]} at {self.off}")
        n = prod(shape[1:])
        nb = n * DTSZ[dtype]
        off = (self.off + 31) // 32 * 32
        assert off + nb <= self.arena_bytes, f"arena overflow allocating {name}: {off + nb}"
        self.off = off + nb
        self.peak = max(self.peak, self.off)
        h = self.arena if dtype == F32 else self.arena.bitcast(dtype)
        e0 = off // DTSZ[dtype]
        v = h[0:shape[0], e0:e0 + n]
        if len(shape) > 2:
            names = " ".join(f"d{i}" for i in range(len(shape) - 1))
            kw = {f"d{i}": shape[i + 1] for i in range(len(shape) - 1)}
            v = v.rearrange(f"p ({names}) -> p {names}", **kw)
        b = Buf(name, v, self.floor)
        b.off = off
        self.bufs.append(b)
        return b

    def mark(self):
        return self.off

    def release(self, mark):
        self.barrier()
        self.bufs = [b for b in self.bufs if b.off < mark]
        self.off = mark

    def bank(self):
        b = self.banks[self.bank_i]
        self.bank_i = (self.bank_i + 1) % 8
        return b

    def dbuf(self, key):
        if key not in self.dram_bufs:
            self.dram_bufs[key] = Buf(str(key), None, None)
        return self.dram_bufs[key]

    def op(self, eng, fn, reads=(), writes=(), name=None, dma=False, cost=None):
        o = Op(eng, fn, dma, name)
        if cost is not None:
            o.cost = cost
        deps = set()
        ex = [b for b in reads if b.excl]
        if ex:
            reads = [b for b in reads if not b.excl]
            writes = list(writes) + [b for b in ex if b not in writes]
        for b in reads:
            if b.last_w is not None:
                deps.add(b.last_w)
        for b in writes:
            if b.last_w is not None:
                deps.add(b.last_w)
            for r in b.readers:
                deps.add(r)
        o.deps = list(deps)
        o.idx = len(self.ops)
        for d in o.deps:
            d.has_dep = True
        for b in reads:
            b.readers.append(o)
        for b in writes:
            b.last_w = o
            b.readers = []
        self.ops.append(o)
        return o

    def dma(self, q, out, in_, reads=(), writes=(), name=None, **kw):
        nbytes = prod(out.shape) * 4
        return self.op(q, lambda e: e.dma_start(out=out, in_=in_, **kw), reads, writes, name=name, dma=True,
                       cost=2.0 + nbytes / 150e3)

    def barrier(self):
        allb = self.bufs + self.banks + list(self.dram_bufs.values())
        o = self.op("sp", None, reads=[], writes=allb, name="barrier")
        self.floor = o
        return o

    def list_schedule(self, ops):
        import heapq
        LAT = SCHED_LAT
        out = []
        seg = []
        segs = []
        for o in ops:
            if o.fn is None:
                segs.append(seg)
                segs.append([o])
                seg = []
            else:
                seg.append(o)
        segs.append(seg)
        finish = {}
        for seg in segs:
            if len(seg) <= 1:
                for o in seg:
                    finish[o] = 0.0
                    out.append(o)
                continue
            inseg = set(seg)
            indeg = {}
            users = {}
            for o in seg:
                n = 0
                for d in o.deps:
                    if d in inseg:
                        n += 1
                        users.setdefault(d, []).append(o)
                indeg[o] = n
            tail = {}
            for o in reversed(seg):
                t = 0.0
                for u in users.get(o, ()):
                    if tail[u] > t:
                        t = tail[u]
                tail[o] = t + o.cost + 0.2
            eng_time = {e: 0.0 for e in ENGS}
            ready_at = {}
            heap = []
            for o in seg:
                if indeg[o] == 0:
                    ready_at[o] = 0.0
                    heapq.heappush(heap, (0.0, o.idx, o))
            while heap:
                best = None
                cand = []
                while heap and len(cand) < 24:
                    cand.append(heapq.heappop(heap))
                bi = None
                for ci, (ra, idx, o) in enumerate(cand):
                    stt = max(ra, eng_time[o.eng])
                    key = (stt - SCHED_CP * tail[o], idx)
                    if best is None or key < best:
                        best, bi = key, ci
                ra, idx, o = cand.pop(bi)
                for c in cand:
                    heapq.heappush(heap, c)
                stt = max(ra, eng_time[o.eng])
                if o.is_dma:
                    eng_time[o.eng] = stt + 0.15
                    fin_t = stt + o.cost
                else:
                    fin_t = stt + o.cost
                    eng_time[o.eng] = fin_t
                finish[o] = fin_t
                out.append(o)
                for u in users.get(o, ()):
                    t = fin_t + (LAT if u.eng != o.eng else 0.3)
                    if ready_at.get(u, 0.0) < t:
                        ready_at[u] = t
                    indeg[u] -= 1
                    if indeg[u] == 0:
                        heapq.heappush(heap, (ready_at[u], u.idx, u))
            self.est_time = getattr(self, "est_time", 0.0) + max(eng_time.values())
        for i, o in enumerate(out):
            o.idx = i
        return out

    def emit(self, final_bufs):
        nc, es = self.nc, self.es
        fin = self.op("sp", None, reads=list(final_bufs), name="final")
        if getattr(self, "reorder", True):
            self.ops = self.list_schedule(self.ops)
        cnt, dma_n, semkeys = {}, {}, []
        for o in self.ops:
            if not o.has_dep:
                continue
            if o.is_dma:
                n = dma_n.get(o.eng, 0)
                dma_n[o.eng] = n + 1
                key = ("dma", o.eng, n % NDMA_SEM)
                cnt[key] = cnt.get(key, 0) + 16
                o.sig = (key, cnt[key])
            else:
                tot = cnt.get(("n", o.eng), 0)
                cnt[("n", o.eng)] = tot + 1
                key = ("c", o.eng, tot // EPOCH)
                o.sig = (key, tot % EPOCH + 1)
            if o.sig[0] not in semkeys:
                semkeys.append(o.sig[0])
        sems = {k: es.enter_context(nc.semaphore("s_" + "_".join(str(x) for x in k))) for k in semkeys}
        per_eng = {e: [] for e in ENGS}
        seen = {e: {} for e in ENGS}
        nwaits = 0
        for o in self.ops:
            s = seen[o.eng]
            need = {}
            for d in o.deps:
                k, c = d.sig
                if s.get(k, 0) < c:
                    need[k] = max(need.get(k, 0), c)
            if o.is_dma and o.sig is not None:
                k, c = o.sig
                if c > 16 and s.get(k, 0) < c - 16:
                    need[k] = max(need.get(k, 0), c - 16)
            for d in o.deps:
                for k, c in d.vc.items():
                    if s.get(k, 0) < c:
                        s[k] = c
            for k, c in need.items():
                if s.get(k, 0) < c:
                    s[k] = c
            o.deps = need
            nwaits += len(need)
            vc = dict(s)
            if o.sig is not None:
                vc[o.sig[0]] = max(vc.get(o.sig[0], 0), o.sig[1])
                if not o.is_dma:
                    for ep in range(o.sig[0][2]):
                        vc[("c", o.eng, ep)] = EPOCH
            o.vc = vc
            per_eng[o.eng].append(o)

        def body_for(engname):
            def body(eng):
                for o in per_eng[engname]:
                    for k, c in o.deps.items():
                        eng.wait_ge(sems[k], c)
                    if o.fn is not None:
                        ins = o.fn(eng)
                        if o.sig is not None:
                            ins.then_inc(sems[o.sig[0]], 16 if o.is_dma else 1)
                    elif o.sig is not None:
                        eng.nop().then_inc(sems[o.sig[0]], 1)
            return body

        with nc.Block() as block:
            block.sync(body_for("sp"))
            block.scalar(body_for("act"))
            block.vector(body_for("dve"))
            block.gpsimd(body_for("pool"))
            block.tensor(body_for("pe"))
        return {"ops": len(self.ops), "waits": nwaits, "sems": len(sems),
                "per_eng": {e: len(v) for e, v in per_eng.items()}, "sbuf_peak": self.peak}


def _c(out, base=0.25, per=1.0 / 1000.0):
    return base + prod(out.shape[1:]) * per


def ACT(S, r, w, out, in_, func, bias=None, scale=None):
    kw = {}
    if bias is not None:
        kw["bias"] = bias
    if scale is not None:
        kw["scale"] = scale
    return S.op("act", lambda e: e.activation(out, in_, func, **kw), r, w, cost=_c(out, 0.3, 1 / 1200.0))


def TS(S, eng, r, w, out, in0, s1, s2, op0, op1=None):
    if op1 is None:
        return S.op(eng, lambda e: e.tensor_scalar(out, in0, s1, None, op0), r, w, cost=_c(out))
    return S.op(eng, lambda e: e.tensor_scalar(out, in0, s1, s2, op0, op1), r, w, cost=_c(out))


def TT(S, eng, r, w, out, in0, in1, op):
    return S.op(eng, lambda e: e.tensor_tensor(out, in0, in1, op), r, w, cost=_c(out))


def STT(S, eng, r, w, out, in0, scalar, in1, op0, op1):
    return S.op(eng, lambda e: e.scalar_tensor_tensor(out, in0, scalar, in1, op0, op1), r, w, cost=_c(out))


def CP(S, eng, r, w, out, in_):
    if eng == "act":
        return S.op("act", lambda e: e.copy(out, in_), r, w, cost=_c(out, 0.3, 1 / 1200.0))
    return S.op(eng, lambda e: e.tensor_copy(out, in_), r, w, cost=_c(out))


def MSET(S, eng, w, out, val):
    return S.op(eng, lambda e: e.memset(out, val), [], w)


def MM(S, r, w, mms):
    mms = list(mms)

    def fn(e):
        ins = None
        for (o, l, rh, st, sp) in mms:
            ins = e.matmul(o, l, rh, start=st, stop=sp)
        return ins
    cost = 0.1
    for (o, l, rh, st, sp) in mms:
        cost += max(64, prod(rh.shape[1:])) / 2400.0 * (4.0 if rh.dtype == F32 else 1.0) + 0.02
    return S.op("pe", fn, r, w, cost=cost)


def TR(S, r, w, trs):
    trs = list(trs)

    def fn(e):
        ins = None
        for (o, i, idt) in trs:
            ins = e.transpose(o, i, idt)
        return ins
    return S.op("pe", fn, r, w, cost=0.1 + 0.12 * len(trs))


def bc(ap, shape):
    return ap.to_broadcast([int(s) for s in shape])


POOL_W = (2, 4, 8, 16)


def make_consts():
    cols = {}
    parts = []
    off = [0]

    def add(name, arr):
        a = np.zeros((128, arr.shape[1]), np.float32)
        a[:arr.shape[0]] = arr
        cols[name] = (off[0], arr.shape[1])
        off[0] += arr.shape[1]
        parts.append(a)

    idx = np.arange(128)
    s_, t_ = idx[:, None], idx[None, :]
    add("ident", np.eye(128, dtype=np.float32))
    add("ones", np.ones((128, 128), np.float32))
    add("tri0", (s_ <= t_).astype(np.float32))
    add("tri1", (s_ >= t_).astype(np.float32))
    add("neg0", np.where(s_ <= t_, 0.0, NEG).astype(np.float32))
    add("neg1", np.where(s_ >= t_, 0.0, NEG).astype(np.float32))
    n1 = off[0]
    for g, w in enumerate(POOL_W):
        h = w // 2
        band = ((s_ >= t_ - h) & (s_ < t_ + h)).astype(np.float32)
        cnt_int = np.full(128, float(w))
        cnt_first = (idx + h) - np.maximum(idx - h, 0)
        cnt_last = np.minimum(idx + h, 128) - (idx - h)
        eye = np.eye(128, dtype=np.float32)
        add(f"pA{g}int", band / cnt_int[None, :] - eye)
        add(f"pA{g}first", band / cnt_first[None, :] - eye)
        add(f"pA{g}last", band / cnt_last[None, :] - eye)
        sp = np.arange(8)[:, None]
        add(f"pP{g}", (((sp - 8) >= t_ - h) & ((sp - 8) < t_ + h)).astype(np.float32) / w)
        add(f"pN{g}", (((128 + sp) >= t_ - h) & ((128 + sp) < t_ + h)).astype(np.float32) / w)
    add("jrow", np.tile(np.arange(256, dtype=np.float32)[None, :], (128, 1)))
    add("pcol", (idx % 64).astype(np.float32)[:, None])
    full = np.concatenate(parts, axis=1)
    cols2 = {kk: (o - n1, n) for kk, (o, n) in cols.items() if o >= n1}
    cols1 = {kk: (o, n) for kk, (o, n) in cols.items() if o < n1}
    return full[:, :n1].copy(), cols1, full[:, n1:].copy(), cols2


class Cfg:
    def __init__(self, Ts=4096, NP=4, Tp=256, L=2, debug=()):
        self.Ts, self.NP, self.Tp, self.L = Ts, NP, Tp, L
        self.debug = tuple(debug)
        self.seqs = [("s", Ts, 0, 0)] + [(f"p{j}", Tp, 1, Ts + j * Tp) for j in range(NP)]
        self.Ttot = Ts + NP * Tp


INPUT_SPECS = lambda c: [
    ("xs", [c.Ts, D]), ("xp", [c.NP * c.Tp, D]),
    ("st_c", [2, 2, 4, 64, 64]), ("st_n", [2, 2, 4, 64]), ("st_m", [2, 8]), ("st_s", [2, 2, 4, 64, 128]),
    ("cond", [2, D]),
    ("w_ada", [2, D, 6 * D]), ("b_ada", [2, 6 * D]), ("w_in", [2, D, DIN]), ("b_ig", [2, 8]), ("b_fg", [2, 8]),
    ("mnorm_g", [2, 256]), ("w_pool", [2, 4, 64, 64]), ("pool_scale", [2, 256]), ("w_sp", [2, 4, 128, 128]),
    ("b_sp", [2, 4, 128]), ("sconv_w", [2, 3, 768]), ("sconv_b", [2, 768]), ("dt_bias", [2, 8]),
    ("a_log", [2, 8]), ("ssd_d", [2, 4]), ("snorm_g", [2, 256]), ("w_out", [2, D, D]),
    ("ln1_g", [2, D]), ("ln1_b", [2, D]), ("w_up", [2, D, 2 * DFF]), ("fconv_w", [2, 3, 2 * DFF]),
    ("fconv_b", [2, 2 * DFF]), ("w_dn", [2, DFF, D]), ("ln2_g", [2, D]), ("ln2_b", [2, D]),
]
OUTPUT_SPECS = lambda c: [
    ("ys", [c.Ts, D]), ("yp", [c.NP * c.Tp, D]), ("oc", [c.NP, 2, 2, 4, 64, 64]), ("on", [c.NP, 2, 2, 4, 64]),
    ("om", [c.NP, 2, 8]), ("os", [c.NP, 2, 2, 4, 64, 128]),
]


class K:
    pass


def build(cfg):
    nc = bass.Bass("TRN2", target_bir_lowering=False)
    cst_np, ccols, cst2_np, ccols2 = make_consts()
    I = {}
    for name, shape in INPUT_SPECS(cfg):
        I[name] = nc.dram_tensor(name, shape, F32, kind="ExternalInput")
    I["cst"] = nc.dram_tensor("cst", list(cst_np.shape), F32, kind="ExternalInput")
    I["cst2"] = nc.dram_tensor("cst2", list(cst2_np.shape), F32, kind="ExternalInput")
    O = {}
    for name, shape in OUTPUT_SPECS(cfg):
        O[name] = nc.dram_tensor(name, shape, F32, kind="ExternalOutput")
    DBG = {}
    for name, shape in cfg.debug:
        DBG[name] = nc.dram_tensor("dbg_" + name, shape, F32, kind="ExternalOutput")
    ntile = cfg.Ttot // 128
    XA = nc.dram_tensor("scr_xa", [cfg.Ttot, D], F32, kind="Internal")
    XB = nc.dram_tensor("scr_xb", [cfg.Ttot, D], F32, kind="Internal")
    HF = nc.dram_tensor("scr_hf", [cfg.Ttot, 512], F32, kind="Internal")
    HT = nc.dram_tensor("scr_ht", [ntile, 128, 8 * 128], BF16, kind="Internal")
    ED = nc.dram_tensor("scr_e", [64, 512], F32, kind="Internal")
    STB = nc.dram_tensor("scr_stb", [ntile, 128, 2304], BF16, kind="Internal")
    STG = nc.dram_tensor("scr_stg", [ntile, 128, 280], F32, kind="Internal")

    with ExitStack() as es:
        S = Sched(nc, es, 204 * 1024)
        k = K()
        k.S, k.cfg, k.I, k.O, k.DBG = S, cfg, I, O, DBG
        k.XA, k.XB, k.HF, k.HT, k.ED = XA, XB, HF, HT, ED
        k.STB, k.STG = STB, STG
        k.final_bufs = []
        k.ccols2 = ccols2
        k.W2 = cst2_np.shape[1]
        setup(k, ccols)
        for l in range(cfg.L):
            layer(k, l)
        stats = S.emit(k.final_bufs)
    return nc, stats


def cview(k, name, rows=128):
    if name in k.ccols:
        o, n = k.ccols[name]
        return k.cst[0:rows, o:o + n]
    o, n = k.ccols2[name]
    return k.cst2[0:rows, o:o + n]


def load_cst2(k):
    S = k.S
    b = S.sb("cst2", [128, k.W2])
    k.cst2b = b
    k.cst2 = b.v
    S.dma("sp", b[:, :], k.I["cst2"][:, :], writes=[b])
    return b


def setup(k, ccols):
    S, I, cfg = k.S, k.I, k.cfg
    k.ccols = ccols
    W = sum(n for (_, n) in ccols.values())
    cstb = S.sb("cst", [128, W])
    k.cstb = cstb
    k.cst = cstb.v
    S.dma("sp", cstb[:, :], I["cst"][:, :], writes=[cstb])
    k.identb = S.sb("identb", [128, 128], BF16)
    CP(S, "dve", [cstb], [k.identb], k.identb[:, :], cview(k, "ident"))
    k.cm05 = S.sb("cm05", [128, 1])
    MSET(S, "pool", [k.cm05], k.cm05[:, :], -0.5)
    k.cm1 = S.sb("cm1", [128, 1])
    MSET(S, "pool", [k.cm1], k.cm1[:, :], -1.0)

    L = cfg.L
    k.modT = S.sb("modT", [128, L, 48, 2])
    layer_consts_alloc(k)
    m1 = S.mark()
    c2b = load_cst2(k)
    fr = S.sb("pe_fr", [64, 256])
    ang = S.sb("pe_ang", [64, 256])
    et = S.sb("pe_e", [64, 512])
    et2 = S.sb("pe_e2", [64, 512])
    sq = S.sb("pe_sq", [64, 256])
    ACT(S, [c2b], [fr], fr[:, :], cview(k, "jrow", 64), AF.Exp, scale=-math.log(10000.0) / 256.0)
    TS(S, "dve", [fr, c2b], [ang], ang[:, :], fr[:, :], cview(k, "pcol", 64), None, ALU.mult)
    ACT(S, [ang], [et], et[:, 0:256], ang[:, :], AF.Sin, scale=1.0 / 32.0)
    ACT(S, [ang], [et], et[:, 256:512], ang[:, :], AF.Sin, scale=-1.0 / 32.0, bias=math.pi / 2.0)
    cur, nxt = et, et2
    for it in range(5):
        TT(S, "dve", [cur], [sq], sq[:, :], cur[:, 0:256], cur[:, 0:256], ALU.mult)
        STT(S, "dve", [cur], [nxt], nxt[:, 0:256], cur[:, 0:256], 2.0, cur[:, 256:512], ALU.mult, ALU.mult)
        TS(S, "dve", [sq], [nxt], nxt[:, 256:512], sq[:, :], -2.0, 1.0, ALU.mult, ALU.add)
        cur, nxt = nxt, cur
    et = cur
    edb = S.dbuf("ED")
    S.dma("sp", k.ED[:, :], et[:, :], reads=[et], writes=[edb])

    condT = S.sb("condT", [128, 8, 2])
    for c in range(2):
        S.dma("sp", condT[:, :, c], I["cond"][c].rearrange("(kc p) -> p kc", p=128), writes=[condT],
              allow_slow_non_contiguous=True)
    esg = S.sb("cond_e", [128, 8, 2])
    ACT(S, [condT], [esg], esg[:, :, :], condT[:, :, :], AF.Exp, scale=-1.0)
    TS(S, "dve", [esg], [esg], esg[:, :, :], esg[:, :, :], 1.0, None, ALU.add)
    TT(S, "pool", [esg, k.cm1], [esg], esg[:, :, :], esg[:, :, :], bc(k.cm1[:, 0:1].unsqueeze(2), [128, 8, 2]), ALU.pow)
    TT(S, "pool", [condT, esg], [condT], condT[:, :, :], condT[:, :, :], esg[:, :, :], ALU.mult)
    badaT = S.sb("badaT", [128, L, 48])
    k.cf_st = S.sb("cf_st0", [128, 128])
    for l in range(L):
        colform(k, badaT, badaT[:, l, :], I["b_ada"][l].rearrange("(j p) -> j p", p=128), 48)
    wst = [S.sb(f"wada_st{j}", [128, 8, 512]) for j in range(2)]
    n = 0
    for l in range(L):
        for cb in range(12):
            st = wst[n % 2]
            n += 1
            S.dma("sp", st[:, :, :], I["w_ada"][l, :, cb * 512:(cb + 1) * 512].rearrange("(kc p) n -> p kc n", p=128),
                  writes=[st])
            pb = S.bank()
            mms = []
            for sub in range(4):
                for kc in range(8):
                    mms.append((pb[:, sub * 2:sub * 2 + 2], st[:, kc, sub * 128:(sub + 1) * 128], condT[:, kc, :],
                                kc == 0, kc == 7))
            MM(S, [st, condT], [pb], mms)
            TT(S, "dve", [pb, badaT], [k.modT],
               k.modT[:, l, cb * 4:cb * 4 + 4, :],
               pb[:, 0:8].rearrange("p (s c) -> p s c", c=2),
               bc(badaT[:, l, cb * 4:cb * 4 + 4].unsqueeze(2), [128, 4, 2]), ALU.add)
    for l in range(L):
        for grp in (1, 4):
            TS(S, "dve", [k.modT], [k.modT], k.modT[:, l, grp * 8:(grp + 1) * 8, :],
               k.modT[:, l, grp * 8:(grp + 1) * 8, :], 1.0, None, ALU.add)
    S.release(m1)


class nc_allow:
    def __init__(self, k):
        pass

    def __enter__(self):
        return self

    def __exit__(self, *a):
        return False


def bview(bank, dtype=F32):
    return bank.v if dtype == F32 else bank.v.bitcast(dtype)


def layer_consts_alloc(k):
    S = k.S
    k.bif = S.sb("bif", [128, 16])
    k.coef = S.sb("coef", [128, 2, 8])
    k.dtb = S.sb("dtb", [128, 8])
    k.dsk = S.sb("dsk", [128, 4])
    k.mng = S.sb("mng", [128, 256])
    k.sng = S.sb("sng", [128, 256])
    k.psc = S.sb("psc", [128, 2])
    k.wpb = S.sb("wpb", [128, 2, 128], BF16)
    k.wsT = S.sb("wsT", [128, 4, 128], BF16)
    k.bsT = S.sb("bsT", [128, 4])
    k.scw = S.sb("scw", [128, 4, 6])
    k.fcw = S.sb("fcw", [128, 4, 44])
    k.lng = S.sb("lng", [128, D])
    k.lnb = S.sb("lnb", [128, D])
    k.gbc = S.sb("gbc", [128, D])
    k.ttmp = [S.sb(f"ttmp{j}", [128, D]) for j in range(1)]
    k.small = {}


def colform(k, dst_buf, dst_ap, src_ap, nb):
    S = k.S
    st = k.cf_st
    S.dma("sp", st[0:nb, :], src_ap, writes=[st])
    pb = S.bank()
    TR(S, [st, k.cstb], [pb], [(pb[:, 0:nb], st[0:nb, :], cview(k, "ident")[0:nb, 0:nb])])
    CP(S, "dve", [pb], [dst_buf], dst_ap, pb[:, 0:nb])


def load_layer_consts(k, l):
    S, I = k.S, k.I
    m = S.mark()
    k.cf_st = S.sb("cf_st", [128, 128])
    row = lambda name, a, b: I[name][l:l + 1, a:b].partition_broadcast(128)
    S.dma("sp", k.bif[:, 0:8], row("b_ig", 0, 8), writes=[k.bif])
    S.dma("sp", k.bif[:, 8:16], row("b_fg", 0, 8), writes=[k.bif])
    TS(S, "dve", [k.bif], [k.bif], k.bif[:, 8:16], k.bif[:, 8:16], -1.0, None, ALU.mult)
    al = S.sb("al_tmp", [128, 8])
    S.dma("sp", al[:, :], row("a_log", 0, 8), writes=[al])
    ACT(S, [al], [al], al[:, :], al[:, :], AF.Exp)
    MSET(S, "pool", [k.coef], k.coef[:, :, :], -1.0)
    TS(S, "dve", [al, k.coef], [k.coef], k.coef[:, :, 4:8], al[:, :].rearrange("p (d h) -> p d h", d=2), -1.0, None, ALU.mult)
    S.dma("sp", k.dtb[:, :], row("dt_bias", 0, 8), writes=[k.dtb])
    S.dma("sp", k.dsk[:, :], row("ssd_d", 0, 4), writes=[k.dsk])
    S.dma("sp", k.mng[:, :], row("mnorm_g", 0, 256), writes=[k.mng])
    S.dma("sp", k.sng[:, :], row("snorm_g", 0, 256), writes=[k.sng])
    colform(k, k.psc, k.psc[:, :], I["pool_scale"][l].rearrange("(j p) -> j p", p=128), 2)
    wp32 = S.sb("wp32", [128, 2, 128])
    MSET(S, "pool", [wp32], wp32[:, :, :], 0.0)
    for g in range(4):
        pr = slice((g % 2) * 64, (g % 2) * 64 + 64)
        S.dma("sp", wp32[pr, g // 2, (g % 2) * 64:(g % 2) * 64 + 64], I["w_pool"][l, g], writes=[wp32])
    CP(S, "dve", [wp32], [k.wpb], k.wpb[:, :, :], wp32[:, :, :])
    ws32 = S.sb("ws32", [128, 4, 128])
    S.dma("sp", ws32[:, :, :], I["w_sp"][l].rearrange("h t s -> t h s"), writes=[ws32])
    pb = S.bank()
    TR(S, [ws32, k.cstb], [pb], [(pb[:, h * 128:(h + 1) * 128], ws32[:, h, :], cview(k, "ident")) for h in range(4)])
    CP(S, "act", [pb], [k.wsT], k.wsT[:, :, :], pb[:, :].rearrange("p (h t) -> p h t", h=4))
    colform(k, k.bsT, k.bsT[:, :], I["b_sp"][l], 4)
    for tap in range(3):
        colform(k, k.scw, k.scw[:, tap, :], I["sconv_w"][l, tap].rearrange("(b p) -> b p", p=128), 6)
        colform(k, k.fcw, k.fcw[:, tap, :], I["fconv_w"][l, tap].rearrange("(b p) -> b p", p=128), 44)
    colform(k, k.scw, k.scw[:, 3, :], I["sconv_b"][l].rearrange("(b p) -> b p", p=128), 6)
    colform(k, k.fcw, k.fcw[:, 3, :], I["fconv_b"][l].rearrange("(b p) -> b p", p=128), 44)
    S.release(m)


def load_weight(k, dst, src2d, nkc, ncols, scope_stage):
    S = k.S
    engs = ("dve", "act")
    piece = 2840
    for kc in range(nkc):
        for c0 in range(0, ncols, piece):
            c1 = min(ncols, c0 + piece)
            st = scope_stage[k.wl_n % 2]
            S.dma("sp", st[:, 0:c1 - c0], src2d[kc * 128:(kc + 1) * 128, c0:c1], writes=[st])
            CP(S, engs[k.wl_n % 2], [st], [dst], dst[:, kc, c0:c1], st[:, 0:c1 - c0])
            k.wl_n += 1


def gate_table(k, l, grp, cond):
    S = k.S
    dg = k.ttmp[0]
    for j in range(8):
        TS(S, "dve", [k.cstb, k.modT], [dg], dg[:, 0:128], cview(k, "ident"), k.modT[:, l, grp * 8 + j, cond:cond + 1], None, ALU.mult)
        if j % 4 == 0:
            pb = S.bank()
        MM(S, [dg, k.cstb], [pb], [(pb[:, (j % 4) * 128:(j % 4 + 1) * 128], cview(k, "ones"), dg[:, 0:128], True, True)])
        if j % 4 == 3:
            CP(S, "act", [pb], [k.gbc], k.gbc[:, (j // 4) * 512:(j // 4 + 1) * 512], pb[:, :])


def make_hT(k, l, which, cond, rows, xbuf, dsts, dst_buf, src_bufs, pos_tile=None):
    S = k.S
    n = 0
    for r, nr in rows:
        S.dma("sp", xbuf[n:n + nr, :], r, reads=src_bufs, writes=[xbuf])
        n += nr
    if pos_tile is not None:
        TT(S, "pool", [xbuf, pos_tile], [xbuf], xbuf[0:n, :], xbuf[0:n, :], pos_tile[0:n, :], ALU.add)
    sm = k.hsm
    st, mv, ve, rstd, xnb = sm["st"], sm["mv"], sm["ve"], sm["rstd"], sm["xnb"]
    S.op("dve", lambda e: e.bn_stats(st[0:n, 0, :], xbuf[0:n, 0:512]), [xbuf], [st])
    S.op("dve", lambda e: e.bn_stats(st[0:n, 1, :], xbuf[0:n, 512:1024]), [xbuf], [st])
    S.op("dve", lambda e: e.bn_aggr(mv[0:n, :], st[0:n, :, :].rearrange("p a b -> p (a b)")), [st], [mv])
    TS(S, "dve", [mv], [ve], ve[0:n, :], mv[0:n, 1:2], EPS, None, ALU.add)
    TT(S, "pool", [ve, k.cm05], [rstd], rstd[0:n, :], ve[0:n, :], k.cm05[0:n, :], ALU.pow)
    TS(S, "dve", [xbuf, mv, rstd], [xnb], xnb[0:n, :], xbuf[0:n, :], mv[0:n, 0:1], rstd[0:n, 0:1], ALU.subtract, ALU.mult)
    pb = S.bank()
    pv = bview(pb, BF16)
    TR(S, [xnb, k.identb], [pb],
       [(pv[:, kc * 128:kc * 128 + n], xnb[0:n, kc * 128:(kc + 1) * 128], k.identb[0:n, 0:n]) for kc in range(8)])
    gsh, gsc = (0, 1) if which == 1 else (3, 4)
    for kc in range(8):
        sc = k.modT[:, l, gsc * 8 + kc, cond:cond + 1]
        sh = k.modT[:, l, gsh * 8 + kc, cond:cond + 1]
        if kc % 2 == 0:
            ACT(S, [pb, k.modT], [dst_buf], dsts[kc], pv[:, kc * 128:kc * 128 + n], AF.Identity, bias=sh, scale=sc)
        else:
            TS(S, "dve", [pb, k.modT], [dst_buf], dsts[kc], pv[:, kc * 128:kc * 128 + n], sc, sh, ALU.mult, ALU.add)


def alloc_hsm(k):
    S = k.S
    k.hsm = {"st": S.sb("h_st", [128, 2, 6]), "mv": S.sb("h_mv", [128, 2]), "ve": S.sb("h_ve", [128, 1]),
             "rstd": S.sb("h_rstd", [128, 1]), "xnb": S.sb("h_xnb", [128, D], BF16)}


def resid_ln(k, x_buf, psum_halves, out_buf, nb_small):
    S = k.S
    t0, t1 = k.ttmp[0], out_buf
    for hlf, pb in enumerate(psum_halves):
        sl = slice(hlf * 512, (hlf + 1) * 512)
        TT(S, "dve", [pb, k.gbc], [t0], t0[:, sl], pb[:, :], k.gbc[:, sl], ALU.mult)
    STT(S, "dve", [x_buf, t0], [t0], t0[:, :], x_buf[:, :], ALPHA, t0[:, :], ALU.mult, ALU.add)
    st, mv, ve, rstd, nb = nb_small["st"], nb_small["mv"], nb_small["ve"], nb_small["rstd"], nb_small["nb"]
    S.op("dve", lambda e: e.bn_stats(st[:, 0, :], t0[:, 0:512]), [t0], [st])
    S.op("dve", lambda e: e.bn_stats(st[:, 1, :], t0[:, 512:1024]), [t0], [st])
    S.op("dve", lambda e: e.bn_aggr(mv[:, :], st[:, :, :].rearrange("p a b -> p (a b)")), [st], [mv])
    TS(S, "dve", [mv], [ve], ve[:, :], mv[:, 1:2], EPS, None, ALU.add)
    TT(S, "pool", [ve, k.cm05], [rstd], rstd[:, :], ve[:, :], k.cm05[:, :], ALU.pow)
    STT(S, "dve", [mv, rstd], [nb], nb[:, :], mv[:, 0:1], -1.0, rstd[:, :], ALU.mult, ALU.mult)
    ACT(S, [t0, rstd, nb], [t1], t1[:, :], t0[:, :], AF.Identity, bias=nb[:, 0:1], scale=rstd[:, 0:1])
    TT(S, "pool", [t1, k.lng], [t1], t1[:, :], t1[:, :], k.lng[:, :], ALU.mult)
    TT(S, "pool", [t1, k.lnb], [out_buf], out_buf[:, :], t1[:, :], k.lnb[:, :], ALU.add)


def alloc_rsm(k):
    S = k.S
    return {"st": S.sb("r_st", [128, 2, 6]), "mv": S.sb("r_mv", [128, 2]), "ve": S.sb("r_ve", [128, 1]),
            "rstd": S.sb("r_rstd", [128, 1]), "nb": S.sb("r_nb", [128, 1])}


def seq_src_dst(k, l, phase):
    cfg = k.cfg
    mode = getattr(cfg, "mode", "full")
    if phase == "A":
        src = None if l == 0 else k.XB
        dst = k.XA if mode == "full" else None
    else:
        src = k.XA if mode == "full" else None
        dst = None if l == cfg.L - 1 else k.XB
    return src, dst


def rows_ap(k, handle, which_io, r0, n):
    cfg = k.cfg
    if handle is not None:
        return handle[r0:r0 + n, :]
    if r0 < cfg.Ts:
        t = k.I["xs"] if which_io == "in" else k.O["ys"]
        return t[r0:r0 + n, :]
    t = k.I["xp"] if which_io == "in" else k.O["yp"]
    return t[r0 - cfg.Ts:r0 - cfg.Ts + n, :]


def phaseB(k, l):
    S, I, cfg = k.S, k.I, k.cfg
    m = S.mark()
    w_up = S.sb("w_up", [128, 8, 2 * DFF], BF16)
    w_dn = S.sb("w_dn", [128, 22, D], BF16)
    m2 = S.mark()
    stage = [S.sb(f"wstage{j}", [128, 2840]) for j in range(2)]
    k.wl_n = 0
    load_weight(k, w_up, I["w_up"][l], 8, 2 * DFF, stage)
    load_weight(k, w_dn, I["w_dn"][l], 22, D, stage)
    S.release(m2)
    S.dma("sp", k.lng[:, :], I["ln2_g"][l:l + 1, :].partition_broadcast(128), writes=[k.lng])
    S.dma("sp", k.lnb[:, :], I["ln2_b"][l:l + 1, :].partition_broadcast(128), writes=[k.lnb])
    alloc_hsm(k)
    rsm = alloc_rsm(k)
    SEG = 256
    h2T = [S.sb(f"h2T{j}", [128, 8, SEG + 2], BF16) for j in range(2)]
    actT = S.sb("actT", [128, 22, SEG], BF16)
    xt = [S.sb(f"xtB{j}", [128, D]) for j in range(4)]
    xh = k.ttmp[0]
    NBUF = 3
    cg = [S.sb(f"cg{j}", [128, SEG]) for j in range(NBUF)]
    cv = [S.sb(f"cv{j}", [128, SEG]) for j in range(NBUF)]
    th = [S.sb(f"th{j}", [128, SEG]) for j in range(NBUF)]
    src, dst = seq_src_dst(k, l, "B")
    segs = []
    for (sname, T, cond, off) in cfg.seqs:
        for t0 in range(0, T, SEG):
            segs.append((T, cond, off, t0))
    state = {}

    def prep(si):
        T, cond, off, t0 = segs[si]
        hT = h2T[si % 2]
        r0 = off + t0
        xts = []
        for j in range(SEG // 128):
            xb = xt[(2 * si + j) % 4]
            xts.append(xb)
            make_hT(k, l, 2, cond, [(rows_ap(k, src, "in", r0 + 128 * j, 128), 128)], xb,
                    [hT[:, kc, 1 + 128 * j:1 + 128 * (j + 1)] for kc in range(8)], hT, [])
        rows, cols = [], []
        if t0 > 0:
            rows.append((rows_ap(k, src, "in", r0 - 1, 1), 1))
            cols.append(0)
        else:
            MSET(S, "pool", [hT], hT[:, :, 0:1], 0.0)
        if t0 + SEG < T:
            rows.append((rows_ap(k, src, "in", r0 + SEG, 1), 1))
            cols.append(SEG + 1)
        else:
            MSET(S, "pool", [hT], hT[:, :, SEG + 1:SEG + 2], 0.0)
        if len(rows) == 2:
            make_hT(k, l, 2, cond, rows, xh, [hT[:, kc, 0:SEG + 2:SEG + 1] for kc in range(8)], hT, [])
        elif len(rows) == 1:
            c = cols[0]
            make_hT(k, l, 2, cond, rows, xh, [hT[:, kc, c:c + 1] for kc in range(8)], hT, [])
        state[si] = xts

    def ffn(si):
        T, cond, off, t0 = segs[si]
        hT = h2T[si % 2]
        r0 = off + t0
        xts = state.pop(si)
        for c in range(22):
            pg, pv = S.bank(), S.bank()
            MM(S, [w_up, hT], [pg], [(pg[:, 0:SEG + 2], w_up[:, kc, c * 128:(c + 1) * 128], hT[:, kc, :], kc == 0, kc == 7)
                                      for kc in range(8)])
            MM(S, [w_up, hT], [pv], [(pv[:, 0:SEG + 2], w_up[:, kc, DFF + c * 128:DFF + (c + 1) * 128], hT[:, kc, :], kc == 0, kc == 7)
                                      for kc in range(8)])
            g_, v_, t_ = cg[c % NBUF], cv[c % NBUF], th[c % NBUF]
            fw = k.fcw
            ACT(S, [pg, fw], [g_], g_[:, :], pg[:, 1:SEG + 1], AF.Identity, bias=fw[:, 3, c:c + 1], scale=fw[:, 1, c:c + 1])
            STT(S, "dve", [pg, fw, g_], [g_], g_[:, :], pg[:, 0:SEG], fw[:, 0, c:c + 1], g_[:, :], ALU.mult, ALU.add)
            STT(S, "dve", [pg, fw, g_], [g_], g_[:, :], pg[:, 2:SEG + 2], fw[:, 2, c:c + 1], g_[:, :], ALU.mult, ALU.add)
            cc = 22 + c
            ACT(S, [pv, fw], [v_], v_[:, :], pv[:, 1:SEG + 1], AF.Identity, bias=fw[:, 3, cc:cc + 1], scale=fw[:, 1, cc:cc + 1])
            STT(S, "dve", [pv, fw, v_], [v_], v_[:, :], pv[:, 0:SEG], fw[:, 0, cc:cc + 1], v_[:, :], ALU.mult, ALU.add)
            STT(S, "dve", [pv, fw, v_], [v_], v_[:, :], pv[:, 2:SEG + 2], fw[:, 2, cc:cc + 1], v_[:, :], ALU.mult, ALU.add)
            ACT(S, [g_], [t_], t_[:, :], g_[:, :], AF.Silu)
            TT(S, "pool", [t_, v_], [actT], actT[:, c, :], t_[:, :], v_[:, :], ALU.mult)
        for j in range(SEG // 128):
            p0, p1 = S.bank(), S.bank()
            for hlf, pb in enumerate((p0, p1)):
                MM(S, [actT, w_dn], [pb], [(pb[:, :], actT[:, c, 128 * j:128 * (j + 1)], w_dn[:, c, hlf * 512:(hlf + 1) * 512],
                                              c == 0, c == 21) for c in range(22)])
            ob = xts[j]
            resid_ln(k, xts[j], (p0, p1), ob, rsm)
            db = S.dbuf(("xout", l, (r0 + 128 * j) // 128))
            S.dma("pool", rows_ap(k, dst, "out", r0 + 128 * j, 128), ob[:, :], reads=[ob], writes=[db])
            if dst is None:
                k.final_bufs.append(db)

    cur_cond = None
    prep(0)
    for si in range(len(segs)):
        if si + 1 < len(segs):
            prep(si + 1)
        if segs[si][1] != cur_cond:
            cur_cond = segs[si][1]
            gate_table(k, l, 5, cur_cond)
        ffn(si)
    S.release(m)


def layer(k, l):
    cfg = k.cfg
    load_layer_consts(k, l)
    mode = getattr(cfg, "mode", "full")
    if mode in ("full", "A"):
        phaseA(k, l)
    if mode in ("full", "B"):
        phaseB(k, l)


def shard_inputs(inp, cfg, core):
    f = lambda a: np.ascontiguousarray(np.asarray(a), dtype=np.float32)
    NP = cfg.NP
    cst, _, cst2, _ = make_consts()
    m = {
        "xs": f(inp["x_sample"][core]),
        "xp": f(inp["x_prompt"][NP * core:NP * (core + 1)]).reshape(NP * cfg.Tp, D),
        "st_c": f(inp["state_mlstm_c"][core]), "st_n": f(inp["state_mlstm_n"][core]),
        "st_m": f(inp["state_mlstm_m"][core]).reshape(2, 8), "st_s": f(inp["state_ssd"][core]),
        "cond": f(np.stack([np.asarray(inp["c"])[core], np.asarray(inp["c_ctx"])], 0)),
        "w_ada": f(inp["w_ada"]), "b_ada": f(inp["b_ada"]), "w_in": f(inp["w_in"]),
        "b_ig": f(inp["b_igate"]).reshape(2, 8), "b_fg": f(inp["b_fgate"]).reshape(2, 8),
        "mnorm_g": f(inp["mlstm_norm_g"]), "w_pool": f(inp["w_pool"]), "pool_scale": f(inp["pool_scale"]),
        "w_sp": f(inp["w_spatial"]), "b_sp": f(inp["b_spatial"]), "sconv_w": f(inp["ssd_conv_w"]),
        "sconv_b": f(inp["ssd_conv_b"]), "dt_bias": f(inp["ssd_dt_bias"]).reshape(2, 8),
        "a_log": f(inp["ssd_a_log"]).reshape(2, 8), "ssd_d": f(inp["ssd_d"]), "snorm_g": f(inp["ssd_norm_g"]),
        "w_out": f(inp["w_out"]), "ln1_g": f(inp["ln1_g"]), "ln1_b": f(inp["ln1_b"]), "w_up": f(inp["ffn_w_up"]),
        "fconv_w": f(inp["ffn_conv_w"]), "fconv_b": f(inp["ffn_conv_b"]), "w_dn": f(inp["ffn_w_down"]),
        "ln2_g": f(inp["ln2_g"]), "ln2_b": f(inp["ln2_b"]), "cst": cst, "cst2": cst2,
    }
    return m


def phaseA(k, l):
    S, I, cfg = k.S, k.I, k.cfg
    m = S.mark()
    w_in = S.sb("w_in", [128, 8, DIN], BF16)
    w_out = S.sb("w_out", [128, 8, D], BF16)
    m2 = S.mark()
    stage = [S.sb(f"wstageA{j}", [128, 2840]) for j in range(2)]
    k.wl_n = 0
    load_weight(k, w_in, I["w_in"][l], 8, DIN, stage)
    load_weight(k, w_out, I["w_out"][l], 8, D, stage)
    S.release(m2)
    c2b = load_cst2(k)
    S.dma("sp", k.lng[:, :], I["ln1_g"][l:l + 1, :].partition_broadcast(128), writes=[k.lng])
    S.dma("sp", k.lnb[:, :], I["ln1_b"][l:l + 1, :].partition_broadcast(128), writes=[k.lnb])
    alloc_hsm(k)
    rsm = alloc_rsm(k)
    a = K()
    a.w_in, a.w_out, a.c2b, a.rsm, a.l = w_in, w_out, c2b, rsm, l
    a.hb = [S.sb(f"hb{j}", [128, 8, 130], BF16) for j in range(3)]
    a.xq = [S.sb(f"xq{j}", [128, D]) for j in range(2)]
    a.pet = [S.sb(f"petile{j}", [128, D]) for j in range(2)] if l == 0 else None
    if l == 0:
        edb = S.dbuf("ED")
        for j in range(2):
            S.dma("sp", a.pet[j][0:64, 512:1024], k.ED[:, :], reads=[edb], writes=[a.pet[j]])
            S.dma("sp", a.pet[j][64:128, 512:1024], k.ED[:, :], reads=[edb], writes=[a.pet[j]])
    sb = S.sb
    a.Cn, a.Cnb = sb("Cn", [128, 2, 65]), sb("Cnb", [128, 2, 66], BF16)
    a.Hs, a.Hsb = sb("Hs", [128, 4, 64]), sb("Hsb", [128, 4, 64], BF16)
    a.p = []
    for par in range(2):
        q = K()
        a.p.append(q)
        q.qkT = sb(f"qkT{par}", [128, 4, 128], BF16)
        q.k_tm = sb(f"k_tm{par}", [128, 256], BF16)
        q.v_sb = sb(f"v_sb{par}", [128, 256], BF16)
        q.XBCb = sb(f"XBCb{par}", [128, 6, 128], BF16)
        q.x_tm, q.B_tm = sb(f"x_tm{par}", [128, 256], BF16), sb(f"B_tm{par}", [128, 2, 128], BF16)
        q.stash_bufs = [q.qkT, q.k_tm, q.v_sb, q.XBCb, q.x_tm, q.B_tm]
        e0 = q.qkT.off // 2
        q.stash_ap = S.arena.bitcast(BF16)[0:128, e0:e0 + 2304]
        assert q.B_tm.off + 512 == q.qkT.off + 4608, "stash group must be contiguous"
        q.og = sb(f"og{par}", [128, 280])
        q.G8, q.E8, q.SP8, q.igb = sb(f"G8{par}", [128, 8]), sb(f"E8{par}", [128, 8]), sb(f"SP8{par}", [128, 8]), sb(f"igb{par}", [128, 4])
        q.r8, q.logdec, q.cum, q.e8 = sb(f"r8{par}", [128, 8]), sb(f"logdec{par}", [128, 8]), sb(f"cum{par}", [128, 8]), sb(f"e8{par}", [128, 8])
        q.wend, q.aL, q.tmp8 = sb(f"wend{par}", [128, 8]), sb(f"aL{par}", [128, 8]), sb(f"tmp8{par}", [128, 8])
        q.L1 = sb(f"L1{par}", [128, 8, 128])
        q.DIFF = sb(f"DIFF{par}", [128, 8, 128])
        q.PTm = sb(f"PTm{par}", [128, 4, 128], BF16)
        q.PTs = sb(f"PTs{par}", [128, 4, 128], BF16)
        q.xt_m, q.xh_m = sb(f"xt_m{par}", [128, 4, 66], BF16), sb(f"xh_m{par}", [128, 4, 66], BF16)
        q.XBC, q.XBCe = sb(f"XBC{par}", [128, 6, 128]), sb(f"XBCe{par}", [128, 6, 128])
        q.xt_s, q.xh_s = sb(f"xt_s{par}", [128, 4, 64], BF16), sb(f"xh_s{par}", [128, 4, 64], BF16)
        q.NUM = sb(f"NUM{par}", [128, 4, 65])
        q.den = sb(f"den{par}", [128, 4])
        q.Ysc = sb(f"Ysc{par}", [128, 4, 64])
    a.HY = [sb(f"HY{j}", [128, 512]) for j in range(2)]
    a.HYf = [sb(f"HYf{j}", [128, 512]) for j in range(2)]
    a.eo, a.z_sb, a.ez, a.gu = sb("eo", [128, 256]), sb("z_sb", [128, 256]), sb("ez", [128, 256]), sb("gu", [128, 256])
    a.gvb = sb("gvb", [128, 256], BF16)
    a.pc, a.pcP, a.pcN = sb("pc", [128, 256]), sb("pcP", [8, 256]), sb("pcN", [8, 256])
    a.plb, a.plT = sb("plb", [128, 256], BF16), sb("plT", [128, 2, 128], BF16)
    a.fin1, a.fin2, a.fin3 = sb("fin1", [128, 256]), sb("fin2", [128, 256]), sb("fin3", [128, 256])
    a.st4, a.st4b = sb("st4", [128, 4]), sb("st4b", [128, 4])
    a.yall = sb("yall", [128, 3, 256], BF16)
    a.concatT = sb("concatT", [128, 8, 128], BF16)
    a.gst, a.gmv, a.gve, a.grs = sb("gst", [128, 6]), sb("gmv", [128, 2]), sb("gve", [128, 1]), sb("grs", [128, 1])
    a.mrun = sb("mrun", [4, 1])
    a.mt = sb("mt", [4, 2])
    for par in range(2):
        a.p[par].dec = sb(f"dec{par}", [128, 8])
    a.sio = sb("sio", [128, 4, 128])
    src, dst = seq_src_dst(k, l, "A")
    a.src, a.dst = src, dst
    cur_cond = None
    for si, (sname, T, cond, off) in enumerate(cfg.seqs):
        if cond != cur_cond:
            gate_table(k, l, 2, cond)
            cur_cond = cond
        runseq(k, a, si, T, cond, off)
    S.release(m)


def runseq(k, a, si, T, cond, off):
    S, cfg, l = k.S, k.cfg, a.l
    nt = T // 128
    is_sample = (si == 0)
    tile0 = off // 128
    w_in = a.w_in

    def hbuf(i):
        return a.hb[i % 3]

    def fix_halo(lo, hi):
        CP(S, "pool", [hbuf(hi)], [hbuf(lo)], hbuf(lo)[:, :, 129:130], hbuf(hi)[:, :, 1:2])
        CP(S, "pool", [hbuf(lo)], [hbuf(hi)], hbuf(hi)[:, :, 0:1], hbuf(lo)[:, :, 128:129])

    def ensure1(i):
        hb = hbuf(i)
        xb = a.xq[i % 2]
        pos = None
        if l == 0 and is_sample:
            pos = a.pet[i % 2]
            edb = S.dbuf("ED")
            S.dma("sp", pos[0:64, 0:512], k.ED[2 * i:2 * i + 1, :].partition_broadcast(64), reads=[edb], writes=[pos])
            S.dma("sp", pos[64:128, 0:512], k.ED[2 * i + 1:2 * i + 2, :].partition_broadcast(64), reads=[edb], writes=[pos])
        make_hT(k, l, 1, cond, [(rows_ap(k, a.src, "in", off + 128 * i, 128), 128)], xb,
                [hb[:, kc, 1:129] for kc in range(8)], hb, [], pos_tile=pos)
        db = S.dbuf(("HT", tile0 + i))
        S.dma("pool", k.HT[tile0 + i].rearrange("p (kc t) -> p kc t", kc=8), hb[:, :, 1:129], reads=[hb], writes=[db])
        if i == 0:
            MSET(S, "pool", [hb], hb[:, :, 0:1], 0.0)
        else:
            fix_halo(i - 1, i)
        if i == nt - 1:
            MSET(S, "pool", [hb], hb[:, :, 129:130], 0.0)

    def ensure2(i):
        hb = hbuf(i)
        db = S.dbuf(("HT", tile0 + i))
        S.dma("sp", hb[:, :, 1:129], k.HT[tile0 + i].rearrange("p (kc t) -> p kc t", kc=8), reads=[db], writes=[hb])
        xb = a.xq[i % 2]
        S.dma("sp", xb[:, :], rows_ap(k, a.src, "in", off + 128 * i, 128), writes=[xb])
        if l == 0 and is_sample:
            pos = a.pet[i % 2]
            edb = S.dbuf("ED")
            S.dma("sp", pos[0:64, 0:512], k.ED[2 * i:2 * i + 1, :].partition_broadcast(64), reads=[edb], writes=[pos])
            S.dma("sp", pos[64:128, 0:512], k.ED[2 * i + 1:2 * i + 2, :].partition_broadcast(64), reads=[edb], writes=[pos])
            TT(S, "pool", [xb, pos], [xb], xb[:, :], xb[:, :], pos[:, :], ALU.add)
        hf = a.HYf[i % 2]
        S.dma("sp", hf[:, :], k.HF[off + 128 * i:off + 128 * (i + 1), :], reads=[S.dbuf(("HF", tile0 + i))], writes=[hf])
        q = a.p[i % 2]
        S.dma("sp", q.stash_ap, k.STB[tile0 + i], reads=[S.dbuf(("STB", tile0 + i))], writes=q.stash_bufs)
        S.dma("sp", q.og[:, :], k.STG[tile0 + i], reads=[S.dbuf(("STG", tile0 + i))], writes=[q.og])
        if i == nt - 1:
            MSET(S, "pool", [hb], hb[:, :, 129:130], 0.0)
        else:
            fix_halo(i, i + 1)
        if i == 0:
            MSET(S, "pool", [hb], hb[:, :, 0:1], 0.0)

    for d in ((0,) if getattr(cfg, "stop", 99) <= 4 else (0, 1)):
        init_state(k, a, si, d, is_sample)
        order = list(range(nt)) if d == 0 else list(range(nt - 1, -1, -1))
        ens = ensure1 if d == 0 else ensure2
        ens(order[0])
        for n, i in enumerate(order):
            if n + 1 < len(order):
                ens(order[n + 1])
            tileA(k, a, si, T, cond, off, i, d, nt, is_sample)
        if not is_sample and getattr(cfg, "stop", 99) > 5:
            final_state(k, a, si, d)


def init_state(k, a, si, d, is_sample):
    S, I, l = k.S, k.I, a.l
    if not is_sample:
        MSET(S, "pool", [a.Cn], a.Cn[:, :, :], 0.0)
        MSET(S, "pool", [a.Cnb], a.Cnb[:, :, :], 0.0)
        MSET(S, "pool", [a.Hs], a.Hs[:, :, :], 0.0)
        MSET(S, "pool", [a.Hsb], a.Hsb[:, :, :], 0.0)
        MSET(S, "pool", [a.mrun], a.mrun[:, :], 0.0)
        return
    for h in range(4):
        pr = slice((h % 2) * 64, (h % 2) * 64 + 64)
        S.dma("sp", a.Cn[pr, h // 2, 0:64], I["st_c"][l, d, h], writes=[a.Cn])
        S.dma("sp", a.Cn[pr, h // 2, 64:65], I["st_n"][l, d, h].rearrange("(p o) -> p o", o=1), writes=[a.Cn])
    S.dma("sp", a.st4[:, :], I["st_m"][l:l + 1, 4 * d:4 * d + 4].partition_broadcast(128), writes=[a.st4])
    ACT(S, [a.st4], [a.st4b], a.st4b[:, :], a.st4[:, :], AF.Exp)
    for h in range(4):
        pr = slice((h % 2) * 64, (h % 2) * 64 + 64)
        TS(S, "dve", [a.Cn, a.st4b], [a.Cn], a.Cn[pr, h // 2, :], a.Cn[pr, h // 2, :], a.st4b[pr, h:h + 1], None, ALU.mult)
    CP(S, "pool", [a.Cn], [a.Cnb], a.Cnb[:, :, 0:65], a.Cn[:, :, :])
    S.dma("sp", a.sio[0:64, :, :], I["st_s"][l, d].rearrange("h p n -> p h n"), writes=[a.sio])
    pb = S.bank()
    TR(S, [a.sio, k.cstb], [pb], [(pb[:, h * 64:(h + 1) * 64], a.sio[0:64, h, :], cview(k, "ident")[0:64, 0:64]) for h in range(4)])
    CP(S, "dve", [pb], [a.Hs], a.Hs[:, :, :], pb[:, 0:256].rearrange("p (h q) -> p h q", h=4))
    CP(S, "act", [pb], [a.Hsb], a.Hsb[:, :, :], pb[:, 0:256].rearrange("p (h q) -> p h q", h=4))


def final_state(k, a, si, d):
    S, O, l = k.S, k.O, a.l
    j = si - 1
    dg = a.p[0].tmp8
    TS(S, "dve", [k.cstb, a.mrun], [dg], dg[0:4, 0:4], cview(k, "ident")[0:4, 0:4], a.mrun[0:4, 0:1], None, ALU.mult)
    pb = S.bank()
    MM(S, [dg, k.cstb], [pb], [(pb[:, 0:4], cview(k, "ones")[0:4, :], dg[0:4, 0:4], True, True)])
    ACT(S, [pb], [a.st4b], a.st4b[:, :], pb[:, 0:4], AF.Exp, scale=-1.0)
    stg = a.sio
    sv = stg[:, 0:2, 0:65]
    for h in range(4):
        pr = slice((h % 2) * 64, (h % 2) * 64 + 64)
        TS(S, "dve", [a.Cn, a.st4b], [stg], stg[pr, h // 2, 0:65], a.Cn[pr, h // 2, :], a.st4b[pr, h:h + 1], None, ALU.mult)
    outs = []
    for h in range(4):
        pr = slice((h % 2) * 64, (h % 2) * 64 + 64)
        db = S.dbuf(("oc", j, l, d, h))
        S.dma("pool", O["oc"][j, l, d, h], stg[pr, h // 2, 0:64], reads=[stg], writes=[db])
        db2 = S.dbuf(("on", j, l, d, h))
        S.dma("pool", O["on"][j, l, d, h].rearrange("(p o) -> p o", o=1), stg[pr, h // 2, 64:65], reads=[stg], writes=[db2])
        outs += [db, db2]
    db = S.dbuf(("om", j, l, d))
    S.dma("pool", O["om"][j, l, 4 * d:4 * d + 4].rearrange("(p o) -> p o", o=1), a.mrun[0:4, 0:1], reads=[a.mrun], writes=[db])
    outs.append(db)
    pb2 = S.bank()
    TR(S, [a.Hs, k.cstb], [pb2], [(pb2[0:64, h * 128:(h + 1) * 128], a.Hs[:, h, :], cview(k, "ident")) for h in range(4)])
    CP(S, "dve", [pb2, stg], [stg], stg[0:64, :, :], pb2[0:64, :].rearrange("p (h n) -> p h n", h=4))
    db = S.dbuf(("os", j, l, d))
    S.dma("pool", O["os"][j, l, d].rearrange("h p n -> p h n"), stg[0:64, :, :], reads=[stg], writes=[db])
    outs.append(db)
    k.final_bufs += outs


def tileA(k, a, si, T, cond, off, i, d, nt, is_sample):
    S, l = k.S, a.l
    PS = a.p[i % 2]
    w_in = a.w_in
    hb = a.hb[i % 3]
    hcur = lambda kc: hb[:, kc, 1:129]
    tri = cview(k, "tri%d" % d)
    neg = cview(k, "neg%d" % d)
    endc = 127 if d == 0 else 0
    full = (d == 1)
    cst = k.cstb

    og = PS.og
    ps1 = ps2 = None
    if not full:
        ps1, ps2 = S.bank(), S.bank()
        MM(S, [hb, w_in], [ps1], [(ps1[:, 0:512], hcur(kc), w_in[:, kc, 256:768], kc == 0, kc == 7) for kc in range(8)])
        MM(S, [hb, w_in], [ps2], [(ps2[:, 0:272], hcur(kc), w_in[:, kc, 768:1040], kc == 0, kc == 7) for kc in range(8)]
           + [(ps2[:, 272:280], hcur(kc), w_in[:, kc, 2832:2840], kc == 0, kc == 7) for kc in range(8)])
        CP(S, "act", [ps2], [og], og[:, :], ps2[:, 0:280])
        ACT(S, [ps1], [PS.k_tm], PS.k_tm[:, :], ps1[:, 0:256], AF.Identity, scale=0.125)
        CP(S, "act", [ps1], [PS.v_sb], PS.v_sb[:, :], ps1[:, 256:512])
    G8, E8, SP8, igb, r8, logdec, cum, e8, wend, aL, tmp8 = (PS.G8, PS.E8, PS.SP8, PS.igb, PS.r8, PS.logdec, PS.cum, PS.e8,
                                                             PS.wend, PS.aL, PS.tmp8)
    STT(S, "dve", [og, k.bif], [G8], G8[:, 0:4], og[:, 264 + 4 * d:268 + 4 * d], -1.0, k.bif[:, 8 + 4 * d:12 + 4 * d], ALU.mult, ALU.add)
    TT(S, "dve", [og, k.dtb], [G8], G8[:, 4:8], og[:, 272 + 4 * d:276 + 4 * d], k.dtb[:, 4 * d:4 * d + 4], ALU.add)
    TT(S, "dve", [og, k.bif], [igb], igb[:, :], og[:, 256 + 4 * d:260 + 4 * d], k.bif[:, 4 * d:4 * d + 4], ALU.add)
    if full:
        ACT(S, [og], [a.eo], a.eo[:, :], og[:, 0:256], AF.Exp, scale=-1.0)
    ACT(S, [G8], [E8], E8[:, :], G8[:, :], AF.Exp)
    ACT(S, [E8], [SP8], SP8[:, :], E8[:, :], AF.Ln, bias=1.0)
    ACT(S, [igb], [r8], r8[:, 0:4], igb[:, :], AF.Exp)
    CP(S, "pool", [SP8], [r8], r8[:, 4:8], SP8[:, 4:8])
    TT(S, "dve", [SP8, k.coef], [logdec], logdec[:, :], SP8[:, :], k.coef[:, d, :], ALU.mult)
    CP(S, "act", [logdec], [PS.L1], PS.L1[:, :, :], bc(logdec[:, 0:8].unsqueeze(2), [128, 8, 128]))
    psL = [S.bank(), S.bank()]
    for hh in range(2):
        MM(S, [PS.L1, cst], [psL[hh]], [(psL[hh][:, q * 128:(q + 1) * 128], PS.L1[:, hh * 4 + q, :], tri, True, True) for q in range(4)])
    psC = S.bank()
    MM(S, [logdec, cst], [psC], [(psC[:, 0:8], tri, logdec[:, 0:8], True, True)])
    CP(S, "dve", [psC], [cum], cum[:, :], psC[:, 0:8])
    for h in range(8):
        pl = psL[h // 4]
        q = h % 4
        STT(S, "dve", [pl, cum, cst], [PS.DIFF], PS.DIFF[:, h, :], pl[:, q * 128:(q + 1) * 128], cum[:, h:h + 1], neg, ALU.subtract, ALU.add)
    ACT(S, [PS.DIFF], [PS.DIFF], PS.DIFF[:, :, :], PS.DIFF[:, :, :], AF.Exp)
    ACT(S, [cum], [e8], e8[:, :], cum[:, :], AF.Exp)
    for hh in range(2):
        TT(S, "dve", [psL[hh], cum], [tmp8], tmp8[:, hh * 4:hh * 4 + 4], psL[hh][:, endc:512:128], cum[:, hh * 4:hh * 4 + 4], ALU.subtract)
        ACT(S, [psL[hh]], [aL], aL[:, hh * 4:hh * 4 + 4], psL[hh][:, endc:512:128], AF.Exp)
    ACT(S, [tmp8], [wend], wend[:, :], tmp8[:, :], AF.Exp)
    TT(S, "dve", [wend, r8], [wend], wend[:, :], wend[:, :], r8[:, :], ALU.mult)
    if not is_sample:
        TT(S, "dve", [tmp8, igb], [PS.dec], PS.dec[:, 0:4], tmp8[:, 0:4], igb[:, :], ALU.add)
        TT(S, "dve", [tmp8, cum], [PS.dec], PS.dec[:, 4:8], tmp8[:, 0:4], cum[:, 0:4], ALU.add)
        pm = S.bank()
        TR(S, [PS.dec, cst], [pm], [(pm[0:4, 0:128], PS.dec[:, 0:4], cview(k, "ident")),
                                   (pm[0:4, 128:256], PS.dec[:, 4:8], cview(k, "ident"))])
        S.op("dve", lambda e: e.tensor_reduce(a.mt[0:4, 0:1], pm[0:4, 0:128], AX.X, ALU.max), [pm], [a.mt])
        TT(S, "dve", [pm, a.mrun], [a.mt], a.mt[0:4, 1:2], pm[0:4, 128:129], a.mrun[0:4, 0:1], ALU.add)
        TT(S, "dve", [a.mt], [a.mrun], a.mrun[0:4, 0:1], a.mt[0:4, 0:1], a.mt[0:4, 1:2], ALU.max)

    if getattr(k.cfg, "stop", 99) <= 1:
        return
    qkT = PS.qkT
    if not full:
        psQ = [S.bank(), S.bank()]
        for hh in range(2):
            MM(S, [hb, w_in], [psQ[hh]], [(psQ[hh][:, q * 130:(q + 1) * 130], w_in[:, kc, (hh * 2 + q) * 128:(hh * 2 + q + 1) * 128],
                                            hb[:, kc, 0:130], kc == 0, kc == 7) for q in range(2) for kc in range(8)])
        CP(S, "act", [psQ[0]], [qkT], qkT[:, 0:2, :], psQ[0][:, 0:260].rearrange("p (b t) -> p b t", b=2)[:, :, 1:129])
        ACT(S, [psQ[1]], [qkT], qkT[:, 2:4, :], psQ[1][:, 0:260].rearrange("p (b t) -> p b t", b=2)[:, :, 1:129], AF.Identity, scale=0.125)
    v4 = PS.v_sb[:, :].rearrange("p (h e) -> p h e", h=4)
    TT(S, "dve", [PS.v_sb, r8], [PS.xt_m], PS.xt_m[:, :, 0:64], v4, bc(r8[:, 0:4].unsqueeze(2), [128, 4, 64]), ALU.mult)
    if getattr(k.cfg, "stop", 99) <= 1.12:
        return
    CP(S, "pool", [r8], [PS.xt_m], PS.xt_m[:, :, 64:65], r8[:, 0:4].unsqueeze(2))
    if getattr(k.cfg, "stop", 99) <= 1.15:
        return
    EXP = getattr(k.cfg, "exp", "")
    if EXP != "noTT":
        TT(S, "dve", [PS.v_sb, wend], [PS.xh_m], PS.xh_m[:, :, 0:64], v4, bc((r8 if EXP == "r8" else wend)[:, 0:4].unsqueeze(2), [128, 4, 64]), ALU.mult)
    if EXP != "noCP":
        CP(S, "pool", [wend], [PS.xh_m], PS.xh_m[:, :, 64:65], wend[:, 0:4].unsqueeze(2))
    if getattr(k.cfg, "stop", 99) <= 1.2:
        return
    psS = [S.bank(), S.bank()]
    hp = lambda h: slice((h % 2) * 64, (h % 2) * 64 + 64)
    for par in range(2):
        MM(S, [qkT], [psS[par]], [(psS[par][:, (h // 2) * 128:(h // 2 + 1) * 128], qkT[hp(h), 2 + h // 2, :], qkT[hp(h), h // 2, :], True, True)
                                  for h in (par, par + 2)])
    for par in range(2):
        TT(S, "dve", [psS[par], PS.DIFF], [PS.PTm], PS.PTm[:, par:4:2, :], psS[par][:, 0:256].rearrange("p (h t) -> p h t", h=2),
           PS.DIFF[:, par:4:2, :], ALU.mult)
    if getattr(k.cfg, "stop", 99) <= 1.4:
        return
    psO = S.bank()
    psI = [S.bank(), S.bank()]
    MM(S, [PS.PTm, PS.xt_m], [psO], [(psO[:, h * 65:h * 65 + 65], PS.PTm[:, h, :], PS.xt_m[:, h, 0:65], True, True) for h in range(4)])
    for par in range(2):
        MM(S, [qkT, a.Cnb], [psI[par]], [(psI[par][:, (h // 2) * 65:(h // 2) * 65 + 65], qkT[hp(h), h // 2, :], a.Cnb[hp(h), h // 2, 0:65], True, True)
                                         for h in (par, par + 2)])
    NUM = PS.NUM
    for par in range(2):
        TT(S, "dve", [psI[par], e8], [NUM], NUM[:, par:4:2, :], psI[par][:, 0:130].rearrange("p (h e) -> p h e", h=2),
           bc(e8[:, par:4:2].unsqueeze(2), [128, 2, 65]), ALU.mult)
    TT(S, "dve", [psO, NUM], [NUM], NUM[:, :, :], psO[:, 0:260].rearrange("p (h e) -> p h e", h=4), NUM[:, :, :], ALU.add)
    if getattr(k.cfg, "stop", 99) <= 1.6:
        return
    HY = a.HY[i % 2]
    ACT(S, [NUM], [PS.den], PS.den[:, :].unsqueeze(2), NUM[:, :, 64:65], AF.Abs)
    TS(S, "dve", [PS.den], [PS.den], PS.den[:, :], PS.den[:, :], 1.0, None, ALU.max)
    TT(S, "pool", [PS.den, k.cm1], [PS.den], PS.den[:, :], PS.den[:, :], bc(k.cm1[:, 0:1], [128, 4]), ALU.pow)
    TT(S, "pool", [NUM, PS.den], [HY], HY[:, 0:256].rearrange("p (h e) -> p h e", h=4), NUM[:, :, 0:64],
       bc(PS.den[:, :].unsqueeze(2), [128, 4, 64]), ALU.mult)
    if getattr(k.cfg, "stop", 99) <= 1.8:
        return
    psU = S.bank()
    MM(S, [PS.k_tm, PS.xh_m], [psU], [(psU[:, h * 65:h * 65 + 65], PS.k_tm[:, (h // 2) * 128:(h // 2 + 1) * 128], PS.xh_m[:, h, 0:65], True, True)
                                     for h in range(4)])
    for h in range(4):
        STT(S, "dve", [a.Cn, aL, psU], [a.Cn], a.Cn[hp(h), h // 2, :], a.Cn[hp(h), h // 2, :], aL[hp(h), h:h + 1],
            psU[hp(h), h * 65:h * 65 + 65], ALU.mult, ALU.add)
    CP(S, "act", [a.Cn], [a.Cnb], a.Cnb[:, :, 0:65], a.Cn[:, :, :])

    if getattr(k.cfg, "stop", 99) <= 2:
        return
    XBC, XBCe, XBCb = PS.XBC, PS.XBCe, PS.XBCb
    if not full:
        psX = [S.bank(), S.bank()]
        for hh in range(2):
            MM(S, [hb, w_in], [psX[hh]], [(psX[hh][:, q * 130:q * 130 + 130], w_in[:, kc, 2064 + (hh * 3 + q) * 128:2064 + (hh * 3 + q + 1) * 128],
                                            hb[:, kc, 0:130], kc == 0, kc == 7) for q in range(3) for kc in range(8)])
        for b in range(6):
            pb, c0 = psX[b // 3], (b % 3) * 130
            ACT(S, [pb, k.scw], [XBC], XBC[:, b, :], pb[:, c0 + 1:c0 + 129], AF.Identity, bias=k.scw[:, 3, b:b + 1], scale=k.scw[:, 1, b:b + 1])
            STT(S, "dve", [pb, k.scw, XBC], [XBC], XBC[:, b, :], pb[:, c0:c0 + 128], k.scw[:, 0, b:b + 1], XBC[:, b, :], ALU.mult, ALU.add)
            STT(S, "dve", [pb, k.scw, XBC], [XBC], XBC[:, b, :], pb[:, c0 + 2:c0 + 130], k.scw[:, 2, b:b + 1], XBC[:, b, :], ALU.mult, ALU.add)
        ACT(S, [XBC], [XBCe], XBCe[:, :, :], XBC[:, :, :], AF.Exp, scale=-1.0)
        ACT(S, [XBCe], [XBCe], XBCe[:, :, :], XBCe[:, :, :], AF.Ln, bias=1.0)
        ACT(S, [XBCe], [XBCe], XBCe[:, :, :], XBCe[:, :, :], AF.Exp, scale=-1.0)
        TT(S, "pool", [XBC, XBCe], [XBCb], XBCb[:, :, :], XBC[:, :, :], XBCe[:, :, :], ALU.mult)
        psT = S.bank()
        pTv = bview(psT, BF16)
        TR(S, [XBCb, k.identb], [psT], [(pTv[:, b * 128:(b + 1) * 128], XBCb[:, b, :], k.identb[:, :]) for b in range(4)])
        CP(S, "act", [psT], [PS.x_tm], PS.x_tm[:, :], pTv[:, 0:256])
        CP(S, "act", [psT], [PS.B_tm], PS.B_tm[:, :, :], pTv[:, 256:512].rearrange("p (g n) -> p g n", g=2))
    x4 = PS.x_tm[:, :].rearrange("p (h e) -> p h e", h=4)
    TT(S, "pool", [PS.x_tm, r8], [PS.xt_s], PS.xt_s[:, :, :], x4, bc(r8[:, 4:8].unsqueeze(2), [128, 4, 64]), ALU.mult)
    TT(S, "pool", [PS.x_tm, wend], [PS.xh_s], PS.xh_s[:, :, :], x4, bc(wend[:, 4:8].unsqueeze(2), [128, 4, 64]), ALU.mult)
    psS2 = S.bank()
    MM(S, [XBCb], [psS2], [(psS2[:, g * 128:(g + 1) * 128], XBCb[:, 2 + g, :], XBCb[:, 4 + g, :], True, True) for g in range(2)])
    for g in range(2):
        TT(S, "dve", [psS2, PS.DIFF], [PS.PTs], PS.PTs[:, 2 * g:2 * g + 2, :],
           bc(psS2[:, g * 128:(g + 1) * 128].unsqueeze(1), [128, 2, 128]), PS.DIFF[:, 4 + 2 * g:6 + 2 * g, :], ALU.mult)
    psY = S.bank()
    MM(S, [PS.PTs, PS.xt_s, XBCb, a.Hsb], [psY],
       [(psY[:, h * 64:(h + 1) * 64], PS.PTs[:, h, :], PS.xt_s[:, h, :], True, True) for h in range(4)]
       + [(psY[:, 256 + g * 128:256 + (g + 1) * 128], XBCb[:, 4 + g, :], a.Hsb[:, 2 * g:2 * g + 2, :].rearrange("p h e -> p (h e)"), True, True)
          for g in range(2)])
    Ysc = PS.Ysc
    TT(S, "dve", [psY, e8], [Ysc], Ysc[:, :, :], psY[:, 256:512].rearrange("p (h e) -> p h e", h=4),
       bc(e8[:, 4:8].unsqueeze(2), [128, 4, 64]), ALU.mult)
    TT(S, "dve", [psY, Ysc], [HY], HY[:, 256:512], psY[:, 0:256], Ysc[:, :, :].rearrange("p h e -> p (h e)"), ALU.add)
    psU2 = S.bank()
    MM(S, [PS.B_tm, PS.xh_s], [psU2], [(psU2[:, g * 128:(g + 1) * 128], PS.B_tm[:, g, :], PS.xh_s[:, 2 * g:2 * g + 2, :].rearrange("p h e -> p (h e)"),
                                       True, True) for g in range(2)])
    TT(S, "dve", [a.Hs, aL], [a.Hs], a.Hs[:, :, :], a.Hs[:, :, :], bc(aL[:, 4:8].unsqueeze(2), [128, 4, 64]), ALU.mult)
    TT(S, "dve", [a.Hs, psU2], [a.Hs], a.Hs[:, :, :], psU2[:, 0:256].rearrange("p (h e) -> p h e", h=4), a.Hs[:, :, :], ALU.add)
    CP(S, "act", [a.Hs], [a.Hsb], a.Hsb[:, :, :], a.Hs[:, :, :])

    if getattr(k.cfg, "stop", 99) <= 3:
        return
    tile_g = (off // 128) + i
    if not full:
        S.dma("pool", k.STB[tile_g], PS.stash_ap, reads=PS.stash_bufs, writes=[S.dbuf(("STB", tile_g))])
        S.dma("pool", k.STG[tile_g], og[:, :], reads=[og], writes=[S.dbuf(("STG", tile_g))])
        db = S.dbuf(("HF", tile_g))
        S.dma("pool", k.HF[off + 128 * i:off + 128 * (i + 1), :], HY[:, :], reads=[HY], writes=[db])
        return
    finalizeA(k, a, si, T, cond, off, i, nt, ps1, ps2, HY, PS)


def finalizeA(k, a, si, T, cond, off, i, nt, ps1, ps2, HY, pset):
    S, l = k.S, a.l
    w_in, w_out = a.w_in, a.w_out
    hb = a.hb[i % 3]
    hcur = lambda kc: hb[:, kc, 1:129]
    cst = k.cstb
    HYf = a.HYf[i % 2]
    f1, f2, f3 = a.fin1, a.fin2, a.fin3
    v4 = lambda ap: ap.rearrange("p (h e) -> p h e", h=4)
    ym, yg, ys = a.yall[:, 0, :], a.yall[:, 1, :], a.yall[:, 2, :]

    ps3, ps4 = S.bank(), S.bank()
    MM(S, [hb, w_in], [ps3], [(ps3[:, 0:512], hcur(kc), w_in[:, kc, 1296:1808], kc == 0, kc == 7) for kc in range(8)])
    MM(S, [hb, w_in], [ps4], [(ps4[:, 0:256], hcur(kc), w_in[:, kc, 1808:2064], kc == 0, kc == 7) for kc in range(8)]
       + [(ps4[:, 256:512], hcur(kc), w_in[:, kc, 1040:1296], kc == 0, kc == 7) for kc in range(8)])
    has_p, has_n = i > 0, i < nt - 1
    ps5 = S.bank()
    mm5 = []
    if has_p:
        hp_ = a.hb[(i - 1) % 3]
        mm5 += [(ps5[0:8, 0:256], hp_[:, kc, 121:129], w_in[:, kc, 1040:1296], kc == 0, kc == 7) for kc in range(8)]
    if has_n:
        hn_ = a.hb[(i + 1) % 3]
        mm5 += [(ps5[0:8, 256:512], hn_[:, kc, 1:9], w_in[:, kc, 1040:1296], kc == 0, kc == 7) for kc in range(8)]
    if mm5:
        rd = [w_in] + ([a.hb[(i - 1) % 3]] if has_p else []) + ([a.hb[(i + 1) % 3]] if has_n else [])
        MM(S, rd, [ps5], mm5)

    TT(S, "pool", [HY, HYf], [f1], f1[:, :], HY[:, 0:256], HYf[:, 0:256], ALU.add)
    S.op("dve", lambda e: e.tensor_reduce(a.st4[:, :], v4(f1[:, :]), AX.X, ALU.add), [f1], [a.st4])
    TS(S, "dve", [a.st4], [a.st4], a.st4[:, :], a.st4[:, :], 1.0 / 64.0, None, ALU.mult)
    TT(S, "pool", [f1, a.st4], [f1], v4(f1[:, :]), v4(f1[:, :]), bc(a.st4[:, :].unsqueeze(2), [128, 4, 64]), ALU.subtract)
    TT(S, "pool", [f1], [f2], f2[:, :], f1[:, :], f1[:, :], ALU.mult)
    S.op("dve", lambda e: e.tensor_reduce(a.st4b[:, :], v4(f2[:, :]), AX.X, ALU.add), [f2], [a.st4b])
    TS(S, "dve", [a.st4b], [a.st4b], a.st4b[:, :], a.st4b[:, :], 1.0 / 64.0, EPS, ALU.mult, ALU.add)
    TT(S, "pool", [a.st4b, k.cm05], [a.st4b], a.st4b[:, :], a.st4b[:, :], bc(k.cm05[:, 0:1], [128, 4]), ALU.pow)
    TT(S, "pool", [f1, a.st4b], [f1], v4(f1[:, :]), v4(f1[:, :]), bc(a.st4b[:, :].unsqueeze(2), [128, 4, 64]), ALU.mult)
    TT(S, "pool", [f1, k.mng], [f1], f1[:, :], f1[:, :], k.mng[:, :], ALU.mult)
    ACT(S, [a.eo], [a.eo], a.eo[:, :], a.eo[:, :], AF.Ln, bias=1.0)
    ACT(S, [a.eo], [a.eo], a.eo[:, :], a.eo[:, :], AF.Exp, scale=-1.0)
    TT(S, "pool", [f1, a.eo], [a.yall], ym, f1[:, :], a.eo[:, :], ALU.mult)

    CP(S, "act", [ps4], [a.z_sb], a.z_sb[:, :], ps4[:, 0:256])
    ACT(S, [ps4], [a.ez], a.ez[:, :], ps4[:, 0:256], AF.Exp, scale=-1.0)
    TT(S, "pool", [HY, HYf], [f2], f2[:, :], HY[:, 256:512], HYf[:, 256:512], ALU.add)
    TT(S, "pool", [pset.x_tm, k.dsk], [f3], v4(f3[:, :]), v4(pset.x_tm[:, :]), bc(k.dsk[:, :].unsqueeze(2), [128, 4, 64]), ALU.mult)
    TT(S, "pool", [f2, f3], [f2], f2[:, :], f2[:, :], f3[:, :], ALU.add)
    ACT(S, [a.ez], [a.ez], a.ez[:, :], a.ez[:, :], AF.Ln, bias=1.0)
    ACT(S, [a.ez], [a.ez], a.ez[:, :], a.ez[:, :], AF.Exp, scale=-1.0)
    TT(S, "pool", [a.ez, a.z_sb], [a.ez], a.ez[:, :], a.ez[:, :], a.z_sb[:, :], ALU.mult)
    TT(S, "pool", [f2, a.ez], [f2], f2[:, :], f2[:, :], a.ez[:, :], ALU.mult)
    TT(S, "pool", [f2], [f3], f3[:, :], f2[:, :], f2[:, :], ALU.mult)
    S.op("dve", lambda e: e.tensor_reduce(a.st4[:, 0:2], f3[:, :].rearrange("p (g e) -> p g e", g=2), AX.X, ALU.add), [f3], [a.st4])
    TS(S, "dve", [a.st4], [a.st4], a.st4[:, 0:2], a.st4[:, 0:2], 1.0 / 128.0, EPS, ALU.mult, ALU.add)
    TT(S, "pool", [a.st4, k.cm05], [a.st4], a.st4[:, 0:2], a.st4[:, 0:2], bc(k.cm05[:, 0:1], [128, 2]), ALU.pow)
    TT(S, "pool", [f2, a.st4], [f2], f2[:, :].rearrange("p (g e) -> p g e", g=2), f2[:, :].rearrange("p (g e) -> p g e", g=2),
       bc(a.st4[:, 0:2].unsqueeze(2), [128, 2, 128]), ALU.mult)
    TT(S, "pool", [f2, k.sng], [a.yall], ys, f2[:, :], k.sng[:, :], ALU.mult)

    CP(S, "act", [ps3], [a.gu], a.gu[:, :], ps3[:, 0:256])
    S.op("dve", lambda e: e.bn_stats(a.gst[:, :], ps3[:, 256:512]), [ps3], [a.gst])
    S.op("dve", lambda e: e.bn_aggr(a.gmv[:, :], a.gst[:, :]), [a.gst], [a.gmv])
    TS(S, "dve", [a.gmv], [a.gve], a.gve[:, :], a.gmv[:, 1:2], EPS, None, ALU.add)
    TT(S, "pool", [a.gve, k.cm05], [a.grs], a.grs[:, :], a.gve[:, :], k.cm05[:, :], ALU.pow)
    TS(S, "dve", [ps3, a.gmv, a.grs], [a.gvb], a.gvb[:, :], ps3[:, 256:512], a.gmv[:, 0:1], a.grs[:, 0:1], ALU.subtract, ALU.mult)
    psG = S.bank()
    MM(S, [k.wsT, a.gvb], [psG], [(psG[:, h * 64:(h + 1) * 64], k.wsT[:, h, :], a.gvb[:, h * 64:(h + 1) * 64], True, True) for h in range(4)])
    TT(S, "dve", [psG, k.bsT], [f3], v4(f3[:, :]), v4(psG[:, 0:256]), bc(k.bsT[:, :].unsqueeze(2), [128, 4, 64]), ALU.add)
    TT(S, "pool", [f3, a.gu], [a.yall], yg, f3[:, :], a.gu[:, :], ALU.mult)

    CP(S, "act", [ps4], [a.pc], a.pc[:, :], ps4[:, 256:512])
    if has_p:
        CP(S, "act", [ps5], [a.pcP], a.pcP[:, :], ps5[0:8, 0:256])
    if has_n:
        CP(S, "act", [ps5], [a.pcN], a.pcN[:, :], ps5[0:8, 256:512])
    var = "int" if (has_p and has_n) else ("first" if has_n else ("last" if has_p else "int"))
    psP = S.bank()
    mmp = []
    for g in range(4):
        o_ = psP[:, g * 64:(g + 1) * 64]
        seqm = [(cview(k, f"pA{g}{var}"), a.pc[:, g * 64:(g + 1) * 64])]
        if has_p:
            seqm.append((cview(k, f"pP{g}", 8), a.pcP[0:8, g * 64:(g + 1) * 64]))
        if has_n:
            seqm.append((cview(k, f"pN{g}", 8), a.pcN[0:8, g * 64:(g + 1) * 64]))
        for n_, (lh, rh) in enumerate(seqm):
            mmp.append((o_, lh, rh, n_ == 0, n_ == len(seqm) - 1))
    MM(S, [a.c2b, a.pc, a.pcP, a.pcN], [psP], mmp)
    CP(S, "act", [psP], [a.plb], a.plb[:, :], psP[:, 0:256])
    psT2 = S.bank()
    t2v = bview(psT2, BF16)
    TR(S, [a.plb, k.identb], [psT2], [(t2v[:, j * 128:(j + 1) * 128], a.plb[:, j * 128:(j + 1) * 128], k.identb[:, :]) for j in range(2)])
    CP(S, "dve", [psT2], [a.plT], a.plT[:, :, :], t2v[:, 0:256].rearrange("p (j t) -> p j t", j=2))
    psW = S.bank()
    MM(S, [k.wpb, a.plT], [psW], [(psW[:, j * 128:(j + 1) * 128], k.wpb[:, j, :], a.plT[:, j, :], True, True) for j in range(2)])
    cT = a.concatT
    for j in range(2):
        ACT(S, [psW, k.psc], [cT], cT[:, 2 + j, :], psW[:, j * 128:(j + 1) * 128], AF.Identity, scale=k.psc[:, j:j + 1])

    psT3 = S.bank()
    t3v = bview(psT3, BF16)
    TR(S, [a.yall, k.identb], [psT3], [(t3v[:, (m3 * 2 + j) * 128:(m3 * 2 + j + 1) * 128], a.yall[:, m3, j * 128:(j + 1) * 128], k.identb[:, :])
                                      for m3 in range(3) for j in range(2)])
    CP(S, "dve", [psT3], [cT], cT[:, 0:2, :], t3v[:, 0:256].rearrange("p (j t) -> p j t", j=2))
    CP(S, "act", [psT3], [cT], cT[:, 4:8, :], t3v[:, 256:768].rearrange("p (j t) -> p j t", j=4))

    p0, p1 = S.bank(), S.bank()
    for hlf, pb in enumerate((p0, p1)):
        MM(S, [cT, w_out], [pb], [(pb[:, :], cT[:, kc, :], w_out[:, kc, hlf * 512:(hlf + 1) * 512], kc == 0, kc == 7) for kc in range(8)])
    xb = a.xq[i % 2]
    resid_ln(k, xb, (p0, p1), xb, a.rsm)
    r0 = off + 128 * i
    db = S.dbuf(("xoutA", l, r0 // 128))
    S.dma("pool", rows_ap(k, a.dst, "out", r0, 128), xb[:, :], reads=[xb], writes=[db])
    if a.dst is None:
        k.final_bufs.append(db)


_CACHE = {}


def gather_outputs(results, cfg, n):
    NP, Tp, Ts = cfg.NP, cfg.Tp, cfg.Ts
    y_p = np.concatenate([r["yp"].reshape(NP, Tp, D) for r in results], 0)
    y_s = np.stack([r["ys"].reshape(Ts, D) for r in results], 0)
    oc = np.concatenate([r["oc"] for r in results], 0)
    on = np.concatenate([r["on"] for r in results], 0)
    om = np.concatenate([r["om"].reshape(NP, 2, 2, 4) for r in results], 0)
    os_ = np.concatenate([r["os"] for r in results], 0)
    f = lambda a: np.ascontiguousarray(a, dtype=np.float32)
    return (f(y_p), f(y_s), f(oc), f(on), f(om), f(os_))


def kernel(**inputs):
    n = 8
    xs = np.asarray(inputs["x_sample"])
    xp = np.asarray(inputs["x_prompt"])
    cfg = Cfg(Ts=xs.shape[1], NP=xp.shape[0] // n, Tp=xp.shape[1], L=2)
    key = (cfg.Ts, cfg.NP, cfg.Tp)
    if key not in _CACHE:
        _CACHE[key] = build(cfg)
    nc, _ = _CACHE[key]
    in_maps = [shard_inputs(inputs, cfg, c) for c in range(n)]
    res = run_bass_kernel_spmd(nc, in_maps, core_ids=list(range(n)))
    return gather_outputs(res.results, cfg, n)
```

```python
import math
import numpy as np
import ml_dtypes
from contextlib import ExitStack
import concourse.bass as bass
import concourse.mybir as mybir
from concourse.bass_utils import run_bass_kernel_spmd

F32 = mybir.dt.float32
BF16 = mybir.dt.bfloat16
AF = mybir.ActivationFunctionType
ALU = mybir.AluOpType
AX = mybir.AxisListType
DTSZ = {F32: 4, BF16: 2}

D = 1024
DIN = 2840
DFF = 2816
EPS = 1e-5
ALPHA = 4.0 ** 0.25
NEG = -30000.0

ENGS = ("pe", "act", "dve", "pool", "sp")
EPOCH = 30000
NDMA_SEM = 8
SCHED_CP_B = 0.05
SCHED_CP = 0.2
SCHED_LAT = 0.1


def prod(l):
    r = 1
    for x in l:
        r *= int(x)
    return r


class Buf:
    __slots__ = ("name", "v", "last_w", "readers", "off", "excl")

    def __init__(self, name, v, floor=None):
        self.off = -1
        self.excl = False
        self.name = name
        self.v = v
        self.last_w = floor
        self.readers = []

    def __getitem__(self, k):
        return self.v[k]


class Op:
    __slots__ = ("eng", "fn", "deps", "is_dma", "idx", "sig", "has_dep", "vc", "name", "cost", "cp")

    def __init__(self, eng, fn, is_dma, name):
        self.cost = 0.4
        self.eng = eng
        self.fn = fn
        self.is_dma = is_dma
        self.deps = []
        self.sig = None
        self.has_dep = False
        self.vc = None
        self.name = name


class Sched:
    def __init__(self, nc, es, arena_bytes):
        self.nc = nc
        self.es = es
        self.ops = []
        self.floor = None
        self.bufs = []
        self.arena = es.enter_context(nc.sbuf_tensor("arena", [128, arena_bytes // 4], F32))
        self.arena_bytes = arena_bytes
        self.off = 0
        self.peak = 0
        self.banks = []
        for i in range(8):
            t = es.enter_context(nc.psum_tensor(f"bank{i}", [128, 512], F32))
            self.banks.append(Buf(f"bank{i}", t))
            self.banks[-1].excl = True
        self.bank_i = 0
        self.dram_bufs = {}

    def sb(self, name, shape, dtype=F32):
        shape = [int(s) for s in shape]
        if getattr(self, "verbose", False):
            print(f"  sb {name} {shape} {prod(shape[1:]) * DTSZ[dtype]} at {self.off}")
        n = prod(shape[1:])
        nb = n * DTSZ[dtype]
        off = (self.off + 31) // 32 * 32
        assert off + nb <= self.arena_bytes, f"arena overflow allocating {name}: {off + nb}"
        self.off = off + nb
        self.peak = max(self.peak, self.off)
        h = self.arena if dtype == F32 else self.arena.bitcast(dtype)
        e0 = off // DTSZ[dtype]
        v = h[0:shape[0], e0:e0 + n]
        if len(shape) > 2:
            names = " ".join(f"d{i}" for i in range(len(shape) - 1))
            kw = {f"d{i}": shape[i + 1] for i in range(len(shape) - 1)}
            v = v.rearrange(f"p ({names}) -> p {names}", **kw)
        b = Buf(name, v, self.floor)
        b.off = off
        self.bufs.append(b)
        return b

    def mark(self):
        return self.off

    def release(self, mark):
        self.barrier()
        self.bufs = [b for b in self.bufs if b.off < mark]
        self.off = mark

    def bank(self):
        b = self.banks[self.bank_i]
        self.bank_i = (self.bank_i + 1) % 8
        return b

    def dbuf(self, key):
        if key not in self.dram_bufs:
            self.dram_bufs[key] = Buf(str(key), None, None)
        return self.dram_bufs[key]

    def op(self, eng, fn, reads=(), writes=(), name=None, dma=False, cost=None):
        o = Op(eng, fn, dma, name)
        o.cp = getattr(self, "cp", SCHED_CP)
        if cost is not None:
            o.cost = cost
        deps = set()
        ex = [b for b in reads if b.excl]
        if ex:
            reads = [b for b in reads if not b.excl]
            writes = list(writes) + [b for b in ex if b not in writes]
        for b in reads:
            if b.last_w is not None:
                deps.add(b.last_w)
        for b in writes:
            if b.last_w is not None:
                deps.add(b.last_w)
            for r in b.readers:
                deps.add(r)
        o.deps = list(deps)
        o.idx = len(self.ops)
        for d in o.deps:
            d.has_dep = True
        for b in reads:
            b.readers.append(o)
        for b in writes:
            b.last_w = o
            b.readers = []
        self.ops.append(o)
        return o

    def dma(self, q, out, in_, reads=(), writes=(), name=None, **kw):
        nbytes = prod(out.shape) * 4
        return self.op(q, lambda e: e.dma_start(out=out, in_=in_, **kw), reads, writes, name=name, dma=True,
                       cost=2.0 + nbytes / 150e3)

    def barrier(self):
        allb = self.bufs + self.banks + list(self.dram_bufs.values())
        o = self.op("sp", None, reads=[], writes=allb, name="barrier")
        self.floor = o
        return o

    def list_schedule(self, ops):
        import heapq
        LAT = SCHED_LAT
        out = []
        seg = []
        segs = []
        for o in ops:
            if o.fn is None:
                segs.append(seg)
                segs.append([o])
                seg = []
            else:
                seg.append(o)
        segs.append(seg)
        finish = {}
        for seg in segs:
            if len(seg) <= 1:
                for o in seg:
                    finish[o] = 0.0
                    out.append(o)
                continue
            inseg = set(seg)
            seg_cp = seg[len(seg) // 2].cp
            indeg = {}
            users = {}
            for o in seg:
                n = 0
                for d in o.deps:
                    if d in inseg:
                        n += 1
                        users.setdefault(d, []).append(o)
                indeg[o] = n
            tail = {}
            for o in reversed(seg):
                t = 0.0
                for u in users.get(o, ()):
                    if tail[u] > t:
                        t = tail[u]
                tail[o] = t + o.cost + 0.2
            eng_time = {e: 0.0 for e in ENGS}
            ready_at = {}
            heap = []
            for o in seg:
                if indeg[o] == 0:
                    ready_at[o] = 0.0
                    heapq.heappush(heap, (0.0, o.idx, o))
            while heap:
                best = None
                cand = []
                while heap and len(cand) < 24:
                    cand.append(heapq.heappop(heap))
                bi = None
                for ci, (ra, idx, o) in enumerate(cand):
                    stt = max(ra, eng_time[o.eng])
                    key = (stt - seg_cp * tail[o], idx)
                    if best is None or key < best:
                        best, bi = key, ci
                ra, idx, o = cand.pop(bi)
                for c in cand:
                    heapq.heappush(heap, c)
                stt = max(ra, eng_time[o.eng])
                if o.is_dma:
                    eng_time[o.eng] = stt + 0.15
                    fin_t = stt + o.cost
                else:
                    fin_t = stt + o.cost
                    eng_time[o.eng] = fin_t
                finish[o] = fin_t
                out.append(o)
                for u in users.get(o, ()):
                    t = fin_t + (LAT if u.eng != o.eng else 0.3)
                    if ready_at.get(u, 0.0) < t:
                        ready_at[u] = t
                    indeg[u] -= 1
                    if indeg[u] == 0:
                        heapq.heappush(heap, (ready_at[u], u.idx, u))
            self.est_time = getattr(self, "est_time", 0.0) + max(eng_time.values())
        for i, o in enumerate(out):
            o.idx = i
        return out

    def emit(self, final_bufs):
        nc, es = self.nc, self.es
        fin = self.op("sp", None, reads=list(final_bufs), name="final")
        if getattr(self, "reorder", True):
            self.ops = self.list_schedule(self.ops)
        cnt, dma_n, semkeys = {}, {}, []
        for o in self.ops:
            if not o.has_dep:
                continue
            if o.is_dma:
                n = dma_n.get(o.eng, 0)
                dma_n[o.eng] = n + 1
                key = ("dma", o.eng, n % NDMA_SEM)
                cnt[key] = cnt.get(key, 0) + 16
                o.sig = (key, cnt[key])
            else:
                tot = cnt.get(("n", o.eng), 0)
                cnt[("n", o.eng)] = tot + 1
                key = ("c", o.eng, tot // EPOCH)
                o.sig = (key, tot % EPOCH + 1)
            if o.sig[0] not in semkeys:
                semkeys.append(o.sig[0])
        sems = {k: es.enter_context(nc.semaphore("s_" + "_".join(str(x) for x in k))) for k in semkeys}
        per_eng = {e: [] for e in ENGS}
        seen = {e: {} for e in ENGS}
        nwaits = 0
        for o in self.ops:
            s = seen[o.eng]
            need = {}
            for d in o.deps:
                k, c = d.sig
                if s.get(k, 0) < c:
                    need[k] = max(need.get(k, 0), c)
            if o.is_dma and o.sig is not None:
                k, c = o.sig
                if c > 16 and s.get(k, 0) < c - 16:
                    need[k] = max(need.get(k, 0), c - 16)
            for d in o.deps:
                for k, c in d.vc.items():
                    if s.get(k, 0) < c:
                        s[k] = c
            for k, c in need.items():
                if s.get(k, 0) < c:
                    s[k] = c
            o.deps = need
            nwaits += len(need)
            vc = dict(s)
            if o.sig is not None:
                vc[o.sig[0]] = max(vc.get(o.sig[0], 0), o.sig[1])
                if not o.is_dma:
                    for ep in range(o.sig[0][2]):
                        vc[("c", o.eng, ep)] = EPOCH
            o.vc = vc
            per_eng[o.eng].append(o)

        def body_for(engname):
            def body(eng):
                for o in per_eng[engname]:
                    for k, c in o.deps.items():
                        eng.wait_ge(sems[k], c)
                    if o.fn is not None:
                        ins = o.fn(eng)
                        if o.sig is not None:
                            ins.then_inc(sems[o.sig[0]], 16 if o.is_dma else 1)
                    elif o.sig is not None:
                        eng.nop().then_inc(sems[o.sig[0]], 1)
            return body

        with nc.Block() as block:
            block.sync(body_for("sp"))
            block.scalar(body_for("act"))
            block.vector(body_for("dve"))
            block.gpsimd(body_for("pool"))
            block.tensor(body_for("pe"))
        return {"ops": len(self.ops), "waits": nwaits, "sems": len(sems),
                "per_eng": {e: len(v) for e, v in per_eng.items()}, "sbuf_peak": self.peak}


def _c(out, base=0.25, per=1.0 / 1000.0):
    return base + prod(out.shape[1:]) * per


def ACT(S, r, w, out, in_, func, bias=None, scale=None):
    kw = {}
    if bias is not None:
        kw["bias"] = bias
    if scale is not None:
        kw["scale"] = scale
    return S.op("act", lambda e: e.activation(out, in_, func, **kw), r, w, cost=_c(out, 0.3, 1 / 1200.0))


def TS(S, eng, r, w, out, in0, s1, s2, op0, op1=None):
    if op1 is None:
        return S.op(eng, lambda e: e.tensor_scalar(out, in0, s1, None, op0), r, w, cost=_c(out))
    return S.op(eng, lambda e: e.tensor_scalar(out, in0, s1, s2, op0, op1), r, w, cost=_c(out))


def TT(S, eng, r, w, out, in0, in1, op):
    return S.op(eng, lambda e: e.tensor_tensor(out, in0, in1, op), r, w, cost=_c(out))


def STT(S, eng, r, w, out, in0, scalar, in1, op0, op1):
    return S.op(eng, lambda e: e.scalar_tensor_tensor(out, in0, scalar, in1, op0, op1), r, w, cost=_c(out))


def CP(S, eng, r, w, out, in_):
    if eng == "act":
        return S.op("act", lambda e: e.copy(out, in_), r, w, cost=_c(out, 0.3, 1 / 1200.0))
    return S.op(eng, lambda e: e.tensor_copy(out, in_), r, w, cost=_c(out))


def MSET(S, eng, w, out, val):
    return S.op(eng, lambda e: e.memset(out, val), [], w)


def MM(S, r, w, mms):
    mms = list(mms)

    def fn(e):
        ins = None
        for (o, l, rh, st, sp) in mms:
            ins = e.matmul(o, l, rh, start=st, stop=sp)
        return ins
    cost = 0.1
    for (o, l, rh, st, sp) in mms:
        cost += max(64, prod(rh.shape[1:])) / 2400.0 * (4.0 if rh.dtype == F32 else 1.0) + 0.02
    return S.op("pe", fn, r, w, cost=cost)


def TR(S, r, w, trs):
    trs = list(trs)

    def fn(e):
        ins = None
        for (o, i, idt) in trs:
            ins = e.transpose(o, i, idt)
        return ins
    return S.op("pe", fn, r, w, cost=0.1 + 0.12 * len(trs))


def bc(ap, shape):
    return ap.to_broadcast([int(s) for s in shape])


POOL_W = (2, 4, 8, 16)


def make_consts():
    cols = {}
    parts = []
    off = [0]

    def add(name, arr):
        a = np.zeros((128, arr.shape[1]), np.float32)
        a[:arr.shape[0]] = arr
        cols[name] = (off[0], arr.shape[1])
        off[0] += arr.shape[1]
        parts.append(a)

    idx = np.arange(128)
    s_, t_ = idx[:, None], idx[None, :]
    add("ident", np.eye(128, dtype=np.float32))
    add("ones", np.ones((128, 128), np.float32))
    add("tri0", (s_ <= t_).astype(np.float32))
    add("tri1", (s_ >= t_).astype(np.float32))
    add("neg0", np.where(s_ <= t_, 0.0, NEG).astype(np.float32))
    add("neg1", np.where(s_ >= t_, 0.0, NEG).astype(np.float32))
    n1 = off[0]
    for g, w in enumerate(POOL_W):
        h = w // 2
        band = ((s_ >= t_ - h) & (s_ < t_ + h)).astype(np.float32)
        cnt_int = np.full(128, float(w))
        cnt_first = (idx + h) - np.maximum(idx - h, 0)
        cnt_last = np.minimum(idx + h, 128) - (idx - h)
        eye = np.eye(128, dtype=np.float32)
        add(f"pA{g}int", band / cnt_int[None, :] - eye)
        add(f"pA{g}first", band / cnt_first[None, :] - eye)
        add(f"pA{g}last", band / cnt_last[None, :] - eye)
        sp = np.arange(8)[:, None]
        add(f"pP{g}", (((sp - 8) >= t_ - h) & ((sp - 8) < t_ + h)).astype(np.float32) / w)
        add(f"pN{g}", (((128 + sp) >= t_ - h) & ((128 + sp) < t_ + h)).astype(np.float32) / w)
    add("jrow", np.tile(np.arange(256, dtype=np.float32)[None, :], (128, 1)))
    add("pcol", (idx % 64).astype(np.float32)[:, None])
    full = np.concatenate(parts, axis=1)
    cols2 = {kk: (o - n1, n) for kk, (o, n) in cols.items() if o >= n1}
    cols1 = {kk: (o, n) for kk, (o, n) in cols.items() if o < n1}
    return full[:, :n1].copy(), cols1, full[:, n1:].copy(), cols2


class Cfg:
    def __init__(self, Ts=4096, NP=4, Tp=256, L=2, debug=()):
        self.Ts, self.NP, self.Tp, self.L = Ts, NP, Tp, L
        self.debug = tuple(debug)
        self.seqs = [("s", Ts, 0, 0)] + [(f"p{j}", Tp, 1, Ts + j * Tp) for j in range(NP)]
        self.Ttot = Ts + NP * Tp


INPUT_SPECS = lambda c: [
    ("xs", [c.Ts, D]), ("xp", [c.NP * c.Tp, D]),
    ("st_c", [2, 2, 4, 64, 64]), ("st_n", [2, 2, 4, 64]), ("st_m", [2, 8]), ("st_s", [2, 2, 4, 64, 128]),
    ("cond", [2, D]),
    ("w_ada", [2, D, 6 * D]), ("b_ada", [2, 6 * D]), ("w_in", [2, D, DIN]), ("b_ig", [2, 8]), ("b_fg", [2, 8]),
    ("mnorm_g", [2, 256]), ("w_pool", [2, 4, 64, 64]), ("pool_scale", [2, 256]), ("w_sp", [2, 4, 128, 128]),
    ("b_sp", [2, 4, 128]), ("sconv_w", [2, 3, 768]), ("sconv_b", [2, 768]), ("dt_bias", [2, 8]),
    ("a_log", [2, 8]), ("ssd_d", [2, 4]), ("snorm_g", [2, 256]), ("w_out", [2, D, D]),
    ("ln1_g", [2, D]), ("ln1_b", [2, D]), ("w_up", [2, D, 2 * DFF]), ("fconv_w", [2, 3, 2 * DFF]),
    ("fconv_b", [2, 2 * DFF]), ("w_dn", [2, DFF, D]), ("ln2_g", [2, D]), ("ln2_b", [2, D]),
]
OUTPUT_SPECS = lambda c: [
    ("ys", [c.Ts, D]), ("yp", [c.NP * c.Tp, D]), ("oc", [c.NP, 2, 2, 4, 64, 64]), ("on", [c.NP, 2, 2, 4, 64]),
    ("om", [c.NP, 2, 8]), ("os", [c.NP, 2, 2, 4, 64, 128]),
]


class K:
    pass


def build(cfg):
    nc = bass.Bass("TRN2", target_bir_lowering=False)
    cst_np, ccols, cst2_np, ccols2 = make_consts()
    I = {}
    for name, shape in INPUT_SPECS(cfg):
        I[name] = nc.dram_tensor(name, shape, F32, kind="ExternalInput")
    I["cst"] = nc.dram_tensor("cst", list(cst_np.shape), F32, kind="ExternalInput")
    I["cst2"] = nc.dram_tensor("cst2", list(cst2_np.shape), F32, kind="ExternalInput")
    O = {}
    for name, shape in OUTPUT_SPECS(cfg):
        O[name] = nc.dram_tensor(name, shape, F32, kind="ExternalOutput")
    DBG = {}
    for name, shape in cfg.debug:
        DBG[name] = nc.dram_tensor("dbg_" + name, shape, F32, kind="ExternalOutput")
    ntile = cfg.Ttot // 128
    XA = nc.dram_tensor("scr_xa", [cfg.Ttot, D], F32, kind="Internal")
    XB = nc.dram_tensor("scr_xb", [cfg.Ttot, D], F32, kind="Internal")
    HF = nc.dram_tensor("scr_hf", [cfg.Ttot, 512], F32, kind="Internal")
    HT = nc.dram_tensor("scr_ht", [ntile, 128, 8 * 128], BF16, kind="Internal")
    ED = nc.dram_tensor("scr_e", [64, 512], F32, kind="Internal")
    STB = nc.dram_tensor("scr_stb", [ntile, 128, 2304], BF16, kind="Internal")
    STG = nc.dram_tensor("scr_stg", [ntile, 128, 280], F32, kind="Internal")

    with ExitStack() as es:
        S = Sched(nc, es, 204 * 1024)
        k = K()
        k.S, k.cfg, k.I, k.O, k.DBG = S, cfg, I, O, DBG
        k.XA, k.XB, k.HF, k.HT, k.ED = XA, XB, HF, HT, ED
        k.STB, k.STG = STB, STG
        k.final_bufs = []
        k.ccols2 = ccols2
        k.W2 = cst2_np.shape[1]
        setup(k, ccols)
        for l in range(cfg.L):
            layer(k, l)
        stats = S.emit(k.final_bufs)
    return nc, stats


def cview(k, name, rows=128):
    if name in k.ccols:
        o, n = k.ccols[name]
        return k.cst[0:rows, o:o + n]
    o, n = k.ccols2[name]
    return k.cst2[0:rows, o:o + n]


def load_cst2(k):
    S = k.S
    b = S.sb("cst2", [128, k.W2])
    k.cst2b = b
    k.cst2 = b.v
    S.dma("sp", b[:, :], k.I["cst2"][:, :], writes=[b])
    return b


def setup(k, ccols):
    S, I, cfg = k.S, k.I, k.cfg
    k.ccols = ccols
    W = sum(n for (_, n) in ccols.values())
    cstb = S.sb("cst", [128, W])
    k.cstb = cstb
    k.cst = cstb.v
    S.dma("sp", cstb[:, :], I["cst"][:, :], writes=[cstb])
    k.identb = S.sb("identb", [128, 128], BF16)
    CP(S, "dve", [cstb], [k.identb], k.identb[:, :], cview(k, "ident"))
    k.cm05 = S.sb("cm05", [128, 1])
    MSET(S, "pool", [k.cm05], k.cm05[:, :], -0.5)
    k.cm1 = S.sb("cm1", [128, 1])
    MSET(S, "pool", [k.cm1], k.cm1[:, :], -1.0)

    L = cfg.L
    k.modT = S.sb("modT", [128, L, 48, 2])
    layer_consts_alloc(k)
    m1 = S.mark()
    c2b = load_cst2(k)
    fr = S.sb("pe_fr", [64, 256])
    ang = S.sb("pe_ang", [64, 256])
    et = S.sb("pe_e", [64, 512])
    et2 = S.sb("pe_e2", [64, 512])
    sq = S.sb("pe_sq", [64, 256])
    ACT(S, [c2b], [fr], fr[:, :], cview(k, "jrow", 64), AF.Exp, scale=-math.log(10000.0) / 256.0)
    TS(S, "dve", [fr, c2b], [ang], ang[:, :], fr[:, :], cview(k, "pcol", 64), None, ALU.mult)
    ACT(S, [ang], [et], et[:, 0:256], ang[:, :], AF.Sin, scale=1.0 / 32.0)
    ACT(S, [ang], [et], et[:, 256:512], ang[:, :], AF.Sin, scale=-1.0 / 32.0, bias=math.pi / 2.0)
    cur, nxt = et, et2
    for it in range(5):
        TT(S, "dve", [cur], [sq], sq[:, :], cur[:, 0:256], cur[:, 0:256], ALU.mult)
        STT(S, "dve", [cur], [nxt], nxt[:, 0:256], cur[:, 0:256], 2.0, cur[:, 256:512], ALU.mult, ALU.mult)
        TS(S, "dve", [sq], [nxt], nxt[:, 256:512], sq[:, :], -2.0, 1.0, ALU.mult, ALU.add)
        cur, nxt = nxt, cur
    et = cur
    edb = S.dbuf("ED")
    S.dma("sp", k.ED[:, :], et[:, :], reads=[et], writes=[edb])

    condT = S.sb("condT", [128, 8, 2])
    for c in range(2):
        S.dma("sp", condT[:, :, c], I["cond"][c].rearrange("(kc p) -> p kc", p=128), writes=[condT],
              allow_slow_non_contiguous=True)
    esg = S.sb("cond_e", [128, 8, 2])
    ACT(S, [condT], [esg], esg[:, :, :], condT[:, :, :], AF.Exp, scale=-1.0)
    TS(S, "dve", [esg], [esg], esg[:, :, :], esg[:, :, :], 1.0, None, ALU.add)
    TT(S, "pool", [esg, k.cm1], [esg], esg[:, :, :], esg[:, :, :], bc(k.cm1[:, 0:1].unsqueeze(2), [128, 8, 2]), ALU.pow)
    TT(S, "pool", [condT, esg], [condT], condT[:, :, :], condT[:, :, :], esg[:, :, :], ALU.mult)
    badaT = S.sb("badaT", [128, L, 48])
    k.cf_st = [S.sb(f"cf_sx{j}", [128, 128]) for j in range(2)]
    for l in range(L):
        colform(k, badaT, badaT[:, l, :], I["b_ada"][l].rearrange("(j p) -> j p", p=128), 48)
    wst = [S.sb(f"wada_st{j}", [128, 8, 512]) for j in range(4)]
    n = 0
    for l in range(L):
        for cb in range(12):
            st = wst[n % 4]
            n += 1
            S.dma("sp", st[:, :, :], I["w_ada"][l, :, cb * 512:(cb + 1) * 512].rearrange("(kc p) n -> p kc n", p=128),
                  writes=[st])
            pb = S.bank()
            mms = []
            for sub in range(4):
                for kc in range(8):
                    mms.append((pb[:, sub * 2:sub * 2 + 2], st[:, kc, sub * 128:(sub + 1) * 128], condT[:, kc, :],
                                kc == 0, kc == 7))
            MM(S, [st, condT], [pb], mms)
            TT(S, "dve", [pb, badaT], [k.modT],
               k.modT[:, l, cb * 4:cb * 4 + 4, :],
               pb[:, 0:8].rearrange("p (s c) -> p s c", c=2),
               bc(badaT[:, l, cb * 4:cb * 4 + 4].unsqueeze(2), [128, 4, 2]), ALU.add)
    for l in range(L):
        for grp in (1, 4):
            TS(S, "dve", [k.modT], [k.modT], k.modT[:, l, grp * 8:(grp + 1) * 8, :],
               k.modT[:, l, grp * 8:(grp + 1) * 8, :], 1.0, None, ALU.add)
    S.release(m1)


class nc_allow:
    def __init__(self, k):
        pass

    def __enter__(self):
        return self

    def __exit__(self, *a):
        return False


def bview(bank, dtype=F32):
    return bank.v if dtype == F32 else bank.v.bitcast(dtype)


def layer_consts_alloc(k):
    S = k.S
    k.bif = S.sb("bif", [128, 16])
    k.coef = S.sb("coef", [128, 2, 8])
    k.dtb = S.sb("dtb", [128, 8])
    k.dsk = S.sb("dsk", [128, 4])
    k.mng = S.sb("mng", [128, 256])
    k.sng = S.sb("sng", [128, 256])
    k.psc = S.sb("psc", [128, 2])
    k.wpb = S.sb("wpb", [128, 2, 128], BF16)
    k.wsT = S.sb("wsT", [128, 4, 128], BF16)
    k.bsT = S.sb("bsT", [128, 4])
    k.scw = S.sb("scw", [128, 4, 6])
    k.fcw = S.sb("fcw", [128, 4, 44])
    k.lng = S.sb("lng", [128, D])
    k.lnb = S.sb("lnb", [128, D])
    k.gbc = S.sb("gbc", [128, D])
    k.ttmp = [S.sb(f"ttmp{j}", [128, D]) for j in range(1)]
    k.small = {}


def colform(k, dst_buf, dst_ap, src_ap, nb):
    S = k.S
    k.cf_n = getattr(k, "cf_n", 0) + 1
    st = k.cf_st[k.cf_n % len(k.cf_st)]
    S.dma("sp", st[0:nb, :], src_ap, writes=[st])
    pb = S.bank()
    TR(S, [st, k.cstb], [pb], [(pb[:, 0:nb], st[0:nb, :], cview(k, "ident")[0:nb, 0:nb])])
    CP(S, "dve", [pb], [dst_buf], dst_ap, pb[:, 0:nb])


def load_layer_consts(k, l):
    S, I = k.S, k.I
    m = S.mark()
    k.cf_st = [S.sb(f"cf_st{j}", [128, 128]) for j in range(4)]
    row = lambda name, a, b: I[name][l:l + 1, a:b].partition_broadcast(128)
    S.dma("sp", k.bif[:, 0:8], row("b_ig", 0, 8), writes=[k.bif])
    S.dma("sp", k.bif[:, 8:16], row("b_fg", 0, 8), writes=[k.bif])
    TS(S, "dve", [k.bif], [k.bif], k.bif[:, 8:16], k.bif[:, 8:16], -1.0, None, ALU.mult)
    al = S.sb("al_tmp", [128, 8])
    S.dma("sp", al[:, :], row("a_log", 0, 8), writes=[al])
    ACT(S, [al], [al], al[:, :], al[:, :], AF.Exp)
    MSET(S, "pool", [k.coef], k.coef[:, :, :], -1.0)
    TS(S, "dve", [al, k.coef], [k.coef], k.coef[:, :, 4:8], al[:, :].rearrange("p (d h) -> p d h", d=2), -1.0, None, ALU.mult)
    S.dma("sp", k.dtb[:, :], row("dt_bias", 0, 8), writes=[k.dtb])
    S.dma("sp", k.dsk[:, :], row("ssd_d", 0, 4), writes=[k.dsk])
    S.dma("sp", k.mng[:, :], row("mnorm_g", 0, 256), writes=[k.mng])
    S.dma("sp", k.sng[:, :], row("snorm_g", 0, 256), writes=[k.sng])
    colform(k, k.psc, k.psc[:, :], I["pool_scale"][l].rearrange("(j p) -> j p", p=128), 2)
    wp32 = S.sb("wp32", [128, 2, 128])
    MSET(S, "pool", [wp32], wp32[:, :, :], 0.0)
    for g in range(4):
        pr = slice((g % 2) * 64, (g % 2) * 64 + 64)
        S.dma("sp", wp32[pr, g // 2, (g % 2) * 64:(g % 2) * 64 + 64], I["w_pool"][l, g], writes=[wp32])
    CP(S, "dve", [wp32], [k.wpb], k.wpb[:, :, :], wp32[:, :, :])
    ws32 = S.sb("ws32", [128, 4, 128])
    S.dma("sp", ws32[:, :, :], I["w_sp"][l].rearrange("h t s -> t h s"), writes=[ws32])
    pb = S.bank()
    TR(S, [ws32, k.cstb], [pb], [(pb[:, h * 128:(h + 1) * 128], ws32[:, h, :], cview(k, "ident")) for h in range(4)])
    CP(S, "act", [pb], [k.wsT], k.wsT[:, :, :], pb[:, :].rearrange("p (h t) -> p h t", h=4))
    colform(k, k.bsT, k.bsT[:, :], I["b_sp"][l], 4)
    for tap in range(3):
        colform(k, k.scw, k.scw[:, tap, :], I["sconv_w"][l, tap].rearrange("(b p) -> b p", p=128), 6)
        colform(k, k.fcw, k.fcw[:, tap, :], I["fconv_w"][l, tap].rearrange("(b p) -> b p", p=128), 44)
    colform(k, k.scw, k.scw[:, 3, :], I["sconv_b"][l].rearrange("(b p) -> b p", p=128), 6)
    colform(k, k.fcw, k.fcw[:, 3, :], I["fconv_b"][l].rearrange("(b p) -> b p", p=128), 44)
    S.release(m)


def load_weight(k, dst, src2d, nkc, ncols, scope_stage):
    S = k.S
    engs = ("dve", "act")
    piece = 2840
    for kc in range(nkc):
        for c0 in range(0, ncols, piece):
            c1 = min(ncols, c0 + piece)
            st = scope_stage[k.wl_n % len(scope_stage)]
            S.dma("sp", st[:, 0:c1 - c0], src2d[kc * 128:(kc + 1) * 128, c0:c1], writes=[st])
            CP(S, engs[k.wl_n % 2], [st], [dst], dst[:, kc, c0:c1], st[:, 0:c1 - c0])
            k.wl_n += 1


def gate_table(k, l, grp, cond):
    S = k.S
    dg = k.ttmp[0]
    for j in range(8):
        TS(S, "dve", [k.cstb, k.modT], [dg], dg[:, 0:128], cview(k, "ident"), k.modT[:, l, grp * 8 + j, cond:cond + 1], None, ALU.mult)
        if j % 4 == 0:
            pb = S.bank()
        MM(S, [dg, k.cstb], [pb], [(pb[:, (j % 4) * 128:(j % 4 + 1) * 128], cview(k, "ones"), dg[:, 0:128], True, True)])
        if j % 4 == 3:
            CP(S, "act", [pb], [k.gbc], k.gbc[:, (j // 4) * 512:(j // 4 + 1) * 512], pb[:, :])


def make_hT(k, l, which, cond, rows, xbuf, dsts, dst_buf, src_bufs, pos_tile=None):
    S = k.S
    n = 0
    for r, nr in rows:
        S.dma("sp", xbuf[n:n + nr, :], r, reads=src_bufs, writes=[xbuf])
        n += nr
    if pos_tile is not None:
        TT(S, "pool", [xbuf, pos_tile], [xbuf], xbuf[0:n, :], xbuf[0:n, :], pos_tile[0:n, :], ALU.add)
    sm = k.hsm
    st, mv, ve, rstd, xnb = sm["st"], sm["mv"], sm["ve"], sm["rstd"], sm["xnb"]
    S.op("dve", lambda e: e.bn_stats(st[0:n, 0, :], xbuf[0:n, 0:512]), [xbuf], [st])
    S.op("dve", lambda e: e.bn_stats(st[0:n, 1, :], xbuf[0:n, 512:1024]), [xbuf], [st])
    S.op("dve", lambda e: e.bn_aggr(mv[0:n, :], st[0:n, :, :].rearrange("p a b -> p (a b)")), [st], [mv])
    TS(S, "dve", [mv], [ve], ve[0:n, :], mv[0:n, 1:2], EPS, None, ALU.add)
    TT(S, "pool", [ve, k.cm05], [rstd], rstd[0:n, :], ve[0:n, :], k.cm05[0:n, :], ALU.pow)
    TS(S, "dve", [xbuf, mv, rstd], [xnb], xnb[0:n, :], xbuf[0:n, :], mv[0:n, 0:1], rstd[0:n, 0:1], ALU.subtract, ALU.mult)
    pb = S.bank()
    pv = bview(pb, BF16)
    TR(S, [xnb, k.identb], [pb],
       [(pv[:, kc * 128:kc * 128 + n], xnb[0:n, kc * 128:(kc + 1) * 128], k.identb[0:n, 0:n]) for kc in range(8)])
    gsh, gsc = (0, 1) if which == 1 else (3, 4)
    for kc in range(8):
        sc = k.modT[:, l, gsc * 8 + kc, cond:cond + 1]
        sh = k.modT[:, l, gsh * 8 + kc, cond:cond + 1]
        if kc % 2 == 0:
            ACT(S, [pb, k.modT], [dst_buf], dsts[kc], pv[:, kc * 128:kc * 128 + n], AF.Identity, bias=sh, scale=sc)
        else:
            TS(S, "dve", [pb, k.modT], [dst_buf], dsts[kc], pv[:, kc * 128:kc * 128 + n], sc, sh, ALU.mult, ALU.add)


def alloc_hsm(k):
    S = k.S
    k.hsm = {"st": S.sb("h_st", [128, 2, 6]), "mv": S.sb("h_mv", [128, 2]), "ve": S.sb("h_ve", [128, 1]),
             "rstd": S.sb("h_rstd", [128, 1]), "xnb": S.sb("h_xnb", [128, D], BF16)}


def resid_ln(k, x_buf, psum_halves, out_buf, nb_small):
    S = k.S
    t0, t1 = k.ttmp[0], out_buf
    for hlf, pb in enumerate(psum_halves):
        sl = slice(hlf * 512, (hlf + 1) * 512)
        TT(S, "dve", [pb, k.gbc], [t0], t0[:, sl], pb[:, :], k.gbc[:, sl], ALU.mult)
    STT(S, "dve", [x_buf, t0], [t0], t0[:, :], x_buf[:, :], ALPHA, t0[:, :], ALU.mult, ALU.add)
    st, mv, ve, rstd, nb = nb_small["st"], nb_small["mv"], nb_small["ve"], nb_small["rstd"], nb_small["nb"]
    S.op("dve", lambda e: e.bn_stats(st[:, 0, :], t0[:, 0:512]), [t0], [st])
    S.op("dve", lambda e: e.bn_stats(st[:, 1, :], t0[:, 512:1024]), [t0], [st])
    S.op("dve", lambda e: e.bn_aggr(mv[:, :], st[:, :, :].rearrange("p a b -> p (a b)")), [st], [mv])
    TS(S, "dve", [mv], [ve], ve[:, :], mv[:, 1:2], EPS, None, ALU.add)
    TT(S, "pool", [ve, k.cm05], [rstd], rstd[:, :], ve[:, :], k.cm05[:, :], ALU.pow)
    STT(S, "dve", [mv, rstd], [nb], nb[:, :], mv[:, 0:1], -1.0, rstd[:, :], ALU.mult, ALU.mult)
    ACT(S, [t0, rstd, nb], [t1], t1[:, :], t0[:, :], AF.Identity, bias=nb[:, 0:1], scale=rstd[:, 0:1])
    TT(S, "pool", [t1, k.lng], [t1], t1[:, :], t1[:, :], k.lng[:, :], ALU.mult)
    TT(S, "pool", [t1, k.lnb], [out_buf], out_buf[:, :], t1[:, :], k.lnb[:, :], ALU.add)


def alloc_rsm(k):
    S = k.S
    return {"st": S.sb("r_st", [128, 2, 6]), "mv": S.sb("r_mv", [128, 2]), "ve": S.sb("r_ve", [128, 1]),
            "rstd": S.sb("r_rstd", [128, 1]), "nb": S.sb("r_nb", [128, 1])}


def seq_src_dst(k, l, phase):
    cfg = k.cfg
    mode = getattr(cfg, "mode", "full")
    if phase == "A":
        src = None if l == 0 else k.XB
        dst = k.XA if mode == "full" else None
    else:
        src = k.XA if mode == "full" else None
        dst = None if l == cfg.L - 1 else k.XB
    return src, dst


def rows_ap(k, handle, which_io, r0, n):
    cfg = k.cfg
    if handle is not None:
        return handle[r0:r0 + n, :]
    if r0 < cfg.Ts:
        t = k.I["xs"] if which_io == "in" else k.O["ys"]
        return t[r0:r0 + n, :]
    t = k.I["xp"] if which_io == "in" else k.O["yp"]
    return t[r0 - cfg.Ts:r0 - cfg.Ts + n, :]


def phaseB(k, l):
    S, I, cfg = k.S, k.I, k.cfg
    S.cp = SCHED_CP_B
    m = S.mark()
    w_up = S.sb("w_up", [128, 8, 2 * DFF], BF16)
    w_dn = S.sb("w_dn", [128, 22, D], BF16)
    m2 = S.mark()
    stage = [S.sb(f"wstage{j}", [128, 2840]) for j in range(3)]
    k.wl_n = 0
    load_weight(k, w_up, I["w_up"][l], 8, 2 * DFF, stage)
    load_weight(k, w_dn, I["w_dn"][l], 22, D, stage)
    S.release(m2)
    S.dma("sp", k.lng[:, :], I["ln2_g"][l:l + 1, :].partition_broadcast(128), writes=[k.lng])
    S.dma("sp", k.lnb[:, :], I["ln2_b"][l:l + 1, :].partition_broadcast(128), writes=[k.lnb])
    alloc_hsm(k)
    rsm = alloc_rsm(k)
    SEG = 256
    h2T = [S.sb(f"h2T{j}", [128, 8, SEG + 2], BF16) for j in range(2)]
    actT = S.sb("actT", [128, 22, SEG], BF16)
    xt = [S.sb(f"xtB{j}", [128, D]) for j in range(4)]
    xh = k.ttmp[0]
    NBUF = 3
    cg = [S.sb(f"cg{j}", [128, SEG]) for j in range(NBUF)]
    cv = [S.sb(f"cv{j}", [128, SEG]) for j in range(NBUF)]
    th = [S.sb(f"th{j}", [128, SEG]) for j in range(NBUF)]
    src, dst = seq_src_dst(k, l, "B")
    segs = []
    for (sname, T, cond, off) in cfg.seqs:
        for t0 in range(0, T, SEG):
            segs.append((T, cond, off, t0))
    state = {}

    def prep(si):
        T, cond, off, t0 = segs[si]
        hT = h2T[si % 2]
        r0 = off + t0
        xts = []
        for j in range(SEG // 128):
            xb = xt[(2 * si + j) % 4]
            xts.append(xb)
            make_hT(k, l, 2, cond, [(rows_ap(k, src, "in", r0 + 128 * j, 128), 128)], xb,
                    [hT[:, kc, 1 + 128 * j:1 + 128 * (j + 1)] for kc in range(8)], hT, [])
        rows, cols = [], []
        if t0 > 0:
            rows.append((rows_ap(k, src, "in", r0 - 1, 1), 1))
            cols.append(0)
        else:
            MSET(S, "pool", [hT], hT[:, :, 0:1], 0.0)
        if t0 + SEG < T:
            rows.append((rows_ap(k, src, "in", r0 + SEG, 1), 1))
            cols.append(SEG + 1)
        else:
            MSET(S, "pool", [hT], hT[:, :, SEG + 1:SEG + 2], 0.0)
        if len(rows) == 2:
            make_hT(k, l, 2, cond, rows, xh, [hT[:, kc, 0:SEG + 2:SEG + 1] for kc in range(8)], hT, [])
        elif len(rows) == 1:
            c = cols[0]
            make_hT(k, l, 2, cond, rows, xh, [hT[:, kc, c:c + 1] for kc in range(8)], hT, [])
        state[si] = xts

    def ffn(si):
        T, cond, off, t0 = segs[si]
        hT = h2T[si % 2]
        r0 = off + t0
        xts = state.pop(si)
        for c in range(22):
            pg, pv = S.bank(), S.bank()
            MM(S, [w_up, hT], [pg], [(pg[:, 0:SEG + 2], w_up[:, kc, c * 128:(c + 1) * 128], hT[:, kc, :], kc == 0, kc == 7)
                                      for kc in range(8)])
            MM(S, [w_up, hT], [pv], [(pv[:, 0:SEG + 2], w_up[:, kc, DFF + c * 128:DFF + (c + 1) * 128], hT[:, kc, :], kc == 0, kc == 7)
                                      for kc in range(8)])
            g_, v_, t_ = cg[c % NBUF], cv[c % NBUF], th[c % NBUF]
            fw = k.fcw
            ACT(S, [pg, fw], [g_], g_[:, :], pg[:, 1:SEG + 1], AF.Identity, bias=fw[:, 3, c:c + 1], scale=fw[:, 1, c:c + 1])
            STT(S, "dve", [pg, fw, g_], [g_], g_[:, :], pg[:, 0:SEG], fw[:, 0, c:c + 1], g_[:, :], ALU.mult, ALU.add)
            STT(S, "dve", [pg, fw, g_], [g_], g_[:, :], pg[:, 2:SEG + 2], fw[:, 2, c:c + 1], g_[:, :], ALU.mult, ALU.add)
            cc = 22 + c
            ACT(S, [pv, fw], [v_], v_[:, :], pv[:, 1:SEG + 1], AF.Identity, bias=fw[:, 3, cc:cc + 1], scale=fw[:, 1, cc:cc + 1])
            STT(S, "dve", [pv, fw, v_], [v_], v_[:, :], pv[:, 0:SEG], fw[:, 0, cc:cc + 1], v_[:, :], ALU.mult, ALU.add)
            STT(S, "dve", [pv, fw, v_], [v_], v_[:, :], pv[:, 2:SEG + 2], fw[:, 2, cc:cc + 1], v_[:, :], ALU.mult, ALU.add)
            ACT(S, [g_], [t_], t_[:, :], g_[:, :], AF.Silu)
            TT(S, "pool", [t_, v_], [actT], actT[:, c, :], t_[:, :], v_[:, :], ALU.mult)
        for j in range(SEG // 128):
            p0, p1 = S.bank(), S.bank()
            for hlf, pb in enumerate((p0, p1)):
                MM(S, [actT, w_dn], [pb], [(pb[:, :], actT[:, c, 128 * j:128 * (j + 1)], w_dn[:, c, hlf * 512:(hlf + 1) * 512],
                                              c == 0, c == 21) for c in range(22)])
            ob = xts[j]
            resid_ln(k, xts[j], (p0, p1), ob, rsm)
            db = S.dbuf(("xout", l, (r0 + 128 * j) // 128))
            S.dma("pool", rows_ap(k, dst, "out", r0 + 128 * j, 128), ob[:, :], reads=[ob], writes=[db])
            if dst is None:
                k.final_bufs.append(db)

    cur_cond = None
    prep(0)
    for si in range(len(segs)):
        if si + 1 < len(segs):
            prep(si + 1)
        if segs[si][1] != cur_cond:
            cur_cond = segs[si][1]
            gate_table(k, l, 5, cur_cond)
        ffn(si)
    S.release(m)


def layer(k, l):
    cfg = k.cfg
    load_layer_consts(k, l)
    mode = getattr(cfg, "mode", "full")
    if mode in ("full", "A"):
        phaseA(k, l)
    if mode in ("full", "B"):
        phaseB(k, l)


def shard_inputs(inp, cfg, core):
    f = lambda a: np.ascontiguousarray(np.asarray(a), dtype=np.float32)
    NP = cfg.NP
    cst, _, cst2, _ = make_consts()
    m = {
        "xs": f(inp["x_sample"][core]),
        "xp": f(inp["x_prompt"][NP * core:NP * (core + 1)]).reshape(NP * cfg.Tp, D),
        "st_c": f(inp["state_mlstm_c"][core]), "st_n": f(inp["state_mlstm_n"][core]),
        "st_m": f(inp["state_mlstm_m"][core]).reshape(2, 8), "st_s": f(inp["state_ssd"][core]),
        "cond": f(np.stack([np.asarray(inp["c"])[core], np.asarray(inp["c_ctx"])], 0)),
        "w_ada": f(inp["w_ada"]), "b_ada": f(inp["b_ada"]), "w_in": f(inp["w_in"]),
        "b_ig": f(inp["b_igate"]).reshape(2, 8), "b_fg": f(inp["b_fgate"]).reshape(2, 8),
        "mnorm_g": f(inp["mlstm_norm_g"]), "w_pool": f(inp["w_pool"]), "pool_scale": f(inp["pool_scale"]),
        "w_sp": f(inp["w_spatial"]), "b_sp": f(inp["b_spatial"]), "sconv_w": f(inp["ssd_conv_w"]),
        "sconv_b": f(inp["ssd_conv_b"]), "dt_bias": f(inp["ssd_dt_bias"]).reshape(2, 8),
        "a_log": f(inp["ssd_a_log"]).reshape(2, 8), "ssd_d": f(inp["ssd_d"]), "snorm_g": f(inp["ssd_norm_g"]),
        "w_out": f(inp["w_out"]), "ln1_g": f(inp["ln1_g"]), "ln1_b": f(inp["ln1_b"]), "w_up": f(inp["ffn_w_up"]),
        "fconv_w": f(inp["ffn_conv_w"]), "fconv_b": f(inp["ffn_conv_b"]), "w_dn": f(inp["ffn_w_down"]),
        "ln2_g": f(inp["ln2_g"]), "ln2_b": f(inp["ln2_b"]), "cst": cst, "cst2": cst2,
    }
    return m


def phaseA(k, l):
    S, I, cfg = k.S, k.I, k.cfg
    S.cp = SCHED_CP
    m = S.mark()
    w_in = S.sb("w_in", [128, 8, DIN], BF16)
    w_out = S.sb("w_out", [128, 8, D], BF16)
    m2 = S.mark()
    stage = [S.sb(f"wstageA{j}", [128, 2840]) for j in range(6)]
    k.wl_n = 0
    load_weight(k, w_in, I["w_in"][l], 8, DIN, stage)
    load_weight(k, w_out, I["w_out"][l], 8, D, stage)
    S.release(m2)
    c2b = load_cst2(k)
    S.dma("sp", k.lng[:, :], I["ln1_g"][l:l + 1, :].partition_broadcast(128), writes=[k.lng])
    S.dma("sp", k.lnb[:, :], I["ln1_b"][l:l + 1, :].partition_broadcast(128), writes=[k.lnb])
    alloc_hsm(k)
    rsm = alloc_rsm(k)
    a = K()
    a.w_in, a.w_out, a.c2b, a.rsm, a.l = w_in, w_out, c2b, rsm, l
    a.hb = [S.sb(f"hb{j}", [128, 8, 130], BF16) for j in range(3)]
    a.xq = [S.sb(f"xq{j}", [128, D]) for j in range(2)]
    a.pet = [S.sb(f"petile{j}", [128, D]) for j in range(2)] if l == 0 else None
    if l == 0:
        edb = S.dbuf("ED")
        for j in range(2):
            S.dma("sp", a.pet[j][0:64, 512:1024], k.ED[:, :], reads=[edb], writes=[a.pet[j]])
            S.dma("sp", a.pet[j][64:128, 512:1024], k.ED[:, :], reads=[edb], writes=[a.pet[j]])
    sb = S.sb
    a.Cn, a.Cnb = sb("Cn", [128, 2, 65]), sb("Cnb", [128, 2, 66], BF16)
    a.Hs, a.Hsb = sb("Hs", [128, 4, 64]), sb("Hsb", [128, 4, 64], BF16)
    a.p = []
    for par in range(2):
        q = K()
        a.p.append(q)
        q.qkT = sb(f"qkT{par}", [128, 4, 128], BF16)
        q.k_tm = sb(f"k_tm{par}", [128, 256], BF16)
        q.v_sb = sb(f"v_sb{par}", [128, 256], BF16)
        q.XBCb = sb(f"XBCb{par}", [128, 6, 128], BF16)
        q.x_tm, q.B_tm = sb(f"x_tm{par}", [128, 256], BF16), sb(f"B_tm{par}", [128, 2, 128], BF16)
        q.stash_bufs = [q.qkT, q.k_tm, q.v_sb, q.XBCb, q.x_tm, q.B_tm]
        e0 = q.qkT.off // 2
        q.stash_ap = S.arena.bitcast(BF16)[0:128, e0:e0 + 2304]
        assert q.B_tm.off + 512 == q.qkT.off + 4608, "stash group must be contiguous"
        q.og = sb(f"og{par}", [128, 280])
        q.G8, q.E8, q.SP8, q.igb = sb(f"G8{par}", [128, 8]), sb(f"E8{par}", [128, 8]), sb(f"SP8{par}", [128, 8]), sb(f"igb{par}", [128, 4])
        q.r8, q.logdec, q.cum, q.e8 = sb(f"r8{par}", [128, 8]), sb(f"logdec{par}", [128, 8]), sb(f"cum{par}", [128, 8]), sb(f"e8{par}", [128, 8])
        q.wend, q.aL, q.tmp8 = sb(f"wend{par}", [128, 8]), sb(f"aL{par}", [128, 8]), sb(f"tmp8{par}", [128, 8])
        q.L1 = sb(f"L1{par}", [128, 8, 128])
        q.DIFF = sb(f"DIFF{par}", [128, 8, 128])
        q.PTm = sb(f"PTm{par}", [128, 4, 128], BF16)
        q.PTs = sb(f"PTs{par}", [128, 4, 128], BF16)
        q.xt_m, q.xh_m = sb(f"xt_m{par}", [128, 4, 66], BF16), sb(f"xh_m{par}", [128, 4, 66], BF16)
        q.XBC, q.XBCe = sb(f"XBC{par}", [128, 6, 128]), sb(f"XBCe{par}", [128, 6, 128])
        q.xt_s, q.xh_s = sb(f"xt_s{par}", [128, 4, 64], BF16), sb(f"xh_s{par}", [128, 4, 64], BF16)
        q.NUM = sb(f"NUM{par}", [128, 4, 65])
        q.den = sb(f"den{par}", [128, 4])
        q.Ysc = sb(f"Ysc{par}", [128, 4, 64])
    a.HY = [sb(f"HY{j}", [128, 512]) for j in range(2)]
    a.HYf = [sb(f"HYf{j}", [128, 512]) for j in range(2)]
    a.eo, a.z_sb, a.ez, a.gu = sb("eo", [128, 256]), sb("z_sb", [128, 256]), sb("ez", [128, 256]), sb("gu", [128, 256])
    a.gvb = sb("gvb", [128, 256], BF16)
    a.pc, a.pcP, a.pcN = sb("pc", [128, 256]), sb("pcP", [8, 256]), sb("pcN", [8, 256])
    a.plb, a.plT = sb("plb", [128, 256], BF16), sb("plT", [128, 2, 128], BF16)
    a.fin1, a.fin2, a.fin3 = sb("fin1", [128, 256]), sb("fin2", [128, 256]), sb("fin3", [128, 256])
    a.st4, a.st4b = sb("st4", [128, 4]), sb("st4b", [128, 4])
    a.yall = sb("yall", [128, 3, 256], BF16)
    a.concatT = sb("concatT", [128, 8, 128], BF16)
    a.gst, a.gmv, a.gve, a.grs = sb("gst", [128, 6]), sb("gmv", [128, 2]), sb("gve", [128, 1]), sb("grs", [128, 1])
    a.mrun = sb("mrun", [4, 1])
    a.mt = sb("mt", [4, 2])
    for par in range(2):
        a.p[par].dec = sb(f"dec{par}", [128, 8])
    a.sio = sb("sio", [128, 4, 128])
    src, dst = seq_src_dst(k, l, "A")
    a.src, a.dst = src, dst
    cur_cond = None
    for si, (sname, T, cond, off) in enumerate(cfg.seqs):
        if cond != cur_cond:
            gate_table(k, l, 2, cond)
            cur_cond = cond
        runseq(k, a, si, T, cond, off)
    S.release(m)


def runseq(k, a, si, T, cond, off):
    S, cfg, l = k.S, k.cfg, a.l
    nt = T // 128
    is_sample = (si == 0)
    tile0 = off // 128
    w_in = a.w_in

    def hbuf(i):
        return a.hb[i % 3]

    def fix_halo(lo, hi):
        CP(S, "pool", [hbuf(hi)], [hbuf(lo)], hbuf(lo)[:, :, 129:130], hbuf(hi)[:, :, 1:2])
        CP(S, "pool", [hbuf(lo)], [hbuf(hi)], hbuf(hi)[:, :, 0:1], hbuf(lo)[:, :, 128:129])

    def ensure1(i):
        hb = hbuf(i)
        xb = a.xq[i % 2]
        pos = None
        if l == 0 and is_sample:
            pos = a.pet[i % 2]
            edb = S.dbuf("ED")
            S.dma("sp", pos[0:64, 0:512], k.ED[2 * i:2 * i + 1, :].partition_broadcast(64), reads=[edb], writes=[pos])
            S.dma("sp", pos[64:128, 0:512], k.ED[2 * i + 1:2 * i + 2, :].partition_broadcast(64), reads=[edb], writes=[pos])
        make_hT(k, l, 1, cond, [(rows_ap(k, a.src, "in", off + 128 * i, 128), 128)], xb,
                [hb[:, kc, 1:129] for kc in range(8)], hb, [], pos_tile=pos)
        db = S.dbuf(("HT", tile0 + i))
        S.dma("pool", k.HT[tile0 + i].rearrange("p (kc t) -> p kc t", kc=8), hb[:, :, 1:129], reads=[hb], writes=[db])
        if i == 0:
            MSET(S, "pool", [hb], hb[:, :, 0:1], 0.0)
        else:
            fix_halo(i - 1, i)
        if i == nt - 1:
            MSET(S, "pool", [hb], hb[:, :, 129:130], 0.0)

    def ensure2(i):
        hb = hbuf(i)
        db = S.dbuf(("HT", tile0 + i))
        S.dma("sp", hb[:, :, 1:129], k.HT[tile0 + i].rearrange("p (kc t) -> p kc t", kc=8), reads=[db], writes=[hb])
        xb = a.xq[i % 2]
        S.dma("sp", xb[:, :], rows_ap(k, a.src, "in", off + 128 * i, 128), writes=[xb])
        if l == 0 and is_sample:
            pos = a.pet[i % 2]
            edb = S.dbuf("ED")
            S.dma("sp", pos[0:64, 0:512], k.ED[2 * i:2 * i + 1, :].partition_broadcast(64), reads=[edb], writes=[pos])
            S.dma("sp", pos[64:128, 0:512], k.ED[2 * i + 1:2 * i + 2, :].partition_broadcast(64), reads=[edb], writes=[pos])
            TT(S, "pool", [xb, pos], [xb], xb[:, :], xb[:, :], pos[:, :], ALU.add)
        hf = a.HYf[i % 2]
        S.dma("sp", hf[:, :], k.HF[off + 128 * i:off + 128 * (i + 1), :], reads=[S.dbuf(("HF", tile0 + i))], writes=[hf])
        q = a.p[i % 2]
        S.dma("sp", q.stash_ap, k.STB[tile0 + i], reads=[S.dbuf(("STB", tile0 + i))], writes=q.stash_bufs)
        S.dma("sp", q.og[:, :], k.STG[tile0 + i], reads=[S.dbuf(("STG", tile0 + i))], writes=[q.og])
        if i == nt - 1:
            MSET(S, "pool", [hb], hb[:, :, 129:130], 0.0)
        else:
            fix_halo(i, i + 1)
        if i == 0:
            MSET(S, "pool", [hb], hb[:, :, 0:1], 0.0)

    for d in ((0,) if getattr(cfg, "stop", 99) <= 4 else (0, 1)):
        init_state(k, a, si, d, is_sample)
        order = list(range(nt)) if d == 0 else list(range(nt - 1, -1, -1))
        ens = ensure1 if d == 0 else ensure2
        ens(order[0])
        for n, i in enumerate(order):
            if n + 1 < len(order):
                ens(order[n + 1])
            tileA(k, a, si, T, cond, off, i, d, nt, is_sample)
        if not is_sample and getattr(cfg, "stop", 99) > 5:
            final_state(k, a, si, d)


def init_state(k, a, si, d, is_sample):
    S, I, l = k.S, k.I, a.l
    if not is_sample:
        MSET(S, "pool", [a.Cn], a.Cn[:, :, :], 0.0)
        MSET(S, "pool", [a.Cnb], a.Cnb[:, :, :], 0.0)
        MSET(S, "pool", [a.Hs], a.Hs[:, :, :], 0.0)
        MSET(S, "pool", [a.Hsb], a.Hsb[:, :, :], 0.0)
        MSET(S, "pool", [a.mrun], a.mrun[:, :], 0.0)
        return
    for h in range(4):
        pr = slice((h % 2) * 64, (h % 2) * 64 + 64)
        S.dma("sp", a.Cn[pr, h // 2, 0:64], I["st_c"][l, d, h], writes=[a.Cn])
        S.dma("sp", a.Cn[pr, h // 2, 64:65], I["st_n"][l, d, h].rearrange("(p o) -> p o", o=1), writes=[a.Cn])
    S.dma("sp", a.st4[:, :], I["st_m"][l:l + 1, 4 * d:4 * d + 4].partition_broadcast(128), writes=[a.st4])
    ACT(S, [a.st4], [a.st4b], a.st4b[:, :], a.st4[:, :], AF.Exp)
    for h in range(4):
        pr = slice((h % 2) * 64, (h % 2) * 64 + 64)
        TS(S, "dve", [a.Cn, a.st4b], [a.Cn], a.Cn[pr, h // 2, :], a.Cn[pr, h // 2, :], a.st4b[pr, h:h + 1], None, ALU.mult)
    CP(S, "pool", [a.Cn], [a.Cnb], a.Cnb[:, :, 0:65], a.Cn[:, :, :])
    S.dma("sp", a.sio[0:64, :, :], I["st_s"][l, d].rearrange("h p n -> p h n"), writes=[a.sio])
    pb = S.bank()
    TR(S, [a.sio, k.cstb], [pb], [(pb[:, h * 64:(h + 1) * 64], a.sio[0:64, h, :], cview(k, "ident")[0:64, 0:64]) for h in range(4)])
    CP(S, "dve", [pb], [a.Hs], a.Hs[:, :, :], pb[:, 0:256].rearrange("p (h q) -> p h q", h=4))
    CP(S, "act", [pb], [a.Hsb], a.Hsb[:, :, :], pb[:, 0:256].rearrange("p (h q) -> p h q", h=4))


def final_state(k, a, si, d):
    S, O, l = k.S, k.O, a.l
    j = si - 1
    dg = a.p[0].tmp8
    TS(S, "dve", [k.cstb, a.mrun], [dg], dg[0:4, 0:4], cview(k, "ident")[0:4, 0:4], a.mrun[0:4, 0:1], None, ALU.mult)
    pb = S.bank()
    MM(S, [dg, k.cstb], [pb], [(pb[:, 0:4], cview(k, "ones")[0:4, :], dg[0:4, 0:4], True, True)])
    ACT(S, [pb], [a.st4b], a.st4b[:, :], pb[:, 0:4], AF.Exp, scale=-1.0)
    stg = a.sio
    sv = stg[:, 0:2, 0:65]
    for h in range(4):
        pr = slice((h % 2) * 64, (h % 2) * 64 + 64)
        TS(S, "dve", [a.Cn, a.st4b], [stg], stg[pr, h // 2, 0:65], a.Cn[pr, h // 2, :], a.st4b[pr, h:h + 1], None, ALU.mult)
    outs = []
    for h in range(4):
        pr = slice((h % 2) * 64, (h % 2) * 64 + 64)
        db = S.dbuf(("oc", j, l, d, h))
        S.dma("pool", O["oc"][j, l, d, h], stg[pr, h // 2, 0:64], reads=[stg], writes=[db])
        db2 = S.dbuf(("on", j, l, d, h))
        S.dma("pool", O["on"][j, l, d, h].rearrange("(p o) -> p o", o=1), stg[pr, h // 2, 64:65], reads=[stg], writes=[db2])
        outs += [db, db2]
    db = S.dbuf(("om", j, l, d))
    S.dma("pool", O["om"][j, l, 4 * d:4 * d + 4].rearrange("(p o) -> p o", o=1), a.mrun[0:4, 0:1], reads=[a.mrun], writes=[db])
    outs.append(db)
    pb2 = S.bank()
    TR(S, [a.Hs, k.cstb], [pb2], [(pb2[0:64, h * 128:(h + 1) * 128], a.Hs[:, h, :], cview(k, "ident")) for h in range(4)])
    CP(S, "dve", [pb2, stg], [stg], stg[0:64, :, :], pb2[0:64, :].rearrange("p (h n) -> p h n", h=4))
    db = S.dbuf(("os", j, l, d))
    S.dma("pool", O["os"][j, l, d].rearrange("h p n -> p h n"), stg[0:64, :, :], reads=[stg], writes=[db])
    outs.append(db)
    k.final_bufs += outs


def tileA(k, a, si, T, cond, off, i, d, nt, is_sample):
    S, l = k.S, a.l
    PS = a.p[i % 2]
    w_in = a.w_in
    hb = a.hb[i % 3]
    hcur = lambda kc: hb[:, kc, 1:129]
    tri = cview(k, "tri%d" % d)
    neg = cview(k, "neg%d" % d)
    endc = 127 if d == 0 else 0
    full = (d == 1)
    cst = k.cstb

    og = PS.og
    ps1 = ps2 = None
    if not full:
        ps1, ps2 = S.bank(), S.bank()
        MM(S, [hb, w_in], [ps1], [(ps1[:, 0:512], hcur(kc), w_in[:, kc, 256:768], kc == 0, kc == 7) for kc in range(8)])
        MM(S, [hb, w_in], [ps2], [(ps2[:, 0:272], hcur(kc), w_in[:, kc, 768:1040], kc == 0, kc == 7) for kc in range(8)]
           + [(ps2[:, 272:280], hcur(kc), w_in[:, kc, 2832:2840], kc == 0, kc == 7) for kc in range(8)])
        CP(S, "act", [ps2], [og], og[:, :], ps2[:, 0:280])
        ACT(S, [ps1], [PS.k_tm], PS.k_tm[:, :], ps1[:, 0:256], AF.Identity, scale=0.125)
        CP(S, "act", [ps1], [PS.v_sb], PS.v_sb[:, :], ps1[:, 256:512])
    G8, E8, SP8, igb, r8, logdec, cum, e8, wend, aL, tmp8 = (PS.G8, PS.E8, PS.SP8, PS.igb, PS.r8, PS.logdec, PS.cum, PS.e8,
                                                             PS.wend, PS.aL, PS.tmp8)
    STT(S, "dve", [og, k.bif], [G8], G8[:, 0:4], og[:, 264 + 4 * d:268 + 4 * d], -1.0, k.bif[:, 8 + 4 * d:12 + 4 * d], ALU.mult, ALU.add)
    TT(S, "dve", [og, k.dtb], [G8], G8[:, 4:8], og[:, 272 + 4 * d:276 + 4 * d], k.dtb[:, 4 * d:4 * d + 4], ALU.add)
    TT(S, "dve", [og, k.bif], [igb], igb[:, :], og[:, 256 + 4 * d:260 + 4 * d], k.bif[:, 4 * d:4 * d + 4], ALU.add)
    if full:
        ACT(S, [og], [a.eo], a.eo[:, :], og[:, 0:256], AF.Exp, scale=-1.0)
    ACT(S, [G8], [E8], E8[:, :], G8[:, :], AF.Exp)
    ACT(S, [E8], [SP8], SP8[:, :], E8[:, :], AF.Ln, bias=1.0)
    ACT(S, [igb], [r8], r8[:, 0:4], igb[:, :], AF.Exp)
    CP(S, "pool", [SP8], [r8], r8[:, 4:8], SP8[:, 4:8])
    TT(S, "dve", [SP8, k.coef], [logdec], logdec[:, :], SP8[:, :], k.coef[:, d, :], ALU.mult)
    CP(S, "act", [logdec], [PS.L1], PS.L1[:, :, :], bc(logdec[:, 0:8].unsqueeze(2), [128, 8, 128]))
    psL = [S.bank(), S.bank()]
    for hh in range(2):
        MM(S, [PS.L1, cst], [psL[hh]], [(psL[hh][:, q * 128:(q + 1) * 128], PS.L1[:, hh * 4 + q, :], tri, True, True) for q in range(4)])
    psC = S.bank()
    MM(S, [logdec, cst], [psC], [(psC[:, 0:8], tri, logdec[:, 0:8], True, True)])
    CP(S, "dve", [psC], [cum], cum[:, :], psC[:, 0:8])
    for h in range(8):
        pl = psL[h // 4]
        q = h % 4
        STT(S, "dve", [pl, cum, cst], [PS.DIFF], PS.DIFF[:, h, :], pl[:, q * 128:(q + 1) * 128], cum[:, h:h + 1], neg, ALU.subtract, ALU.add)
    ACT(S, [PS.DIFF], [PS.DIFF], PS.DIFF[:, :, :], PS.DIFF[:, :, :], AF.Exp)
    ACT(S, [cum], [e8], e8[:, :], cum[:, :], AF.Exp)
    for hh in range(2):
        TT(S, "dve", [psL[hh], cum], [tmp8], tmp8[:, hh * 4:hh * 4 + 4], psL[hh][:, endc:512:128], cum[:, hh * 4:hh * 4 + 4], ALU.subtract)
        ACT(S, [psL[hh]], [aL], aL[:, hh * 4:hh * 4 + 4], psL[hh][:, endc:512:128], AF.Exp)
    ACT(S, [tmp8], [wend], wend[:, :], tmp8[:, :], AF.Exp)
    TT(S, "dve", [wend, r8], [wend], wend[:, :], wend[:, :], r8[:, :], ALU.mult)
    if not is_sample:
        TT(S, "dve", [tmp8, igb], [PS.dec], PS.dec[:, 0:4], tmp8[:, 0:4], igb[:, :], ALU.add)
        TT(S, "dve", [tmp8, cum], [PS.dec], PS.dec[:, 4:8], tmp8[:, 0:4], cum[:, 0:4], ALU.add)
        pm = S.bank()
        TR(S, [PS.dec, cst], [pm], [(pm[0:4, 0:128], PS.dec[:, 0:4], cview(k, "ident")),
                                   (pm[0:4, 128:256], PS.dec[:, 4:8], cview(k, "ident"))])
        S.op("dve", lambda e: e.tensor_reduce(a.mt[0:4, 0:1], pm[0:4, 0:128], AX.X, ALU.max), [pm], [a.mt])
        TT(S, "dve", [pm, a.mrun], [a.mt], a.mt[0:4, 1:2], pm[0:4, 128:129], a.mrun[0:4, 0:1], ALU.add)
        TT(S, "dve", [a.mt], [a.mrun], a.mrun[0:4, 0:1], a.mt[0:4, 0:1], a.mt[0:4, 1:2], ALU.max)

    if getattr(k.cfg, "stop", 99) <= 1:
        return
    qkT = PS.qkT
    if not full:
        psQ = [S.bank(), S.bank()]
        for hh in range(2):
            MM(S, [hb, w_in], [psQ[hh]], [(psQ[hh][:, q * 130:(q + 1) * 130], w_in[:, kc, (hh * 2 + q) * 128:(hh * 2 + q + 1) * 128],
                                            hb[:, kc, 0:130], kc == 0, kc == 7) for q in range(2) for kc in range(8)])
        CP(S, "act", [psQ[0]], [qkT], qkT[:, 0:2, :], psQ[0][:, 0:260].rearrange("p (b t) -> p b t", b=2)[:, :, 1:129])
        ACT(S, [psQ[1]], [qkT], qkT[:, 2:4, :], psQ[1][:, 0:260].rearrange("p (b t) -> p b t", b=2)[:, :, 1:129], AF.Identity, scale=0.125)
    v4 = PS.v_sb[:, :].rearrange("p (h e) -> p h e", h=4)
    TT(S, "dve", [PS.v_sb, r8], [PS.xt_m], PS.xt_m[:, :, 0:64], v4, bc(r8[:, 0:4].unsqueeze(2), [128, 4, 64]), ALU.mult)
    if getattr(k.cfg, "stop", 99) <= 1.12:
        return
    CP(S, "pool", [r8], [PS.xt_m], PS.xt_m[:, :, 64:65], r8[:, 0:4].unsqueeze(2))
    if getattr(k.cfg, "stop", 99) <= 1.15:
        return
    EXP = getattr(k.cfg, "exp", "")
    if EXP != "noTT":
        TT(S, "dve", [PS.v_sb, wend], [PS.xh_m], PS.xh_m[:, :, 0:64], v4, bc((r8 if EXP == "r8" else wend)[:, 0:4].unsqueeze(2), [128, 4, 64]), ALU.mult)
    if EXP != "noCP":
        CP(S, "pool", [wend], [PS.xh_m], PS.xh_m[:, :, 64:65], wend[:, 0:4].unsqueeze(2))
    if getattr(k.cfg, "stop", 99) <= 1.2:
        return
    psS = [S.bank(), S.bank()]
    hp = lambda h: slice((h % 2) * 64, (h % 2) * 64 + 64)
    for par in range(2):
        MM(S, [qkT], [psS[par]], [(psS[par][:, (h // 2) * 128:(h // 2 + 1) * 128], qkT[hp(h), 2 + h // 2, :], qkT[hp(h), h // 2, :], True, True)
                                  for h in (par, par + 2)])
    for par in range(2):
        TT(S, "dve", [psS[par], PS.DIFF], [PS.PTm], PS.PTm[:, par:4:2, :], psS[par][:, 0:256].rearrange("p (h t) -> p h t", h=2),
           PS.DIFF[:, par:4:2, :], ALU.mult)
    if getattr(k.cfg, "stop", 99) <= 1.4:
        return
    psO = S.bank()
    psI = [S.bank(), S.bank()]
    MM(S, [PS.PTm, PS.xt_m], [psO], [(psO[:, h * 65:h * 65 + 65], PS.PTm[:, h, :], PS.xt_m[:, h, 0:65], True, True) for h in range(4)])
    for par in range(2):
        MM(S, [qkT, a.Cnb], [psI[par]], [(psI[par][:, (h // 2) * 65:(h // 2) * 65 + 65], qkT[hp(h), h // 2, :], a.Cnb[hp(h), h // 2, 0:65], True, True)
                                         for h in (par, par + 2)])
    NUM = PS.NUM
    for par in range(2):
        TT(S, "dve", [psI[par], e8], [NUM], NUM[:, par:4:2, :], psI[par][:, 0:130].rearrange("p (h e) -> p h e", h=2),
           bc(e8[:, par:4:2].unsqueeze(2), [128, 2, 65]), ALU.mult)
    TT(S, "dve", [psO, NUM], [NUM], NUM[:, :, :], psO[:, 0:260].rearrange("p (h e) -> p h e", h=4), NUM[:, :, :], ALU.add)
    if getattr(k.cfg, "stop", 99) <= 1.6:
        return
    HY = a.HY[i % 2]
    ACT(S, [NUM], [PS.den], PS.den[:, :].unsqueeze(2), NUM[:, :, 64:65], AF.Abs)
    TS(S, "dve", [PS.den], [PS.den], PS.den[:, :], PS.den[:, :], 1.0, None, ALU.max)
    TT(S, "pool", [PS.den, k.cm1], [PS.den], PS.den[:, :], PS.den[:, :], bc(k.cm1[:, 0:1], [128, 4]), ALU.pow)
    TT(S, "pool", [NUM, PS.den], [HY], HY[:, 0:256].rearrange("p (h e) -> p h e", h=4), NUM[:, :, 0:64],
       bc(PS.den[:, :].unsqueeze(2), [128, 4, 64]), ALU.mult)
    if getattr(k.cfg, "stop", 99) <= 1.8:
        return
    psU = S.bank()
    MM(S, [PS.k_tm, PS.xh_m], [psU], [(psU[:, h * 65:h * 65 + 65], PS.k_tm[:, (h // 2) * 128:(h // 2 + 1) * 128], PS.xh_m[:, h, 0:65], True, True)
                                     for h in range(4)])
    for h in range(4):
        STT(S, "dve", [a.Cn, aL, psU], [a.Cn], a.Cn[hp(h), h // 2, :], a.Cn[hp(h), h // 2, :], aL[hp(h), h:h + 1],
            psU[hp(h), h * 65:h * 65 + 65], ALU.mult, ALU.add)
    CP(S, "act", [a.Cn], [a.Cnb], a.Cnb[:, :, 0:65], a.Cn[:, :, :])

    if getattr(k.cfg, "stop", 99) <= 2:
        return
    XBC, XBCe, XBCb = PS.XBC, PS.XBCe, PS.XBCb
    if not full:
        psX = [S.bank(), S.bank()]
        for hh in range(2):
            MM(S, [hb, w_in], [psX[hh]], [(psX[hh][:, q * 130:q * 130 + 130], w_in[:, kc, 2064 + (hh * 3 + q) * 128:2064 + (hh * 3 + q + 1) * 128],
                                            hb[:, kc, 0:130], kc == 0, kc == 7) for q in range(3) for kc in range(8)])
        for b in range(6):
            pb, c0 = psX[b // 3], (b % 3) * 130
            ACT(S, [pb, k.scw], [XBC], XBC[:, b, :], pb[:, c0 + 1:c0 + 129], AF.Identity, bias=k.scw[:, 3, b:b + 1], scale=k.scw[:, 1, b:b + 1])
            STT(S, "dve", [pb, k.scw, XBC], [XBC], XBC[:, b, :], pb[:, c0:c0 + 128], k.scw[:, 0, b:b + 1], XBC[:, b, :], ALU.mult, ALU.add)
            STT(S, "dve", [pb, k.scw, XBC], [XBC], XBC[:, b, :], pb[:, c0 + 2:c0 + 130], k.scw[:, 2, b:b + 1], XBC[:, b, :], ALU.mult, ALU.add)
        ACT(S, [XBC], [XBCe], XBCe[:, :, :], XBC[:, :, :], AF.Exp, scale=-1.0)
        ACT(S, [XBCe], [XBCe], XBCe[:, :, :], XBCe[:, :, :], AF.Ln, bias=1.0)
        ACT(S, [XBCe], [XBCe], XBCe[:, :, :], XBCe[:, :, :], AF.Exp, scale=-1.0)
        TT(S, "pool", [XBC, XBCe], [XBCb], XBCb[:, :, :], XBC[:, :, :], XBCe[:, :, :], ALU.mult)
        psT = S.bank()
        pTv = bview(psT, BF16)
        TR(S, [XBCb, k.identb], [psT], [(pTv[:, b * 128:(b + 1) * 128], XBCb[:, b, :], k.identb[:, :]) for b in range(4)])
        CP(S, "act", [psT], [PS.x_tm], PS.x_tm[:, :], pTv[:, 0:256])
        CP(S, "act", [psT], [PS.B_tm], PS.B_tm[:, :, :], pTv[:, 256:512].rearrange("p (g n) -> p g n", g=2))
    x4 = PS.x_tm[:, :].rearrange("p (h e) -> p h e", h=4)
    TT(S, "pool", [PS.x_tm, r8], [PS.xt_s], PS.xt_s[:, :, :], x4, bc(r8[:, 4:8].unsqueeze(2), [128, 4, 64]), ALU.mult)
    TT(S, "pool", [PS.x_tm, wend], [PS.xh_s], PS.xh_s[:, :, :], x4, bc(wend[:, 4:8].unsqueeze(2), [128, 4, 64]), ALU.mult)
    psS2 = S.bank()
    MM(S, [XBCb], [psS2], [(psS2[:, g * 128:(g + 1) * 128], XBCb[:, 2 + g, :], XBCb[:, 4 + g, :], True, True) for g in range(2)])
    for g in range(2):
        TT(S, "dve", [psS2, PS.DIFF], [PS.PTs], PS.PTs[:, 2 * g:2 * g + 2, :],
           bc(psS2[:, g * 128:(g + 1) * 128].unsqueeze(1), [128, 2, 128]), PS.DIFF[:, 4 + 2 * g:6 + 2 * g, :], ALU.mult)
    psY = S.bank()
    MM(S, [PS.PTs, PS.xt_s, XBCb, a.Hsb], [psY],
       [(psY[:, h * 64:(h + 1) * 64], PS.PTs[:, h, :], PS.xt_s[:, h, :], True, True) for h in range(4)]
       + [(psY[:, 256 + g * 128:256 + (g + 1) * 128], XBCb[:, 4 + g, :], a.Hsb[:, 2 * g:2 * g + 2, :].rearrange("p h e -> p (h e)"), True, True)
          for g in range(2)])
    Ysc = PS.Ysc
    TT(S, "dve", [psY, e8], [Ysc], Ysc[:, :, :], psY[:, 256:512].rearrange("p (h e) -> p h e", h=4),
       bc(e8[:, 4:8].unsqueeze(2), [128, 4, 64]), ALU.mult)
    TT(S, "dve", [psY, Ysc], [HY], HY[:, 256:512], psY[:, 0:256], Ysc[:, :, :].rearrange("p h e -> p (h e)"), ALU.add)
    psU2 = S.bank()
    MM(S, [PS.B_tm, PS.xh_s], [psU2], [(psU2[:, g * 128:(g + 1) * 128], PS.B_tm[:, g, :], PS.xh_s[:, 2 * g:2 * g + 2, :].rearrange("p h e -> p (h e)"),
                                       True, True) for g in range(2)])
    TT(S, "dve", [a.Hs, aL], [a.Hs], a.Hs[:, :, :], a.Hs[:, :, :], bc(aL[:, 4:8].unsqueeze(2), [128, 4, 64]), ALU.mult)
    TT(S, "dve", [a.Hs, psU2], [a.Hs], a.Hs[:, :, :], psU2[:, 0:256].rearrange("p (h e) -> p h e", h=4), a.Hs[:, :, :], ALU.add)
    CP(S, "act", [a.Hs], [a.Hsb], a.Hsb[:, :, :], a.Hs[:, :, :])

    if getattr(k.cfg, "stop", 99) <= 3:
        return
    tile_g = (off // 128) + i
    if not full:
        S.dma("pool", k.STB[tile_g], PS.stash_ap, reads=PS.stash_bufs, writes=[S.dbuf(("STB", tile_g))])
        S.dma("pool", k.STG[tile_g], og[:, :], reads=[og], writes=[S.dbuf(("STG", tile_g))])
        db = S.dbuf(("HF", tile_g))
        S.dma("pool", k.HF[off + 128 * i:off + 128 * (i + 1), :], HY[:, :], reads=[HY], writes=[db])
        return
    finalizeA(k, a, si, T, cond, off, i, nt, ps1, ps2, HY, PS)


def finalizeA(k, a, si, T, cond, off, i, nt, ps1, ps2, HY, pset):
    S, l = k.S, a.l
    w_in, w_out = a.w_in, a.w_out
    hb = a.hb[i % 3]
    hcur = lambda kc: hb[:, kc, 1:129]
    cst = k.cstb
    HYf = a.HYf[i % 2]
    f1, f2, f3 = a.fin1, a.fin2, a.fin3
    v4 = lambda ap: ap.rearrange("p (h e) -> p h e", h=4)
    ym, yg, ys = a.yall[:, 0, :], a.yall[:, 1, :], a.yall[:, 2, :]

    ps3, ps4 = S.bank(), S.bank()
    MM(S, [hb, w_in], [ps3], [(ps3[:, 0:512], hcur(kc), w_in[:, kc, 1296:1808], kc == 0, kc == 7) for kc in range(8)])
    MM(S, [hb, w_in], [ps4], [(ps4[:, 0:256], hcur(kc), w_in[:, kc, 1808:2064], kc == 0, kc == 7) for kc in range(8)]
       + [(ps4[:, 256:512], hcur(kc), w_in[:, kc, 1040:1296], kc == 0, kc == 7) for kc in range(8)])
    has_p, has_n = i > 0, i < nt - 1
    ps5 = S.bank()
    mm5 = []
    if has_p:
        hp_ = a.hb[(i - 1) % 3]
        mm5 += [(ps5[0:8, 0:256], hp_[:, kc, 121:129], w_in[:, kc, 1040:1296], kc == 0, kc == 7) for kc in range(8)]
    if has_n:
        hn_ = a.hb[(i + 1) % 3]
        mm5 += [(ps5[0:8, 256:512], hn_[:, kc, 1:9], w_in[:, kc, 1040:1296], kc == 0, kc == 7) for kc in range(8)]
    if mm5:
        rd = [w_in] + ([a.hb[(i - 1) % 3]] if has_p else []) + ([a.hb[(i + 1) % 3]] if has_n else [])
        MM(S, rd, [ps5], mm5)

    TT(S, "pool", [HY, HYf], [f1], f1[:, :], HY[:, 0:256], HYf[:, 0:256], ALU.add)
    S.op("dve", lambda e: e.tensor_reduce(a.st4[:, :], v4(f1[:, :]), AX.X, ALU.add), [f1], [a.st4])
    TS(S, "dve", [a.st4], [a.st4], a.st4[:, :], a.st4[:, :], 1.0 / 64.0, None, ALU.mult)
    TT(S, "pool", [f1, a.st4], [f1], v4(f1[:, :]), v4(f1[:, :]), bc(a.st4[:, :].unsqueeze(2), [128, 4, 64]), ALU.subtract)
    TT(S, "pool", [f1], [f2], f2[:, :], f1[:, :], f1[:, :], ALU.mult)
    S.op("dve", lambda e: e.tensor_reduce(a.st4b[:, :], v4(f2[:, :]), AX.X, ALU.add), [f2], [a.st4b])
    TS(S, "dve", [a.st4b], [a.st4b], a.st4b[:, :], a.st4b[:, :], 1.0 / 64.0, EPS, ALU.mult, ALU.add)
    TT(S, "pool", [a.st4b, k.cm05], [a.st4b], a.st4b[:, :], a.st4b[:, :], bc(k.cm05[:, 0:1], [128, 4]), ALU.pow)
    TT(S, "pool", [f1, a.st4b], [f1], v4(f1[:, :]), v4(f1[:, :]), bc(a.st4b[:, :].unsqueeze(2), [128, 4, 64]), ALU.mult)
    TT(S, "pool", [f1, k.mng], [f1], f1[:, :], f1[:, :], k.mng[:, :], ALU.mult)
    ACT(S, [a.eo], [a.eo], a.eo[:, :], a.eo[:, :], AF.Ln, bias=1.0)
    ACT(S, [a.eo], [a.eo], a.eo[:, :], a.eo[:, :], AF.Exp, scale=-1.0)
    TT(S, "pool", [f1, a.eo], [a.yall], ym, f1[:, :], a.eo[:, :], ALU.mult)

    CP(S, "act", [ps4], [a.z_sb], a.z_sb[:, :], ps4[:, 0:256])
    ACT(S, [ps4], [a.ez], a.ez[:, :], ps4[:, 0:256], AF.Exp, scale=-1.0)
    TT(S, "pool", [HY, HYf], [f2], f2[:, :], HY[:, 256:512], HYf[:, 256:512], ALU.add)
    TT(S, "pool", [pset.x_tm, k.dsk], [f3], v4(f3[:, :]), v4(pset.x_tm[:, :]), bc(k.dsk[:, :].unsqueeze(2), [128, 4, 64]), ALU.mult)
    TT(S, "pool", [f2, f3], [f2], f2[:, :], f2[:, :], f3[:, :], ALU.add)
    ACT(S, [a.ez], [a.ez], a.ez[:, :], a.ez[:, :], AF.Ln, bias=1.0)
    ACT(S, [a.ez], [a.ez], a.ez[:, :], a.ez[:, :], AF.Exp, scale=-1.0)
    TT(S, "pool", [a.ez, a.z_sb], [a.ez], a.ez[:, :], a.ez[:, :], a.z_sb[:, :], ALU.mult)
    TT(S, "pool", [f2, a.ez], [f2], f2[:, :], f2[:, :], a.ez[:, :], ALU.mult)
    TT(S, "pool", [f2], [f3], f3[:, :], f2[:, :], f2[:, :], ALU.mult)
    S.op("dve", lambda e: e.tensor_reduce(a.st4[:, 0:2], f3[:, :].rearrange("p (g e) -> p g e", g=2), AX.X, ALU.add), [f3], [a.st4])
    TS(S, "dve", [a.st4], [a.st4], a.st4[:, 0:2], a.st4[:, 0:2], 1.0 / 128.0, EPS, ALU.mult, ALU.add)
    TT(S, "pool", [a.st4, k.cm05], [a.st4], a.st4[:, 0:2], a.st4[:, 0:2], bc(k.cm05[:, 0:1], [128, 2]), ALU.pow)
    TT(S, "pool", [f2, a.st4], [f2], f2[:, :].rearrange("p (g e) -> p g e", g=2), f2[:, :].rearrange("p (g e) -> p g e", g=2),
       bc(a.st4[:, 0:2].unsqueeze(2), [128, 2, 128]), ALU.mult)
    TT(S, "pool", [f2, k.sng], [a.yall], ys, f2[:, :], k.sng[:, :], ALU.mult)

    CP(S, "act", [ps3], [a.gu], a.gu[:, :], ps3[:, 0:256])
    S.op("dve", lambda e: e.bn_stats(a.gst[:, :], ps3[:, 256:512]), [ps3], [a.gst])
    S.op("dve", lambda e: e.bn_aggr(a.gmv[:, :], a.gst[:, :]), [a.gst], [a.gmv])
    TS(S, "dve", [a.gmv], [a.gve], a.gve[:, :], a.gmv[:, 1:2], EPS, None, ALU.add)
    TT(S, "pool", [a.gve, k.cm05], [a.grs], a.grs[:, :], a.gve[:, :], k.cm05[:, :], ALU.pow)
    TS(S, "dve", [ps3, a.gmv, a.grs], [a.gvb], a.gvb[:, :], ps3[:, 256:512], a.gmv[:, 0:1], a.grs[:, 0:1], ALU.subtract, ALU.mult)
    psG = S.bank()
    MM(S, [k.wsT, a.gvb], [psG], [(psG[:, h * 64:(h + 1) * 64], k.wsT[:, h, :], a.gvb[:, h * 64:(h + 1) * 64], True, True) for h in range(4)])
    TT(S, "dve", [psG, k.bsT], [f3], v4(f3[:, :]), v4(psG[:, 0:256]), bc(k.bsT[:, :].unsqueeze(2), [128, 4, 64]), ALU.add)
    TT(S, "pool", [f3, a.gu], [a.yall], yg, f3[:, :], a.gu[:, :], ALU.mult)

    CP(S, "act", [ps4], [a.pc], a.pc[:, :], ps4[:, 256:512])
    if has_p:
        CP(S, "act", [ps5], [a.pcP], a.pcP[:, :], ps5[0:8, 0:256])
    if has_n:
        CP(S, "act", [ps5], [a.pcN], a.pcN[:, :], ps5[0:8, 256:512])
    var = "int" if (has_p and has_n) else ("first" if has_n else ("last" if has_p else "int"))
    psP = S.bank()
    mmp = []
    for g in range(4):
        o_ = psP[:, g * 64:(g + 1) * 64]
        seqm = [(cview(k, f"pA{g}{var}"), a.pc[:, g * 64:(g + 1) * 64])]
        if has_p:
            seqm.append((cview(k, f"pP{g}", 8), a.pcP[0:8, g * 64:(g + 1) * 64]))
        if has_n:
            seqm.append((cview(k, f"pN{g}", 8), a.pcN[0:8, g * 64:(g + 1) * 64]))
        for n_, (lh, rh) in enumerate(seqm):
            mmp.append((o_, lh, rh, n_ == 0, n_ == len(seqm) - 1))
    MM(S, [a.c2b, a.pc, a.pcP, a.pcN], [psP], mmp)
    CP(S, "act", [psP], [a.plb], a.plb[:, :], psP[:, 0:256])
    psT2 = S.bank()
    t2v = bview(psT2, BF16)
    TR(S, [a.plb, k.identb], [psT2], [(t2v[:, j * 128:(j + 1) * 128], a.plb[:, j * 128:(j + 1) * 128], k.identb[:, :]) for j in range(2)])
    CP(S, "dve", [psT2], [a.plT], a.plT[:, :, :], t2v[:, 0:256].rearrange("p (j t) -> p j t", j=2))
    psW = S.bank()
    MM(S, [k.wpb, a.plT], [psW], [(psW[:, j * 128:(j + 1) * 128], k.wpb[:, j, :], a.plT[:, j, :], True, True) for j in range(2)])
    cT = a.concatT
    for j in range(2):
        ACT(S, [psW, k.psc], [cT], cT[:, 2 + j, :], psW[:, j * 128:(j + 1) * 128], AF.Identity, scale=k.psc[:, j:j + 1])

    psT3 = S.bank()
    t3v = bview(psT3, BF16)
    TR(S, [a.yall, k.identb], [psT3], [(t3v[:, (m3 * 2 + j) * 128:(m3 * 2 + j + 1) * 128], a.yall[:, m3, j * 128:(j + 1) * 128], k.identb[:, :])
                                      for m3 in range(3) for j in range(2)])
    CP(S, "dve", [psT3], [cT], cT[:, 0:2, :], t3v[:, 0:256].rearrange("p (j t) -> p j t", j=2))
    CP(S, "act", [psT3], [cT], cT[:, 4:8, :], t3v[:, 256:768].rearrange("p (j t) -> p j t", j=4))

    p0, p1 = S.bank(), S.bank()
    for hlf, pb in enumerate((p0, p1)):
        MM(S, [cT, w_out], [pb], [(pb[:, :], cT[:, kc, :], w_out[:, kc, hlf * 512:(hlf + 1) * 512], kc == 0, kc == 7) for kc in range(8)])
    xb = a.xq[i % 2]
    resid_ln(k, xb, (p0, p1), xb, a.rsm)
    r0 = off + 128 * i
    db = S.dbuf(("xoutA", l, r0 // 128))
    S.dma("pool", rows_ap(k, a.dst, "out", r0, 128), xb[:, :], reads=[xb], writes=[db])
    if a.dst is None:
        k.final_bufs.append(db)


_CACHE = {}


def gather_outputs(results, cfg, n):
    NP, Tp, Ts = cfg.NP, cfg.Tp, cfg.Ts
    y_p = np.concatenate([r["yp"].reshape(NP, Tp, D) for r in results], 0)
    y_s = np.stack([r["ys"].reshape(Ts, D) for r in results], 0)
    oc = np.concatenate([r["oc"] for r in results], 0)
    on = np.concatenate([r["on"] for r in results], 0)
    om = np.concatenate([r["om"].reshape(NP, 2, 2, 4) for r in results], 0)
    os_ = np.concatenate([r["os"] for r in results], 0)
    f = lambda a: np.ascontiguousarray(a, dtype=np.float32)
    return (f(y_p), f(y_s), f(oc), f(on), f(om), f(os_))


def kernel(**inputs):
    n = 8
    xs = np.asarray(inputs["x_sample"])
    xp = np.asarray(inputs["x_prompt"])
    cfg = Cfg(Ts=xs.shape[1], NP=xp.shape[0] // n, Tp=xp.shape[1], L=2)
    key = (cfg.Ts, cfg.NP, cfg.Tp)
    if key not in _CACHE:
        _CACHE[key] = build(cfg)
    nc, _ = _CACHE[key]
    in_maps = [shard_inputs(inputs, cfg, c) for c in range(n)]
    res = run_bass_kernel_spmd(nc, in_maps, core_ids=list(range(n)))
    return gather_outputs(res.results, cfg, n)
```

```python
import math
import numpy as np
import ml_dtypes
from contextlib import ExitStack
import concourse.bass as bass
import concourse.mybir as mybir
from concourse.bass_utils import run_bass_kernel_spmd

F32 = mybir.dt.float32
BF16 = mybir.dt.bfloat16
AF = mybir.ActivationFunctionType
ALU = mybir.AluOpType
AX = mybir.AxisListType
DTSZ = {F32: 4, BF16: 2}

D = 1024
DIN = 2840
DFF = 2816
EPS = 1e-5
ALPHA = 4.0 ** 0.25
NEG = -30000.0

ENGS = ("pe", "act", "dve", "pool", "sp")
EPOCH = 30000
NDMA_SEM = 8
SCHED_CP_B = 0.05
BANK_GROUPS = {0: (0, 5), 1: (5, 8)}
SCHED_CP = 0.2
SCHED_LAT = 0.1


def prod(l):
    r = 1
    for x in l:
        r *= int(x)
    return r


class Buf:
    __slots__ = ("name", "v", "last_w", "readers", "off", "excl")

    def __init__(self, name, v, floor=None):
        self.off = -1
        self.excl = False
        self.name = name
        self.v = v
        self.last_w = floor
        self.readers = []

    def __getitem__(self, k):
        return self.v[k]


class Op:
    __slots__ = ("eng", "fn", "deps", "is_dma", "idx", "sig", "has_dep", "vc", "name", "cost", "cp")

    def __init__(self, eng, fn, is_dma, name):
        self.cost = 0.4
        self.eng = eng
        self.fn = fn
        self.is_dma = is_dma
        self.deps = []
        self.sig = None
        self.has_dep = False
        self.vc = None
        self.name = name


class Sched:
    def __init__(self, nc, es, arena_bytes):
        self.nc = nc
        self.es = es
        self.ops = []
        self.floor = None
        self.bufs = []
        self.arena = es.enter_context(nc.sbuf_tensor("arena", [128, arena_bytes // 4], F32))
        self.arena_bytes = arena_bytes
        self.off = 0
        self.peak = 0
        self.banks = []
        for i in range(8):
            t = es.enter_context(nc.psum_tensor(f"bank{i}", [128, 512], F32))
            self.banks.append(Buf(f"bank{i}", t))
            self.banks[-1].excl = True
        self.bank_i = 0
        self.bank_g = {}
        self.dram_bufs = {}

    def sb(self, name, shape, dtype=F32):
        shape = [int(s) for s in shape]
        if getattr(self, "verbose", False):
            print(f"  sb {name} {shape} {prod(shape[1:]) * DTSZ[dtype]} at {self.off}")
        n = prod(shape[1:])
        nb = n * DTSZ[dtype]
        off = (self.off + 31) // 32 * 32
        assert off + nb <= self.arena_bytes, f"arena overflow allocating {name}: {off + nb}"
        self.off = off + nb
        self.peak = max(self.peak, self.off)
        h = self.arena if dtype == F32 else self.arena.bitcast(dtype)
        e0 = off // DTSZ[dtype]
        v = h[0:shape[0], e0:e0 + n]
        if len(shape) > 2:
            names = " ".join(f"d{i}" for i in range(len(shape) - 1))
            kw = {f"d{i}": shape[i + 1] for i in range(len(shape) - 1)}
            v = v.rearrange(f"p ({names}) -> p {names}", **kw)
        b = Buf(name, v, self.floor)
        b.off = off
        self.bufs.append(b)
        return b

    def mark(self):
        return self.off

    def release(self, mark):
        self.barrier()
        self.bufs = [b for b in self.bufs if b.off < mark]
        self.off = mark

    def bank(self, g=None):
        if g is None or not BANK_GROUPS:
            b = self.banks[self.bank_i]
            self.bank_i = (self.bank_i + 1) % 8
            return b
        lo, hi = BANK_GROUPS[g]
        gi = self.bank_g.get(g, 0)
        self.bank_g[g] = gi + 1
        return self.banks[lo + gi % (hi - lo)]

    def dbuf(self, key):
        if key not in self.dram_bufs:
            self.dram_bufs[key] = Buf(str(key), None, None)
        return self.dram_bufs[key]

    def op(self, eng, fn, reads=(), writes=(), name=None, dma=False, cost=None):
        o = Op(eng, fn, dma, name)
        o.cp = getattr(self, "cp", SCHED_CP)
        if cost is not None:
            o.cost = cost
        deps = set()
        ex = [b for b in reads if b.excl]
        if ex:
            reads = [b for b in reads if not b.excl]
            writes = list(writes) + [b for b in ex if b not in writes]
        for b in reads:
            if b.last_w is not None:
                deps.add(b.last_w)
        for b in writes:
            if b.last_w is not None:
                deps.add(b.last_w)
            for r in b.readers:
                deps.add(r)
        o.deps = list(deps)
        o.idx = len(self.ops)
        for d in o.deps:
            d.has_dep = True
        for b in reads:
            b.readers.append(o)
        for b in writes:
            b.last_w = o
            b.readers = []
        self.ops.append(o)
        return o

    def dma(self, q, out, in_, reads=(), writes=(), name=None, **kw):
        nbytes = prod(out.shape) * 4
        return self.op(q, lambda e: e.dma_start(out=out, in_=in_, **kw), reads, writes, name=name, dma=True,
                       cost=2.0 + nbytes / 150e3)

    def barrier(self):
        allb = self.bufs + self.banks + list(self.dram_bufs.values())
        o = self.op("sp", None, reads=[], writes=allb, name="barrier")
        self.floor = o
        return o

    def list_schedule(self, ops):
        import heapq
        LAT = SCHED_LAT
        out = []
        seg = []
        segs = []
        for o in ops:
            if o.fn is None:
                segs.append(seg)
                segs.append([o])
                seg = []
            else:
                seg.append(o)
        segs.append(seg)
        finish = {}
        for seg in segs:
            if len(seg) <= 1:
                for o in seg:
                    finish[o] = 0.0
                    out.append(o)
                continue
            inseg = set(seg)
            seg_cp = seg[len(seg) // 2].cp
            indeg = {}
            users = {}
            for o in seg:
                n = 0
                for d in o.deps:
                    if d in inseg:
                        n += 1
                        users.setdefault(d, []).append(o)
                indeg[o] = n
            tail = {}
            for o in reversed(seg):
                t = 0.0
                for u in users.get(o, ()):
                    if tail[u] > t:
                        t = tail[u]
                tail[o] = t + o.cost + 0.2
            eng_time = {e: 0.0 for e in ENGS}
            ready_at = {}
            heap = []
            for o in seg:
                if indeg[o] == 0:
                    ready_at[o] = 0.0
                    heapq.heappush(heap, (0.0, o.idx, o))
            while heap:
                best = None
                cand = []
                while heap and len(cand) < 24:
                    cand.append(heapq.heappop(heap))
                bi = None
                for ci, (ra, idx, o) in enumerate(cand):
                    stt = max(ra, eng_time[o.eng])
                    key = (stt - seg_cp * tail[o], idx)
                    if best is None or key < best:
                        best, bi = key, ci
                ra, idx, o = cand.pop(bi)
                for c in cand:
                    heapq.heappush(heap, c)
                stt = max(ra, eng_time[o.eng])
                if o.is_dma:
                    eng_time[o.eng] = stt + 0.15
                    fin_t = stt + o.cost
                else:
                    fin_t = stt + o.cost
                    eng_time[o.eng] = fin_t
                finish[o] = fin_t
                out.append(o)
                for u in users.get(o, ()):
                    t = fin_t + (LAT if u.eng != o.eng else 0.3)
                    if ready_at.get(u, 0.0) < t:
                        ready_at[u] = t
                    indeg[u] -= 1
                    if indeg[u] == 0:
                        heapq.heappush(heap, (ready_at[u], u.idx, u))
            self.est_time = getattr(self, "est_time", 0.0) + max(eng_time.values())
        for i, o in enumerate(out):
            o.idx = i
        return out

    def emit(self, final_bufs):
        nc, es = self.nc, self.es
        fin = self.op("sp", None, reads=list(final_bufs), name="final")
        if getattr(self, "reorder", True):
            self.ops = self.list_schedule(self.ops)
        cnt, dma_n, semkeys = {}, {}, []
        for o in self.ops:
            if not o.has_dep:
                continue
            if o.is_dma:
                n = dma_n.get(o.eng, 0)
                dma_n[o.eng] = n + 1
                key = ("dma", o.eng, n % NDMA_SEM)
                cnt[key] = cnt.get(key, 0) + 16
                o.sig = (key, cnt[key])
            else:
                tot = cnt.get(("n", o.eng), 0)
                cnt[("n", o.eng)] = tot + 1
                key = ("c", o.eng, tot // EPOCH)
                o.sig = (key, tot % EPOCH + 1)
            if o.sig[0] not in semkeys:
                semkeys.append(o.sig[0])
        sems = {k: es.enter_context(nc.semaphore("s_" + "_".join(str(x) for x in k))) for k in semkeys}
        per_eng = {e: [] for e in ENGS}
        seen = {e: {} for e in ENGS}
        nwaits = 0
        for o in self.ops:
            s = seen[o.eng]
            need = {}
            for d in o.deps:
                k, c = d.sig
                if s.get(k, 0) < c:
                    need[k] = max(need.get(k, 0), c)
            if o.is_dma and o.sig is not None:
                k, c = o.sig
                if c > 16 and s.get(k, 0) < c - 16:
                    need[k] = max(need.get(k, 0), c - 16)
            for d in o.deps:
                for k, c in d.vc.items():
                    if s.get(k, 0) < c:
                        s[k] = c
            for k, c in need.items():
                if s.get(k, 0) < c:
                    s[k] = c
            o.deps = need
            nwaits += len(need)
            vc = dict(s)
            if o.sig is not None:
                vc[o.sig[0]] = max(vc.get(o.sig[0], 0), o.sig[1])
                if not o.is_dma:
                    for ep in range(o.sig[0][2]):
                        vc[("c", o.eng, ep)] = EPOCH
            o.vc = vc
            per_eng[o.eng].append(o)

        def body_for(engname):
            def body(eng):
                for o in per_eng[engname]:
                    for k, c in o.deps.items():
                        eng.wait_ge(sems[k], c)
                    if o.fn is not None:
                        ins = o.fn(eng)
                        if o.sig is not None:
                            ins.then_inc(sems[o.sig[0]], 16 if o.is_dma else 1)
                    elif o.sig is not None:
                        eng.nop().then_inc(sems[o.sig[0]], 1)
            return body

        with nc.Block() as block:
            block.sync(body_for("sp"))
            block.scalar(body_for("act"))
            block.vector(body_for("dve"))
            block.gpsimd(body_for("pool"))
            block.tensor(body_for("pe"))
        return {"ops": len(self.ops), "waits": nwaits, "sems": len(sems),
                "per_eng": {e: len(v) for e, v in per_eng.items()}, "sbuf_peak": self.peak}


def _c(out, base=0.25, per=1.0 / 1000.0):
    return base + prod(out.shape[1:]) * per


def ACT(S, r, w, out, in_, func, bias=None, scale=None):
    kw = {}
    if bias is not None:
        kw["bias"] = bias
    if scale is not None:
        kw["scale"] = scale
    return S.op("act", lambda e: e.activation(out, in_, func, **kw), r, w, cost=_c(out, 0.3, 1 / 1200.0))


def TS(S, eng, r, w, out, in0, s1, s2, op0, op1=None):
    if op1 is None:
        return S.op(eng, lambda e: e.tensor_scalar(out, in0, s1, None, op0), r, w, cost=_c(out))
    return S.op(eng, lambda e: e.tensor_scalar(out, in0, s1, s2, op0, op1), r, w, cost=_c(out))


def TT(S, eng, r, w, out, in0, in1, op):
    return S.op(eng, lambda e: e.tensor_tensor(out, in0, in1, op), r, w, cost=_c(out))


def STT(S, eng, r, w, out, in0, scalar, in1, op0, op1):
    return S.op(eng, lambda e: e.scalar_tensor_tensor(out, in0, scalar, in1, op0, op1), r, w, cost=_c(out))


def CP(S, eng, r, w, out, in_):
    if eng == "act":
        return S.op("act", lambda e: e.copy(out, in_), r, w, cost=_c(out, 0.3, 1 / 1200.0))
    return S.op(eng, lambda e: e.tensor_copy(out, in_), r, w, cost=_c(out))


def MSET(S, eng, w, out, val):
    return S.op(eng, lambda e: e.memset(out, val), [], w)


def MM(S, r, w, mms):
    mms = list(mms)

    def fn(e):
        ins = None
        for (o, l, rh, st, sp) in mms:
            ins = e.matmul(o, l, rh, start=st, stop=sp)
        return ins
    cost = 0.1
    for (o, l, rh, st, sp) in mms:
        cost += max(64, prod(rh.shape[1:])) / 2400.0 * (4.0 if rh.dtype == F32 else 1.0) + 0.02
    return S.op("pe", fn, r, w, cost=cost)


def TR(S, r, w, trs):
    trs = list(trs)

    def fn(e):
        ins = None
        for (o, i, idt) in trs:
            ins = e.transpose(o, i, idt)
        return ins
    return S.op("pe", fn, r, w, cost=0.1 + 0.12 * len(trs))


def bc(ap, shape):
    return ap.to_broadcast([int(s) for s in shape])


POOL_W = (2, 4, 8, 16)


def make_consts():
    cols = {}
    parts = []
    off = [0]

    def add(name, arr):
        a = np.zeros((128, arr.shape[1]), np.float32)
        a[:arr.shape[0]] = arr
        cols[name] = (off[0], arr.shape[1])
        off[0] += arr.shape[1]
        parts.append(a)

    idx = np.arange(128)
    s_, t_ = idx[:, None], idx[None, :]
    add("ident", np.eye(128, dtype=np.float32))
    add("ones", np.ones((128, 128), np.float32))
    add("tri0", (s_ <= t_).astype(np.float32))
    add("tri1", (s_ >= t_).astype(np.float32))
    add("neg0", np.where(s_ <= t_, 0.0, NEG).astype(np.float32))
    add("neg1", np.where(s_ >= t_, 0.0, NEG).astype(np.float32))
    n1 = off[0]
    for g, w in enumerate(POOL_W):
        h = w // 2
        band = ((s_ >= t_ - h) & (s_ < t_ + h)).astype(np.float32)
        cnt_int = np.full(128, float(w))
        cnt_first = (idx + h) - np.maximum(idx - h, 0)
        cnt_last = np.minimum(idx + h, 128) - (idx - h)
        eye = np.eye(128, dtype=np.float32)
        add(f"pA{g}int", band / cnt_int[None, :] - eye)
        add(f"pA{g}first", band / cnt_first[None, :] - eye)
        add(f"pA{g}last", band / cnt_last[None, :] - eye)
        sp = np.arange(8)[:, None]
        add(f"pP{g}", (((sp - 8) >= t_ - h) & ((sp - 8) < t_ + h)).astype(np.float32) / w)
        add(f"pN{g}", (((128 + sp) >= t_ - h) & ((128 + sp) < t_ + h)).astype(np.float32) / w)
    add("jrow", np.tile(np.arange(256, dtype=np.float32)[None, :], (128, 1)))
    add("pcol", (idx % 64).astype(np.float32)[:, None])
    full = np.concatenate(parts, axis=1)
    cols2 = {kk: (o - n1, n) for kk, (o, n) in cols.items() if o >= n1}
    cols1 = {kk: (o, n) for kk, (o, n) in cols.items() if o < n1}
    return full[:, :n1].copy(), cols1, full[:, n1:].copy(), cols2


class Cfg:
    def __init__(self, Ts=4096, NP=4, Tp=256, L=2, debug=()):
        self.Ts, self.NP, self.Tp, self.L = Ts, NP, Tp, L
        self.debug = tuple(debug)
        self.seqs = [("s", Ts, 0, 0)] + [(f"p{j}", Tp, 1, Ts + j * Tp) for j in range(NP)]
        self.Ttot = Ts + NP * Tp


INPUT_SPECS = lambda c: [
    ("xs", [c.Ts, D]), ("xp", [c.NP * c.Tp, D]),
    ("st_c", [2, 2, 4, 64, 64]), ("st_n", [2, 2, 4, 64]), ("st_m", [2, 8]), ("st_s", [2, 2, 4, 64, 128]),
    ("cond", [2, D]),
    ("w_ada", [2, D, 6 * D]), ("b_ada", [2, 6 * D]), ("w_in", [2, D, DIN]), ("b_ig", [2, 8]), ("b_fg", [2, 8]),
    ("mnorm_g", [2, 256]), ("w_pool", [2, 4, 64, 64]), ("pool_scale", [2, 256]), ("w_sp", [2, 4, 128, 128]),
    ("b_sp", [2, 4, 128]), ("sconv_w", [2, 3, 768]), ("sconv_b", [2, 768]), ("dt_bias", [2, 8]),
    ("a_log", [2, 8]), ("ssd_d", [2, 4]), ("snorm_g", [2, 256]), ("w_out", [2, D, D]),
    ("ln1_g", [2, D]), ("ln1_b", [2, D]), ("w_up", [2, D, 2 * DFF]), ("fconv_w", [2, 3, 2 * DFF]),
    ("fconv_b", [2, 2 * DFF]), ("w_dn", [2, DFF, D]), ("ln2_g", [2, D]), ("ln2_b", [2, D]),
]
OUTPUT_SPECS = lambda c: [
    ("ys", [c.Ts, D]), ("yp", [c.NP * c.Tp, D]), ("oc", [c.NP, 2, 2, 4, 64, 64]), ("on", [c.NP, 2, 2, 4, 64]),
    ("om", [c.NP, 2, 8]), ("os", [c.NP, 2, 2, 4, 64, 128]),
]


class K:
    pass


def build(cfg):
    nc = bass.Bass("TRN2", target_bir_lowering=False)
    cst_np, ccols, cst2_np, ccols2 = make_consts()
    I = {}
    for name, shape in INPUT_SPECS(cfg):
        I[name] = nc.dram_tensor(name, shape, F32, kind="ExternalInput")
    I["cst"] = nc.dram_tensor("cst", list(cst_np.shape), F32, kind="ExternalInput")
    I["cst2"] = nc.dram_tensor("cst2", list(cst2_np.shape), F32, kind="ExternalInput")
    O = {}
    for name, shape in OUTPUT_SPECS(cfg):
        O[name] = nc.dram_tensor(name, shape, F32, kind="ExternalOutput")
    DBG = {}
    for name, shape in cfg.debug:
        DBG[name] = nc.dram_tensor("dbg_" + name, shape, F32, kind="ExternalOutput")
    ntile = cfg.Ttot // 128
    XA = nc.dram_tensor("scr_xa", [cfg.Ttot, D], F32, kind="Internal")
    XB = nc.dram_tensor("scr_xb", [cfg.Ttot, D], F32, kind="Internal")
    HF = nc.dram_tensor("scr_hf", [cfg.Ttot, 512], F32, kind="Internal")
    HT = nc.dram_tensor("scr_ht", [ntile, 128, 8 * 128], BF16, kind="Internal")
    ED = nc.dram_tensor("scr_e", [64, 512], F32, kind="Internal")
    STB = nc.dram_tensor("scr_stb", [ntile, 128, 2304], BF16, kind="Internal")
    STG = nc.dram_tensor("scr_stg", [ntile, 128, 280], F32, kind="Internal")

    with ExitStack() as es:
        S = Sched(nc, es, 204 * 1024)
        k = K()
        k.S, k.cfg, k.I, k.O, k.DBG = S, cfg, I, O, DBG
        k.XA, k.XB, k.HF, k.HT, k.ED = XA, XB, HF, HT, ED
        k.STB, k.STG = STB, STG
        k.final_bufs = []
        k.ccols2 = ccols2
        k.W2 = cst2_np.shape[1]
        setup(k, ccols)
        for l in range(cfg.L):
            layer(k, l)
        stats = S.emit(k.final_bufs)
    return nc, stats


def cview(k, name, rows=128):
    if name in k.ccols:
        o, n = k.ccols[name]
        return k.cst[0:rows, o:o + n]
    o, n = k.ccols2[name]
    return k.cst2[0:rows, o:o + n]


def load_cst2(k):
    S = k.S
    b = S.sb("cst2", [128, k.W2])
    k.cst2b = b
    k.cst2 = b.v
    S.dma("sp", b[:, :], k.I["cst2"][:, :], writes=[b])
    return b


def setup(k, ccols):
    S, I, cfg = k.S, k.I, k.cfg
    k.ccols = ccols
    W = sum(n for (_, n) in ccols.values())
    cstb = S.sb("cst", [128, W])
    k.cstb = cstb
    k.cst = cstb.v
    S.dma("sp", cstb[:, :], I["cst"][:, :], writes=[cstb])
    k.identb = S.sb("identb", [128, 128], BF16)
    CP(S, "dve", [cstb], [k.identb], k.identb[:, :], cview(k, "ident"))
    k.cm05 = S.sb("cm05", [128, 1])
    MSET(S, "pool", [k.cm05], k.cm05[:, :], -0.5)
    k.cm1 = S.sb("cm1", [128, 1])
    MSET(S, "pool", [k.cm1], k.cm1[:, :], -1.0)

    L = cfg.L
    k.modT = S.sb("modT", [128, L, 48, 2])
    layer_consts_alloc(k)
    m1 = S.mark()
    c2b = load_cst2(k)
    fr = S.sb("pe_fr", [64, 256])
    ang = S.sb("pe_ang", [64, 256])
    et = S.sb("pe_e", [64, 512])
    et2 = S.sb("pe_e2", [64, 512])
    sq = S.sb("pe_sq", [64, 256])
    ACT(S, [c2b], [fr], fr[:, :], cview(k, "jrow", 64), AF.Exp, scale=-math.log(10000.0) / 256.0)
    TS(S, "dve", [fr, c2b], [ang], ang[:, :], fr[:, :], cview(k, "pcol", 64), None, ALU.mult)
    ACT(S, [ang], [et], et[:, 0:256], ang[:, :], AF.Sin, scale=1.0 / 32.0)
    ACT(S, [ang], [et], et[:, 256:512], ang[:, :], AF.Sin, scale=-1.0 / 32.0, bias=math.pi / 2.0)
    cur, nxt = et, et2
    for it in range(5):
        TT(S, "dve", [cur], [sq], sq[:, :], cur[:, 0:256], cur[:, 0:256], ALU.mult)
        STT(S, "dve", [cur], [nxt], nxt[:, 0:256], cur[:, 0:256], 2.0, cur[:, 256:512], ALU.mult, ALU.mult)
        TS(S, "dve", [sq], [nxt], nxt[:, 256:512], sq[:, :], -2.0, 1.0, ALU.mult, ALU.add)
        cur, nxt = nxt, cur
    et = cur
    edb = S.dbuf("ED")
    S.dma("sp", k.ED[:, :], et[:, :], reads=[et], writes=[edb])

    condT = S.sb("condT", [128, 8, 2])
    for c in range(2):
        S.dma("sp", condT[:, :, c], I["cond"][c].rearrange("(kc p) -> p kc", p=128), writes=[condT],
              allow_slow_non_contiguous=True)
    esg = S.sb("cond_e", [128, 8, 2])
    ACT(S, [condT], [esg], esg[:, :, :], condT[:, :, :], AF.Exp, scale=-1.0)
    TS(S, "dve", [esg], [esg], esg[:, :, :], esg[:, :, :], 1.0, None, ALU.add)
    TT(S, "pool", [esg, k.cm1], [esg], esg[:, :, :], esg[:, :, :], bc(k.cm1[:, 0:1].unsqueeze(2), [128, 8, 2]), ALU.pow)
    TT(S, "pool", [condT, esg], [condT], condT[:, :, :], condT[:, :, :], esg[:, :, :], ALU.mult)
    badaT = S.sb("badaT", [128, L, 48])
    k.cf_st = [S.sb(f"cf_sx{j}", [128, 128]) for j in range(2)]
    for l in range(L):
        colform(k, badaT, badaT[:, l, :], I["b_ada"][l].rearrange("(j p) -> j p", p=128), 48)
    wst = [S.sb(f"wada_st{j}", [128, 8, 512]) for j in range(4)]
    n = 0
    for l in range(L):
        for cb in range(12):
            st = wst[n % 4]
            n += 1
            S.dma("sp", st[:, :, :], I["w_ada"][l, :, cb * 512:(cb + 1) * 512].rearrange("(kc p) n -> p kc n", p=128),
                  writes=[st])
            pb = S.bank()
            mms = []
            for sub in range(4):
                for kc in range(8):
                    mms.append((pb[:, sub * 2:sub * 2 + 2], st[:, kc, sub * 128:(sub + 1) * 128], condT[:, kc, :],
                                kc == 0, kc == 7))
            MM(S, [st, condT], [pb], mms)
            TT(S, "dve", [pb, badaT], [k.modT],
               k.modT[:, l, cb * 4:cb * 4 + 4, :],
               pb[:, 0:8].rearrange("p (s c) -> p s c", c=2),
               bc(badaT[:, l, cb * 4:cb * 4 + 4].unsqueeze(2), [128, 4, 2]), ALU.add)
    for l in range(L):
        for grp in (1, 4):
            TS(S, "dve", [k.modT], [k.modT], k.modT[:, l, grp * 8:(grp + 1) * 8, :],
               k.modT[:, l, grp * 8:(grp + 1) * 8, :], 1.0, None, ALU.add)
    S.release(m1)


class nc_allow:
    def __init__(self, k):
        pass

    def __enter__(self):
        return self

    def __exit__(self, *a):
        return False


def bview(bank, dtype=F32):
    return bank.v if dtype == F32 else bank.v.bitcast(dtype)


def layer_consts_alloc(k):
    S = k.S
    k.bif = S.sb("bif", [128, 16])
    k.coef = S.sb("coef", [128, 2, 8])
    k.dtb = S.sb("dtb", [128, 8])
    k.dsk = S.sb("dsk", [128, 4])
    k.mng = S.sb("mng", [128, 256])
    k.sng = S.sb("sng", [128, 256])
    k.psc = S.sb("psc", [128, 2])
    k.wpb = S.sb("wpb", [128, 2, 128], BF16)
    k.wsT = S.sb("wsT", [128, 4, 128], BF16)
    k.bsT = S.sb("bsT", [128, 4])
    k.scw = S.sb("scw", [128, 4, 6])
    k.fcw = S.sb("fcw", [128, 4, 44])
    k.lng = S.sb("lng", [128, D])
    k.lnb = S.sb("lnb", [128, D])
    k.gbc = S.sb("gbc", [128, D])
    k.ttmp = [S.sb(f"ttmp{j}", [128, D]) for j in range(1)]
    k.small = {}


def colform(k, dst_buf, dst_ap, src_ap, nb):
    S = k.S
    k.cf_n = getattr(k, "cf_n", 0) + 1
    st = k.cf_st[k.cf_n % len(k.cf_st)]
    S.dma("sp", st[0:nb, :], src_ap, writes=[st])
    pb = S.bank()
    TR(S, [st, k.cstb], [pb], [(pb[:, 0:nb], st[0:nb, :], cview(k, "ident")[0:nb, 0:nb])])
    CP(S, "dve", [pb], [dst_buf], dst_ap, pb[:, 0:nb])


def load_layer_consts(k, l):
    S, I = k.S, k.I
    m = S.mark()
    k.cf_st = [S.sb(f"cf_st{j}", [128, 128]) for j in range(4)]
    row = lambda name, a, b: I[name][l:l + 1, a:b].partition_broadcast(128)
    S.dma("sp", k.bif[:, 0:8], row("b_ig", 0, 8), writes=[k.bif])
    S.dma("sp", k.bif[:, 8:16], row("b_fg", 0, 8), writes=[k.bif])
    TS(S, "dve", [k.bif], [k.bif], k.bif[:, 8:16], k.bif[:, 8:16], -1.0, None, ALU.mult)
    al = S.sb("al_tmp", [128, 8])
    S.dma("sp", al[:, :], row("a_log", 0, 8), writes=[al])
    ACT(S, [al], [al], al[:, :], al[:, :], AF.Exp)
    MSET(S, "pool", [k.coef], k.coef[:, :, :], -1.0)
    TS(S, "dve", [al, k.coef], [k.coef], k.coef[:, :, 4:8], al[:, :].rearrange("p (d h) -> p d h", d=2), -1.0, None, ALU.mult)
    S.dma("sp", k.dtb[:, :], row("dt_bias", 0, 8), writes=[k.dtb])
    S.dma("sp", k.dsk[:, :], row("ssd_d", 0, 4), writes=[k.dsk])
    S.dma("sp", k.mng[:, :], row("mnorm_g", 0, 256), writes=[k.mng])
    S.dma("sp", k.sng[:, :], row("snorm_g", 0, 256), writes=[k.sng])
    colform(k, k.psc, k.psc[:, :], I["pool_scale"][l].rearrange("(j p) -> j p", p=128), 2)
    wp32 = S.sb("wp32", [128, 2, 128])
    MSET(S, "pool", [wp32], wp32[:, :, :], 0.0)
    for g in range(4):
        pr = slice((g % 2) * 64, (g % 2) * 64 + 64)
        S.dma("sp", wp32[pr, g // 2, (g % 2) * 64:(g % 2) * 64 + 64], I["w_pool"][l, g], writes=[wp32])
    CP(S, "dve", [wp32], [k.wpb], k.wpb[:, :, :], wp32[:, :, :])
    ws32 = S.sb("ws32", [128, 4, 128])
    S.dma("sp", ws32[:, :, :], I["w_sp"][l].rearrange("h t s -> t h s"), writes=[ws32])
    pb = S.bank()
    TR(S, [ws32, k.cstb], [pb], [(pb[:, h * 128:(h + 1) * 128], ws32[:, h, :], cview(k, "ident")) for h in range(4)])
    CP(S, "act", [pb], [k.wsT], k.wsT[:, :, :], pb[:, :].rearrange("p (h t) -> p h t", h=4))
    colform(k, k.bsT, k.bsT[:, :], I["b_sp"][l], 4)
    for tap in range(3):
        colform(k, k.scw, k.scw[:, tap, :], I["sconv_w"][l, tap].rearrange("(b p) -> b p", p=128), 6)
        colform(k, k.fcw, k.fcw[:, tap, :], I["fconv_w"][l, tap].rearrange("(b p) -> b p", p=128), 44)
    colform(k, k.scw, k.scw[:, 3, :], I["sconv_b"][l].rearrange("(b p) -> b p", p=128), 6)
    colform(k, k.fcw, k.fcw[:, 3, :], I["fconv_b"][l].rearrange("(b p) -> b p", p=128), 44)
    S.release(m)


def load_weight(k, dst, src2d, nkc, ncols, scope_stage):
    S = k.S
    engs = ("dve", "act")
    piece = 2840
    for kc in range(nkc):
        for c0 in range(0, ncols, piece):
            c1 = min(ncols, c0 + piece)
            st = scope_stage[k.wl_n % len(scope_stage)]
            S.dma("sp", st[:, 0:c1 - c0], src2d[kc * 128:(kc + 1) * 128, c0:c1], writes=[st])
            CP(S, engs[k.wl_n % 2], [st], [dst], dst[:, kc, c0:c1], st[:, 0:c1 - c0])
            k.wl_n += 1


def gate_table(k, l, grp, cond):
    S = k.S
    dg = k.ttmp[0]
    for j in range(8):
        TS(S, "dve", [k.cstb, k.modT], [dg], dg[:, 0:128], cview(k, "ident"), k.modT[:, l, grp * 8 + j, cond:cond + 1], None, ALU.mult)
        if j % 4 == 0:
            pb = S.bank()
        MM(S, [dg, k.cstb], [pb], [(pb[:, (j % 4) * 128:(j % 4 + 1) * 128], cview(k, "ones"), dg[:, 0:128], True, True)])
        if j % 4 == 3:
            CP(S, "act", [pb], [k.gbc], k.gbc[:, (j // 4) * 512:(j // 4 + 1) * 512], pb[:, :])


def make_hT(k, l, which, cond, rows, xbuf, dsts, dst_buf, src_bufs, pos_tile=None):
    S = k.S
    n = 0
    for r, nr in rows:
        S.dma("sp", xbuf[n:n + nr, :], r, reads=src_bufs, writes=[xbuf])
        n += nr
    if pos_tile is not None:
        TT(S, "pool", [xbuf, pos_tile], [xbuf], xbuf[0:n, :], xbuf[0:n, :], pos_tile[0:n, :], ALU.add)
    sm = k.hsm
    st, mv, ve, rstd, xnb = sm["st"], sm["mv"], sm["ve"], sm["rstd"], sm["xnb"]
    S.op("dve", lambda e: e.bn_stats(st[0:n, 0, :], xbuf[0:n, 0:512]), [xbuf], [st])
    S.op("dve", lambda e: e.bn_stats(st[0:n, 1, :], xbuf[0:n, 512:1024]), [xbuf], [st])
    S.op("dve", lambda e: e.bn_aggr(mv[0:n, :], st[0:n, :, :].rearrange("p a b -> p (a b)")), [st], [mv])
    TS(S, "dve", [mv], [ve], ve[0:n, :], mv[0:n, 1:2], EPS, None, ALU.add)
    TT(S, "pool", [ve, k.cm05], [rstd], rstd[0:n, :], ve[0:n, :], k.cm05[0:n, :], ALU.pow)
    TS(S, "dve", [xbuf, mv, rstd], [xnb], xnb[0:n, :], xbuf[0:n, :], mv[0:n, 0:1], rstd[0:n, 0:1], ALU.subtract, ALU.mult)
    pb = S.bank(getattr(k, "hT_bank_group", None))
    pv = bview(pb, BF16)
    TR(S, [xnb, k.identb], [pb],
       [(pv[:, kc * 128:kc * 128 + n], xnb[0:n, kc * 128:(kc + 1) * 128], k.identb[0:n, 0:n]) for kc in range(8)])
    gsh, gsc = (0, 1) if which == 1 else (3, 4)
    for kc in range(8):
        sc = k.modT[:, l, gsc * 8 + kc, cond:cond + 1]
        sh = k.modT[:, l, gsh * 8 + kc, cond:cond + 1]
        if kc % 2 == 0:
            ACT(S, [pb, k.modT], [dst_buf], dsts[kc], pv[:, kc * 128:kc * 128 + n], AF.Identity, bias=sh, scale=sc)
        else:
            TS(S, "dve", [pb, k.modT], [dst_buf], dsts[kc], pv[:, kc * 128:kc * 128 + n], sc, sh, ALU.mult, ALU.add)


def alloc_hsm(k):
    S = k.S
    k.hsm = {"st": S.sb("h_st", [128, 2, 6]), "mv": S.sb("h_mv", [128, 2]), "ve": S.sb("h_ve", [128, 1]),
             "rstd": S.sb("h_rstd", [128, 1]), "xnb": S.sb("h_xnb", [128, D], BF16)}


def resid_ln(k, x_buf, psum_halves, out_buf, nb_small):
    S = k.S
    t0, t1 = k.ttmp[0], out_buf
    for hlf, pb in enumerate(psum_halves):
        sl = slice(hlf * 512, (hlf + 1) * 512)
        TT(S, "dve", [pb, k.gbc], [t0], t0[:, sl], pb[:, :], k.gbc[:, sl], ALU.mult)
    STT(S, "dve", [x_buf, t0], [t0], t0[:, :], x_buf[:, :], ALPHA, t0[:, :], ALU.mult, ALU.add)
    st, mv, ve, rstd, nb = nb_small["st"], nb_small["mv"], nb_small["ve"], nb_small["rstd"], nb_small["nb"]
    S.op("dve", lambda e: e.bn_stats(st[:, 0, :], t0[:, 0:512]), [t0], [st])
    S.op("dve", lambda e: e.bn_stats(st[:, 1, :], t0[:, 512:1024]), [t0], [st])
    S.op("dve", lambda e: e.bn_aggr(mv[:, :], st[:, :, :].rearrange("p a b -> p (a b)")), [st], [mv])
    TS(S, "dve", [mv], [ve], ve[:, :], mv[:, 1:2], EPS, None, ALU.add)
    TT(S, "pool", [ve, k.cm05], [rstd], rstd[:, :], ve[:, :], k.cm05[:, :], ALU.pow)
    STT(S, "dve", [mv, rstd], [nb], nb[:, :], mv[:, 0:1], -1.0, rstd[:, :], ALU.mult, ALU.mult)
    ACT(S, [t0, rstd, nb], [t1], t1[:, :], t0[:, :], AF.Identity, bias=nb[:, 0:1], scale=rstd[:, 0:1])
    TT(S, "pool", [t1, k.lng], [t1], t1[:, :], t1[:, :], k.lng[:, :], ALU.mult)
    TT(S, "pool", [t1, k.lnb], [out_buf], out_buf[:, :], t1[:, :], k.lnb[:, :], ALU.add)


def alloc_rsm(k):
    S = k.S
    return {"st": S.sb("r_st", [128, 2, 6]), "mv": S.sb("r_mv", [128, 2]), "ve": S.sb("r_ve", [128, 1]),
            "rstd": S.sb("r_rstd", [128, 1]), "nb": S.sb("r_nb", [128, 1])}


def seq_src_dst(k, l, phase):
    cfg = k.cfg
    mode = getattr(cfg, "mode", "full")
    if phase == "A":
        src = None if l == 0 else k.XB
        dst = k.XA if mode == "full" else None
    else:
        src = k.XA if mode == "full" else None
        dst = None if l == cfg.L - 1 else k.XB
    return src, dst


def rows_ap(k, handle, which_io, r0, n):
    cfg = k.cfg
    if handle is not None:
        return handle[r0:r0 + n, :]
    if r0 < cfg.Ts:
        t = k.I["xs"] if which_io == "in" else k.O["ys"]
        return t[r0:r0 + n, :]
    t = k.I["xp"] if which_io == "in" else k.O["yp"]
    return t[r0 - cfg.Ts:r0 - cfg.Ts + n, :]


def phaseB(k, l):
    S, I, cfg = k.S, k.I, k.cfg
    S.cp = SCHED_CP_B
    k.hT_bank_group = None
    m = S.mark()
    w_up = S.sb("w_up", [128, 8, 2 * DFF], BF16)
    w_dn = S.sb("w_dn", [128, 22, D], BF16)
    m2 = S.mark()
    stage = [S.sb(f"wstage{j}", [128, 2840]) for j in range(3)]
    k.wl_n = 0
    load_weight(k, w_up, I["w_up"][l], 8, 2 * DFF, stage)
    load_weight(k, w_dn, I["w_dn"][l], 22, D, stage)
    S.release(m2)
    S.dma("sp", k.lng[:, :], I["ln2_g"][l:l + 1, :].partition_broadcast(128), writes=[k.lng])
    S.dma("sp", k.lnb[:, :], I["ln2_b"][l:l + 1, :].partition_broadcast(128), writes=[k.lnb])
    alloc_hsm(k)
    rsm = alloc_rsm(k)
    SEG = 256
    h2T = [S.sb(f"h2T{j}", [128, 8, SEG + 2], BF16) for j in range(2)]
    actT = S.sb("actT", [128, 22, SEG], BF16)
    xt = [S.sb(f"xtB{j}", [128, D]) for j in range(4)]
    xh = k.ttmp[0]
    NBUF = 3
    cg = [S.sb(f"cg{j}", [128, SEG]) for j in range(NBUF)]
    cv = [S.sb(f"cv{j}", [128, SEG]) for j in range(NBUF)]
    th = [S.sb(f"th{j}", [128, SEG]) for j in range(NBUF)]
    src, dst = seq_src_dst(k, l, "B")
    segs = []
    for (sname, T, cond, off) in cfg.seqs:
        for t0 in range(0, T, SEG):
            segs.append((T, cond, off, t0))
    state = {}

    def prep(si):
        T, cond, off, t0 = segs[si]
        hT = h2T[si % 2]
        r0 = off + t0
        xts = []
        for j in range(SEG // 128):
            xb = xt[(2 * si + j) % 4]
            xts.append(xb)
            make_hT(k, l, 2, cond, [(rows_ap(k, src, "in", r0 + 128 * j, 128), 128)], xb,
                    [hT[:, kc, 1 + 128 * j:1 + 128 * (j + 1)] for kc in range(8)], hT, [])
        rows, cols = [], []
        if t0 > 0:
            rows.append((rows_ap(k, src, "in", r0 - 1, 1), 1))
            cols.append(0)
        else:
            MSET(S, "pool", [hT], hT[:, :, 0:1], 0.0)
        if t0 + SEG < T:
            rows.append((rows_ap(k, src, "in", r0 + SEG, 1), 1))
            cols.append(SEG + 1)
        else:
            MSET(S, "pool", [hT], hT[:, :, SEG + 1:SEG + 2], 0.0)
        if len(rows) == 2:
            make_hT(k, l, 2, cond, rows, xh, [hT[:, kc, 0:SEG + 2:SEG + 1] for kc in range(8)], hT, [])
        elif len(rows) == 1:
            c = cols[0]
            make_hT(k, l, 2, cond, rows, xh, [hT[:, kc, c:c + 1] for kc in range(8)], hT, [])
        state[si] = xts

    def ffn(si):
        T, cond, off, t0 = segs[si]
        hT = h2T[si % 2]
        r0 = off + t0
        xts = state.pop(si)
        for c in range(22):
            pg, pv = S.bank(), S.bank()
            MM(S, [w_up, hT], [pg], [(pg[:, 0:SEG + 2], w_up[:, kc, c * 128:(c + 1) * 128], hT[:, kc, :], kc == 0, kc == 7)
                                      for kc in range(8)])
            MM(S, [w_up, hT], [pv], [(pv[:, 0:SEG + 2], w_up[:, kc, DFF + c * 128:DFF + (c + 1) * 128], hT[:, kc, :], kc == 0, kc == 7)
                                      for kc in range(8)])
            g_, v_, t_ = cg[c % NBUF], cv[c % NBUF], th[c % NBUF]
            fw = k.fcw
            ACT(S, [pg, fw], [g_], g_[:, :], pg[:, 1:SEG + 1], AF.Identity, bias=fw[:, 3, c:c + 1], scale=fw[:, 1, c:c + 1])
            STT(S, "dve", [pg, fw, g_], [g_], g_[:, :], pg[:, 0:SEG], fw[:, 0, c:c + 1], g_[:, :], ALU.mult, ALU.add)
            STT(S, "dve", [pg, fw, g_], [g_], g_[:, :], pg[:, 2:SEG + 2], fw[:, 2, c:c + 1], g_[:, :], ALU.mult, ALU.add)
            cc = 22 + c
            ACT(S, [pv, fw], [v_], v_[:, :], pv[:, 1:SEG + 1], AF.Identity, bias=fw[:, 3, cc:cc + 1], scale=fw[:, 1, cc:cc + 1])
            STT(S, "dve", [pv, fw, v_], [v_], v_[:, :], pv[:, 0:SEG], fw[:, 0, cc:cc + 1], v_[:, :], ALU.mult, ALU.add)
            STT(S, "dve", [pv, fw, v_], [v_], v_[:, :], pv[:, 2:SEG + 2], fw[:, 2, cc:cc + 1], v_[:, :], ALU.mult, ALU.add)
            ACT(S, [g_], [t_], t_[:, :], g_[:, :], AF.Silu)
            TT(S, "pool", [t_, v_], [actT], actT[:, c, :], t_[:, :], v_[:, :], ALU.mult)
        for j in range(SEG // 128):
            p0, p1 = S.bank(), S.bank()
            for hlf, pb in enumerate((p0, p1)):
                MM(S, [actT, w_dn], [pb], [(pb[:, :], actT[:, c, 128 * j:128 * (j + 1)], w_dn[:, c, hlf * 512:(hlf + 1) * 512],
                                              c == 0, c == 21) for c in range(22)])
            ob = xts[j]
            resid_ln(k, xts[j], (p0, p1), ob, rsm)
            db = S.dbuf(("xout", l, (r0 + 128 * j) // 128))
            S.dma("pool", rows_ap(k, dst, "out", r0 + 128 * j, 128), ob[:, :], reads=[ob], writes=[db])
            if dst is None:
                k.final_bufs.append(db)

    cur_cond = None
    prep(0)
    for si in range(len(segs)):
        if si + 1 < len(segs):
            prep(si + 1)
        if segs[si][1] != cur_cond:
            cur_cond = segs[si][1]
            gate_table(k, l, 5, cur_cond)
        ffn(si)
    S.release(m)


def layer(k, l):
    cfg = k.cfg
    load_layer_consts(k, l)
    mode = getattr(cfg, "mode", "full")
    if mode in ("full", "A"):
        phaseA(k, l)
    if mode in ("full", "B"):
        phaseB(k, l)


def shard_inputs(inp, cfg, core):
    f = lambda a: np.ascontiguousarray(np.asarray(a), dtype=np.float32)
    NP = cfg.NP
    cst, _, cst2, _ = make_consts()
    m = {
        "xs": f(inp["x_sample"][core]),
        "xp": f(inp["x_prompt"][NP * core:NP * (core + 1)]).reshape(NP * cfg.Tp, D),
        "st_c": f(inp["state_mlstm_c"][core]), "st_n": f(inp["state_mlstm_n"][core]),
        "st_m": f(inp["state_mlstm_m"][core]).reshape(2, 8), "st_s": f(inp["state_ssd"][core]),
        "cond": f(np.stack([np.asarray(inp["c"])[core], np.asarray(inp["c_ctx"])], 0)),
        "w_ada": f(inp["w_ada"]), "b_ada": f(inp["b_ada"]), "w_in": f(inp["w_in"]),
        "b_ig": f(inp["b_igate"]).reshape(2, 8), "b_fg": f(inp["b_fgate"]).reshape(2, 8),
        "mnorm_g": f(inp["mlstm_norm_g"]), "w_pool": f(inp["w_pool"]), "pool_scale": f(inp["pool_scale"]),
        "w_sp": f(inp["w_spatial"]), "b_sp": f(inp["b_spatial"]), "sconv_w": f(inp["ssd_conv_w"]),
        "sconv_b": f(inp["ssd_conv_b"]), "dt_bias": f(inp["ssd_dt_bias"]).reshape(2, 8),
        "a_log": f(inp["ssd_a_log"]).reshape(2, 8), "ssd_d": f(inp["ssd_d"]), "snorm_g": f(inp["ssd_norm_g"]),
        "w_out": f(inp["w_out"]), "ln1_g": f(inp["ln1_g"]), "ln1_b": f(inp["ln1_b"]), "w_up": f(inp["ffn_w_up"]),
        "fconv_w": f(inp["ffn_conv_w"]), "fconv_b": f(inp["ffn_conv_b"]), "w_dn": f(inp["ffn_w_down"]),
        "ln2_g": f(inp["ln2_g"]), "ln2_b": f(inp["ln2_b"]), "cst": cst, "cst2": cst2,
    }
    return m


def phaseA(k, l):
    S, I, cfg = k.S, k.I, k.cfg
    S.cp = SCHED_CP
    k.hT_bank_group = 0
    m = S.mark()
    w_in = S.sb("w_in", [128, 8, DIN], BF16)
    w_out = S.sb("w_out", [128, 8, D], BF16)
    m2 = S.mark()
    stage = [S.sb(f"wstageA{j}", [128, 2840]) for j in range(6)]
    k.wl_n = 0
    load_weight(k, w_in, I["w_in"][l], 8, DIN, stage)
    load_weight(k, w_out, I["w_out"][l], 8, D, stage)
    S.release(m2)
    c2b = load_cst2(k)
    S.dma("sp", k.lng[:, :], I["ln1_g"][l:l + 1, :].partition_broadcast(128), writes=[k.lng])
    S.dma("sp", k.lnb[:, :], I["ln1_b"][l:l + 1, :].partition_broadcast(128), writes=[k.lnb])
    alloc_hsm(k)
    rsm = alloc_rsm(k)
    a = K()
    a.w_in, a.w_out, a.c2b, a.rsm, a.l = w_in, w_out, c2b, rsm, l
    a.hb = [S.sb(f"hb{j}", [128, 8, 130], BF16) for j in range(3)]
    a.xq = [S.sb(f"xq{j}", [128, D]) for j in range(2)]
    a.pet = [S.sb(f"petile{j}", [128, D]) for j in range(2)] if l == 0 else None
    if l == 0:
        edb = S.dbuf("ED")
        for j in range(2):
            S.dma("sp", a.pet[j][0:64, 512:1024], k.ED[:, :], reads=[edb], writes=[a.pet[j]])
            S.dma("sp", a.pet[j][64:128, 512:1024], k.ED[:, :], reads=[edb], writes=[a.pet[j]])
    sb = S.sb
    a.Cn, a.Cnb = sb("Cn", [128, 2, 65]), sb("Cnb", [128, 2, 66], BF16)
    a.Hs, a.Hsb = sb("Hs", [128, 4, 64]), sb("Hsb", [128, 4, 64], BF16)
    a.p = []
    for par in range(2):
        q = K()
        a.p.append(q)
        q.qkT = sb(f"qkT{par}", [128, 4, 128], BF16)
        q.k_tm = sb(f"k_tm{par}", [128, 256], BF16)
        q.v_sb = sb(f"v_sb{par}", [128, 256], BF16)
        q.XBCb = sb(f"XBCb{par}", [128, 6, 128], BF16)
        q.x_tm, q.B_tm = sb(f"x_tm{par}", [128, 256], BF16), sb(f"B_tm{par}", [128, 2, 128], BF16)
        q.stash_bufs = [q.qkT, q.k_tm, q.v_sb, q.XBCb, q.x_tm, q.B_tm]
        e0 = q.qkT.off // 2
        q.stash_ap = S.arena.bitcast(BF16)[0:128, e0:e0 + 2304]
        assert q.B_tm.off + 512 == q.qkT.off + 4608, "stash group must be contiguous"
        q.og = sb(f"og{par}", [128, 280])
        q.G8, q.E8, q.SP8, q.igb = sb(f"G8{par}", [128, 8]), sb(f"E8{par}", [128, 8]), sb(f"SP8{par}", [128, 8]), sb(f"igb{par}", [128, 4])
        q.r8, q.logdec, q.cum, q.e8 = sb(f"r8{par}", [128, 8]), sb(f"logdec{par}", [128, 8]), sb(f"cum{par}", [128, 8]), sb(f"e8{par}", [128, 8])
        q.wend, q.aL, q.tmp8 = sb(f"wend{par}", [128, 8]), sb(f"aL{par}", [128, 8]), sb(f"tmp8{par}", [128, 8])
        q.L1 = sb(f"L1{par}", [128, 8, 128])
        q.DIFF = sb(f"DIFF{par}", [128, 8, 128])
        q.PTm = sb(f"PTm{par}", [128, 4, 128], BF16)
        q.PTs = sb(f"PTs{par}", [128, 4, 128], BF16)
        q.xt_m, q.xh_m = sb(f"xt_m{par}", [128, 4, 66], BF16), sb(f"xh_m{par}", [128, 4, 66], BF16)
        q.XBC, q.XBCe = sb(f"XBC{par}", [128, 6, 128]), sb(f"XBCe{par}", [128, 6, 128])
        q.xt_s, q.xh_s = sb(f"xt_s{par}", [128, 4, 64], BF16), sb(f"xh_s{par}", [128, 4, 64], BF16)
        q.NUM = sb(f"NUM{par}", [128, 4, 65])
        q.den = sb(f"den{par}", [128, 4])
        q.Ysc = sb(f"Ysc{par}", [128, 4, 64])
    a.HY = [sb(f"HY{j}", [128, 512]) for j in range(2)]
    a.HYf = [sb(f"HYf{j}", [128, 512]) for j in range(2)]
    a.eo, a.z_sb, a.ez, a.gu = sb("eo", [128, 256]), sb("z_sb", [128, 256]), sb("ez", [128, 256]), sb("gu", [128, 256])
    a.gvb = sb("gvb", [128, 256], BF16)
    a.pc, a.pcP, a.pcN = sb("pc", [128, 256]), sb("pcP", [8, 256]), sb("pcN", [8, 256])
    a.plb, a.plT = sb("plb", [128, 256], BF16), sb("plT", [128, 2, 128], BF16)
    a.fin1, a.fin2, a.fin3 = sb("fin1", [128, 256]), sb("fin2", [128, 256]), sb("fin3", [128, 256])
    a.st4, a.st4b = sb("st4", [128, 4]), sb("st4b", [128, 4])
    a.yall = sb("yall", [128, 3, 256], BF16)
    a.concatT = sb("concatT", [128, 8, 128], BF16)
    a.gst, a.gmv, a.gve, a.grs = sb("gst", [128, 6]), sb("gmv", [128, 2]), sb("gve", [128, 1]), sb("grs", [128, 1])
    a.mrun = sb("mrun", [4, 1])
    a.mt = sb("mt", [4, 2])
    for par in range(2):
        a.p[par].dec = sb(f"dec{par}", [128, 8])
    a.sio = sb("sio", [128, 4, 128])
    src, dst = seq_src_dst(k, l, "A")
    a.src, a.dst = src, dst
    cur_cond = None
    for si, (sname, T, cond, off) in enumerate(cfg.seqs):
        if cond != cur_cond:
            gate_table(k, l, 2, cond)
            cur_cond = cond
        runseq(k, a, si, T, cond, off)
    S.release(m)


def runseq(k, a, si, T, cond, off):
    S, cfg, l = k.S, k.cfg, a.l
    nt = T // 128
    is_sample = (si == 0)
    tile0 = off // 128
    w_in = a.w_in

    def hbuf(i):
        return a.hb[i % 3]

    def fix_halo(lo, hi):
        CP(S, "pool", [hbuf(hi)], [hbuf(lo)], hbuf(lo)[:, :, 129:130], hbuf(hi)[:, :, 1:2])
        CP(S, "pool", [hbuf(lo)], [hbuf(hi)], hbuf(hi)[:, :, 0:1], hbuf(lo)[:, :, 128:129])

    def ensure1(i):
        hb = hbuf(i)
        xb = a.xq[i % 2]
        pos = None
        if l == 0 and is_sample:
            pos = a.pet[i % 2]
            edb = S.dbuf("ED")
            S.dma("sp", pos[0:64, 0:512], k.ED[2 * i:2 * i + 1, :].partition_broadcast(64), reads=[edb], writes=[pos])
            S.dma("sp", pos[64:128, 0:512], k.ED[2 * i + 1:2 * i + 2, :].partition_broadcast(64), reads=[edb], writes=[pos])
        make_hT(k, l, 1, cond, [(rows_ap(k, a.src, "in", off + 128 * i, 128), 128)], xb,
                [hb[:, kc, 1:129] for kc in range(8)], hb, [], pos_tile=pos)
        db = S.dbuf(("HT", tile0 + i))
        S.dma("pool", k.HT[tile0 + i].rearrange("p (kc t) -> p kc t", kc=8), hb[:, :, 1:129], reads=[hb], writes=[db])
        if i == 0:
            MSET(S, "pool", [hb], hb[:, :, 0:1], 0.0)
        else:
            fix_halo(i - 1, i)
        if i == nt - 1:
            MSET(S, "pool", [hb], hb[:, :, 129:130], 0.0)

    def ensure2(i):
        hb = hbuf(i)
        db = S.dbuf(("HT", tile0 + i))
        S.dma("sp", hb[:, :, 1:129], k.HT[tile0 + i].rearrange("p (kc t) -> p kc t", kc=8), reads=[db], writes=[hb])
        xb = a.xq[i % 2]
        S.dma("sp", xb[:, :], rows_ap(k, a.src, "in", off + 128 * i, 128), writes=[xb])
        if l == 0 and is_sample:
            pos = a.pet[i % 2]
            edb = S.dbuf("ED")
            S.dma("sp", pos[0:64, 0:512], k.ED[2 * i:2 * i + 1, :].partition_broadcast(64), reads=[edb], writes=[pos])
            S.dma("sp", pos[64:128, 0:512], k.ED[2 * i + 1:2 * i + 2, :].partition_broadcast(64), reads=[edb], writes=[pos])
            TT(S, "pool", [xb, pos], [xb], xb[:, :], xb[:, :], pos[:, :], ALU.add)
        hf = a.HYf[i % 2]
        S.dma("sp", hf[:, :], k.HF[off + 128 * i:off + 128 * (i + 1), :], reads=[S.dbuf(("HF", tile0 + i))], writes=[hf])
        q = a.p[i % 2]
        S.dma("sp", q.stash_ap, k.STB[tile0 + i], reads=[S.dbuf(("STB", tile0 + i))], writes=q.stash_bufs)
        S.dma("sp", q.og[:, :], k.STG[tile0 + i], reads=[S.dbuf(("STG", tile0 + i))], writes=[q.og])
        if i == nt - 1:
            MSET(S, "pool", [hb], hb[:, :, 129:130], 0.0)
        else:
            fix_halo(i, i + 1)
        if i == 0:
            MSET(S, "pool", [hb], hb[:, :, 0:1], 0.0)

    for d in ((0,) if getattr(cfg, "stop", 99) <= 4 else (0, 1)):
        init_state(k, a, si, d, is_sample)
        order = list(range(nt)) if d == 0 else list(range(nt - 1, -1, -1))
        ens = ensure1 if d == 0 else ensure2
        ens(order[0])
        for n, i in enumerate(order):
            if n + 1 < len(order):
                ens(order[n + 1])
            tileA(k, a, si, T, cond, off, i, d, nt, is_sample)
        if not is_sample and getattr(cfg, "stop", 99) > 5:
            final_state(k, a, si, d)


def init_state(k, a, si, d, is_sample):
    S, I, l = k.S, k.I, a.l
    if not is_sample:
        MSET(S, "pool", [a.Cn], a.Cn[:, :, :], 0.0)
        MSET(S, "pool", [a.Cnb], a.Cnb[:, :, :], 0.0)
        MSET(S, "pool", [a.Hs], a.Hs[:, :, :], 0.0)
        MSET(S, "pool", [a.Hsb], a.Hsb[:, :, :], 0.0)
        MSET(S, "pool", [a.mrun], a.mrun[:, :], 0.0)
        return
    for h in range(4):
        pr = slice((h % 2) * 64, (h % 2) * 64 + 64)
        S.dma("sp", a.Cn[pr, h // 2, 0:64], I["st_c"][l, d, h], writes=[a.Cn])
        S.dma("sp", a.Cn[pr, h // 2, 64:65], I["st_n"][l, d, h].rearrange("(p o) -> p o", o=1), writes=[a.Cn])
    S.dma("sp", a.st4[:, :], I["st_m"][l:l + 1, 4 * d:4 * d + 4].partition_broadcast(128), writes=[a.st4])
    ACT(S, [a.st4], [a.st4b], a.st4b[:, :], a.st4[:, :], AF.Exp)
    for h in range(4):
        pr = slice((h % 2) * 64, (h % 2) * 64 + 64)
        TS(S, "dve", [a.Cn, a.st4b], [a.Cn], a.Cn[pr, h // 2, :], a.Cn[pr, h // 2, :], a.st4b[pr, h:h + 1], None, ALU.mult)
    CP(S, "pool", [a.Cn], [a.Cnb], a.Cnb[:, :, 0:65], a.Cn[:, :, :])
    S.dma("sp", a.sio[0:64, :, :], I["st_s"][l, d].rearrange("h p n -> p h n"), writes=[a.sio])
    pb = S.bank()
    TR(S, [a.sio, k.cstb], [pb], [(pb[:, h * 64:(h + 1) * 64], a.sio[0:64, h, :], cview(k, "ident")[0:64, 0:64]) for h in range(4)])
    CP(S, "dve", [pb], [a.Hs], a.Hs[:, :, :], pb[:, 0:256].rearrange("p (h q) -> p h q", h=4))
    CP(S, "act", [pb], [a.Hsb], a.Hsb[:, :, :], pb[:, 0:256].rearrange("p (h q) -> p h q", h=4))


def final_state(k, a, si, d):
    S, O, l = k.S, k.O, a.l
    j = si - 1
    dg = a.p[0].tmp8
    TS(S, "dve", [k.cstb, a.mrun], [dg], dg[0:4, 0:4], cview(k, "ident")[0:4, 0:4], a.mrun[0:4, 0:1], None, ALU.mult)
    pb = S.bank()
    MM(S, [dg, k.cstb], [pb], [(pb[:, 0:4], cview(k, "ones")[0:4, :], dg[0:4, 0:4], True, True)])
    ACT(S, [pb], [a.st4b], a.st4b[:, :], pb[:, 0:4], AF.Exp, scale=-1.0)
    stg = a.sio
    sv = stg[:, 0:2, 0:65]
    for h in range(4):
        pr = slice((h % 2) * 64, (h % 2) * 64 + 64)
        TS(S, "dve", [a.Cn, a.st4b], [stg], stg[pr, h // 2, 0:65], a.Cn[pr, h // 2, :], a.st4b[pr, h:h + 1], None, ALU.mult)
    outs = []
    for h in range(4):
        pr = slice((h % 2) * 64, (h % 2) * 64 + 64)
        db = S.dbuf(("oc", j, l, d, h))
        S.dma("pool", O["oc"][j, l, d, h], stg[pr, h // 2, 0:64], reads=[stg], writes=[db])
        db2 = S.dbuf(("on", j, l, d, h))
        S.dma("pool", O["on"][j, l, d, h].rearrange("(p o) -> p o", o=1), stg[pr, h // 2, 64:65], reads=[stg], writes=[db2])
        outs += [db, db2]
    db = S.dbuf(("om", j, l, d))
    S.dma("pool", O["om"][j, l, 4 * d:4 * d + 4].rearrange("(p o) -> p o", o=1), a.mrun[0:4, 0:1], reads=[a.mrun], writes=[db])
    outs.append(db)
    pb2 = S.bank()
    TR(S, [a.Hs, k.cstb], [pb2], [(pb2[0:64, h * 128:(h + 1) * 128], a.Hs[:, h, :], cview(k, "ident")) for h in range(4)])
    CP(S, "dve", [pb2, stg], [stg], stg[0:64, :, :], pb2[0:64, :].rearrange("p (h n) -> p h n", h=4))
    db = S.dbuf(("os", j, l, d))
    S.dma("pool", O["os"][j, l, d].rearrange("h p n -> p h n"), stg[0:64, :, :], reads=[stg], writes=[db])
    outs.append(db)
    k.final_bufs += outs


def tileA(k, a, si, T, cond, off, i, d, nt, is_sample):
    S, l = k.S, a.l
    PS = a.p[i % 2]
    w_in = a.w_in
    hb = a.hb[i % 3]
    hcur = lambda kc: hb[:, kc, 1:129]
    tri = cview(k, "tri%d" % d)
    neg = cview(k, "neg%d" % d)
    endc = 127 if d == 0 else 0
    full = (d == 1)
    cst = k.cstb

    og = PS.og
    ps1 = ps2 = None
    if not full:
        ps1, ps2 = S.bank(0), S.bank(0)
        MM(S, [hb, w_in], [ps1], [(ps1[:, 0:512], hcur(kc), w_in[:, kc, 256:768], kc == 0, kc == 7) for kc in range(8)])
        MM(S, [hb, w_in], [ps2], [(ps2[:, 0:272], hcur(kc), w_in[:, kc, 768:1040], kc == 0, kc == 7) for kc in range(8)]
           + [(ps2[:, 272:280], hcur(kc), w_in[:, kc, 2832:2840], kc == 0, kc == 7) for kc in range(8)])
        CP(S, "act", [ps2], [og], og[:, :], ps2[:, 0:280])
        ACT(S, [ps1], [PS.k_tm], PS.k_tm[:, :], ps1[:, 0:256], AF.Identity, scale=0.125)
        CP(S, "act", [ps1], [PS.v_sb], PS.v_sb[:, :], ps1[:, 256:512])
    G8, E8, SP8, igb, r8, logdec, cum, e8, wend, aL, tmp8 = (PS.G8, PS.E8, PS.SP8, PS.igb, PS.r8, PS.logdec, PS.cum, PS.e8,
                                                             PS.wend, PS.aL, PS.tmp8)
    STT(S, "dve", [og, k.bif], [G8], G8[:, 0:4], og[:, 264 + 4 * d:268 + 4 * d], -1.0, k.bif[:, 8 + 4 * d:12 + 4 * d], ALU.mult, ALU.add)
    TT(S, "dve", [og, k.dtb], [G8], G8[:, 4:8], og[:, 272 + 4 * d:276 + 4 * d], k.dtb[:, 4 * d:4 * d + 4], ALU.add)
    TT(S, "dve", [og, k.bif], [igb], igb[:, :], og[:, 256 + 4 * d:260 + 4 * d], k.bif[:, 4 * d:4 * d + 4], ALU.add)
    if full:
        ACT(S, [og], [a.eo], a.eo[:, :], og[:, 0:256], AF.Exp, scale=-1.0)
    ACT(S, [G8], [E8], E8[:, :], G8[:, :], AF.Exp)
    ACT(S, [E8], [SP8], SP8[:, :], E8[:, :], AF.Ln, bias=1.0)
    ACT(S, [igb], [r8], r8[:, 0:4], igb[:, :], AF.Exp)
    CP(S, "pool", [SP8], [r8], r8[:, 4:8], SP8[:, 4:8])
    TT(S, "dve", [SP8, k.coef], [logdec], logdec[:, :], SP8[:, :], k.coef[:, d, :], ALU.mult)
    CP(S, "act", [logdec], [PS.L1], PS.L1[:, :, :], bc(logdec[:, 0:8].unsqueeze(2), [128, 8, 128]))
    psL = [S.bank(1), S.bank(1)]
    for hh in range(2):
        MM(S, [PS.L1, cst], [psL[hh]], [(psL[hh][:, q * 128:(q + 1) * 128], PS.L1[:, hh * 4 + q, :], tri, True, True) for q in range(4)])
    psC = S.bank(1)
    MM(S, [logdec, cst], [psC], [(psC[:, 0:8], tri, logdec[:, 0:8], True, True)])
    CP(S, "dve", [psC], [cum], cum[:, :], psC[:, 0:8])
    for h in range(8):
        pl = psL[h // 4]
        q = h % 4
        STT(S, "dve", [pl, cum, cst], [PS.DIFF], PS.DIFF[:, h, :], pl[:, q * 128:(q + 1) * 128], cum[:, h:h + 1], neg, ALU.subtract, ALU.add)
    ACT(S, [PS.DIFF], [PS.DIFF], PS.DIFF[:, :, :], PS.DIFF[:, :, :], AF.Exp)
    ACT(S, [cum], [e8], e8[:, :], cum[:, :], AF.Exp)
    for hh in range(2):
        TT(S, "dve", [psL[hh], cum], [tmp8], tmp8[:, hh * 4:hh * 4 + 4], psL[hh][:, endc:512:128], cum[:, hh * 4:hh * 4 + 4], ALU.subtract)
        ACT(S, [psL[hh]], [aL], aL[:, hh * 4:hh * 4 + 4], psL[hh][:, endc:512:128], AF.Exp)
    ACT(S, [tmp8], [wend], wend[:, :], tmp8[:, :], AF.Exp)
    TT(S, "dve", [wend, r8], [wend], wend[:, :], wend[:, :], r8[:, :], ALU.mult)
    if not is_sample:
        TT(S, "dve", [tmp8, igb], [PS.dec], PS.dec[:, 0:4], tmp8[:, 0:4], igb[:, :], ALU.add)
        TT(S, "dve", [tmp8, cum], [PS.dec], PS.dec[:, 4:8], tmp8[:, 0:4], cum[:, 0:4], ALU.add)
        pm = S.bank(1)
        TR(S, [PS.dec, cst], [pm], [(pm[0:4, 0:128], PS.dec[:, 0:4], cview(k, "ident")),
                                   (pm[0:4, 128:256], PS.dec[:, 4:8], cview(k, "ident"))])
        S.op("dve", lambda e: e.tensor_reduce(a.mt[0:4, 0:1], pm[0:4, 0:128], AX.X, ALU.max), [pm], [a.mt])
        TT(S, "dve", [pm, a.mrun], [a.mt], a.mt[0:4, 1:2], pm[0:4, 128:129], a.mrun[0:4, 0:1], ALU.add)
        TT(S, "dve", [a.mt], [a.mrun], a.mrun[0:4, 0:1], a.mt[0:4, 0:1], a.mt[0:4, 1:2], ALU.max)

    if getattr(k.cfg, "stop", 99) <= 1:
        return
    qkT = PS.qkT
    if not full:
        psQ = [S.bank(0), S.bank(0)]
        for hh in range(2):
            MM(S, [hb, w_in], [psQ[hh]], [(psQ[hh][:, q * 130:(q + 1) * 130], w_in[:, kc, (hh * 2 + q) * 128:(hh * 2 + q + 1) * 128],
                                            hb[:, kc, 0:130], kc == 0, kc == 7) for q in range(2) for kc in range(8)])
        CP(S, "act", [psQ[0]], [qkT], qkT[:, 0:2, :], psQ[0][:, 0:260].rearrange("p (b t) -> p b t", b=2)[:, :, 1:129])
        ACT(S, [psQ[1]], [qkT], qkT[:, 2:4, :], psQ[1][:, 0:260].rearrange("p (b t) -> p b t", b=2)[:, :, 1:129], AF.Identity, scale=0.125)
    v4 = PS.v_sb[:, :].rearrange("p (h e) -> p h e", h=4)
    TT(S, "dve", [PS.v_sb, r8], [PS.xt_m], PS.xt_m[:, :, 0:64], v4, bc(r8[:, 0:4].unsqueeze(2), [128, 4, 64]), ALU.mult)
    if getattr(k.cfg, "stop", 99) <= 1.12:
        return
    CP(S, "pool", [r8], [PS.xt_m], PS.xt_m[:, :, 64:65], r8[:, 0:4].unsqueeze(2))
    if getattr(k.cfg, "stop", 99) <= 1.15:
        return
    EXP = getattr(k.cfg, "exp", "")
    if EXP != "noTT":
        TT(S, "dve", [PS.v_sb, wend], [PS.xh_m], PS.xh_m[:, :, 0:64], v4, bc((r8 if EXP == "r8" else wend)[:, 0:4].unsqueeze(2), [128, 4, 64]), ALU.mult)
    if EXP != "noCP":
        CP(S, "pool", [wend], [PS.xh_m], PS.xh_m[:, :, 64:65], wend[:, 0:4].unsqueeze(2))
    if getattr(k.cfg, "stop", 99) <= 1.2:
        return
    psS = [S.bank(1), S.bank(1)]
    hp = lambda h: slice((h % 2) * 64, (h % 2) * 64 + 64)
    for par in range(2):
        MM(S, [qkT], [psS[par]], [(psS[par][:, (h // 2) * 128:(h // 2 + 1) * 128], qkT[hp(h), 2 + h // 2, :], qkT[hp(h), h // 2, :], True, True)
                                  for h in (par, par + 2)])
    for par in range(2):
        TT(S, "dve", [psS[par], PS.DIFF], [PS.PTm], PS.PTm[:, par:4:2, :], psS[par][:, 0:256].rearrange("p (h t) -> p h t", h=2),
           PS.DIFF[:, par:4:2, :], ALU.mult)
    if getattr(k.cfg, "stop", 99) <= 1.4:
        return
    psO = S.bank(1)
    psI = [S.bank(1), S.bank(1)]
    MM(S, [PS.PTm, PS.xt_m], [psO], [(psO[:, h * 65:h * 65 + 65], PS.PTm[:, h, :], PS.xt_m[:, h, 0:65], True, True) for h in range(4)])
    for par in range(2):
        MM(S, [qkT, a.Cnb], [psI[par]], [(psI[par][:, (h // 2) * 65:(h // 2) * 65 + 65], qkT[hp(h), h // 2, :], a.Cnb[hp(h), h // 2, 0:65], True, True)
                                         for h in (par, par + 2)])
    NUM = PS.NUM
    for par in range(2):
        TT(S, "dve", [psI[par], e8], [NUM], NUM[:, par:4:2, :], psI[par][:, 0:130].rearrange("p (h e) -> p h e", h=2),
           bc(e8[:, par:4:2].unsqueeze(2), [128, 2, 65]), ALU.mult)
    TT(S, "dve", [psO, NUM], [NUM], NUM[:, :, :], psO[:, 0:260].rearrange("p (h e) -> p h e", h=4), NUM[:, :, :], ALU.add)
    if getattr(k.cfg, "stop", 99) <= 1.6:
        return
    HY = a.HY[i % 2]
    ACT(S, [NUM], [PS.den], PS.den[:, :].unsqueeze(2), NUM[:, :, 64:65], AF.Abs)
    TS(S, "dve", [PS.den], [PS.den], PS.den[:, :], PS.den[:, :], 1.0, None, ALU.max)
    TT(S, "pool", [PS.den, k.cm1], [PS.den], PS.den[:, :], PS.den[:, :], bc(k.cm1[:, 0:1], [128, 4]), ALU.pow)
    TT(S, "pool", [NUM, PS.den], [HY], HY[:, 0:256].rearrange("p (h e) -> p h e", h=4), NUM[:, :, 0:64],
       bc(PS.den[:, :].unsqueeze(2), [128, 4, 64]), ALU.mult)
    if getattr(k.cfg, "stop", 99) <= 1.8:
        return
    psU = S.bank(1)
    MM(S, [PS.k_tm, PS.xh_m], [psU], [(psU[:, h * 65:h * 65 + 65], PS.k_tm[:, (h // 2) * 128:(h // 2 + 1) * 128], PS.xh_m[:, h, 0:65], True, True)
                                     for h in range(4)])
    for h in range(4):
        STT(S, "dve", [a.Cn, aL, psU], [a.Cn], a.Cn[hp(h), h // 2, :], a.Cn[hp(h), h // 2, :], aL[hp(h), h:h + 1],
            psU[hp(h), h * 65:h * 65 + 65], ALU.mult, ALU.add)
    CP(S, "act", [a.Cn], [a.Cnb], a.Cnb[:, :, 0:65], a.Cn[:, :, :])

    if getattr(k.cfg, "stop", 99) <= 2:
        return
    XBC, XBCe, XBCb = PS.XBC, PS.XBCe, PS.XBCb
    if not full:
        psX = [S.bank(0), S.bank(0)]
        for hh in range(2):
            MM(S, [hb, w_in], [psX[hh]], [(psX[hh][:, q * 130:q * 130 + 130], w_in[:, kc, 2064 + (hh * 3 + q) * 128:2064 + (hh * 3 + q + 1) * 128],
                                            hb[:, kc, 0:130], kc == 0, kc == 7) for q in range(3) for kc in range(8)])
        for b in range(6):
            pb, c0 = psX[b // 3], (b % 3) * 130
            ACT(S, [pb, k.scw], [XBC], XBC[:, b, :], pb[:, c0 + 1:c0 + 129], AF.Identity, bias=k.scw[:, 3, b:b + 1], scale=k.scw[:, 1, b:b + 1])
            STT(S, "dve", [pb, k.scw, XBC], [XBC], XBC[:, b, :], pb[:, c0:c0 + 128], k.scw[:, 0, b:b + 1], XBC[:, b, :], ALU.mult, ALU.add)
            STT(S, "dve", [pb, k.scw, XBC], [XBC], XBC[:, b, :], pb[:, c0 + 2:c0 + 130], k.scw[:, 2, b:b + 1], XBC[:, b, :], ALU.mult, ALU.add)
        ACT(S, [XBC], [XBCe], XBCe[:, :, :], XBC[:, :, :], AF.Exp, scale=-1.0)
        ACT(S, [XBCe], [XBCe], XBCe[:, :, :], XBCe[:, :, :], AF.Ln, bias=1.0)
        ACT(S, [XBCe], [XBCe], XBCe[:, :, :], XBCe[:, :, :], AF.Exp, scale=-1.0)
        TT(S, "pool", [XBC, XBCe], [XBCb], XBCb[:, :, :], XBC[:, :, :], XBCe[:, :, :], ALU.mult)
        psT = S.bank(0)
        pTv = bview(psT, BF16)
        TR(S, [XBCb, k.identb], [psT], [(pTv[:, b * 128:(b + 1) * 128], XBCb[:, b, :], k.identb[:, :]) for b in range(4)])
        CP(S, "act", [psT], [PS.x_tm], PS.x_tm[:, :], pTv[:, 0:256])
        CP(S, "act", [psT], [PS.B_tm], PS.B_tm[:, :, :], pTv[:, 256:512].rearrange("p (g n) -> p g n", g=2))
    x4 = PS.x_tm[:, :].rearrange("p (h e) -> p h e", h=4)
    TT(S, "pool", [PS.x_tm, r8], [PS.xt_s], PS.xt_s[:, :, :], x4, bc(r8[:, 4:8].unsqueeze(2), [128, 4, 64]), ALU.mult)
    TT(S, "pool", [PS.x_tm, wend], [PS.xh_s], PS.xh_s[:, :, :], x4, bc(wend[:, 4:8].unsqueeze(2), [128, 4, 64]), ALU.mult)
    psS2 = S.bank(1)
    MM(S, [XBCb], [psS2], [(psS2[:, g * 128:(g + 1) * 128], XBCb[:, 2 + g, :], XBCb[:, 4 + g, :], True, True) for g in range(2)])
    for g in range(2):
        TT(S, "dve", [psS2, PS.DIFF], [PS.PTs], PS.PTs[:, 2 * g:2 * g + 2, :],
           bc(psS2[:, g * 128:(g + 1) * 128].unsqueeze(1), [128, 2, 128]), PS.DIFF[:, 4 + 2 * g:6 + 2 * g, :], ALU.mult)
    psY = S.bank(1)
    MM(S, [PS.PTs, PS.xt_s, XBCb, a.Hsb], [psY],
       [(psY[:, h * 64:(h + 1) * 64], PS.PTs[:, h, :], PS.xt_s[:, h, :], True, True) for h in range(4)]
       + [(psY[:, 256 + g * 128:256 + (g + 1) * 128], XBCb[:, 4 + g, :], a.Hsb[:, 2 * g:2 * g + 2, :].rearrange("p h e -> p (h e)"), True, True)
          for g in range(2)])
    Ysc = PS.Ysc
    TT(S, "dve", [psY, e8], [Ysc], Ysc[:, :, :], psY[:, 256:512].rearrange("p (h e) -> p h e", h=4),
       bc(e8[:, 4:8].unsqueeze(2), [128, 4, 64]), ALU.mult)
    TT(S, "dve", [psY, Ysc], [HY], HY[:, 256:512], psY[:, 0:256], Ysc[:, :, :].rearrange("p h e -> p (h e)"), ALU.add)
    psU2 = S.bank(1)
    MM(S, [PS.B_tm, PS.xh_s], [psU2], [(psU2[:, g * 128:(g + 1) * 128], PS.B_tm[:, g, :], PS.xh_s[:, 2 * g:2 * g + 2, :].rearrange("p h e -> p (h e)"),
                                       True, True) for g in range(2)])
    TT(S, "dve", [a.Hs, aL], [a.Hs], a.Hs[:, :, :], a.Hs[:, :, :], bc(aL[:, 4:8].unsqueeze(2), [128, 4, 64]), ALU.mult)
    TT(S, "dve", [a.Hs, psU2], [a.Hs], a.Hs[:, :, :], psU2[:, 0:256].rearrange("p (h e) -> p h e", h=4), a.Hs[:, :, :], ALU.add)
    CP(S, "act", [a.Hs], [a.Hsb], a.Hsb[:, :, :], a.Hs[:, :, :])

    if getattr(k.cfg, "stop", 99) <= 3:
        return
    tile_g = (off // 128) + i
    if not full:
        S.dma("pool", k.STB[tile_g], PS.stash_ap, reads=PS.stash_bufs, writes=[S.dbuf(("STB", tile_g))])
        S.dma("pool", k.STG[tile_g], og[:, :], reads=[og], writes=[S.dbuf(("STG", tile_g))])
        db = S.dbuf(("HF", tile_g))
        S.dma("pool", k.HF[off + 128 * i:off + 128 * (i + 1), :], HY[:, :], reads=[HY], writes=[db])
        return
    finalizeA(k, a, si, T, cond, off, i, nt, ps1, ps2, HY, PS)


def finalizeA(k, a, si, T, cond, off, i, nt, ps1, ps2, HY, pset):
    S, l = k.S, a.l
    w_in, w_out = a.w_in, a.w_out
    hb = a.hb[i % 3]
    hcur = lambda kc: hb[:, kc, 1:129]
    cst = k.cstb
    HYf = a.HYf[i % 2]
    f1, f2, f3 = a.fin1, a.fin2, a.fin3
    v4 = lambda ap: ap.rearrange("p (h e) -> p h e", h=4)
    ym, yg, ys = a.yall[:, 0, :], a.yall[:, 1, :], a.yall[:, 2, :]

    ps3, ps4 = S.bank(0), S.bank(0)
    MM(S, [hb, w_in], [ps3], [(ps3[:, 0:512], hcur(kc), w_in[:, kc, 1296:1808], kc == 0, kc == 7) for kc in range(8)])
    MM(S, [hb, w_in], [ps4], [(ps4[:, 0:256], hcur(kc), w_in[:, kc, 1808:2064], kc == 0, kc == 7) for kc in range(8)]
       + [(ps4[:, 256:512], hcur(kc), w_in[:, kc, 1040:1296], kc == 0, kc == 7) for kc in range(8)])
    has_p, has_n = i > 0, i < nt - 1
    ps5 = S.bank(0)
    mm5 = []
    if has_p:
        hp_ = a.hb[(i - 1) % 3]
        mm5 += [(ps5[0:8, 0:256], hp_[:, kc, 121:129], w_in[:, kc, 1040:1296], kc == 0, kc == 7) for kc in range(8)]
    if has_n:
        hn_ = a.hb[(i + 1) % 3]
        mm5 += [(ps5[0:8, 256:512], hn_[:, kc, 1:9], w_in[:, kc, 1040:1296], kc == 0, kc == 7) for kc in range(8)]
    if mm5:
        rd = [w_in] + ([a.hb[(i - 1) % 3]] if has_p else []) + ([a.hb[(i + 1) % 3]] if has_n else [])
        MM(S, rd, [ps5], mm5)

    TT(S, "pool", [HY, HYf], [f1], f1[:, :], HY[:, 0:256], HYf[:, 0:256], ALU.add)
    S.op("dve", lambda e: e.tensor_reduce(a.st4[:, :], v4(f1[:, :]), AX.X, ALU.add), [f1], [a.st4])
    TS(S, "dve", [a.st4], [a.st4], a.st4[:, :], a.st4[:, :], 1.0 / 64.0, None, ALU.mult)
    TT(S, "pool", [f1, a.st4], [f1], v4(f1[:, :]), v4(f1[:, :]), bc(a.st4[:, :].unsqueeze(2), [128, 4, 64]), ALU.subtract)
    TT(S, "pool", [f1], [f2], f2[:, :], f1[:, :], f1[:, :], ALU.mult)
    S.op("dve", lambda e: e.tensor_reduce(a.st4b[:, :], v4(f2[:, :]), AX.X, ALU.add), [f2], [a.st4b])
    TS(S, "dve", [a.st4b], [a.st4b], a.st4b[:, :], a.st4b[:, :], 1.0 / 64.0, EPS, ALU.mult, ALU.add)
    TT(S, "pool", [a.st4b, k.cm05], [a.st4b], a.st4b[:, :], a.st4b[:, :], bc(k.cm05[:, 0:1], [128, 4]), ALU.pow)
    TT(S, "pool", [f1, a.st4b], [f1], v4(f1[:, :]), v4(f1[:, :]), bc(a.st4b[:, :].unsqueeze(2), [128, 4, 64]), ALU.mult)
    TT(S, "pool", [f1, k.mng], [f1], f1[:, :], f1[:, :], k.mng[:, :], ALU.mult)
    ACT(S, [a.eo], [a.eo], a.eo[:, :], a.eo[:, :], AF.Ln, bias=1.0)
    ACT(S, [a.eo], [a.eo], a.eo[:, :], a.eo[:, :], AF.Exp, scale=-1.0)
    TT(S, "pool", [f1, a.eo], [a.yall], ym, f1[:, :], a.eo[:, :], ALU.mult)

    CP(S, "act", [ps4], [a.z_sb], a.z_sb[:, :], ps4[:, 0:256])
    ACT(S, [ps4], [a.ez], a.ez[:, :], ps4[:, 0:256], AF.Exp, scale=-1.0)
    TT(S, "pool", [HY, HYf], [f2], f2[:, :], HY[:, 256:512], HYf[:, 256:512], ALU.add)
    TT(S, "pool", [pset.x_tm, k.dsk], [f3], v4(f3[:, :]), v4(pset.x_tm[:, :]), bc(k.dsk[:, :].unsqueeze(2), [128, 4, 64]), ALU.mult)
    TT(S, "pool", [f2, f3], [f2], f2[:, :], f2[:, :], f3[:, :], ALU.add)
    ACT(S, [a.ez], [a.ez], a.ez[:, :], a.ez[:, :], AF.Ln, bias=1.0)
    ACT(S, [a.ez], [a.ez], a.ez[:, :], a.ez[:, :], AF.Exp, scale=-1.0)
    TT(S, "pool", [a.ez, a.z_sb], [a.ez], a.ez[:, :], a.ez[:, :], a.z_sb[:, :], ALU.mult)
    TT(S, "pool", [f2, a.ez], [f2], f2[:, :], f2[:, :], a.ez[:, :], ALU.mult)
    TT(S, "pool", [f2], [f3], f3[:, :], f2[:, :], f2[:, :], ALU.mult)
    S.op("dve", lambda e: e.tensor_reduce(a.st4[:, 0:2], f3[:, :].rearrange("p (g e) -> p g e", g=2), AX.X, ALU.add), [f3], [a.st4])
    TS(S, "dve", [a.st4], [a.st4], a.st4[:, 0:2], a.st4[:, 0:2], 1.0 / 128.0, EPS, ALU.mult, ALU.add)
    TT(S, "pool", [a.st4, k.cm05], [a.st4], a.st4[:, 0:2], a.st4[:, 0:2], bc(k.cm05[:, 0:1], [128, 2]), ALU.pow)
    TT(S, "pool", [f2, a.st4], [f2], f2[:, :].rearrange("p (g e) -> p g e", g=2), f2[:, :].rearrange("p (g e) -> p g e", g=2),
       bc(a.st4[:, 0:2].unsqueeze(2), [128, 2, 128]), ALU.mult)
    TT(S, "pool", [f2, k.sng], [a.yall], ys, f2[:, :], k.sng[:, :], ALU.mult)

    CP(S, "act", [ps3], [a.gu], a.gu[:, :], ps3[:, 0:256])
    S.op("dve", lambda e: e.bn_stats(a.gst[:, :], ps3[:, 256:512]), [ps3], [a.gst])
    S.op("dve", lambda e: e.bn_aggr(a.gmv[:, :], a.gst[:, :]), [a.gst], [a.gmv])
    TS(S, "dve", [a.gmv], [a.gve], a.gve[:, :], a.gmv[:, 1:2], EPS, None, ALU.add)
    TT(S, "pool", [a.gve, k.cm05], [a.grs], a.grs[:, :], a.gve[:, :], k.cm05[:, :], ALU.pow)
    TS(S, "dve", [ps3, a.gmv, a.grs], [a.gvb], a.gvb[:, :], ps3[:, 256:512], a.gmv[:, 0:1], a.grs[:, 0:1], ALU.subtract, ALU.mult)
    psG = S.bank(0)
    MM(S, [k.wsT, a.gvb], [psG], [(psG[:, h * 64:(h + 1) * 64], k.wsT[:, h, :], a.gvb[:, h * 64:(h + 1) * 64], True, True) for h in range(4)])
    TT(S, "dve", [psG, k.bsT], [f3], v4(f3[:, :]), v4(psG[:, 0:256]), bc(k.bsT[:, :].unsqueeze(2), [128, 4, 64]), ALU.add)
    TT(S, "pool", [f3, a.gu], [a.yall], yg, f3[:, :], a.gu[:, :], ALU.mult)

    CP(S, "act", [ps4], [a.pc], a.pc[:, :], ps4[:, 256:512])
    if has_p:
        CP(S, "act", [ps5], [a.pcP], a.pcP[:, :], ps5[0:8, 0:256])
    if has_n:
        CP(S, "act", [ps5], [a.pcN], a.pcN[:, :], ps5[0:8, 256:512])
    var = "int" if (has_p and has_n) else ("first" if has_n else ("last" if has_p else "int"))
    psP = S.bank(0)
    mmp = []
    for g in range(4):
        o_ = psP[:, g * 64:(g + 1) * 64]
        seqm = [(cview(k, f"pA{g}{var}"), a.pc[:, g * 64:(g + 1) * 64])]
        if has_p:
            seqm.append((cview(k, f"pP{g}", 8), a.pcP[0:8, g * 64:(g + 1) * 64]))
        if has_n:
            seqm.append((cview(k, f"pN{g}", 8), a.pcN[0:8, g * 64:(g + 1) * 64]))
        for n_, (lh, rh) in enumerate(seqm):
            mmp.append((o_, lh, rh, n_ == 0, n_ == len(seqm) - 1))
    MM(S, [a.c2b, a.pc, a.pcP, a.pcN], [psP], mmp)
    CP(S, "act", [psP], [a.plb], a.plb[:, :], psP[:, 0:256])
    psT2 = S.bank(0)
    t2v = bview(psT2, BF16)
    TR(S, [a.plb, k.identb], [psT2], [(t2v[:, j * 128:(j + 1) * 128], a.plb[:, j * 128:(j + 1) * 128], k.identb[:, :]) for j in range(2)])
    CP(S, "dve", [psT2], [a.plT], a.plT[:, :, :], t2v[:, 0:256].rearrange("p (j t) -> p j t", j=2))
    psW = S.bank(0)
    MM(S, [k.wpb, a.plT], [psW], [(psW[:, j * 128:(j + 1) * 128], k.wpb[:, j, :], a.plT[:, j, :], True, True) for j in range(2)])
    cT = a.concatT
    for j in range(2):
        ACT(S, [psW, k.psc], [cT], cT[:, 2 + j, :], psW[:, j * 128:(j + 1) * 128], AF.Identity, scale=k.psc[:, j:j + 1])

    psT3 = S.bank(0)
    t3v = bview(psT3, BF16)
    TR(S, [a.yall, k.identb], [psT3], [(t3v[:, (m3 * 2 + j) * 128:(m3 * 2 + j + 1) * 128], a.yall[:, m3, j * 128:(j + 1) * 128], k.identb[:, :])
                                      for m3 in range(3) for j in range(2)])
    CP(S, "dve", [psT3], [cT], cT[:, 0:2, :], t3v[:, 0:256].rearrange("p (j t) -> p j t", j=2))
    CP(S, "act", [psT3], [cT], cT[:, 4:8, :], t3v[:, 256:768].rearrange("p (j t) -> p j t", j=4))

    p0, p1 = S.bank(0), S.bank(0)
    for hlf, pb in enumerate((p0, p1)):
        MM(S, [cT, w_out], [pb], [(pb[:, :], cT[:, kc, :], w_out[:, kc, hlf * 512:(hlf + 1) * 512], kc == 0, kc == 7) for kc in range(8)])
    xb = a.xq[i % 2]
    resid_ln(k, xb, (p0, p1), xb, a.rsm)
    r0 = off + 128 * i
    db = S.dbuf(("xoutA", l, r0 // 128))
    S.dma("pool", rows_ap(k, a.dst, "out", r0, 128), xb[:, :], reads=[xb], writes=[db])
    if a.dst is None:
        k.final_bufs.append(db)


_CACHE = {}


def gather_outputs(results, cfg, n):
    NP, Tp, Ts = cfg.NP, cfg.Tp, cfg.Ts
    y_p = np.concatenate([r["yp"].reshape(NP, Tp, D) for r in results], 0)
    y_s = np.stack([r["ys"].reshape(Ts, D) for r in results], 0)
    oc = np.concatenate([r["oc"] for r in results], 0)
    on = np.concatenate([r["on"] for r in results], 0)
    om = np.concatenate([r["om"].reshape(NP, 2, 2, 4) for r in results], 0)
    os_ = np.concatenate([r["os"] for r in results], 0)
    f = lambda a: np.ascontiguousarray(a, dtype=np.float32)
    return (f(y_p), f(y_s), f(oc), f(on), f(om), f(os_))


def kernel(**inputs):
    n = 8
    xs = np.asarray(inputs["x_sample"])
    xp = np.asarray(inputs["x_prompt"])
    cfg = Cfg(Ts=xs.shape[1], NP=xp.shape[0] // n, Tp=xp.shape[1], L=2)
    key = (cfg.Ts, cfg.NP, cfg.Tp)
    if key not in _CACHE:
        _CACHE[key] = build(cfg)
    nc, _ = _CACHE[key]
    in_maps = [shard_inputs(inputs, cfg, c) for c in range(n)]
    res = run_bass_kernel_spmd(nc, in_maps, core_ids=list(range(n)))
    return gather_outputs(res.results, cfg, n)
```

```python
import math
import numpy as np
import ml_dtypes
from contextlib import ExitStack
import concourse.bass as bass
import concourse.mybir as mybir
from concourse.bass_utils import run_bass_kernel_spmd

F32 = mybir.dt.float32
BF16 = mybir.dt.bfloat16
AF = mybir.ActivationFunctionType
ALU = mybir.AluOpType
AX = mybir.AxisListType
DTSZ = {F32: 4, BF16: 2}

D = 1024
DIN = 2840
DFF = 2816
EPS = 1e-5
ALPHA = 4.0 ** 0.25
NEG = -30000.0

ENGS = ("pe", "act", "dve", "pool", "sp")
EPOCH = 30000
NDMA_SEM = 8
SCHED_CP_B = 0.05
B_HT_GROUP = "d"
BANK_GROUPS = {0: (0, 5), 1: (5, 8), "u": (0, 6), "d": (6, 8)}
SCHED_CP = 0.2
SCHED_LAT = 0.1


def prod(l):
    r = 1
    for x in l:
        r *= int(x)
    return r


class Buf:
    __slots__ = ("name", "v", "last_w", "readers", "off", "excl")

    def __init__(self, name, v, floor=None):
        self.off = -1
        self.excl = False
        self.name = name
        self.v = v
        self.last_w = floor
        self.readers = []

    def __getitem__(self, k):
        return self.v[k]


class Op:
    __slots__ = ("eng", "fn", "deps", "is_dma", "idx", "sig", "has_dep", "vc", "name", "cost", "cp")

    def __init__(self, eng, fn, is_dma, name):
        self.cost = 0.4
        self.eng = eng
        self.fn = fn
        self.is_dma = is_dma
        self.deps = []
        self.sig = None
        self.has_dep = False
        self.vc = None
        self.name = name


class Sched:
    def __init__(self, nc, es, arena_bytes):
        self.nc = nc
        self.es = es
        self.ops = []
        self.floor = None
        self.bufs = []
        self.arena = es.enter_context(nc.sbuf_tensor("arena", [128, arena_bytes // 4], F32))
        self.arena_bytes = arena_bytes
        self.off = 0
        self.peak = 0
        self.banks = []
        for i in range(8):
            t = es.enter_context(nc.psum_tensor(f"bank{i}", [128, 512], F32))
            self.banks.append(Buf(f"bank{i}", t))
            self.banks[-1].excl = True
        self.bank_i = 0
        self.bank_g = {}
        self.dram_bufs = {}

    def sb(self, name, shape, dtype=F32):
        shape = [int(s) for s in shape]
        if getattr(self, "verbose", False):
            print(f"  sb {name} {shape} {prod(shape[1:]) * DTSZ[dtype]} at {self.off}")
        n = prod(shape[1:])
        nb = n * DTSZ[dtype]
        off = (self.off + 31) // 32 * 32
        assert off + nb <= self.arena_bytes, f"arena overflow allocating {name}: {off + nb}"
        self.off = off + nb
        self.peak = max(self.peak, self.off)
        h = self.arena if dtype == F32 else self.arena.bitcast(dtype)
        e0 = off // DTSZ[dtype]
        v = h[0:shape[0], e0:e0 + n]
        if len(shape) > 2:
            names = " ".join(f"d{i}" for i in range(len(shape) - 1))
            kw = {f"d{i}": shape[i + 1] for i in range(len(shape) - 1)}
            v = v.rearrange(f"p ({names}) -> p {names}", **kw)
        b = Buf(name, v, self.floor)
        b.off = off
        self.bufs.append(b)
        return b

    def mark(self):
        return self.off

    def release(self, mark):
        self.barrier()
        self.bufs = [b for b in self.bufs if b.off < mark]
        self.off = mark

    def bank(self, g=None):
        if g is None or not BANK_GROUPS:
            b = self.banks[self.bank_i]
            self.bank_i = (self.bank_i + 1) % 8
            return b
        lo, hi = BANK_GROUPS[g]
        gi = self.bank_g.get(g, 0)
        self.bank_g[g] = gi + 1
        return self.banks[lo + gi % (hi - lo)]

    def dbuf(self, key):
        if key not in self.dram_bufs:
            self.dram_bufs[key] = Buf(str(key), None, None)
        return self.dram_bufs[key]

    def op(self, eng, fn, reads=(), writes=(), name=None, dma=False, cost=None):
        o = Op(eng, fn, dma, name)
        o.cp = getattr(self, "cp", SCHED_CP)
        if cost is not None:
            o.cost = cost
        deps = set()
        ex = [b for b in reads if b.excl]
        if ex:
            reads = [b for b in reads if not b.excl]
            writes = list(writes) + [b for b in ex if b not in writes]
        for b in reads:
            if b.last_w is not None:
                deps.add(b.last_w)
        for b in writes:
            if b.last_w is not None:
                deps.add(b.last_w)
            for r in b.readers:
                deps.add(r)
        o.deps = list(deps)
        o.idx = len(self.ops)
        for d in o.deps:
            d.has_dep = True
        for b in reads:
            b.readers.append(o)
        for b in writes:
            b.last_w = o
            b.readers = []
        self.ops.append(o)
        return o

    def dma(self, q, out, in_, reads=(), writes=(), name=None, **kw):
        nbytes = prod(out.shape) * 4
        return self.op(q, lambda e: e.dma_start(out=out, in_=in_, **kw), reads, writes, name=name, dma=True,
                       cost=2.0 + nbytes / 150e3)

    def barrier(self):
        allb = self.bufs + self.banks + list(self.dram_bufs.values())
        o = self.op("sp", None, reads=[], writes=allb, name="barrier")
        self.floor = o
        return o

    def list_schedule(self, ops):
        import heapq
        LAT = SCHED_LAT
        out = []
        seg = []
        segs = []
        for o in ops:
            if o.fn is None:
                segs.append(seg)
                segs.append([o])
                seg = []
            else:
                seg.append(o)
        segs.append(seg)
        finish = {}
        for seg in segs:
            if len(seg) <= 1:
                for o in seg:
                    finish[o] = 0.0
                    out.append(o)
                continue
            inseg = set(seg)
            seg_cp = seg[len(seg) // 2].cp
            indeg = {}
            users = {}
            for o in seg:
                n = 0
                for d in o.deps:
                    if d in inseg:
                        n += 1
                        users.setdefault(d, []).append(o)
                indeg[o] = n
            tail = {}
            for o in reversed(seg):
                t = 0.0
                for u in users.get(o, ()):
                    if tail[u] > t:
                        t = tail[u]
                tail[o] = t + o.cost + 0.2
            eng_time = {e: 0.0 for e in ENGS}
            ready_at = {}
            heap = []
            for o in seg:
                if indeg[o] == 0:
                    ready_at[o] = 0.0
                    heapq.heappush(heap, (0.0, o.idx, o))
            while heap:
                best = None
                cand = []
                while heap and len(cand) < 24:
                    cand.append(heapq.heappop(heap))
                bi = None
                for ci, (ra, idx, o) in enumerate(cand):
                    stt = max(ra, eng_time[o.eng])
                    key = (stt - seg_cp * tail[o], idx)
                    if best is None or key < best:
                        best, bi = key, ci
                ra, idx, o = cand.pop(bi)
                for c in cand:
                    heapq.heappush(heap, c)
                stt = max(ra, eng_time[o.eng])
                if o.is_dma:
                    eng_time[o.eng] = stt + 0.15
                    fin_t = stt + o.cost
                else:
                    fin_t = stt + o.cost
                    eng_time[o.eng] = fin_t
                finish[o] = fin_t
                out.append(o)
                for u in users.get(o, ()):
                    t = fin_t + (LAT if u.eng != o.eng else 0.3)
                    if ready_at.get(u, 0.0) < t:
                        ready_at[u] = t
                    indeg[u] -= 1
                    if indeg[u] == 0:
                        heapq.heappush(heap, (ready_at[u], u.idx, u))
            self.est_time = getattr(self, "est_time", 0.0) + max(eng_time.values())
        for i, o in enumerate(out):
            o.idx = i
        return out

    def emit(self, final_bufs):
        nc, es = self.nc, self.es
        fin = self.op("sp", None, reads=list(final_bufs), name="final")
        if getattr(self, "reorder", True):
            self.ops = self.list_schedule(self.ops)
        cnt, dma_n, semkeys = {}, {}, []
        for o in self.ops:
            if not o.has_dep:
                continue
            if o.is_dma:
                n = dma_n.get(o.eng, 0)
                dma_n[o.eng] = n + 1
                key = ("dma", o.eng, n % NDMA_SEM)
                cnt[key] = cnt.get(key, 0) + 16
                o.sig = (key, cnt[key])
            else:
                tot = cnt.get(("n", o.eng), 0)
                cnt[("n", o.eng)] = tot + 1
                key = ("c", o.eng, tot // EPOCH)
                o.sig = (key, tot % EPOCH + 1)
            if o.sig[0] not in semkeys:
                semkeys.append(o.sig[0])
        sems = {k: es.enter_context(nc.semaphore("s_" + "_".join(str(x) for x in k))) for k in semkeys}
        per_eng = {e: [] for e in ENGS}
        seen = {e: {} for e in ENGS}
        nwaits = 0
        for o in self.ops:
            s = seen[o.eng]
            need = {}
            for d in o.deps:
                k, c = d.sig
                if s.get(k, 0) < c:
                    need[k] = max(need.get(k, 0), c)
            if o.is_dma and o.sig is not None:
                k, c = o.sig
                if c > 16 and s.get(k, 0) < c - 16:
                    need[k] = max(need.get(k, 0), c - 16)
            for d in o.deps:
                for k, c in d.vc.items():
                    if s.get(k, 0) < c:
                        s[k] = c
            for k, c in need.items():
                if s.get(k, 0) < c:
                    s[k] = c
            o.deps = need
            nwaits += len(need)
            vc = dict(s)
            if o.sig is not None:
                vc[o.sig[0]] = max(vc.get(o.sig[0], 0), o.sig[1])
                if not o.is_dma:
                    for ep in range(o.sig[0][2]):
                        vc[("c", o.eng, ep)] = EPOCH
            o.vc = vc
            per_eng[o.eng].append(o)

        def body_for(engname):
            def body(eng):
                for o in per_eng[engname]:
                    for k, c in o.deps.items():
                        eng.wait_ge(sems[k], c)
                    if o.fn is not None:
                        ins = o.fn(eng)
                        if o.sig is not None:
                            ins.then_inc(sems[o.sig[0]], 16 if o.is_dma else 1)
                    elif o.sig is not None:
                        eng.nop().then_inc(sems[o.sig[0]], 1)
            return body

        with nc.Block() as block:
            block.sync(body_for("sp"))
            block.scalar(body_for("act"))
            block.vector(body_for("dve"))
            block.gpsimd(body_for("pool"))
            block.tensor(body_for("pe"))
        return {"ops": len(self.ops), "waits": nwaits, "sems": len(sems),
                "per_eng": {e: len(v) for e, v in per_eng.items()}, "sbuf_peak": self.peak}


def _c(out, base=0.25, per=1.0 / 1000.0):
    return base + prod(out.shape[1:]) * per


def ACT(S, r, w, out, in_, func, bias=None, scale=None):
    kw = {}
    if bias is not None:
        kw["bias"] = bias
    if scale is not None:
        kw["scale"] = scale
    return S.op("act", lambda e: e.activation(out, in_, func, **kw), r, w, cost=_c(out, 0.3, 1 / 1200.0))


def TS(S, eng, r, w, out, in0, s1, s2, op0, op1=None):
    if op1 is None:
        return S.op(eng, lambda e: e.tensor_scalar(out, in0, s1, None, op0), r, w, cost=_c(out))
    return S.op(eng, lambda e: e.tensor_scalar(out, in0, s1, s2, op0, op1), r, w, cost=_c(out))


def TT(S, eng, r, w, out, in0, in1, op):
    return S.op(eng, lambda e: e.tensor_tensor(out, in0, in1, op), r, w, cost=_c(out))


def STT(S, eng, r, w, out, in0, scalar, in1, op0, op1):
    return S.op(eng, lambda e: e.scalar_tensor_tensor(out, in0, scalar, in1, op0, op1), r, w, cost=_c(out))


def CP(S, eng, r, w, out, in_):
    if eng == "act":
        return S.op("act", lambda e: e.copy(out, in_), r, w, cost=_c(out, 0.3, 1 / 1200.0))
    return S.op(eng, lambda e: e.tensor_copy(out, in_), r, w, cost=_c(out))


def MSET(S, eng, w, out, val):
    return S.op(eng, lambda e: e.memset(out, val), [], w)


def MM(S, r, w, mms):
    mms = list(mms)

    def fn(e):
        ins = None
        for (o, l, rh, st, sp) in mms:
            ins = e.matmul(o, l, rh, start=st, stop=sp)
        return ins
    cost = 0.1
    for (o, l, rh, st, sp) in mms:
        cost += max(64, prod(rh.shape[1:])) / 2400.0 * (4.0 if rh.dtype == F32 else 1.0) + 0.02
    return S.op("pe", fn, r, w, cost=cost)


def TR(S, r, w, trs):
    trs = list(trs)

    def fn(e):
        ins = None
        for (o, i, idt) in trs:
            ins = e.transpose(o, i, idt)
        return ins
    return S.op("pe", fn, r, w, cost=0.1 + 0.12 * len(trs))


def bc(ap, shape):
    return ap.to_broadcast([int(s) for s in shape])


POOL_W = (2, 4, 8, 16)


def make_consts():
    cols = {}
    parts = []
    off = [0]

    def add(name, arr):
        a = np.zeros((128, arr.shape[1]), np.float32)
        a[:arr.shape[0]] = arr
        cols[name] = (off[0], arr.shape[1])
        off[0] += arr.shape[1]
        parts.append(a)

    idx = np.arange(128)
    s_, t_ = idx[:, None], idx[None, :]
    add("ident", np.eye(128, dtype=np.float32))
    add("ones", np.ones((128, 128), np.float32))
    add("tri0", (s_ <= t_).astype(np.float32))
    add("tri1", (s_ >= t_).astype(np.float32))
    add("neg0", np.where(s_ <= t_, 0.0, NEG).astype(np.float32))
    add("neg1", np.where(s_ >= t_, 0.0, NEG).astype(np.float32))
    n1 = off[0]
    for g, w in enumerate(POOL_W):
        h = w // 2
        band = ((s_ >= t_ - h) & (s_ < t_ + h)).astype(np.float32)
        cnt_int = np.full(128, float(w))
        cnt_first = (idx + h) - np.maximum(idx - h, 0)
        cnt_last = np.minimum(idx + h, 128) - (idx - h)
        eye = np.eye(128, dtype=np.float32)
        add(f"pA{g}int", band / cnt_int[None, :] - eye)
        add(f"pA{g}first", band / cnt_first[None, :] - eye)
        add(f"pA{g}last", band / cnt_last[None, :] - eye)
        sp = np.arange(8)[:, None]
        add(f"pP{g}", (((sp - 8) >= t_ - h) & ((sp - 8) < t_ + h)).astype(np.float32) / w)
        add(f"pN{g}", (((128 + sp) >= t_ - h) & ((128 + sp) < t_ + h)).astype(np.float32) / w)
    add("jrow", np.tile(np.arange(256, dtype=np.float32)[None, :], (128, 1)))
    add("pcol", (idx % 64).astype(np.float32)[:, None])
    full = np.concatenate(parts, axis=1)
    cols2 = {kk: (o - n1, n) for kk, (o, n) in cols.items() if o >= n1}
    cols1 = {kk: (o, n) for kk, (o, n) in cols.items() if o < n1}
    return full[:, :n1].copy(), cols1, full[:, n1:].copy(), cols2


class Cfg:
    def __init__(self, Ts=4096, NP=4, Tp=256, L=2, debug=()):
        self.Ts, self.NP, self.Tp, self.L = Ts, NP, Tp, L
        self.debug = tuple(debug)
        self.seqs = [("s", Ts, 0, 0)] + [(f"p{j}", Tp, 1, Ts + j * Tp) for j in range(NP)]
        self.Ttot = Ts + NP * Tp


INPUT_SPECS = lambda c: [
    ("xs", [c.Ts, D]), ("xp", [c.NP * c.Tp, D]),
    ("st_c", [2, 2, 4, 64, 64]), ("st_n", [2, 2, 4, 64]), ("st_m", [2, 8]), ("st_s", [2, 2, 4, 64, 128]),
    ("cond", [2, D]),
    ("w_ada", [2, D, 6 * D]), ("b_ada", [2, 6 * D]), ("w_in", [2, D, DIN]), ("b_ig", [2, 8]), ("b_fg", [2, 8]),
    ("mnorm_g", [2, 256]), ("w_pool", [2, 4, 64, 64]), ("pool_scale", [2, 256]), ("w_sp", [2, 4, 128, 128]),
    ("b_sp", [2, 4, 128]), ("sconv_w", [2, 3, 768]), ("sconv_b", [2, 768]), ("dt_bias", [2, 8]),
    ("a_log", [2, 8]), ("ssd_d", [2, 4]), ("snorm_g", [2, 256]), ("w_out", [2, D, D]),
    ("ln1_g", [2, D]), ("ln1_b", [2, D]), ("w_up", [2, D, 2 * DFF]), ("fconv_w", [2, 3, 2 * DFF]),
    ("fconv_b", [2, 2 * DFF]), ("w_dn", [2, DFF, D]), ("ln2_g", [2, D]), ("ln2_b", [2, D]),
]
OUTPUT_SPECS = lambda c: [
    ("ys", [c.Ts, D]), ("yp", [c.NP * c.Tp, D]), ("oc", [c.NP, 2, 2, 4, 64, 64]), ("on", [c.NP, 2, 2, 4, 64]),
    ("om", [c.NP, 2, 8]), ("os", [c.NP, 2, 2, 4, 64, 128]),
]


class K:
    pass


def build(cfg):
    nc = bass.Bass("TRN2", target_bir_lowering=False)
    cst_np, ccols, cst2_np, ccols2 = make_consts()
    I = {}
    for name, shape in INPUT_SPECS(cfg):
        I[name] = nc.dram_tensor(name, shape, F32, kind="ExternalInput")
    I["cst"] = nc.dram_tensor("cst", list(cst_np.shape), F32, kind="ExternalInput")
    I["cst2"] = nc.dram_tensor("cst2", list(cst2_np.shape), F32, kind="ExternalInput")
    O = {}
    for name, shape in OUTPUT_SPECS(cfg):
        O[name] = nc.dram_tensor(name, shape, F32, kind="ExternalOutput")
    DBG = {}
    for name, shape in cfg.debug:
        DBG[name] = nc.dram_tensor("dbg_" + name, shape, F32, kind="ExternalOutput")
    ntile = cfg.Ttot // 128
    XA = nc.dram_tensor("scr_xa", [cfg.Ttot, D], F32, kind="Internal")
    XB = nc.dram_tensor("scr_xb", [cfg.Ttot, D], F32, kind="Internal")
    HF = nc.dram_tensor("scr_hf", [cfg.Ttot, 512], F32, kind="Internal")
    HT = nc.dram_tensor("scr_ht", [ntile, 128, 8 * 128], BF16, kind="Internal")
    ED = nc.dram_tensor("scr_e", [64, 512], F32, kind="Internal")
    STB = nc.dram_tensor("scr_stb", [ntile, 128, 2304], BF16, kind="Internal")
    STG = nc.dram_tensor("scr_stg", [ntile, 128, 280], F32, kind="Internal")

    with ExitStack() as es:
        S = Sched(nc, es, 204 * 1024)
        k = K()
        k.S, k.cfg, k.I, k.O, k.DBG = S, cfg, I, O, DBG
        k.XA, k.XB, k.HF, k.HT, k.ED = XA, XB, HF, HT, ED
        k.STB, k.STG = STB, STG
        k.final_bufs = []
        k.ccols2 = ccols2
        k.W2 = cst2_np.shape[1]
        setup(k, ccols)
        for l in range(cfg.L):
            layer(k, l)
        stats = S.emit(k.final_bufs)
    return nc, stats


def cview(k, name, rows=128):
    if name in k.ccols:
        o, n = k.ccols[name]
        return k.cst[0:rows, o:o + n]
    o, n = k.ccols2[name]
    return k.cst2[0:rows, o:o + n]


def load_cst2(k):
    S = k.S
    b = S.sb("cst2", [128, k.W2])
    k.cst2b = b
    k.cst2 = b.v
    S.dma("sp", b[:, :], k.I["cst2"][:, :], writes=[b])
    return b


def setup(k, ccols):
    S, I, cfg = k.S, k.I, k.cfg
    k.ccols = ccols
    W = sum(n for (_, n) in ccols.values())
    cstb = S.sb("cst", [128, W])
    k.cstb = cstb
    k.cst = cstb.v
    S.dma("sp", cstb[:, :], I["cst"][:, :], writes=[cstb])
    k.identb = S.sb("identb", [128, 128], BF16)
    CP(S, "dve", [cstb], [k.identb], k.identb[:, :], cview(k, "ident"))
    k.cm05 = S.sb("cm05", [128, 1])
    MSET(S, "pool", [k.cm05], k.cm05[:, :], -0.5)
    k.cm1 = S.sb("cm1", [128, 1])
    MSET(S, "pool", [k.cm1], k.cm1[:, :], -1.0)

    L = cfg.L
    k.modT = S.sb("modT", [128, L, 48, 2])
    layer_consts_alloc(k)
    m1 = S.mark()
    c2b = load_cst2(k)
    fr = S.sb("pe_fr", [64, 256])
    ang = S.sb("pe_ang", [64, 256])
    et = S.sb("pe_e", [64, 512])
    et2 = S.sb("pe_e2", [64, 512])
    sq = S.sb("pe_sq", [64, 256])
    ACT(S, [c2b], [fr], fr[:, :], cview(k, "jrow", 64), AF.Exp, scale=-math.log(10000.0) / 256.0)
    TS(S, "dve", [fr, c2b], [ang], ang[:, :], fr[:, :], cview(k, "pcol", 64), None, ALU.mult)
    ACT(S, [ang], [et], et[:, 0:256], ang[:, :], AF.Sin, scale=1.0 / 32.0)
    ACT(S, [ang], [et], et[:, 256:512], ang[:, :], AF.Sin, scale=-1.0 / 32.0, bias=math.pi / 2.0)
    cur, nxt = et, et2
    for it in range(5):
        TT(S, "dve", [cur], [sq], sq[:, :], cur[:, 0:256], cur[:, 0:256], ALU.mult)
        STT(S, "dve", [cur], [nxt], nxt[:, 0:256], cur[:, 0:256], 2.0, cur[:, 256:512], ALU.mult, ALU.mult)
        TS(S, "dve", [sq], [nxt], nxt[:, 256:512], sq[:, :], -2.0, 1.0, ALU.mult, ALU.add)
        cur, nxt = nxt, cur
    et = cur
    edb = S.dbuf("ED")
    S.dma("sp", k.ED[:, :], et[:, :], reads=[et], writes=[edb])

    condT = S.sb("condT", [128, 8, 2])
    for c in range(2):
        S.dma("sp", condT[:, :, c], I["cond"][c].rearrange("(kc p) -> p kc", p=128), writes=[condT],
              allow_slow_non_contiguous=True)
    esg = S.sb("cond_e", [128, 8, 2])
    ACT(S, [condT], [esg], esg[:, :, :], condT[:, :, :], AF.Exp, scale=-1.0)
    TS(S, "dve", [esg], [esg], esg[:, :, :], esg[:, :, :], 1.0, None, ALU.add)
    TT(S, "pool", [esg, k.cm1], [esg], esg[:, :, :], esg[:, :, :], bc(k.cm1[:, 0:1].unsqueeze(2), [128, 8, 2]), ALU.pow)
    TT(S, "pool", [condT, esg], [condT], condT[:, :, :], condT[:, :, :], esg[:, :, :], ALU.mult)
    badaT = S.sb("badaT", [128, L, 48])
    k.cf_st = [S.sb(f"cf_sx{j}", [128, 128]) for j in range(2)]
    for l in range(L):
        colform(k, badaT, badaT[:, l, :], I["b_ada"][l].rearrange("(j p) -> j p", p=128), 48)
    wst = [S.sb(f"wada_st{j}", [128, 8, 512]) for j in range(4)]
    n = 0
    for l in range(L):
        for cb in range(12):
            st = wst[n % 4]
            n += 1
            S.dma("sp", st[:, :, :], I["w_ada"][l, :, cb * 512:(cb + 1) * 512].rearrange("(kc p) n -> p kc n", p=128),
                  writes=[st])
            pb = S.bank()
            mms = []
            for sub in range(4):
                for kc in range(8):
                    mms.append((pb[:, sub * 2:sub * 2 + 2], st[:, kc, sub * 128:(sub + 1) * 128], condT[:, kc, :],
                                kc == 0, kc == 7))
            MM(S, [st, condT], [pb], mms)
            TT(S, "dve", [pb, badaT], [k.modT],
               k.modT[:, l, cb * 4:cb * 4 + 4, :],
               pb[:, 0:8].rearrange("p (s c) -> p s c", c=2),
               bc(badaT[:, l, cb * 4:cb * 4 + 4].unsqueeze(2), [128, 4, 2]), ALU.add)
    for l in range(L):
        for grp in (1, 4):
            TS(S, "dve", [k.modT], [k.modT], k.modT[:, l, grp * 8:(grp + 1) * 8, :],
               k.modT[:, l, grp * 8:(grp + 1) * 8, :], 1.0, None, ALU.add)
    S.release(m1)


class nc_allow:
    def __init__(self, k):
        pass

    def __enter__(self):
        return self

    def __exit__(self, *a):
        return False


def bview(bank, dtype=F32):
    return bank.v if dtype == F32 else bank.v.bitcast(dtype)


def layer_consts_alloc(k):
    S = k.S
    k.bif = S.sb("bif", [128, 16])
    k.coef = S.sb("coef", [128, 2, 8])
    k.dtb = S.sb("dtb", [128, 8])
    k.dsk = S.sb("dsk", [128, 4])
    k.mng = S.sb("mng", [128, 256])
    k.sng = S.sb("sng", [128, 256])
    k.psc = S.sb("psc", [128, 2])
    k.wpb = S.sb("wpb", [128, 2, 128], BF16)
    k.wsT = S.sb("wsT", [128, 4, 128], BF16)
    k.bsT = S.sb("bsT", [128, 4])
    k.scw = S.sb("scw", [128, 4, 6])
    k.fcw = S.sb("fcw", [128, 4, 44])
    k.lng = S.sb("lng", [128, D])
    k.lnb = S.sb("lnb", [128, D])
    k.gbc = S.sb("gbc", [128, D])
    k.ttmp = [S.sb(f"ttmp{j}", [128, D]) for j in range(1)]
    k.small = {}


def colform(k, dst_buf, dst_ap, src_ap, nb):
    S = k.S
    k.cf_n = getattr(k, "cf_n", 0) + 1
    st = k.cf_st[k.cf_n % len(k.cf_st)]
    S.dma("sp", st[0:nb, :], src_ap, writes=[st])
    pb = S.bank()
    TR(S, [st, k.cstb], [pb], [(pb[:, 0:nb], st[0:nb, :], cview(k, "ident")[0:nb, 0:nb])])
    CP(S, "dve", [pb], [dst_buf], dst_ap, pb[:, 0:nb])


def load_layer_consts(k, l):
    S, I = k.S, k.I
    m = S.mark()
    k.cf_st = [S.sb(f"cf_st{j}", [128, 128]) for j in range(4)]
    row = lambda name, a, b: I[name][l:l + 1, a:b].partition_broadcast(128)
    S.dma("sp", k.bif[:, 0:8], row("b_ig", 0, 8), writes=[k.bif])
    S.dma("sp", k.bif[:, 8:16], row("b_fg", 0, 8), writes=[k.bif])
    TS(S, "dve", [k.bif], [k.bif], k.bif[:, 8:16], k.bif[:, 8:16], -1.0, None, ALU.mult)
    al = S.sb("al_tmp", [128, 8])
    S.dma("sp", al[:, :], row("a_log", 0, 8), writes=[al])
    ACT(S, [al], [al], al[:, :], al[:, :], AF.Exp)
    MSET(S, "pool", [k.coef], k.coef[:, :, :], -1.0)
    TS(S, "dve", [al, k.coef], [k.coef], k.coef[:, :, 4:8], al[:, :].rearrange("p (d h) -> p d h", d=2), -1.0, None, ALU.mult)
    S.dma("sp", k.dtb[:, :], row("dt_bias", 0, 8), writes=[k.dtb])
    S.dma("sp", k.dsk[:, :], row("ssd_d", 0, 4), writes=[k.dsk])
    S.dma("sp", k.mng[:, :], row("mnorm_g", 0, 256), writes=[k.mng])
    S.dma("sp", k.sng[:, :], row("snorm_g", 0, 256), writes=[k.sng])
    colform(k, k.psc, k.psc[:, :], I["pool_scale"][l].rearrange("(j p) -> j p", p=128), 2)
    wp32 = S.sb("wp32", [128, 2, 128])
    MSET(S, "pool", [wp32], wp32[:, :, :], 0.0)
    for g in range(4):
        pr = slice((g % 2) * 64, (g % 2) * 64 + 64)
        S.dma("sp", wp32[pr, g // 2, (g % 2) * 64:(g % 2) * 64 + 64], I["w_pool"][l, g], writes=[wp32])
    CP(S, "dve", [wp32], [k.wpb], k.wpb[:, :, :], wp32[:, :, :])
    ws32 = S.sb("ws32", [128, 4, 128])
    S.dma("sp", ws32[:, :, :], I["w_sp"][l].rearrange("h t s -> t h s"), writes=[ws32])
    pb = S.bank()
    TR(S, [ws32, k.cstb], [pb], [(pb[:, h * 128:(h + 1) * 128], ws32[:, h, :], cview(k, "ident")) for h in range(4)])
    CP(S, "act", [pb], [k.wsT], k.wsT[:, :, :], pb[:, :].rearrange("p (h t) -> p h t", h=4))
    colform(k, k.bsT, k.bsT[:, :], I["b_sp"][l], 4)
    for tap in range(3):
        colform(k, k.scw, k.scw[:, tap, :], I["sconv_w"][l, tap].rearrange("(b p) -> b p", p=128), 6)
        colform(k, k.fcw, k.fcw[:, tap, :], I["fconv_w"][l, tap].rearrange("(b p) -> b p", p=128), 44)
    colform(k, k.scw, k.scw[:, 3, :], I["sconv_b"][l].rearrange("(b p) -> b p", p=128), 6)
    colform(k, k.fcw, k.fcw[:, 3, :], I["fconv_b"][l].rearrange("(b p) -> b p", p=128), 44)
    S.release(m)


def load_weight(k, dst, src2d, nkc, ncols, scope_stage):
    S = k.S
    engs = ("dve", "act")
    piece = 2840
    for kc in range(nkc):
        for c0 in range(0, ncols, piece):
            c1 = min(ncols, c0 + piece)
            st = scope_stage[k.wl_n % len(scope_stage)]
            S.dma("sp", st[:, 0:c1 - c0], src2d[kc * 128:(kc + 1) * 128, c0:c1], writes=[st])
            CP(S, engs[k.wl_n % 2], [st], [dst], dst[:, kc, c0:c1], st[:, 0:c1 - c0])
            k.wl_n += 1


def gate_table(k, l, grp, cond):
    S = k.S
    dg = k.ttmp[0]
    for j in range(8):
        TS(S, "dve", [k.cstb, k.modT], [dg], dg[:, 0:128], cview(k, "ident"), k.modT[:, l, grp * 8 + j, cond:cond + 1], None, ALU.mult)
        if j % 4 == 0:
            pb = S.bank()
        MM(S, [dg, k.cstb], [pb], [(pb[:, (j % 4) * 128:(j % 4 + 1) * 128], cview(k, "ones"), dg[:, 0:128], True, True)])
        if j % 4 == 3:
            CP(S, "act", [pb], [k.gbc], k.gbc[:, (j // 4) * 512:(j // 4 + 1) * 512], pb[:, :])


def make_hT(k, l, which, cond, rows, xbuf, dsts, dst_buf, src_bufs, pos_tile=None):
    S = k.S
    n = 0
    for r, nr in rows:
        S.dma("sp", xbuf[n:n + nr, :], r, reads=src_bufs, writes=[xbuf])
        n += nr
    if pos_tile is not None:
        TT(S, "pool", [xbuf, pos_tile], [xbuf], xbuf[0:n, :], xbuf[0:n, :], pos_tile[0:n, :], ALU.add)
    sm = k.hsm
    st, mv, ve, rstd, xnb = sm["st"], sm["mv"], sm["ve"], sm["rstd"], sm["xnb"]
    S.op("dve", lambda e: e.bn_stats(st[0:n, 0, :], xbuf[0:n, 0:512]), [xbuf], [st])
    S.op("dve", lambda e: e.bn_stats(st[0:n, 1, :], xbuf[0:n, 512:1024]), [xbuf], [st])
    S.op("dve", lambda e: e.bn_aggr(mv[0:n, :], st[0:n, :, :].rearrange("p a b -> p (a b)")), [st], [mv])
    TS(S, "dve", [mv], [ve], ve[0:n, :], mv[0:n, 1:2], EPS, None, ALU.add)
    TT(S, "pool", [ve, k.cm05], [rstd], rstd[0:n, :], ve[0:n, :], k.cm05[0:n, :], ALU.pow)
    TS(S, "dve", [xbuf, mv, rstd], [xnb], xnb[0:n, :], xbuf[0:n, :], mv[0:n, 0:1], rstd[0:n, 0:1], ALU.subtract, ALU.mult)
    pb = S.bank(getattr(k, "hT_bank_group", None))
    pv = bview(pb, BF16)
    TR(S, [xnb, k.identb], [pb],
       [(pv[:, kc * 128:kc * 128 + n], xnb[0:n, kc * 128:(kc + 1) * 128], k.identb[0:n, 0:n]) for kc in range(8)])
    gsh, gsc = (0, 1) if which == 1 else (3, 4)
    for kc in range(8):
        sc = k.modT[:, l, gsc * 8 + kc, cond:cond + 1]
        sh = k.modT[:, l, gsh * 8 + kc, cond:cond + 1]
        if kc % 2 == 0:
            ACT(S, [pb, k.modT], [dst_buf], dsts[kc], pv[:, kc * 128:kc * 128 + n], AF.Identity, bias=sh, scale=sc)
        else:
            TS(S, "dve", [pb, k.modT], [dst_buf], dsts[kc], pv[:, kc * 128:kc * 128 + n], sc, sh, ALU.mult, ALU.add)


def alloc_hsm(k):
    S = k.S
    k.hsm = {"st": S.sb("h_st", [128, 2, 6]), "mv": S.sb("h_mv", [128, 2]), "ve": S.sb("h_ve", [128, 1]),
             "rstd": S.sb("h_rstd", [128, 1]), "xnb": S.sb("h_xnb", [128, D], BF16)}


def resid_ln(k, x_buf, psum_halves, out_buf, nb_small):
    S = k.S
    t0, t1 = k.ttmp[0], out_buf
    for hlf, pb in enumerate(psum_halves):
        sl = slice(hlf * 512, (hlf + 1) * 512)
        TT(S, "dve", [pb, k.gbc], [t0], t0[:, sl], pb[:, :], k.gbc[:, sl], ALU.mult)
    STT(S, "dve", [x_buf, t0], [t0], t0[:, :], x_buf[:, :], ALPHA, t0[:, :], ALU.mult, ALU.add)
    st, mv, ve, rstd, nb = nb_small["st"], nb_small["mv"], nb_small["ve"], nb_small["rstd"], nb_small["nb"]
    S.op("dve", lambda e: e.bn_stats(st[:, 0, :], t0[:, 0:512]), [t0], [st])
    S.op("dve", lambda e: e.bn_stats(st[:, 1, :], t0[:, 512:1024]), [t0], [st])
    S.op("dve", lambda e: e.bn_aggr(mv[:, :], st[:, :, :].rearrange("p a b -> p (a b)")), [st], [mv])
    TS(S, "dve", [mv], [ve], ve[:, :], mv[:, 1:2], EPS, None, ALU.add)
    TT(S, "pool", [ve, k.cm05], [rstd], rstd[:, :], ve[:, :], k.cm05[:, :], ALU.pow)
    STT(S, "dve", [mv, rstd], [nb], nb[:, :], mv[:, 0:1], -1.0, rstd[:, :], ALU.mult, ALU.mult)
    ACT(S, [t0, rstd, nb], [t1], t1[:, :], t0[:, :], AF.Identity, bias=nb[:, 0:1], scale=rstd[:, 0:1])
    TT(S, "pool", [t1, k.lng], [t1], t1[:, :], t1[:, :], k.lng[:, :], ALU.mult)
    TT(S, "pool", [t1, k.lnb], [out_buf], out_buf[:, :], t1[:, :], k.lnb[:, :], ALU.add)


def alloc_rsm(k):
    S = k.S
    return {"st": S.sb("r_st", [128, 2, 6]), "mv": S.sb("r_mv", [128, 2]), "ve": S.sb("r_ve", [128, 1]),
            "rstd": S.sb("r_rstd", [128, 1]), "nb": S.sb("r_nb", [128, 1])}


def seq_src_dst(k, l, phase):
    cfg = k.cfg
    mode = getattr(cfg, "mode", "full")
    if phase == "A":
        src = None if l == 0 else k.XB
        dst = k.XA if mode == "full" else None
    else:
        src = k.XA if mode == "full" else None
        dst = None if l == cfg.L - 1 else k.XB
    return src, dst


def rows_ap(k, handle, which_io, r0, n):
    cfg = k.cfg
    if handle is not None:
        return handle[r0:r0 + n, :]
    if r0 < cfg.Ts:
        t = k.I["xs"] if which_io == "in" else k.O["ys"]
        return t[r0:r0 + n, :]
    t = k.I["xp"] if which_io == "in" else k.O["yp"]
    return t[r0 - cfg.Ts:r0 - cfg.Ts + n, :]


def phaseB(k, l):
    S, I, cfg = k.S, k.I, k.cfg
    S.cp = SCHED_CP_B
    k.hT_bank_group = B_HT_GROUP
    m = S.mark()
    w_up = S.sb("w_up", [128, 8, 2 * DFF], BF16)
    w_dn = S.sb("w_dn", [128, 22, D], BF16)
    m2 = S.mark()
    stage = [S.sb(f"wstage{j}", [128, 2840]) for j in range(3)]
    k.wl_n = 0
    load_weight(k, w_up, I["w_up"][l], 8, 2 * DFF, stage)
    load_weight(k, w_dn, I["w_dn"][l], 22, D, stage)
    S.release(m2)
    S.dma("sp", k.lng[:, :], I["ln2_g"][l:l + 1, :].partition_broadcast(128), writes=[k.lng])
    S.dma("sp", k.lnb[:, :], I["ln2_b"][l:l + 1, :].partition_broadcast(128), writes=[k.lnb])
    alloc_hsm(k)
    rsm = alloc_rsm(k)
    SEG = 256
    h2T = [S.sb(f"h2T{j}", [128, 8, SEG + 2], BF16) for j in range(2)]
    actT = S.sb("actT", [128, 22, SEG], BF16)
    xt = [S.sb(f"xtB{j}", [128, D]) for j in range(4)]
    xh = k.ttmp[0]
    NBUF = 3
    cg = [S.sb(f"cg{j}", [128, SEG]) for j in range(NBUF)]
    cv = [S.sb(f"cv{j}", [128, SEG]) for j in range(NBUF)]
    th = [S.sb(f"th{j}", [128, SEG]) for j in range(NBUF)]
    src, dst = seq_src_dst(k, l, "B")
    segs = []
    for (sname, T, cond, off) in cfg.seqs:
        for t0 in range(0, T, SEG):
            segs.append((T, cond, off, t0))
    state = {}

    def prep(si):
        T, cond, off, t0 = segs[si]
        hT = h2T[si % 2]
        r0 = off + t0
        xts = []
        for j in range(SEG // 128):
            xb = xt[(2 * si + j) % 4]
            xts.append(xb)
            make_hT(k, l, 2, cond, [(rows_ap(k, src, "in", r0 + 128 * j, 128), 128)], xb,
                    [hT[:, kc, 1 + 128 * j:1 + 128 * (j + 1)] for kc in range(8)], hT, [])
        rows, cols = [], []
        if t0 > 0:
            rows.append((rows_ap(k, src, "in", r0 - 1, 1), 1))
            cols.append(0)
        else:
            MSET(S, "pool", [hT], hT[:, :, 0:1], 0.0)
        if t0 + SEG < T:
            rows.append((rows_ap(k, src, "in", r0 + SEG, 1), 1))
            cols.append(SEG + 1)
        else:
            MSET(S, "pool", [hT], hT[:, :, SEG + 1:SEG + 2], 0.0)
        if len(rows) == 2:
            make_hT(k, l, 2, cond, rows, xh, [hT[:, kc, 0:SEG + 2:SEG + 1] for kc in range(8)], hT, [])
        elif len(rows) == 1:
            c = cols[0]
            make_hT(k, l, 2, cond, rows, xh, [hT[:, kc, c:c + 1] for kc in range(8)], hT, [])
        state[si] = xts

    def ffn(si):
        T, cond, off, t0 = segs[si]
        hT = h2T[si % 2]
        r0 = off + t0
        xts = state.pop(si)
        for c in range(22):
            pg, pv = S.bank("u"), S.bank("u")
            MM(S, [w_up, hT], [pg], [(pg[:, 0:SEG + 2], w_up[:, kc, c * 128:(c + 1) * 128], hT[:, kc, :], kc == 0, kc == 7)
                                      for kc in range(8)])
            MM(S, [w_up, hT], [pv], [(pv[:, 0:SEG + 2], w_up[:, kc, DFF + c * 128:DFF + (c + 1) * 128], hT[:, kc, :], kc == 0, kc == 7)
                                      for kc in range(8)])
            g_, v_, t_ = cg[c % NBUF], cv[c % NBUF], th[c % NBUF]
            fw = k.fcw
            ACT(S, [pg, fw], [g_], g_[:, :], pg[:, 1:SEG + 1], AF.Identity, bias=fw[:, 3, c:c + 1], scale=fw[:, 1, c:c + 1])
            STT(S, "dve", [pg, fw, g_], [g_], g_[:, :], pg[:, 0:SEG], fw[:, 0, c:c + 1], g_[:, :], ALU.mult, ALU.add)
            STT(S, "dve", [pg, fw, g_], [g_], g_[:, :], pg[:, 2:SEG + 2], fw[:, 2, c:c + 1], g_[:, :], ALU.mult, ALU.add)
            cc = 22 + c
            ACT(S, [pv, fw], [v_], v_[:, :], pv[:, 1:SEG + 1], AF.Identity, bias=fw[:, 3, cc:cc + 1], scale=fw[:, 1, cc:cc + 1])
            STT(S, "dve", [pv, fw, v_], [v_], v_[:, :], pv[:, 0:SEG], fw[:, 0, cc:cc + 1], v_[:, :], ALU.mult, ALU.add)
            STT(S, "dve", [pv, fw, v_], [v_], v_[:, :], pv[:, 2:SEG + 2], fw[:, 2, cc:cc + 1], v_[:, :], ALU.mult, ALU.add)
            ACT(S, [g_], [t_], t_[:, :], g_[:, :], AF.Silu)
            TT(S, "pool", [t_, v_], [actT], actT[:, c, :], t_[:, :], v_[:, :], ALU.mult)
        for j in range(SEG // 128):
            p0, p1 = S.bank("d"), S.bank("d")
            for hlf, pb in enumerate((p0, p1)):
                MM(S, [actT, w_dn], [pb], [(pb[:, :], actT[:, c, 128 * j:128 * (j + 1)], w_dn[:, c, hlf * 512:(hlf + 1) * 512],
                                              c == 0, c == 21) for c in range(22)])
            ob = xts[j]
            resid_ln(k, xts[j], (p0, p1), ob, rsm)
            db = S.dbuf(("xout", l, (r0 + 128 * j) // 128))
            S.dma("pool", rows_ap(k, dst, "out", r0 + 128 * j, 128), ob[:, :], reads=[ob], writes=[db])
            if dst is None:
                k.final_bufs.append(db)

    cur_cond = None
    prep(0)
    for si in range(len(segs)):
        if si + 1 < len(segs):
            prep(si + 1)
        if segs[si][1] != cur_cond:
            cur_cond = segs[si][1]
            gate_table(k, l, 5, cur_cond)
        ffn(si)
    S.release(m)


def layer(k, l):
    cfg = k.cfg
    load_layer_consts(k, l)
    mode = getattr(cfg, "mode", "full")
    if mode in ("full", "A"):
        phaseA(k, l)
    if mode in ("full", "B"):
        phaseB(k, l)


def shard_inputs(inp, cfg, core):
    f = lambda a: np.ascontiguousarray(np.asarray(a), dtype=np.float32)
    NP = cfg.NP
    cst, _, cst2, _ = make_consts()
    m = {
        "xs": f(inp["x_sample"][core]),
        "xp": f(inp["x_prompt"][NP * core:NP * (core + 1)]).reshape(NP * cfg.Tp, D),
        "st_c": f(inp["state_mlstm_c"][core]), "st_n": f(inp["state_mlstm_n"][core]),
        "st_m": f(inp["state_mlstm_m"][core]).reshape(2, 8), "st_s": f(inp["state_ssd"][core]),
        "cond": f(np.stack([np.asarray(inp["c"])[core], np.asarray(inp["c_ctx"])], 0)),
        "w_ada": f(inp["w_ada"]), "b_ada": f(inp["b_ada"]), "w_in": f(inp["w_in"]),
        "b_ig": f(inp["b_igate"]).reshape(2, 8), "b_fg": f(inp["b_fgate"]).reshape(2, 8),
        "mnorm_g": f(inp["mlstm_norm_g"]), "w_pool": f(inp["w_pool"]), "pool_scale": f(inp["pool_scale"]),
        "w_sp": f(inp["w_spatial"]), "b_sp": f(inp["b_spatial"]), "sconv_w": f(inp["ssd_conv_w"]),
        "sconv_b": f(inp["ssd_conv_b"]), "dt_bias": f(inp["ssd_dt_bias"]).reshape(2, 8),
        "a_log": f(inp["ssd_a_log"]).reshape(2, 8), "ssd_d": f(inp["ssd_d"]), "snorm_g": f(inp["ssd_norm_g"]),
        "w_out": f(inp["w_out"]), "ln1_g": f(inp["ln1_g"]), "ln1_b": f(inp["ln1_b"]), "w_up": f(inp["ffn_w_up"]),
        "fconv_w": f(inp["ffn_conv_w"]), "fconv_b": f(inp["ffn_conv_b"]), "w_dn": f(inp["ffn_w_down"]),
        "ln2_g": f(inp["ln2_g"]), "ln2_b": f(inp["ln2_b"]), "cst": cst, "cst2": cst2,
    }
    return m


def phaseA(k, l):
    S, I, cfg = k.S, k.I, k.cfg
    S.cp = SCHED_CP
    k.hT_bank_group = 0
    m = S.mark()
    w_in = S.sb("w_in", [128, 8, DIN], BF16)
    w_out = S.sb("w_out", [128, 8, D], BF16)
    m2 = S.mark()
    stage = [S.sb(f"wstageA{j}", [128, 2840]) for j in range(6)]
    k.wl_n = 0
    load_weight(k, w_in, I["w_in"][l], 8, DIN, stage)
    load_weight(k, w_out, I["w_out"][l], 8, D, stage)
    S.release(m2)
    c2b = load_cst2(k)
    S.dma("sp", k.lng[:, :], I["ln1_g"][l:l + 1, :].partition_broadcast(128), writes=[k.lng])
    S.dma("sp", k.lnb[:, :], I["ln1_b"][l:l + 1, :].partition_broadcast(128), writes=[k.lnb])
    alloc_hsm(k)
    rsm = alloc_rsm(k)
    a = K()
    a.w_in, a.w_out, a.c2b, a.rsm, a.l = w_in, w_out, c2b, rsm, l
    a.hb = [S.sb(f"hb{j}", [128, 8, 130], BF16) for j in range(3)]
    a.xq = [S.sb(f"xq{j}", [128, D]) for j in range(2)]
    a.pet = [S.sb(f"petile{j}", [128, D]) for j in range(2)] if l == 0 else None
    if l == 0:
        edb = S.dbuf("ED")
        for j in range(2):
            S.dma("sp", a.pet[j][0:64, 512:1024], k.ED[:, :], reads=[edb], writes=[a.pet[j]])
            S.dma("sp", a.pet[j][64:128, 512:1024], k.ED[:, :], reads=[edb], writes=[a.pet[j]])
    sb = S.sb
    a.Cn, a.Cnb = sb("Cn", [128, 2, 65]), sb("Cnb", [128, 2, 66], BF16)
    a.Hs, a.Hsb = sb("Hs", [128, 4, 64]), sb("Hsb", [128, 4, 64], BF16)
    a.p = []
    for par in range(2):
        q = K()
        a.p.append(q)
        q.qkT = sb(f"qkT{par}", [128, 4, 128], BF16)
        q.k_tm = sb(f"k_tm{par}", [128, 256], BF16)
        q.v_sb = sb(f"v_sb{par}", [128, 256], BF16)
        q.XBCb = sb(f"XBCb{par}", [128, 6, 128], BF16)
        q.x_tm, q.B_tm = sb(f"x_tm{par}", [128, 256], BF16), sb(f"B_tm{par}", [128, 2, 128], BF16)
        q.stash_bufs = [q.qkT, q.k_tm, q.v_sb, q.XBCb, q.x_tm, q.B_tm]
        e0 = q.qkT.off // 2
        q.stash_ap = S.arena.bitcast(BF16)[0:128, e0:e0 + 2304]
        assert q.B_tm.off + 512 == q.qkT.off + 4608, "stash group must be contiguous"
        q.og = sb(f"og{par}", [128, 280])
        q.G8, q.E8, q.SP8, q.igb = sb(f"G8{par}", [128, 8]), sb(f"E8{par}", [128, 8]), sb(f"SP8{par}", [128, 8]), sb(f"igb{par}", [128, 4])
        q.r8, q.logdec, q.cum, q.e8 = sb(f"r8{par}", [128, 8]), sb(f"logdec{par}", [128, 8]), sb(f"cum{par}", [128, 8]), sb(f"e8{par}", [128, 8])
        q.wend, q.aL, q.tmp8 = sb(f"wend{par}", [128, 8]), sb(f"aL{par}", [128, 8]), sb(f"tmp8{par}", [128, 8])
        q.L1 = sb(f"L1{par}", [128, 8, 128])
        q.DIFF = sb(f"DIFF{par}", [128, 8, 128])
        q.PTm = sb(f"PTm{par}", [128, 4, 128], BF16)
        q.PTs = sb(f"PTs{par}", [128, 4, 128], BF16)
        q.xt_m, q.xh_m = sb(f"xt_m{par}", [128, 4, 66], BF16), sb(f"xh_m{par}", [128, 4, 66], BF16)
        q.XBC, q.XBCe = sb(f"XBC{par}", [128, 6, 128]), sb(f"XBCe{par}", [128, 6, 128])
        q.xt_s, q.xh_s = sb(f"xt_s{par}", [128, 4, 64], BF16), sb(f"xh_s{par}", [128, 4, 64], BF16)
        q.NUM = sb(f"NUM{par}", [128, 4, 65])
        q.den = sb(f"den{par}", [128, 4])
        q.Ysc = sb(f"Ysc{par}", [128, 4, 64])
    a.HY = [sb(f"HY{j}", [128, 512]) for j in range(2)]
    a.HYf = [sb(f"HYf{j}", [128, 512]) for j in range(2)]
    a.eo, a.z_sb, a.ez, a.gu = sb("eo", [128, 256]), sb("z_sb", [128, 256]), sb("ez", [128, 256]), sb("gu", [128, 256])
    a.gvb = sb("gvb", [128, 256], BF16)
    a.pc, a.pcP, a.pcN = sb("pc", [128, 256]), sb("pcP", [8, 256]), sb("pcN", [8, 256])
    a.plb, a.plT = sb("plb", [128, 256], BF16), sb("plT", [128, 2, 128], BF16)
    a.fin1, a.fin2, a.fin3 = sb("fin1", [128, 256]), sb("fin2", [128, 256]), sb("fin3", [128, 256])
    a.st4, a.st4b = sb("st4", [128, 4]), sb("st4b", [128, 4])
    a.yall = sb("yall", [128, 3, 256], BF16)
    a.concatT = sb("concatT", [128, 8, 128], BF16)
    a.gst, a.gmv, a.gve, a.grs = sb("gst", [128, 6]), sb("gmv", [128, 2]), sb("gve", [128, 1]), sb("grs", [128, 1])
    a.mrun = sb("mrun", [4, 1])
    a.mt = sb("mt", [4, 2])
    for par in range(2):
        a.p[par].dec = sb(f"dec{par}", [128, 8])
    a.sio = sb("sio", [128, 4, 128])
    src, dst = seq_src_dst(k, l, "A")
    a.src, a.dst = src, dst
    cur_cond = None
    for si, (sname, T, cond, off) in enumerate(cfg.seqs):
        if cond != cur_cond:
            gate_table(k, l, 2, cond)
            cur_cond = cond
        runseq(k, a, si, T, cond, off)
    S.release(m)


def runseq(k, a, si, T, cond, off):
    S, cfg, l = k.S, k.cfg, a.l
    nt = T // 128
    is_sample = (si == 0)
    tile0 = off // 128
    w_in = a.w_in

    def hbuf(i):
        return a.hb[i % 3]

    def fix_halo(lo, hi):
        CP(S, "pool", [hbuf(hi)], [hbuf(lo)], hbuf(lo)[:, :, 129:130], hbuf(hi)[:, :, 1:2])
        CP(S, "pool", [hbuf(lo)], [hbuf(hi)], hbuf(hi)[:, :, 0:1], hbuf(lo)[:, :, 128:129])

    def ensure1(i):
        hb = hbuf(i)
        xb = a.xq[i % 2]
        pos = None
        if l == 0 and is_sample:
            pos = a.pet[i % 2]
            edb = S.dbuf("ED")
            S.dma("sp", pos[0:64, 0:512], k.ED[2 * i:2 * i + 1, :].partition_broadcast(64), reads=[edb], writes=[pos])
            S.dma("sp", pos[64:128, 0:512], k.ED[2 * i + 1:2 * i + 2, :].partition_broadcast(64), reads=[edb], writes=[pos])
        make_hT(k, l, 1, cond, [(rows_ap(k, a.src, "in", off + 128 * i, 128), 128)], xb,
                [hb[:, kc, 1:129] for kc in range(8)], hb, [], pos_tile=pos)
        db = S.dbuf(("HT", tile0 + i))
        S.dma("pool", k.HT[tile0 + i].rearrange("p (kc t) -> p kc t", kc=8), hb[:, :, 1:129], reads=[hb], writes=[db])
        if i == 0:
            MSET(S, "pool", [hb], hb[:, :, 0:1], 0.0)
        else:
            fix_halo(i - 1, i)
        if i == nt - 1:
            MSET(S, "pool", [hb], hb[:, :, 129:130], 0.0)

    def ensure2(i):
        hb = hbuf(i)
        db = S.dbuf(("HT", tile0 + i))
        S.dma("sp", hb[:, :, 1:129], k.HT[tile0 + i].rearrange("p (kc t) -> p kc t", kc=8), reads=[db], writes=[hb])
        xb = a.xq[i % 2]
        S.dma("sp", xb[:, :], rows_ap(k, a.src, "in", off + 128 * i, 128), writes=[xb])
        if l == 0 and is_sample:
            pos = a.pet[i % 2]
            edb = S.dbuf("ED")
            S.dma("sp", pos[0:64, 0:512], k.ED[2 * i:2 * i + 1, :].partition_broadcast(64), reads=[edb], writes=[pos])
            S.dma("sp", pos[64:128, 0:512], k.ED[2 * i + 1:2 * i + 2, :].partition_broadcast(64), reads=[edb], writes=[pos])
            TT(S, "pool", [xb, pos], [xb], xb[:, :], xb[:, :], pos[:, :], ALU.add)
        hf = a.HYf[i % 2]
        S.dma("sp", hf[:, :], k.HF[off + 128 * i:off + 128 * (i + 1), :], reads=[S.dbuf(("HF", tile0 + i))], writes=[hf])
        q = a.p[i % 2]
        S.dma("sp", q.stash_ap, k.STB[tile0 + i], reads=[S.dbuf(("STB", tile0 + i))], writes=q.stash_bufs)
        S.dma("sp", q.og[:, :], k.STG[tile0 + i], reads=[S.dbuf(("STG", tile0 + i))], writes=[q.og])
        if i == nt - 1:
            MSET(S, "pool", [hb], hb[:, :, 129:130], 0.0)
        else:
            fix_halo(i, i + 1)
        if i == 0:
            MSET(S, "pool", [hb], hb[:, :, 0:1], 0.0)

    for d in ((0,) if getattr(cfg, "stop", 99) <= 4 else (0, 1)):
        init_state(k, a, si, d, is_sample)
        order = list(range(nt)) if d == 0 else list(range(nt - 1, -1, -1))
        ens = ensure1 if d == 0 else ensure2
        ens(order[0])
        for n, i in enumerate(order):
            if n + 1 < len(order):
                ens(order[n + 1])
            tileA(k, a, si, T, cond, off, i, d, nt, is_sample)
        if not is_sample and getattr(cfg, "stop", 99) > 5:
            final_state(k, a, si, d)


def init_state(k, a, si, d, is_sample):
    S, I, l = k.S, k.I, a.l
    if not is_sample:
        MSET(S, "pool", [a.Cn], a.Cn[:, :, :], 0.0)
        MSET(S, "pool", [a.Cnb], a.Cnb[:, :, :], 0.0)
        MSET(S, "pool", [a.Hs], a.Hs[:, :, :], 0.0)
        MSET(S, "pool", [a.Hsb], a.Hsb[:, :, :], 0.0)
        MSET(S, "pool", [a.mrun], a.mrun[:, :], 0.0)
        return
    for h in range(4):
        pr = slice((h % 2) * 64, (h % 2) * 64 + 64)
        S.dma("sp", a.Cn[pr, h // 2, 0:64], I["st_c"][l, d, h], writes=[a.Cn])
        S.dma("sp", a.Cn[pr, h // 2, 64:65], I["st_n"][l, d, h].rearrange("(p o) -> p o", o=1), writes=[a.Cn])
    S.dma("sp", a.st4[:, :], I["st_m"][l:l + 1, 4 * d:4 * d + 4].partition_broadcast(128), writes=[a.st4])
    ACT(S, [a.st4], [a.st4b], a.st4b[:, :], a.st4[:, :], AF.Exp)
    for h in range(4):
        pr = slice((h % 2) * 64, (h % 2) * 64 + 64)
        TS(S, "dve", [a.Cn, a.st4b], [a.Cn], a.Cn[pr, h // 2, :], a.Cn[pr, h // 2, :], a.st4b[pr, h:h + 1], None, ALU.mult)
    CP(S, "pool", [a.Cn], [a.Cnb], a.Cnb[:, :, 0:65], a.Cn[:, :, :])
    S.dma("sp", a.sio[0:64, :, :], I["st_s"][l, d].rearrange("h p n -> p h n"), writes=[a.sio])
    pb = S.bank()
    TR(S, [a.sio, k.cstb], [pb], [(pb[:, h * 64:(h + 1) * 64], a.sio[0:64, h, :], cview(k, "ident")[0:64, 0:64]) for h in range(4)])
    CP(S, "dve", [pb], [a.Hs], a.Hs[:, :, :], pb[:, 0:256].rearrange("p (h q) -> p h q", h=4))
    CP(S, "act", [pb], [a.Hsb], a.Hsb[:, :, :], pb[:, 0:256].rearrange("p (h q) -> p h q", h=4))


def final_state(k, a, si, d):
    S, O, l = k.S, k.O, a.l
    j = si - 1
    dg = a.p[0].tmp8
    TS(S, "dve", [k.cstb, a.mrun], [dg], dg[0:4, 0:4], cview(k, "ident")[0:4, 0:4], a.mrun[0:4, 0:1], None, ALU.mult)
    pb = S.bank()
    MM(S, [dg, k.cstb], [pb], [(pb[:, 0:4], cview(k, "ones")[0:4, :], dg[0:4, 0:4], True, True)])
    ACT(S, [pb], [a.st4b], a.st4b[:, :], pb[:, 0:4], AF.Exp, scale=-1.0)
    stg = a.sio
    sv = stg[:, 0:2, 0:65]
    for h in range(4):
        pr = slice((h % 2) * 64, (h % 2) * 64 + 64)
        TS(S, "dve", [a.Cn, a.st4b], [stg], stg[pr, h // 2, 0:65], a.Cn[pr, h // 2, :], a.st4b[pr, h:h + 1], None, ALU.mult)
    outs = []
    for h in range(4):
        pr = slice((h % 2) * 64, (h % 2) * 64 + 64)
        db = S.dbuf(("oc", j, l, d, h))
        S.dma("pool", O["oc"][j, l, d, h], stg[pr, h // 2, 0:64], reads=[stg], writes=[db])
        db2 = S.dbuf(("on", j, l, d, h))
        S.dma("pool", O["on"][j, l, d, h].rearrange("(p o) -> p o", o=1), stg[pr, h // 2, 64:65], reads=[stg], writes=[db2])
        outs += [db, db2]
    db = S.dbuf(("om", j, l, d))
    S.dma("pool", O["om"][j, l, 4 * d:4 * d + 4].rearrange("(p o) -> p o", o=1), a.mrun[0:4, 0:1], reads=[a.mrun], writes=[db])
    outs.append(db)
    pb2 = S.bank()
    TR(S, [a.Hs, k.cstb], [pb2], [(pb2[0:64, h * 128:(h + 1) * 128], a.Hs[:, h, :], cview(k, "ident")) for h in range(4)])
    CP(S, "dve", [pb2, stg], [stg], stg[0:64, :, :], pb2[0:64, :].rearrange("p (h n) -> p h n", h=4))
    db = S.dbuf(("os", j, l, d))
    S.dma("pool", O["os"][j, l, d].rearrange("h p n -> p h n"), stg[0:64, :, :], reads=[stg], writes=[db])
    outs.append(db)
    k.final_bufs += outs


def tileA(k, a, si, T, cond, off, i, d, nt, is_sample):
    S, l = k.S, a.l
    PS = a.p[i % 2]
    w_in = a.w_in
    hb = a.hb[i % 3]
    hcur = lambda kc: hb[:, kc, 1:129]
    tri = cview(k, "tri%d" % d)
    neg = cview(k, "neg%d" % d)
    endc = 127 if d == 0 else 0
    full = (d == 1)
    cst = k.cstb

    og = PS.og
    ps1 = ps2 = None
    if not full:
        ps1, ps2 = S.bank(0), S.bank(0)
        MM(S, [hb, w_in], [ps1], [(ps1[:, 0:512], hcur(kc), w_in[:, kc, 256:768], kc == 0, kc == 7) for kc in range(8)])
        MM(S, [hb, w_in], [ps2], [(ps2[:, 0:272], hcur(kc), w_in[:, kc, 768:1040], kc == 0, kc == 7) for kc in range(8)]
           + [(ps2[:, 272:280], hcur(kc), w_in[:, kc, 2832:2840], kc == 0, kc == 7) for kc in range(8)])
        CP(S, "act", [ps2], [og], og[:, :], ps2[:, 0:280])
        ACT(S, [ps1], [PS.k_tm], PS.k_tm[:, :], ps1[:, 0:256], AF.Identity, scale=0.125)
        CP(S, "act", [ps1], [PS.v_sb], PS.v_sb[:, :], ps1[:, 256:512])
    G8, E8, SP8, igb, r8, logdec, cum, e8, wend, aL, tmp8 = (PS.G8, PS.E8, PS.SP8, PS.igb, PS.r8, PS.logdec, PS.cum, PS.e8,
                                                             PS.wend, PS.aL, PS.tmp8)
    STT(S, "dve", [og, k.bif], [G8], G8[:, 0:4], og[:, 264 + 4 * d:268 + 4 * d], -1.0, k.bif[:, 8 + 4 * d:12 + 4 * d], ALU.mult, ALU.add)
    TT(S, "dve", [og, k.dtb], [G8], G8[:, 4:8], og[:, 272 + 4 * d:276 + 4 * d], k.dtb[:, 4 * d:4 * d + 4], ALU.add)
    TT(S, "dve", [og, k.bif], [igb], igb[:, :], og[:, 256 + 4 * d:260 + 4 * d], k.bif[:, 4 * d:4 * d + 4], ALU.add)
    if full:
        ACT(S, [og], [a.eo], a.eo[:, :], og[:, 0:256], AF.Exp, scale=-1.0)
    ACT(S, [G8], [E8], E8[:, :], G8[:, :], AF.Exp)
    ACT(S, [E8], [SP8], SP8[:, :], E8[:, :], AF.Ln, bias=1.0)
    ACT(S, [igb], [r8], r8[:, 0:4], igb[:, :], AF.Exp)
    CP(S, "pool", [SP8], [r8], r8[:, 4:8], SP8[:, 4:8])
    TT(S, "dve", [SP8, k.coef], [logdec], logdec[:, :], SP8[:, :], k.coef[:, d, :], ALU.mult)
    CP(S, "act", [logdec], [PS.L1], PS.L1[:, :, :], bc(logdec[:, 0:8].unsqueeze(2), [128, 8, 128]))
    psL = [S.bank(1), S.bank(1)]
    for hh in range(2):
        MM(S, [PS.L1, cst], [psL[hh]], [(psL[hh][:, q * 128:(q + 1) * 128], PS.L1[:, hh * 4 + q, :], tri, True, True) for q in range(4)])
    psC = S.bank(1)
    MM(S, [logdec, cst], [psC], [(psC[:, 0:8], tri, logdec[:, 0:8], True, True)])
    CP(S, "dve", [psC], [cum], cum[:, :], psC[:, 0:8])
    for h in range(8):
        pl = psL[h // 4]
        q = h % 4
        STT(S, "dve", [pl, cum, cst], [PS.DIFF], PS.DIFF[:, h, :], pl[:, q * 128:(q + 1) * 128], cum[:, h:h + 1], neg, ALU.subtract, ALU.add)
    ACT(S, [PS.DIFF], [PS.DIFF], PS.DIFF[:, :, :], PS.DIFF[:, :, :], AF.Exp)
    ACT(S, [cum], [e8], e8[:, :], cum[:, :], AF.Exp)
    for hh in range(2):
        TT(S, "dve", [psL[hh], cum], [tmp8], tmp8[:, hh * 4:hh * 4 + 4], psL[hh][:, endc:512:128], cum[:, hh * 4:hh * 4 + 4], ALU.subtract)
        ACT(S, [psL[hh]], [aL], aL[:, hh * 4:hh * 4 + 4], psL[hh][:, endc:512:128], AF.Exp)
    ACT(S, [tmp8], [wend], wend[:, :], tmp8[:, :], AF.Exp)
    TT(S, "dve", [wend, r8], [wend], wend[:, :], wend[:, :], r8[:, :], ALU.mult)
    if not is_sample:
        TT(S, "dve", [tmp8, igb], [PS.dec], PS.dec[:, 0:4], tmp8[:, 0:4], igb[:, :], ALU.add)
        TT(S, "dve", [tmp8, cum], [PS.dec], PS.dec[:, 4:8], tmp8[:, 0:4], cum[:, 0:4], ALU.add)
        pm = S.bank(1)
        TR(S, [PS.dec, cst], [pm], [(pm[0:4, 0:128], PS.dec[:, 0:4], cview(k, "ident")),
                                   (pm[0:4, 128:256], PS.dec[:, 4:8], cview(k, "ident"))])
        S.op("dve", lambda e: e.tensor_reduce(a.mt[0:4, 0:1], pm[0:4, 0:128], AX.X, ALU.max), [pm], [a.mt])
        TT(S, "dve", [pm, a.mrun], [a.mt], a.mt[0:4, 1:2], pm[0:4, 128:129], a.mrun[0:4, 0:1], ALU.add)
        TT(S, "dve", [a.mt], [a.mrun], a.mrun[0:4, 0:1], a.mt[0:4, 0:1], a.mt[0:4, 1:2], ALU.max)

    if getattr(k.cfg, "stop", 99) <= 1:
        return
    qkT = PS.qkT
    if not full:
        psQ = [S.bank(0), S.bank(0)]
        for hh in range(2):
            MM(S, [hb, w_in], [psQ[hh]], [(psQ[hh][:, q * 130:(q + 1) * 130], w_in[:, kc, (hh * 2 + q) * 128:(hh * 2 + q + 1) * 128],
                                            hb[:, kc, 0:130], kc == 0, kc == 7) for q in range(2) for kc in range(8)])
        CP(S, "act", [psQ[0]], [qkT], qkT[:, 0:2, :], psQ[0][:, 0:260].rearrange("p (b t) -> p b t", b=2)[:, :, 1:129])
        ACT(S, [psQ[1]], [qkT], qkT[:, 2:4, :], psQ[1][:, 0:260].rearrange("p (b t) -> p b t", b=2)[:, :, 1:129], AF.Identity, scale=0.125)
    v4 = PS.v_sb[:, :].rearrange("p (h e) -> p h e", h=4)
    TT(S, "dve", [PS.v_sb, r8], [PS.xt_m], PS.xt_m[:, :, 0:64], v4, bc(r8[:, 0:4].unsqueeze(2), [128, 4, 64]), ALU.mult)
    if getattr(k.cfg, "stop", 99) <= 1.12:
        return
    CP(S, "pool", [r8], [PS.xt_m], PS.xt_m[:, :, 64:65], r8[:, 0:4].unsqueeze(2))
    if getattr(k.cfg, "stop", 99) <= 1.15:
        return
    EXP = getattr(k.cfg, "exp", "")
    if EXP != "noTT":
        TT(S, "dve", [PS.v_sb, wend], [PS.xh_m], PS.xh_m[:, :, 0:64], v4, bc((r8 if EXP == "r8" else wend)[:, 0:4].unsqueeze(2), [128, 4, 64]), ALU.mult)
    if EXP != "noCP":
        CP(S, "pool", [wend], [PS.xh_m], PS.xh_m[:, :, 64:65], wend[:, 0:4].unsqueeze(2))
    if getattr(k.cfg, "stop", 99) <= 1.2:
        return
    psS = [S.bank(1), S.bank(1)]
    hp = lambda h: slice((h % 2) * 64, (h % 2) * 64 + 64)
    for par in range(2):
        MM(S, [qkT], [psS[par]], [(psS[par][:, (h // 2) * 128:(h // 2 + 1) * 128], qkT[hp(h), 2 + h // 2, :], qkT[hp(h), h // 2, :], True, True)
                                  for h in (par, par + 2)])
    for par in range(2):
        TT(S, "dve", [psS[par], PS.DIFF], [PS.PTm], PS.PTm[:, par:4:2, :], psS[par][:, 0:256].rearrange("p (h t) -> p h t", h=2),
           PS.DIFF[:, par:4:2, :], ALU.mult)
    if getattr(k.cfg, "stop", 99) <= 1.4:
        return
    psO = S.bank(1)
    psI = [S.bank(1), S.bank(1)]
    MM(S, [PS.PTm, PS.xt_m], [psO], [(psO[:, h * 65:h * 65 + 65], PS.PTm[:, h, :], PS.xt_m[:, h, 0:65], True, True) for h in range(4)])
    for par in range(2):
        MM(S, [qkT, a.Cnb], [psI[par]], [(psI[par][:, (h // 2) * 65:(h // 2) * 65 + 65], qkT[hp(h), h // 2, :], a.Cnb[hp(h), h // 2, 0:65], True, True)
                                         for h in (par, par + 2)])
    NUM = PS.NUM
    for par in range(2):
        TT(S, "dve", [psI[par], e8], [NUM], NUM[:, par:4:2, :], psI[par][:, 0:130].rearrange("p (h e) -> p h e", h=2),
           bc(e8[:, par:4:2].unsqueeze(2), [128, 2, 65]), ALU.mult)
    TT(S, "dve", [psO, NUM], [NUM], NUM[:, :, :], psO[:, 0:260].rearrange("p (h e) -> p h e", h=4), NUM[:, :, :], ALU.add)
    if getattr(k.cfg, "stop", 99) <= 1.6:
        return
    HY = a.HY[i % 2]
    ACT(S, [NUM], [PS.den], PS.den[:, :].unsqueeze(2), NUM[:, :, 64:65], AF.Abs)
    TS(S, "dve", [PS.den], [PS.den], PS.den[:, :], PS.den[:, :], 1.0, None, ALU.max)
    TT(S, "pool", [PS.den, k.cm1], [PS.den], PS.den[:, :], PS.den[:, :], bc(k.cm1[:, 0:1], [128, 4]), ALU.pow)
    TT(S, "pool", [NUM, PS.den], [HY], HY[:, 0:256].rearrange("p (h e) -> p h e", h=4), NUM[:, :, 0:64],
       bc(PS.den[:, :].unsqueeze(2), [128, 4, 64]), ALU.mult)
    if getattr(k.cfg, "stop", 99) <= 1.8:
        return
    psU = S.bank(1)
    MM(S, [PS.k_tm, PS.xh_m], [psU], [(psU[:, h * 65:h * 65 + 65], PS.k_tm[:, (h // 2) * 128:(h // 2 + 1) * 128], PS.xh_m[:, h, 0:65], True, True)
                                     for h in range(4)])
    for h in range(4):
        STT(S, "dve", [a.Cn, aL, psU], [a.Cn], a.Cn[hp(h), h // 2, :], a.Cn[hp(h), h // 2, :], aL[hp(h), h:h + 1],
            psU[hp(h), h * 65:h * 65 + 65], ALU.mult, ALU.add)
    CP(S, "act", [a.Cn], [a.Cnb], a.Cnb[:, :, 0:65], a.Cn[:, :, :])

    if getattr(k.cfg, "stop", 99) <= 2:
        return
    XBC, XBCe, XBCb = PS.XBC, PS.XBCe, PS.XBCb
    if not full:
        psX = [S.bank(0), S.bank(0)]
        for hh in range(2):
            MM(S, [hb, w_in], [psX[hh]], [(psX[hh][:, q * 130:q * 130 + 130], w_in[:, kc, 2064 + (hh * 3 + q) * 128:2064 + (hh * 3 + q + 1) * 128],
                                            hb[:, kc, 0:130], kc == 0, kc == 7) for q in range(3) for kc in range(8)])
        for b in range(6):
            pb, c0 = psX[b // 3], (b % 3) * 130
            ACT(S, [pb, k.scw], [XBC], XBC[:, b, :], pb[:, c0 + 1:c0 + 129], AF.Identity, bias=k.scw[:, 3, b:b + 1], scale=k.scw[:, 1, b:b + 1])
            STT(S, "dve", [pb, k.scw, XBC], [XBC], XBC[:, b, :], pb[:, c0:c0 + 128], k.scw[:, 0, b:b + 1], XBC[:, b, :], ALU.mult, ALU.add)
            STT(S, "dve", [pb, k.scw, XBC], [XBC], XBC[:, b, :], pb[:, c0 + 2:c0 + 130], k.scw[:, 2, b:b + 1], XBC[:, b, :], ALU.mult, ALU.add)
        ACT(S, [XBC], [XBCe], XBCe[:, :, :], XBC[:, :, :], AF.Exp, scale=-1.0)
        ACT(S, [XBCe], [XBCe], XBCe[:, :, :], XBCe[:, :, :], AF.Ln, bias=1.0)
        ACT(S, [XBCe], [XBCe], XBCe[:, :, :], XBCe[:, :, :], AF.Exp, scale=-1.0)
        TT(S, "pool", [XBC, XBCe], [XBCb], XBCb[:, :, :], XBC[:, :, :], XBCe[:, :, :], ALU.mult)
        psT = S.bank(0)
        pTv = bview(psT, BF16)
        TR(S, [XBCb, k.identb], [psT], [(pTv[:, b * 128:(b + 1) * 128], XBCb[:, b, :], k.identb[:, :]) for b in range(4)])
        CP(S, "act", [psT], [PS.x_tm], PS.x_tm[:, :], pTv[:, 0:256])
        CP(S, "act", [psT], [PS.B_tm], PS.B_tm[:, :, :], pTv[:, 256:512].rearrange("p (g n) -> p g n", g=2))
    x4 = PS.x_tm[:, :].rearrange("p (h e) -> p h e", h=4)
    TT(S, "pool", [PS.x_tm, r8], [PS.xt_s], PS.xt_s[:, :, :], x4, bc(r8[:, 4:8].unsqueeze(2), [128, 4, 64]), ALU.mult)
    TT(S, "pool", [PS.x_tm, wend], [PS.xh_s], PS.xh_s[:, :, :], x4, bc(wend[:, 4:8].unsqueeze(2), [128, 4, 64]), ALU.mult)
    psS2 = S.bank(1)
    MM(S, [XBCb], [psS2], [(psS2[:, g * 128:(g + 1) * 128], XBCb[:, 2 + g, :], XBCb[:, 4 + g, :], True, True) for g in range(2)])
    for g in range(2):
        TT(S, "dve", [psS2, PS.DIFF], [PS.PTs], PS.PTs[:, 2 * g:2 * g + 2, :],
           bc(psS2[:, g * 128:(g + 1) * 128].unsqueeze(1), [128, 2, 128]), PS.DIFF[:, 4 + 2 * g:6 + 2 * g, :], ALU.mult)
    psY = S.bank(1)
    MM(S, [PS.PTs, PS.xt_s, XBCb, a.Hsb], [psY],
       [(psY[:, h * 64:(h + 1) * 64], PS.PTs[:, h, :], PS.xt_s[:, h, :], True, True) for h in range(4)]
       + [(psY[:, 256 + g * 128:256 + (g + 1) * 128], XBCb[:, 4 + g, :], a.Hsb[:, 2 * g:2 * g + 2, :].rearrange("p h e -> p (h e)"), True, True)
          for g in range(2)])
    Ysc = PS.Ysc
    TT(S, "dve", [psY, e8], [Ysc], Ysc[:, :, :], psY[:, 256:512].rearrange("p (h e) -> p h e", h=4),
       bc(e8[:, 4:8].unsqueeze(2), [128, 4, 64]), ALU.mult)
    TT(S, "dve", [psY, Ysc], [HY], HY[:, 256:512], psY[:, 0:256], Ysc[:, :, :].rearrange("p h e -> p (h e)"), ALU.add)
    psU2 = S.bank(1)
    MM(S, [PS.B_tm, PS.xh_s], [psU2], [(psU2[:, g * 128:(g + 1) * 128], PS.B_tm[:, g, :], PS.xh_s[:, 2 * g:2 * g + 2, :].rearrange("p h e -> p (h e)"),
                                       True, True) for g in range(2)])
    TT(S, "dve", [a.Hs, aL], [a.Hs], a.Hs[:, :, :], a.Hs[:, :, :], bc(aL[:, 4:8].unsqueeze(2), [128, 4, 64]), ALU.mult)
    TT(S, "dve", [a.Hs, psU2], [a.Hs], a.Hs[:, :, :], psU2[:, 0:256].rearrange("p (h e) -> p h e", h=4), a.Hs[:, :, :], ALU.add)
    CP(S, "act", [a.Hs], [a.Hsb], a.Hsb[:, :, :], a.Hs[:, :, :])

    if getattr(k.cfg, "stop", 99) <= 3:
        return
    tile_g = (off // 128) + i
    if not full:
        S.dma("pool", k.STB[tile_g], PS.stash_ap, reads=PS.stash_bufs, writes=[S.dbuf(("STB", tile_g))])
        S.dma("pool", k.STG[tile_g], og[:, :], reads=[og], writes=[S.dbuf(("STG", tile_g))])
        db = S.dbuf(("HF", tile_g))
        S.dma("pool", k.HF[off + 128 * i:off + 128 * (i + 1), :], HY[:, :], reads=[HY], writes=[db])
        return
    finalizeA(k, a, si, T, cond, off, i, nt, ps1, ps2, HY, PS)


def finalizeA(k, a, si, T, cond, off, i, nt, ps1, ps2, HY, pset):
    S, l = k.S, a.l
    w_in, w_out = a.w_in, a.w_out
    hb = a.hb[i % 3]
    hcur = lambda kc: hb[:, kc, 1:129]
    cst = k.cstb
    HYf = a.HYf[i % 2]
    f1, f2, f3 = a.fin1, a.fin2, a.fin3
    v4 = lambda ap: ap.rearrange("p (h e) -> p h e", h=4)
    ym, yg, ys = a.yall[:, 0, :], a.yall[:, 1, :], a.yall[:, 2, :]

    ps3, ps4 = S.bank(0), S.bank(0)
    MM(S, [hb, w_in], [ps3], [(ps3[:, 0:512], hcur(kc), w_in[:, kc, 1296:1808], kc == 0, kc == 7) for kc in range(8)])
    MM(S, [hb, w_in], [ps4], [(ps4[:, 0:256], hcur(kc), w_in[:, kc, 1808:2064], kc == 0, kc == 7) for kc in range(8)]
       + [(ps4[:, 256:512], hcur(kc), w_in[:, kc, 1040:1296], kc == 0, kc == 7) for kc in range(8)])
    has_p, has_n = i > 0, i < nt - 1
    ps5 = S.bank(0)
    mm5 = []
    if has_p:
        hp_ = a.hb[(i - 1) % 3]
        mm5 += [(ps5[0:8, 0:256], hp_[:, kc, 121:129], w_in[:, kc, 1040:1296], kc == 0, kc == 7) for kc in range(8)]
    if has_n:
        hn_ = a.hb[(i + 1) % 3]
        mm5 += [(ps5[0:8, 256:512], hn_[:, kc, 1:9], w_in[:, kc, 1040:1296], kc == 0, kc == 7) for kc in range(8)]
    if mm5:
        rd = [w_in] + ([a.hb[(i - 1) % 3]] if has_p else []) + ([a.hb[(i + 1) % 3]] if has_n else [])
        MM(S, rd, [ps5], mm5)

    TT(S, "pool", [HY, HYf], [f1], f1[:, :], HY[:, 0:256], HYf[:, 0:256], ALU.add)
    S.op("dve", lambda e: e.tensor_reduce(a.st4[:, :], v4(f1[:, :]), AX.X, ALU.add), [f1], [a.st4])
    TS(S, "dve", [a.st4], [a.st4], a.st4[:, :], a.st4[:, :], 1.0 / 64.0, None, ALU.mult)
    TT(S, "pool", [f1, a.st4], [f1], v4(f1[:, :]), v4(f1[:, :]), bc(a.st4[:, :].unsqueeze(2), [128, 4, 64]), ALU.subtract)
    TT(S, "pool", [f1], [f2], f2[:, :], f1[:, :], f1[:, :], ALU.mult)
    S.op("dve", lambda e: e.tensor_reduce(a.st4b[:, :], v4(f2[:, :]), AX.X, ALU.add), [f2], [a.st4b])
    TS(S, "dve", [a.st4b], [a.st4b], a.st4b[:, :], a.st4b[:, :], 1.0 / 64.0, EPS, ALU.mult, ALU.add)
    TT(S, "pool", [a.st4b, k.cm05], [a.st4b], a.st4b[:, :], a.st4b[:, :], bc(k.cm05[:, 0:1], [128, 4]), ALU.pow)
    TT(S, "pool", [f1, a.st4b], [f1], v4(f1[:, :]), v4(f1[:, :]), bc(a.st4b[:, :].unsqueeze(2), [128, 4, 64]), ALU.mult)
    TT(S, "pool", [f1, k.mng], [f1], f1[:, :], f1[:, :], k.mng[:, :], ALU.mult)
    ACT(S, [a.eo], [a.eo], a.eo[:, :], a.eo[:, :], AF.Ln, bias=1.0)
    ACT(S, [a.eo], [a.eo], a.eo[:, :], a.eo[:, :], AF.Exp, scale=-1.0)
    TT(S, "pool", [f1, a.eo], [a.yall], ym, f1[:, :], a.eo[:, :], ALU.mult)

    CP(S, "act", [ps4], [a.z_sb], a.z_sb[:, :], ps4[:, 0:256])
    ACT(S, [ps4], [a.ez], a.ez[:, :], ps4[:, 0:256], AF.Exp, scale=-1.0)
    TT(S, "pool", [HY, HYf], [f2], f2[:, :], HY[:, 256:512], HYf[:, 256:512], ALU.add)
    TT(S, "pool", [pset.x_tm, k.dsk], [f3], v4(f3[:, :]), v4(pset.x_tm[:, :]), bc(k.dsk[:, :].unsqueeze(2), [128, 4, 64]), ALU.mult)
    TT(S, "pool", [f2, f3], [f2], f2[:, :], f2[:, :], f3[:, :], ALU.add)
    ACT(S, [a.ez], [a.ez], a.ez[:, :], a.ez[:, :], AF.Ln, bias=1.0)
    ACT(S, [a.ez], [a.ez], a.ez[:, :], a.ez[:, :], AF.Exp, scale=-1.0)
    TT(S, "pool", [a.ez, a.z_sb], [a.ez], a.ez[:, :], a.ez[:, :], a.z_sb[:, :], ALU.mult)
    TT(S, "pool", [f2, a.ez], [f2], f2[:, :], f2[:, :], a.ez[:, :], ALU.mult)
    TT(S, "pool", [f2], [f3], f3[:, :], f2[:, :], f2[:, :], ALU.mult)
    S.op("dve", lambda e: e.tensor_reduce(a.st4[:, 0:2], f3[:, :].rearrange("p (g e) -> p g e", g=2), AX.X, ALU.add), [f3], [a.st4])
    TS(S, "dve", [a.st4], [a.st4], a.st4[:, 0:2], a.st4[:, 0:2], 1.0 / 128.0, EPS, ALU.mult, ALU.add)
    TT(S, "pool", [a.st4, k.cm05], [a.st4], a.st4[:, 0:2], a.st4[:, 0:2], bc(k.cm05[:, 0:1], [128, 2]), ALU.pow)
    TT(S, "pool", [f2, a.st4], [f2], f2[:, :].rearrange("p (g e) -> p g e", g=2), f2[:, :].rearrange("p (g e) -> p g e", g=2),
       bc(a.st4[:, 0:2].unsqueeze(2), [128, 2, 128]), ALU.mult)
    TT(S, "pool", [f2, k.sng], [a.yall], ys, f2[:, :], k.sng[:, :], ALU.mult)

    CP(S, "act", [ps3], [a.gu], a.gu[:, :], ps3[:, 0:256])
    S.op("dve", lambda e: e.bn_stats(a.gst[:, :], ps3[:, 256:512]), [ps3], [a.gst])
    S.op("dve", lambda e: e.bn_aggr(a.gmv[:, :], a.gst[:, :]), [a.gst], [a.gmv])
    TS(S, "dve", [a.gmv], [a.gve], a.gve[:, :], a.gmv[:, 1:2], EPS, None, ALU.add)
    TT(S, "pool", [a.gve, k.cm05], [a.grs], a.grs[:, :], a.gve[:, :], k.cm05[:, :], ALU.pow)
    TS(S, "dve", [ps3, a.gmv, a.grs], [a.gvb], a.gvb[:, :], ps3[:, 256:512], a.gmv[:, 0:1], a.grs[:, 0:1], ALU.subtract, ALU.mult)
    psG = S.bank(0)
    MM(S, [k.wsT, a.gvb], [psG], [(psG[:, h * 64:(h + 1) * 64], k.wsT[:, h, :], a.gvb[:, h * 64:(h + 1) * 64], True, True) for h in range(4)])
    TT(S, "dve", [psG, k.bsT], [f3], v4(f3[:, :]), v4(psG[:, 0:256]), bc(k.bsT[:, :].unsqueeze(2), [128, 4, 64]), ALU.add)
    TT(S, "pool", [f3, a.gu], [a.yall], yg, f3[:, :], a.gu[:, :], ALU.mult)

    CP(S, "act", [ps4], [a.pc], a.pc[:, :], ps4[:, 256:512])
    if has_p:
        CP(S, "act", [ps5], [a.pcP], a.pcP[:, :], ps5[0:8, 0:256])
    if has_n:
        CP(S, "act", [ps5], [a.pcN], a.pcN[:, :], ps5[0:8, 256:512])
    var = "int" if (has_p and has_n) else ("first" if has_n else ("last" if has_p else "int"))
    psP = S.bank(0)
    mmp = []
    for g in range(4):
        o_ = psP[:, g * 64:(g + 1) * 64]
        seqm = [(cview(k, f"pA{g}{var}"), a.pc[:, g * 64:(g + 1) * 64])]
        if has_p:
            seqm.append((cview(k, f"pP{g}", 8), a.pcP[0:8, g * 64:(g + 1) * 64]))
        if has_n:
            seqm.append((cview(k, f"pN{g}", 8), a.pcN[0:8, g * 64:(g + 1) * 64]))
        for n_, (lh, rh) in enumerate(seqm):
            mmp.append((o_, lh, rh, n_ == 0, n_ == len(seqm) - 1))
    MM(S, [a.c2b, a.pc, a.pcP, a.pcN], [psP], mmp)
    CP(S, "act", [psP], [a.plb], a.plb[:, :], psP[:, 0:256])
    psT2 = S.bank(0)
    t2v = bview(psT2, BF16)
    TR(S, [a.plb, k.identb], [psT2], [(t2v[:, j * 128:(j + 1) * 128], a.plb[:, j * 128:(j + 1) * 128], k.identb[:, :]) for j in range(2)])
    CP(S, "dve", [psT2], [a.plT], a.plT[:, :, :], t2v[:, 0:256].rearrange("p (j t) -> p j t", j=2))
    psW = S.bank(0)
    MM(S, [k.wpb, a.plT], [psW], [(psW[:, j * 128:(j + 1) * 128], k.wpb[:, j, :], a.plT[:, j, :], True, True) for j in range(2)])
    cT = a.concatT
    for j in range(2):
        ACT(S, [psW, k.psc], [cT], cT[:, 2 + j, :], psW[:, j * 128:(j + 1) * 128], AF.Identity, scale=k.psc[:, j:j + 1])

    psT3 = S.bank(0)
    t3v = bview(psT3, BF16)
    TR(S, [a.yall, k.identb], [psT3], [(t3v[:, (m3 * 2 + j) * 128:(m3 * 2 + j + 1) * 128], a.yall[:, m3, j * 128:(j + 1) * 128], k.identb[:, :])
                                      for m3 in range(3) for j in range(2)])
    CP(S, "dve", [psT3], [cT], cT[:, 0:2, :], t3v[:, 0:256].rearrange("p (j t) -> p j t", j=2))
    CP(S, "act", [psT3], [cT], cT[:, 4:8, :], t3v[:, 256:768].rearrange("p (j t) -> p j t", j=4))

    p0, p1 = S.bank(0), S.bank(0)
    for hlf, pb in enumerate((p0, p1)):
        MM(S, [cT, w_out], [pb], [(pb[:, :], cT[:, kc, :], w_out[:, kc, hlf * 512:(hlf + 1) * 512], kc == 0, kc == 7) for kc in range(8)])
    xb = a.xq[i % 2]
    resid_ln(k, xb, (p0, p1), xb, a.rsm)
    r0 = off + 128 * i
    db = S.dbuf(("xoutA", l, r0 // 128))
    S.dma("pool", rows_ap(k, a.dst, "out", r0, 128), xb[:, :], reads=[xb], writes=[db])
    if a.dst is None:
        k.final_bufs.append(db)


_CACHE = {}


def gather_outputs(results, cfg, n):
    NP, Tp, Ts = cfg.NP, cfg.Tp, cfg.Ts
    y_p = np.concatenate([r["yp"].reshape(NP, Tp, D) for r in results], 0)
    y_s = np.stack([r["ys"].reshape(Ts, D) for r in results], 0)
    oc = np.concatenate([r["oc"] for r in results], 0)
    on = np.concatenate([r["on"] for r in results], 0)
    om = np.concatenate([r["om"].reshape(NP, 2, 2, 4) for r in results], 0)
    os_ = np.concatenate([r["os"] for r in results], 0)
    f = lambda a: np.ascontiguousarray(a, dtype=np.float32)
    return (f(y_p), f(y_s), f(oc), f(on), f(om), f(os_))


def kernel(**inputs):
    n = 8
    xs = np.asarray(inputs["x_sample"])
    xp = np.asarray(inputs["x_prompt"])
    cfg = Cfg(Ts=xs.shape[1], NP=xp.shape[0] // n, Tp=xp.shape[1], L=2)
    key = (cfg.Ts, cfg.NP, cfg.Tp)
    if key not in _CACHE:
        _CACHE[key] = build(cfg)
    nc, _ = _CACHE[key]
    in_maps = [shard_inputs(inputs, cfg, c) for c in range(n)]
    res = run_bass_kernel_spmd(nc, in_maps, core_ids=list(range(n)))
    return gather_outputs(res.results, cfg, n)
```
